# Optimizing a Trainium2 kernel written in Bass

```python
import math
import jax, jax.numpy as jnp
from jax import lax
import numpy as np

D_MODEL = 1024
BATCH = 8
SEQ = 4096
DEPTH = 2

PLE_DIM = 256
N_BRANCHES = 4
BRANCH_WIDTH = D_MODEL // 4
HEAD_DIM = 64
N_HEADS = BRANCH_WIDTH // HEAD_DIM
S5_GROUP_CH = 16
S5_GROUPS = BRANCH_WIDTH // S5_GROUP_CH
S5_STATE = 64
RWKV_DECAY_LORA = 64
RWKV_AAA_LORA = 64
RWKV_GATE_LORA = 128
RWKV_PROJ = 3 * BRANCH_WIDTH + RWKV_DECAY_LORA + RWKV_AAA_LORA + RWKV_GATE_LORA
D_FF = 4 * D_MODEL
QBLOCK = 128
LN_EPS = 1e-5
GN_EPS = 64e-5
DEEPNORM_ALPHA = (2 * DEPTH) ** 0.25
DEEPNORM_BETA = (8 * DEPTH) ** -0.25
IN_SPLIT_SIZES = (BRANCH_WIDTH, BRANCH_WIDTH, BRANCH_WIDTH, BRANCH_WIDTH, N_HEADS, RWKV_PROJ,
                  BRANCH_WIDTH, BRANCH_WIDTH, BRANCH_WIDTH, N_BRANCHES * D_MODEL)
D_IN = sum(IN_SPLIT_SIZES)
RWKV_SPLIT_SIZES = (BRANCH_WIDTH, BRANCH_WIDTH, BRANCH_WIDTH, RWKV_DECAY_LORA, RWKV_AAA_LORA, RWKV_GATE_LORA)

kernel_name = 'hybrid_s5_fox_rwkv7_stickbreaking_deepnorm'


def _split(t, sizes):
    points = np.cumsum(sizes)[:-1].tolist()
    return jnp.split(t, points, axis=-1)


def _layer_norm(t, g, b):
    tf = t.astype(jnp.float32)
    mu = jnp.mean(tf, axis=-1, keepdims=True)
    var = jnp.mean(jnp.square(tf - mu), axis=-1, keepdims=True)
    return ((tf - mu) * lax.rsqrt(var + LN_EPS) * g + b).astype(t.dtype)


def _heads(t):
    return t.reshape(t.shape[:-1] + (N_HEADS, HEAD_DIM))


def _query_blocks(t):
    b, s = t.shape[:2]
    t = t.reshape((b, s // QBLOCK, QBLOCK) + t.shape[2:])
    return jnp.moveaxis(t, 1, 0)


def _merge_blocks(t):
    t = jnp.moveaxis(t, 0, 1)
    return t.reshape((t.shape[0], t.shape[1] * t.shape[2]) + t.shape[3:])


def _s5_mixer(u, lam_re, lam_im, log_dt, b_re, b_im, c_re, c_im, d, glu_w, glu_b):
    bsz, s = u.shape[:2]
    f32 = jnp.float32
    ug = u.astype(f32).reshape(bsz, s, S5_GROUPS, S5_GROUP_CH)
    dt = jnp.exp(log_dt.astype(f32))[:, None]
    lam_re = lam_re.astype(f32)
    lam_im = lam_im.astype(f32)
    mag = jnp.exp(lam_re * dt)
    ang = lam_im * dt
    lb_re = mag * jnp.cos(ang)
    lb_im = mag * jnp.sin(ang)
    den = jnp.square(lam_re) + jnp.square(lam_im)
    nr = lb_re - 1.0
    f_re = (nr * lam_re + lb_im * lam_im) / den
    f_im = (lb_im * lam_re - nr * lam_im) / den
    bb_re = f_re[..., None] * b_re - f_im[..., None] * b_im
    bb_im = f_re[..., None] * b_im + f_im[..., None] * b_re
    bu_re = jnp.einsum('bsgh,gph->bsgp', ug, bb_re)
    bu_im = jnp.einsum('bsgh,gph->bsgp', ug, bb_im)
    a_re = jnp.broadcast_to(lb_re, bu_re.shape)
    a_im = jnp.broadcast_to(lb_im, bu_im.shape)

    def combine(e1, e2):
        a1r, a1i, b1r, b1i = e1
        a2r, a2i, b2r, b2i = e2
        return (a2r * a1r - a2i * a1i,
                a2r * a1i + a2i * a1r,
                a2r * b1r - a2i * b1i + b2r,
                a2r * b1i + a2i * b1r + b2i)

    _, _, x_re, x_im = lax.associative_scan(combine, (a_re, a_im, bu_re, bu_im), axis=1)
    y = (jnp.einsum('bsgp,ghp->bsgh', x_re, c_re) - jnp.einsum('bsgp,ghp->bsgh', x_im, c_im)
         + d * ug)
    y = jax.nn.gelu(y.reshape(bsz, s, BRANCH_WIDTH))
    y = y * jax.nn.sigmoid(y @ glu_w + glu_b)
    return y.astype(u.dtype)


def _forgetting_attention(q, k, v, f_logit):
    s = q.shape[1]
    f32 = jnp.float32
    scale = HEAD_DIM ** -0.5
    c = jnp.cumsum(jax.nn.log_sigmoid(f_logit.astype(f32)), axis=1)
    kf = k.astype(f32)
    vf = v.astype(f32)
    c_key = jnp.moveaxis(c, 1, 2)
    kpos = jnp.arange(s)

    def block(args):
        i, qb, cb = args
        qpos = i * QBLOCK + jnp.arange(QBLOCK)
        logits = jnp.einsum('bqhd,bkhd->bhqk', qb.astype(f32), kf) * scale
        logits = logits + jnp.moveaxis(cb, 1, 2)[..., None] - c_key[:, :, None, :]
        logits = jnp.where(kpos[None, :] <= qpos[:, None], logits, -jnp.inf)
        probs = jax.nn.softmax(logits, axis=-1)
        return jnp.einsum('bhqk,bkhd->bqhd', probs, vf)

    out = lax.map(block, (jnp.arange(s // QBLOCK), _query_blocks(q), _query_blocks(c)))
    return _merge_blocks(out).astype(q.dtype)


def _stick_breaking_attention(q, k, v):
    s = q.shape[1]
    f32 = jnp.float32
    scale = HEAD_DIM ** -0.5
    kf = k.astype(f32)
    vf = v.astype(f32)
    kpos = jnp.arange(s)

    def block(args):
        i, qb = args
        qpos = i * QBLOCK + jnp.arange(QBLOCK)
        z = jnp.einsum('bqhd,bkhd->bhqk', qb.astype(f32), kf) * scale
        mask = kpos[None, :] < qpos[:, None]
        log_rest = jnp.where(mask, jax.nn.log_sigmoid(-z), 0.0)
        between = lax.cumsum(log_rest, axis=3, reverse=True) - log_rest
        weights = jnp.where(mask, jnp.exp(jax.nn.log_sigmoid(z) + between), 0.0)
        return jnp.einsum('bhqk,bkhd->bqhd', weights, vf)

    out = lax.map(block, (jnp.arange(s // QBLOCK), _query_blocks(q)))
    return _merge_blocks(out).astype(q.dtype)


def _rwkv7_time_mix(proj, mu, w0, w2, a0, a2, g2, k_k, k_a, r_k, lnx_g, lnx_b):
    bsz = proj.shape[0]
    f32 = jnp.float32
    prev = jnp.pad(proj, ((0, 0), (1, 0), (0, 0)))[:, :-1]
    xs = proj + (prev - proj) * mu
    r, k, v, w1, a1, g1 = _split(xs, RWKV_SPLIT_SIZES)
    w = -jax.nn.softplus(-(w0 + jnp.tanh(w1) @ w2)) - 0.5
    decay = jnp.exp(-jnp.exp(w.astype(f32)))
    a = jax.nn.sigmoid(a0 + a1 @ a2)
    g = jax.nn.sigmoid(g1) @ g2
    kk = _heads((k * k_k).astype(f32))
    kk = kk / jnp.maximum(jnp.sqrt(jnp.sum(jnp.square(kk), axis=-1, keepdims=True)), 1e-12)
    k = k * (1.0 + (a - 1.0) * k_a)
    rh, kh, vh, ah, dh = [_heads(t.astype(f32)) for t in (r, k, v, a, decay)]

    def step(state, inp):
        r_t, w_t, k_t, v_t, kk_t, a_t = inp
        sa = jnp.einsum('bhvk,bhk->bhv', state, -kk_t)
        state = (state * w_t[:, :, None, :] + sa[..., None] * (kk_t * a_t)[:, :, None, :]
                 + v_t[..., None] * k_t[:, :, None, :])
        return state, jnp.einsum('bhvk,bhk->bhv', state, r_t)

    tm = lambda t: jnp.moveaxis(t, 1, 0)
    init = jnp.zeros((bsz, N_HEADS, HEAD_DIM, HEAD_DIM), f32)
    _, y = lax.scan(step, init, (tm(rh), tm(dh), tm(kh), tm(vh), tm(kk), tm(ah)))
    y = jnp.moveaxis(y, 0, 1)
    ym = jnp.mean(y, axis=-1, keepdims=True)
    yv = jnp.mean(jnp.square(y - ym), axis=-1, keepdims=True)
    yn = (y - ym) * lax.rsqrt(yv + GN_EPS)
    yn = yn.reshape(y.shape[:2] + (BRANCH_WIDTH,)) * lnx_g + lnx_b
    bonus = (jnp.sum(rh * kh * r_k, axis=-1, keepdims=True) * vh).reshape(yn.shape)
    return ((yn + bonus) * g).astype(proj.dtype)


def setup_inputs(seed: int = 0) -> dict:
    key = jax.random.key(seed)
    keys = iter(jax.random.split(key, 40))
    f32 = jnp.float32
    nrm = lambda shape, scale: scale * jax.random.normal(next(keys), shape, f32)
    uni = lambda shape, lo, hi: jax.random.uniform(next(keys), shape, f32, minval=lo, maxval=hi)
    L, G, P, H16, BW = DEPTH, S5_GROUPS, S5_STATE, S5_GROUP_CH, BRANCH_WIDTH
    beta = DEEPNORM_BETA
    return {
        'x': nrm((BATCH, SEQ, D_MODEL), 1.0),
        'p': nrm((DEPTH, BATCH, SEQ, PLE_DIM), 1.0),
        'w_in': nrm((L, D_MODEL, D_IN), D_MODEL ** -0.5),
        's5_lambda_re': -0.5 + nrm((L, G, P), 0.01),
        's5_lambda_im': jnp.broadcast_to(math.pi * jnp.arange(P, dtype=f32), (L, G, P)),
        's5_log_dt': uni((L, G), math.log(1e-3), math.log(1e-1)),
        's5_b_re': nrm((L, G, P, H16), (2.0 * H16) ** -0.5),
        's5_b_im': nrm((L, G, P, H16), (2.0 * H16) ** -0.5),
        's5_c_re': nrm((L, G, H16, P), (2.0 * P) ** -0.5),
        's5_c_im': nrm((L, G, H16, P), (2.0 * P) ** -0.5),
        's5_d': nrm((L, G, H16), 1.0),
        's5_glu_w': nrm((L, BW, BW), BW ** -0.5),
        's5_glu_b': nrm((L, BW), 0.01),
        'fox_f_bias': 3.0 + nrm((L, N_HEADS), 0.5),
        'rwkv_mu': uni((L, RWKV_PROJ), 0.0, 1.0),
        'rwkv_w0': uni((L, BW), -6.0, 1.0),
        'rwkv_w2': nrm((L, RWKV_DECAY_LORA, BW), 0.3 * RWKV_DECAY_LORA ** -0.5),
        'rwkv_a0': nrm((L, BW), 0.1),
        'rwkv_a2': nrm((L, RWKV_AAA_LORA, BW), 0.3 * RWKV_AAA_LORA ** -0.5),
        'rwkv_g2': nrm((L, RWKV_GATE_LORA, BW), RWKV_GATE_LORA ** -0.5),
        'rwkv_k_k': 0.85 + nrm((L, BW), 0.05),
        'rwkv_k_a': 1.0 + nrm((L, BW), 0.05),
        'rwkv_r_k': nrm((L, N_HEADS, HEAD_DIM), 0.1),
        'rwkv_lnx_g': 1.0 + nrm((L, BW), 0.01),
        'rwkv_lnx_b': nrm((L, BW), 0.01),
        'w_branch': nrm((L, N_BRANCHES, BW, D_MODEL), beta * BW ** -0.5),
        'w_out': nrm((L, D_MODEL, D_MODEL), beta * D_MODEL ** -0.5),
        'ln1_g': 1.0 + nrm((L, D_MODEL), 0.01),
        'ln1_b': nrm((L, D_MODEL), 0.01),
        'mlp_w1': nrm((L, D_MODEL, D_FF), beta * D_MODEL ** -0.5),
        'mlp_w2': nrm((L, D_FF, D_MODEL), beta * D_FF ** -0.5),
        'ple_w': nrm((L, PLE_DIM, D_MODEL), beta * PLE_DIM ** -0.5),
        'ple_gate_w': nrm((L, D_MODEL, D_MODEL), D_MODEL ** -0.5),
        'ln2_g': 1.0 + nrm((L, D_MODEL), 0.01),
        'ln2_b': nrm((L, D_MODEL), 0.01),
    }


def reference(x, p, w_in, s5_lambda_re, s5_lambda_im, s5_log_dt, s5_b_re, s5_b_im, s5_c_re, s5_c_im,
              s5_d, s5_glu_w, s5_glu_b, fox_f_bias, rwkv_mu, rwkv_w0, rwkv_w2, rwkv_a0, rwkv_a2, rwkv_g2,
              rwkv_k_k, rwkv_k_a, rwkv_r_k, rwkv_lnx_g, rwkv_lnx_b, w_branch, w_out, ln1_g, ln1_b,
              mlp_w1, mlp_w2, ple_w, ple_gate_w, ln2_g, ln2_b):
    bsz, s, _ = x.shape
    h = x
    for i in range(DEPTH):
        proj = h @ w_in[i]
        (s5_u, fq, fk, fv, ff, rw, sq, sk, sv, gates) = _split(proj, IN_SPLIT_SIZES)
        y_s5 = _s5_mixer(s5_u, s5_lambda_re[i], s5_lambda_im[i], s5_log_dt[i], s5_b_re[i], s5_b_im[i],
                         s5_c_re[i], s5_c_im[i], s5_d[i], s5_glu_w[i], s5_glu_b[i])
        y_fox = _forgetting_attention(_heads(fq), _heads(fk), _heads(fv),
                                      ff + fox_f_bias[i]).reshape(bsz, s, BRANCH_WIDTH)
        y_rwkv = _rwkv7_time_mix(rw, rwkv_mu[i], rwkv_w0[i], rwkv_w2[i], rwkv_a0[i], rwkv_a2[i],
                                 rwkv_g2[i], rwkv_k_k[i], rwkv_k_a[i], rwkv_r_k[i],
                                 rwkv_lnx_g[i], rwkv_lnx_b[i])
        y_sb = _stick_breaking_attention(_heads(sq), _heads(sk), _heads(sv)).reshape(bsz, s, BRANCH_WIDTH)
        gates = gates.reshape(bsz, s, N_BRANCHES, D_MODEL)
        branches = (y_s5, y_fox, y_rwkv, y_sb)
        merged = sum(jax.nn.sigmoid(gates[:, :, n]) * (branches[n] @ w_branch[i, n])
                     for n in range(N_BRANCHES))
        h = _layer_norm(DEEPNORM_ALPHA * h + merged @ w_out[i], ln1_g[i], ln1_b[i])
        ffn = jnp.square(jax.nn.relu(h @ mlp_w1[i])) @ mlp_w2[i]
        ple = jax.nn.sigmoid(h @ ple_gate_w[i]) * (p[i] @ ple_w[i])
        h = _layer_norm(DEEPNORM_ALPHA * h + ffn + ple, ln2_g[i], ln2_b[i])
    return h
```

```python
import math
import contextlib
import numpy as np
import concourse.bass as bass
import concourse.mybir as mybir
from concourse.bass_utils import run_bass_kernel_spmd

F32 = mybir.dt.float32
BF16 = mybir.dt.bfloat16
AF = mybir.ActivationFunctionType
ALU = mybir.AluOpType
AX = mybir.AxisListType

T = 4096
D = 1024
NT = 8
NB = 32
DEPTH = 2
ALPHA = (2 * DEPTH) ** 0.25
LN_EPS = 1e-5
GN_EPS = 64e-5
PI = math.pi


class Dep:
    __slots__ = ("w", "r", "excl")

    def __init__(self):
        self.w = {}
        self.r = {}
        self.excl = False


class Buf:
    def __init__(self, t, excl=False):
        self.t = t
        self.dep = Dep()
        self.dep.excl = excl

    def __getitem__(self, k):
        return self.t[k]


class Ring:
    def __init__(self, bufs):
        self.bufs = bufs
        self.i = 0

    def next(self):
        b = self.bufs[self.i % len(self.bufs)]
        self.i += 1
        return b


class Sched:
    EPOCH = 30000
    NDMA = 6
    NEPOCH = 8
    qmap = {"pool": "sp"}

    def __init__(self, nc):
        self.nc = nc
        self.eng = {"pe": nc.tensor, "act": nc.scalar, "dve": nc.vector,
                    "pool": nc.gpsimd, "sp": nc.sync}
        self.sem = {}
        self.cnt = {}
        self.seen = {e: {} for e in self.eng}
        self.nsem = 0
        self.ninst = 0
        self.allsems = []
        self.epool = {e: [self._alloc("e_%s_%d" % (e, i)) for i in range(self.NEPOCH)] for e in ("pe", "act", "dve", "pool")}
        self.dq = {}
        for q in ("sp",):
            self.dq[q] = {"sems": [self._alloc("dq_%s_%d" % (q, i)) for i in range(self.NDMA)], "n": 0}
        for e in ("pe", "act", "dve", "pool"):
            self._new_epoch(e)

    def _alloc(self, name):
        self.nsem += 1
        h = self.nc.alloc_semaphore(name)
        self.allsems.append(h)
        return h

    def _new_epoch(self, e):
        self.sem[e] = self.epool[e].pop(0)
        self.cnt[e] = 0

    def _wait(self, eng, sem, val):
        key = id(sem)
        if self.seen[eng].get(key, 0) >= val:
            return
        self.eng[eng].wait_ge(sem, val)
        self.seen[eng][key] = val

    def _gather(self, eng, rd, wr, same_ok=True):
        need = {}

        def add(p):
            pe, sem, val = p
            if pe == eng and (same_ok or eng == "pe"):
                return
            k = id(sem)
            if k not in need or need[k][1] < val:
                need[k] = (sem, val)
        for d in rd:
            for p in d.w.values():
                add(p)
        for d in wr:
            for p in d.w.values():
                add(p)
            for p in d.r.values():
                add(p)
        for sem, val in need.values():
            self._wait(eng, sem, val)

    @staticmethod
    def _deps(lst):
        return [x.dep if isinstance(x, Buf) else x for x in lst]

    @staticmethod
    def _merge(dct, tok):
        k = id(tok[1])
        if k not in dct or dct[k][2] < tok[2]:
            dct[k] = tok

    def I(self, eng, fn, rd=(), wr=()):
        rd = self._deps(rd)
        wr = self._deps(wr)
        ex = [d for d in rd if d.excl and d not in wr]
        if ex:
            rd = [d for d in rd if not d.excl]
            wr = list(wr) + ex
        self._gather(eng, rd, wr, same_ok=False)
        if self.cnt[eng] >= self.EPOCH:
            self._new_epoch(eng)
        ins = fn(self.eng[eng])
        self.ninst += 1
        self.cnt[eng] += 1
        ins.then_inc(self.sem[eng], 1)
        tok = (eng, self.sem[eng], self.cnt[eng])
        for d in rd:
            self._merge(d.r, tok)
        for d in wr:
            d.w = {id(tok[1]): tok}
            d.r = {}
        return ins

    def dma(self, q, out, in_, rd=(), wr=()):
        rd = self._deps(rd)
        wr = self._deps(wr)
        q = self.qmap.get(q, q)
        dq = self.dq[q]
        j = dq["n"]
        dq["n"] += 1
        sem = dq["sems"][j % self.NDMA]
        val = 16 * (j // self.NDMA + 1)
        if val > 16:
            self._wait(q, sem, val - 16)
        self._gather(q, rd, wr, same_ok=False)
        ins = self.eng[q].dma_start(out=out, in_=in_)
        self.ninst += 1
        ins.then_inc(sem, 16)
        tok = ("dma", sem, val)
        for d in rd:
            self._merge(d.r, tok)
        for d in wr:
            self._merge(d.w, tok)
            d.r = {}
        return ins

    def barrier(self, force=False):
        for w in ("pe", "act", "dve", "pool", "sp"):
            for q, dq in self.dq.items():
                n = dq["n"]
                for i in range(self.NDMA):
                    cnt_i = len(range(i, n, self.NDMA))
                    if cnt_i > 0:
                        self._wait(w, dq["sems"][i], 16 * cnt_i)
            for e in ("pe", "act", "dve", "pool"):
                if e != w and self.cnt[e] > 0:
                    self._wait(w, self.sem[e], self.cnt[e])

    def finish(self):
        for q, dq in self.dq.items():
            n = dq["n"]
            for i in range(self.NDMA):
                cnt_i = len(range(i, n, self.NDMA))
                if cnt_i > 0:
                    self._wait("sp", dq["sems"][i], 16 * cnt_i)
        for e in ("pe", "act", "dve", "pool"):
            if self.cnt[e] > 0:
                self._wait("sp", self.sem[e], self.cnt[e])


def _chunkcol(v):
    v = np.asarray(v, np.float32)
    return np.ascontiguousarray(v.reshape(-1, 128).T)


def prep_layer(inp, l):
    f = np.float32
    w_in = np.asarray(inp["w_in"][l], f)
    o = {}
    o["winF"] = np.ascontiguousarray(np.concatenate(
        [w_in[:, 0:256], w_in[:, 256:512], w_in[:, 512:768], w_in[:, 1028:2052],
         w_in[:, 2052:2308], w_in[:, 2308:2564], w_in[:, 2820:6916]], axis=1))
    o["winT"] = np.ascontiguousarray(np.concatenate([w_in[:, 768:1024], w_in[:, 2564:2820]], axis=1))
    o["wff"] = np.ascontiguousarray(w_in[:, 1024:1028])
    lre = np.asarray(inp["s5_lambda_re"][l], f)
    lim = np.asarray(inp["s5_lambda_im"][l], f)
    ldt = np.repeat(np.asarray(inp["s5_log_dt"][l], f)[:, None], 64, axis=1)
    def gp(a):
        return a.reshape(8, 2, 64).transpose(1, 2, 0).reshape(128, 8)
    o["s5par"] = np.ascontiguousarray(np.stack([gp(lre), gp(lim), gp(ldt)], axis=2))
    b_re = np.asarray(inp["s5_b_re"][l], f)
    b_im = np.asarray(inp["s5_b_im"][l], f)
    c_re = np.asarray(inp["s5_c_re"][l], f)
    c_im = np.asarray(inp["s5_c_im"][l], f)
    Bw = np.zeros((8, 128, 2, 128), f)
    Cw = np.zeros((8, 128, 2, 128), f)
    for g in range(16):
        rt = g // 2
        k0 = (g % 8) * 16
        m0 = (g % 2) * 64
        Bw[rt, k0:k0 + 16, 0, m0:m0 + 64] = b_re[g].T
        Bw[rt, k0:k0 + 16, 1, m0:m0 + 64] = b_im[g].T
        Cw[rt, m0:m0 + 64, 0, k0:k0 + 16] = c_re[g].T
        Cw[rt, m0:m0 + 64, 1, k0:k0 + 16] = c_im[g].T
    o["s5B"] = np.ascontiguousarray(Bw.transpose(1, 0, 2, 3))
    o["s5C"] = np.ascontiguousarray(Cw.transpose(1, 0, 2, 3))
    o["s5D"] = _chunkcol(np.asarray(inp["s5_d"][l], f).reshape(-1))
    o["gluw"] = np.asarray(inp["s5_glu_w"][l], f)
    o["glub"] = _chunkcol(inp["s5_glu_b"][l])
    o["foxb"] = np.asarray(inp["fox_f_bias"][l], f).reshape(4, 1)
    o["mu"] = _chunkcol(inp["rwkv_mu"][l])
    vec = [inp["rwkv_w0"][l], inp["rwkv_a0"][l], inp["rwkv_k_k"][l], inp["rwkv_k_a"][l],
           np.asarray(inp["rwkv_r_k"][l]).reshape(-1), inp["rwkv_lnx_g"][l], inp["rwkv_lnx_b"][l]]
    o["rvec"] = np.ascontiguousarray(np.stack([_chunkcol(v) for v in vec], axis=2))
    o["wa2"] = np.ascontiguousarray(np.concatenate([np.asarray(inp["rwkv_w2"][l], f),
                                                     np.asarray(inp["rwkv_a2"][l], f)], axis=0))
    o["g2"] = np.asarray(inp["rwkv_g2"][l], f)
    o["wbr"] = np.ascontiguousarray(np.asarray(inp["w_branch"][l], f).reshape(1024, 1024))
    o["wout"] = np.asarray(inp["w_out"][l], f)
    o["ln1"] = np.ascontiguousarray(np.stack([_chunkcol(inp["ln1_g"][l]), _chunkcol(inp["ln1_b"][l])], axis=2))
    o["ln2"] = np.ascontiguousarray(np.stack([_chunkcol(inp["ln2_g"][l]), _chunkcol(inp["ln2_b"][l])], axis=2))
    o["w1"] = np.asarray(inp["mlp_w1"][l], f)
    o["w2m"] = np.asarray(inp["mlp_w2"][l], f)
    o["plew"] = np.asarray(inp["ple_w"][l], f)
    o["pgw"] = np.asarray(inp["ple_gate_w"][l], f)
    return {k: np.ascontiguousarray(v, dtype=f) for k, v in o.items()}


LAYER_SHAPES = {
    "winF": [1024, 6400], "winT": [1024, 512], "wff": [1024, 4], "s5par": [128, 8, 3],
    "s5B": [128, 8, 2, 128], "s5C": [128, 8, 2, 128], "s5D": [128, 2], "gluw": [256, 256], "glub": [128, 2],
    "foxb": [4, 1], "mu": [128, 8], "rvec": [128, 2, 7], "wa2": [128, 256], "g2": [128, 256],
    "wbr": [1024, 1024], "wout": [1024, 1024], "ln1": [128, 8, 2], "ln2": [128, 8, 2],
    "w1": [1024, 4096], "w2m": [4096, 1024], "plew": [256, 1024], "pgw": [1024, 1024],
}


class K:
    pass


@contextlib.contextmanager
def phase(k):
    with contextlib.ExitStack() as st:
        yield st
        k.S.barrier()


def build_program(depth=DEPTH, dbg=(), stop_after=None, only=None, nsteps=T):
    nc = bass.Bass("TRN2", target_bir_lowering=False)
    S = Sched(nc)
    k = K()
    k.nc, k.S = nc, S
    k.dbg = {}
    x_in = nc.dram_tensor("x", [T, D], F32, kind="ExternalInput").ap()
    p_in = nc.dram_tensor("p", [DEPTH, T, 256], F32, kind="ExternalInput").ap()
    out = nc.dram_tensor("out", [T, D], F32, kind="ExternalOutput").ap()
    W = []
    for l in range(DEPTH):
        W.append({n: nc.dram_tensor("%s_%d" % (n, l), s, F32, kind="ExternalInput").ap()
                  for n, s in LAYER_SHAPES.items()})
    k.stop_after = stop_after
    k.only = only
    k.nsteps = nsteps

    def scratch(name, shape, dt):
        kind = "ExternalOutput" if name in dbg else "Internal"
        import os
        if os.environ.get("KNOB_TINY") and name != "hres":
            shape = [2, 2]
        return Buf(nc.dram_tensor(name, shape, dt, kind=kind).ap())
    k.hres = scratch("hres", [128, 8, T], F32)
    k.h1res = scratch("h1res", [128, 8, T], F32)
    k.gsc = scratch("gsc", [32, 128, T], BF16)
    k.ycat = scratch("ycat", [4, 128, 2, T], BF16)
    k.rwxs = scratch("rwxs", [8, 128, T], F32)
    k.sc5 = scratch("sc5", [2, 128, 5, T], F32)
    k.bon = scratch("bon", [2, 128, T], F32)
    k.gg = scratch("gg", [2, 128, T], F32)
    k.vtok = scratch("vtok", [T, 2, 128], F32)
    k.uTf = scratch("uTf", [128, 2, T], F32)
    k.qk = scratch("qk", [4, 128, 2, T], BF16)
    k.vv = scratch("vv", [2, 128, NB, 256], BF16)
    k.negc = scratch("negc", [4, T], F32)
    k.hTs = scratch("hTs", [128, 8, T], BF16)
    k.pTs = scratch("pTs", [128, 2, T], BF16)
    k.w1s = scratch("w1s", [128, 8, 4096], BF16)
    k.w2s = scratch("w2s", [128, 32, 1024], BF16)

    es = contextlib.ExitStack()

    uid = [0]

    def sb(stack, name, shape, dt):
        uid[0] += 1
        return Buf(stack.enter_context(nc.sbuf_tensor("%s_u%d" % (name, uid[0]), shape, dt)))

    def ps(stack, name, shape, dt=F32):
        uid[0] += 1
        full = [128, 512] if dt == F32 else [128, 1024]
        t = stack.enter_context(nc.psum_tensor("%s_u%d" % (name, uid[0]), full, dt))
        n = 1
        for d_ in shape[1:]:
            n *= d_
        if len(shape) == 2:
            v = t[0:shape[0], 0:n]
        else:
            assert len(shape) == 3
            v = t[0:shape[0], 0:n].rearrange("p (a b) -> p a b", a=shape[1])
        return Buf(v, excl=True)
    k.sb, k.ps = sb, ps

    with es:
        k.identf = sb(es, "identf", [128, 128], F32)
        k.identb = sb(es, "identb", [128, 128], BF16)
        k.onesm = sb(es, "onesm", [128, 128], F32)
        k.blk64 = sb(es, "blk64", [128, 128], F32)
        k.nmi = sb(es, "nmi", [128, 128], F32)
        k.nms = sb(es, "nms", [128, 128], F32)
        k.m01 = sb(es, "m01", [128, 128], F32)
        k.ones = sb(es, "ones", [128, 512], F32)
        k.sel4 = sb(es, "sel4", [4, 4, 128], F32)
        k.epsln = sb(es, "epsln", [128, 1], F32)
        k.epsgn = sb(es, "epsgn", [128, 1], F32)
        k.one1 = sb(es, "one1", [128, 1], F32)
        k.mhalf = sb(es, "mhalf", [128, 1], F32)
        I = S.I
        I("pool", lambda e: e.memset(k.epsln[:], LN_EPS), wr=[k.epsln])
        I("pool", lambda e: e.memset(k.epsgn[:], GN_EPS), wr=[k.epsgn])
        I("pool", lambda e: e.memset(k.one1[:], 1.0), wr=[k.one1])
        I("pool", lambda e: e.memset(k.mhalf[:], -0.5), wr=[k.mhalf])
        I("pool", lambda e: e.memset(k.identf[:], 0.0), wr=[k.identf])
        I("pool", lambda e: e.affine_select(out=k.identf[:], in_=k.identf[:], compare_op=ALU.not_equal, fill=1.0,
                                            base=0, pattern=[[-1, 128]], channel_multiplier=1),
          rd=[k.identf], wr=[k.identf])
        I("pool", lambda e: e.tensor_copy(k.identb[:], k.identf[:]), rd=[k.identf], wr=[k.identb])
        I("pool", lambda e: e.memset(k.onesm[:], 1.0 / 1024.0), wr=[k.onesm])
        I("pool", lambda e: e.memset(k.ones[:], 1.0), wr=[k.ones])
        I("pool", lambda e: e.memset(k.blk64[:], 0.0), wr=[k.blk64])
        I("pool", lambda e: e.memset(k.blk64[0:64, 0:64], 1.0), wr=[k.blk64])
        I("pool", lambda e: e.memset(k.blk64[64:128, 64:128], 1.0), wr=[k.blk64])
        I("pool", lambda e: e.memset(k.nmi[:], 0.0), wr=[k.nmi])
        I("pool", lambda e: e.affine_select(out=k.nmi[:], in_=k.nmi[:], compare_op=ALU.is_ge, fill=-1e30,
                                            base=0, pattern=[[-1, 128]], channel_multiplier=1), rd=[k.nmi], wr=[k.nmi])
        I("pool", lambda e: e.memset(k.nms[:], 0.0), wr=[k.nms])
        I("pool", lambda e: e.affine_select(out=k.nms[:], in_=k.nms[:], compare_op=ALU.is_gt, fill=-1e30,
                                            base=0, pattern=[[-1, 128]], channel_multiplier=1), rd=[k.nms], wr=[k.nms])
        I("pool", lambda e: e.affine_select(out=k.m01[:], in_=k.ones[:, 0:128], compare_op=ALU.is_gt, fill=0.0,
                                            base=0, pattern=[[-1, 128]], channel_multiplier=1), rd=[k.ones], wr=[k.m01])
        I("pool", lambda e: e.affine_select(out=k.sel4[:], in_=k.ones[0:4, :].rearrange("p (h m) -> p h m", h=4),
                                            compare_op=ALU.is_equal, fill=0.0, base=0, pattern=[[-1, 4], [0, 128]],
                                            channel_multiplier=1), rd=[k.ones], wr=[k.sel4])

        if stop_after != "consts":
            for l in range(depth):
                layer(k, l, W[l], x_in, p_in[l], out, last=(l == depth - 1))
        S.finish()
    k.ninst = S.ninst
    return nc, k


def dbg_dump(k, name, src_buf, src_ap):
    if name in k.dbg:
        k.S.dma("sp", k.dbg[name], src_ap, rd=[src_buf], wr=[Dep()])


def mm(k, out_ap, lhsT, rhs, start, stop, rd, wr):
    return k.S.I("pe", lambda e: e.matmul(out_ap, lhsT, rhs, start=start, stop=stop), rd=rd, wr=wr)


def tr(k, out_ap, in_ap, ident_ap, rd, wr):
    return k.S.I("pe", lambda e: e.transpose(out_ap, in_ap, ident_ap), rd=rd, wr=wr)


def phase0(k, x_in, hT):
    import os
    lvl = int(os.environ.get("KNOB_P0", "9"))
    nblk = int(os.environ.get("KNOB_P0N", str(NB)))
    S, I = k.S, k.S.I
    with phase(k) as st:
        xin = Ring([k.sb(st, "p0x%d" % i, [128, D], F32) for i in range(2)])
        stg = Ring([k.sb(st, "p0s%d" % i, [128, 8, 128], F32) for i in range(2)])
        pss = Ring([k.ps(st, "p0p%d" % i, [128, 4, 128], F32) for i in range(4)])
        for blk in range(nblk):
            xi = xin.next()
            S.dma("sp", xi[:], x_in[blk * 128:(blk + 1) * 128, :], wr=[xi])
            sg = stg.next()
            if lvl < 1:
                continue
            for half in range(2):
                pt = pss.next()
                for j in range(4):
                    c = half * 4 + j
                    tr(k, pt[:, j, :], xi[:, c * 128:(c + 1) * 128], k.identf[:], rd=[xi, k.identf], wr=[pt])
                if lvl >= 2:
                    I("act", lambda e: e.activation(out=hT[:, half * 4:half * 4 + 4, blk * 128:(blk + 1) * 128],
                                                    in_=pt[:], func=AF.Copy), rd=[pt], wr=[hT])
                if lvl >= 3:
                    I("dve", lambda e: e.tensor_copy(sg[:, half * 4:half * 4 + 4, :], pt[:]), rd=[pt], wr=[sg])
            if lvl >= 4:
                S.dma("pool", k.hres[:, :, blk * 128:(blk + 1) * 128], sg[:], rd=[sg], wr=[k.hres])


def cast_weight(k, st, src, dst_buf, dst_ap, K_, N_, tag, to_dram=False):
    S, I = k.S, k.S.I
    kc = K_ // 128
    ncol = max(1, min(N_, 4096 // kc))
    f32r = Ring([k.sb(st, "cw%s_f%d" % (tag, i), [128, kc, ncol], F32) for i in range(2)])
    if to_dram:
        bfr = Ring([k.sb(st, "cw%s_b%d" % (tag, i), [128, kc, ncol], BF16) for i in range(2)])
    srcv = src.rearrange("(c p) n -> p c n", p=128)
    engs = ["pool", "dve"]
    i = 0
    for n0 in range(0, N_, ncol):
        n1 = min(N_, n0 + ncol)
        w = n1 - n0
        fb = f32r.next()
        S.dma("sp", fb[:, :, 0:w], srcv[:, :, n0:n1], wr=[fb])
        if to_dram:
            bb = bfr.next()
            I(engs[i % 2], lambda e: e.tensor_copy(bb[:, :, 0:w], fb[:, :, 0:w]), rd=[fb], wr=[bb])
            S.dma("pool", dst_ap[:, :, n0:n1], bb[:, :, 0:w], rd=[bb], wr=[dst_buf])
        else:
            I(engs[i % 2], lambda e: e.tensor_copy(dst_ap[:, :, n0:n1], fb[:, :, 0:w]), rd=[fb], wr=[dst_buf])
        i += 1


def layer_norm_tile(k, st_bufs, z, gb, eps, outs):
    S, I = k.S, k.S.I
    mps, msb, sq, vps = st_bufs
    for c in range(8):
        mm(k, mps[:], k.onesm[:], z[:, c, :], c == 0, c == 7, rd=[k.onesm, z], wr=[mps])
    I("act", lambda e: e.activation(out=msb[:], in_=mps[:], func=AF.Copy), rd=[mps], wr=[msb])
    I("dve", lambda e: e.tensor_tensor(out=z[:], in0=z[:], in1=msb[:, None, :].broadcast_to([128, 8, 512]),
                                       op=ALU.subtract), rd=[z, msb], wr=[z])
    I("act", lambda e: e.activation(out=sq[:], in_=z[:], func=AF.Square), rd=[z], wr=[sq])
    for c in range(8):
        mm(k, vps[:], k.onesm[:], sq[:, c, :], c == 0, c == 7, rd=[k.onesm, sq], wr=[vps])
    I("act", lambda e: e.activation(out=msb[:], in_=vps[:], func=AF.Ln, bias=k.epsln[:, 0:1], scale=1.0),
      rd=[vps, k.epsln], wr=[msb])
    I("act", lambda e: e.activation(out=msb[:], in_=msb[:], func=AF.Exp, scale=-0.5), rd=[msb], wr=[msb])
    I("dve", lambda e: e.tensor_tensor(out=z[:], in0=z[:], in1=msb[:, None, :].broadcast_to([128, 8, 512]),
                                       op=ALU.mult), rd=[z, msb], wr=[z])
    for c in range(8):
        outs(c, z[:, c, :], gb[:, c, 0:1], gb[:, c, 1:2])


def inproj(k, l, W, hT):
    S, I = k.S, k.S.I
    with phase(k) as st:
        wf = Ring([k.sb(st, "ipwf%d" % i, [128, 8, 128], F32) for i in range(2)])
        wb = Ring([k.sb(st, "ipwb%d" % i, [128, 8, 128], BF16) for i in range(2)])
        pss = Ring([k.ps(st, "ipps%d" % i, [128, 512], F32) for i in range(4)])
        stb = Ring([k.sb(st, "ipsb%d" % i, [128, 512], BF16) for i in range(3)])
        stf = Ring([k.sb(st, "ipsf%d" % i, [128, 512], F32) for i in range(2)])
        raw = Ring([k.sb(st, "ipraw%d" % i, [128, T + 1], F32) for i in range(2)])
        xs = Ring([k.sb(st, "ipxs%d" % i, [128, T], F32) for i in range(1)])
        mu = k.sb(st, "ipmu", [128, 8], F32)
        S.dma("sp", mu[:], W["mu"], wr=[mu])
        for r_ in raw.bufs:
            I("pool", lambda e: e.memset(r_[:, 0:1], 0.0), wr=[r_])
        winF = W["winF"].rearrange("(c p) n -> p c n", p=128)
        ev = 0
        for j in range(50):
            if k.stop_after == "ip_fm%d" % j:
                return
            f_ = wf.next()
            S.dma("sp", f_[:], winF[:, :, j * 128:(j + 1) * 128], wr=[f_])
            b_ = wb.next()
            I("pool", lambda e: e.tensor_copy(b_[:], f_[:]), rd=[f_], wr=[b_])
            if 6 <= j < 14:
                rw = raw.next()
            for tt in range(NT):
                pt = pss.next()
                for c in range(8):
                    mm(k, pt[:], b_[:, c, :], hT[:, c, tt * 512:(tt + 1) * 512], c == 0, c == 7,
                       rd=[b_, hT], wr=[pt])
                eng = "act" if ev % 2 == 0 else "dve"
                ev += 1
                tsl = slice(tt * 512, (tt + 1) * 512)
                if j < 2:
                    sf = stf.next()
                    if eng == "act":
                        I("act", lambda e: e.activation(out=sf[:], in_=pt[:], func=AF.Copy), rd=[pt], wr=[sf])
                    else:
                        I("dve", lambda e: e.tensor_copy(sf[:], pt[:]), rd=[pt], wr=[sf])
                    S.dma("pool", k.uTf[:, j, tsl], sf[:], rd=[sf], wr=[k.uTf])
                elif j < 6 or 14 <= j < 18:
                    which = (j - 2) // 2 if j < 6 else 2 + (j - 14) // 2
                    cc = j % 2
                    scale = 0.125 if which in (0, 2) else 1.0
                    sb_ = stb.next()
                    I("act", lambda e: e.activation(out=sb_[:], in_=pt[:], func=AF.Copy, scale=scale), rd=[pt], wr=[sb_])
                    S.dma("pool", k.qk[which, :, cc, tsl], sb_[:], rd=[sb_], wr=[k.qk])
                elif j < 14:
                    if eng == "act":
                        I("act", lambda e: e.activation(out=rw[:, 1 + tt * 512:1 + (tt + 1) * 512], in_=pt[:], func=AF.Copy),
                          rd=[pt], wr=[rw])
                    else:
                        I("dve", lambda e: e.tensor_copy(rw[:, 1 + tt * 512:1 + (tt + 1) * 512], pt[:]), rd=[pt], wr=[rw])
                else:
                    gc = j - 18
                    sb_ = stb.next()
                    I("act", lambda e: e.activation(out=sb_[:], in_=pt[:], func=AF.Sigmoid), rd=[pt], wr=[sb_])
                    S.dma("pool", k.gsc[gc, :, tsl], sb_[:], rd=[sb_], wr=[k.gsc])
            if 6 <= j < 14:
                jj = j - 6
                x_ = xs.next()
                I("pool", lambda e: e.tensor_tensor(out=x_[:], in0=rw[:, 0:T], in1=rw[:, 1:T + 1], op=ALU.subtract),
                  rd=[rw], wr=[x_])
                I("dve", lambda e: e.scalar_tensor_tensor(out=x_[:], in0=x_[:], scalar=mu[:, jj:jj + 1], in1=rw[:, 1:T + 1],
                                                          op0=ALU.mult, op1=ALU.add), rd=[x_, rw, mu], wr=[x_])
                S.dma("sp", k.rwxs[jj], x_[:], rd=[x_], wr=[k.rwxs])
        if k.stop_after == "ip_fm":
            return
        wtf = k.sb(st, "ipwtf", [128, 8, 512], F32)
        wtb = k.sb(st, "ipwtb", [128, 8, 512], BF16)
        S.dma("sp", wtf[:], W["winT"].rearrange("(c p) n -> p c n", p=128), wr=[wtf])
        I("pool", lambda e: e.tensor_copy(wtb[:], wtf[:]), rd=[wtf], wr=[wtb])
        for blk in range(NB):
            pt = pss.next()
            for c in range(8):
                mm(k, pt[:], hT[:, c, blk * 128:(blk + 1) * 128], wtb[:, c, :], c == 0, c == 7, rd=[wtb, hT], wr=[pt])
            sb_ = stb.next()
            if blk % 2 == 0:
                I("act", lambda e: e.activation(out=sb_[:], in_=pt[:], func=AF.Copy), rd=[pt], wr=[sb_])
            else:
                I("dve", lambda e: e.tensor_copy(sb_[:], pt[:]), rd=[pt], wr=[sb_])
            S.dma("pool", k.vv[:, :, blk, :].rearrange("w p n -> p w n"), sb_[:].rearrange("p (w n) -> p w n", w=2),
                  rd=[sb_], wr=[k.vv])
        if k.stop_after == "ip_tm":
            return
        wff_f = k.sb(st, "ipwff", [128, 8, 4], F32)
        wff_b = k.sb(st, "ipwffb", [128, 8, 4], BF16)
        fb = k.sb(st, "ipfb", [4, 1], F32)
        fl = k.sb(st, "ipfl", [4, T], F32)
        S.dma("sp", wff_f[:], W["wff"].rearrange("(c p) n -> p c n", p=128), wr=[wff_f])
        S.dma("sp", fb[:], W["foxb"], wr=[fb])
        I("pool", lambda e: e.tensor_copy(wff_b[:], wff_f[:]), rd=[wff_f], wr=[wff_b])
        I("dve", lambda e: e.tensor_scalar(out=fb[:], in0=fb[:], scalar1=-1.0, scalar2=None, op0=ALU.mult), rd=[fb], wr=[fb])
        for tt in range(NT):
            pt = pss.next()
            for c in range(8):
                mm(k, pt[0:4, :], wff_b[:, c, :], hT[:, c, tt * 512:(tt + 1) * 512], c == 0, c == 7, rd=[wff_b, hT], wr=[pt])
            I("act", lambda e: e.activation(out=fl[:, tt * 512:(tt + 1) * 512], in_=pt[0:4, :], func=AF.Exp,
                                            bias=fb[:, 0:1], scale=-1.0), rd=[pt, fb], wr=[fl])
        I("act", lambda e: e.activation(out=fl[:], in_=fl[:], func=AF.Ln, bias=k.one1[0:4, 0:1], scale=1.0),
          rd=[fl, k.one1], wr=[fl])
        for tt in range(NT):
            init = 0.0 if tt == 0 else fl[:, tt * 512 - 1:tt * 512]
            I("dve", lambda e: e.tensor_tensor_scan(out=fl[:, tt * 512:(tt + 1) * 512], data0=k.ones[0:4, :],
                                                    data1=fl[:, tt * 512:(tt + 1) * 512], initial=init,
                                                    op0=ALU.mult, op1=ALU.add), rd=[fl, k.ones], wr=[fl])
        S.dma("sp", k.negc[:], fl[:], rd=[fl], wr=[k.negc])


def layer(k, l, W, x_in, p_l, out, last):
    S, I = k.S, k.S.I
    with phase(k) as st:
        hT = k.sb(st, "hT%d" % l, [128, 8, T], BF16)
        if l == 0:
            phase0(k, x_in, hT)
            if k.stop_after == "phase0":
                return
        else:
            S.dma("sp", hT[:], k.hTs[:], rd=[k.hTs], wr=[hT])
        inproj(k, l, W, hT)
    if k.stop_after is not None and k.stop_after.startswith("ip") or k.stop_after == "inproj":
        return
    if k.only in (None, "s5"):
        s5_phase(k, l, W)
    if k.stop_after == "s5":
        return
    if k.only in (None, "fox"):
        attn_phase(k, l, 0)
    if k.only in (None, "sb"):
        attn_phase(k, l, 1)
    if k.stop_after == "attn":
        return
    if k.only in (None, "rwkv"):
        rwkv_prep(k, l, W)
        rwkv_rec(k, l, W, nsteps=k.nsteps)
    if k.stop_after == "rwkv":
        return
    precast_mlp(k, l, W)
    with phase(k) as st:
        h1T = k.sb(st, "h1T%d" % l, [128, 8, T], BF16)
        merge_phase(k, l, W, h1T)
        if k.stop_after == "merge":
            return
        mlp_phase(k, l, W, h1T, p_l, out, last)


def range_reduce(k, out, in_, shift, qi, qf, rd, wr):
    I = k.S.I
    TWO_PI = 2.0 * PI
    I("dve", lambda e: e.tensor_scalar(out=out, in0=in_, scalar1=float(shift), scalar2=None, op0=ALU.add), rd=rd, wr=wr)
    I("dve", lambda e: e.tensor_scalar(out=qi, in0=out, scalar1=1.0 / TWO_PI, scalar2=0.5, op0=ALU.mult, op1=ALU.add), rd=rd, wr=wr)
    I("dve", lambda e: e.tensor_copy(qf, qi), rd=rd, wr=wr)
    I("dve", lambda e: e.scalar_tensor_tensor(out=out, in0=qf, scalar=-TWO_PI, in1=out, op0=ALU.mult, op1=ALU.add), rd=rd, wr=wr)
    I("dve", lambda e: e.tensor_scalar(out=qf, in0=out, scalar1=-PI, scalar2=None, op0=ALU.is_lt), rd=rd, wr=wr)
    I("dve", lambda e: e.scalar_tensor_tensor(out=out, in0=qf, scalar=TWO_PI, in1=out, op0=ALU.mult, op1=ALU.add), rd=rd, wr=wr)
    I("dve", lambda e: e.tensor_scalar(out=out, in0=out, scalar1=-3.1415925, scalar2=3.1415925, op0=ALU.max, op1=ALU.min), rd=rd, wr=wr)


def s5_phase(k, l, W):
    S, I = k.S, k.S.I
    TWO_PI = 2.0 * PI
    with phase(k) as st:
        sb, ps = k.sb, k.ps
        par = sb(st, "s5par", [128, 8, 3], F32)
        S.dma("sp", par[:], W["s5par"], wr=[par])
        Bb = sb(st, "s5Bb", [128, 8, 2, 128], BF16)
        Cb = sb(st, "s5Cb", [128, 8, 2, 128], BF16)
        gwb = sb(st, "s5gwb", [128, 2, 256], BF16)
        Dc = sb(st, "s5D", [128, 2], F32)
        S.dma("sp", Dc[:], W["s5D"], wr=[Dc])
        with phase(k) as st2:
            Bf = sb(st2, "s5Bf", [128, 8, 2, 128], F32)
            Cf = sb(st2, "s5Cf", [128, 8, 2, 128], F32)
            gwf = sb(st2, "s5gwf", [128, 2, 256], F32)
            S.dma("sp", Bf[:], W["s5B"], wr=[Bf])
            S.dma("sp", Cf[:], W["s5C"], wr=[Cf])
            S.dma("sp", gwf[:], W["gluw"].rearrange("(c p) n -> p c n", p=128), wr=[gwf])
            I("pool", lambda e: e.tensor_copy(Bb[:], Bf[:]), rd=[Bf], wr=[Bb])
            I("pool", lambda e: e.tensor_copy(Cb[:, :, 0, :], Cf[:, :, 0, :]), rd=[Cf], wr=[Cb])
            I("pool", lambda e: e.tensor_scalar(out=Cb[:, :, 1, :], in0=Cf[:, :, 1, :], scalar1=-1.0, scalar2=None, op0=ALU.mult),
              rd=[Cf], wr=[Cb])
            I("pool", lambda e: e.tensor_copy(gwb[:], gwf[:]), rd=[gwf], wr=[gwb])
        gb = sb(st, "s5gb", [128, 2], F32)
        S.dma("sp", gb[:], W["glub"], wr=[gb])
        sm = sb(st, "s5sm", [128, 16, 8], F32)
        def sl(i):
            return sm[:, i, :]
        lre, lim, ldt = par[:, :, 0], par[:, :, 1], par[:, :, 2]
        DT, A_, TH, NA, EM, SN, CS, LBR, LBI, DEN, FRE, FIM, NFRE, T1, T2, THR = range(16)
        dsm = Dep()
        def dv(fn):
            I("dve", fn, rd=[par, dsm], wr=[dsm])
        def ac(fn):
            I("act", fn, rd=[par, dsm], wr=[dsm])
        def pl(fn):
            I("pool", fn, rd=[par, dsm], wr=[dsm])
        ac(lambda e: e.activation(out=sl(DT), in_=ldt, func=AF.Exp))
        dv(lambda e: e.tensor_tensor(out=sl(A_), in0=lre, in1=sl(DT), op=ALU.mult))
        dv(lambda e: e.tensor_tensor(out=sl(TH), in0=lim, in1=sl(DT), op=ALU.mult))
        dv(lambda e: e.tensor_scalar(out=sl(NA), in0=sl(A_), scalar1=-1.0, scalar2=None, op0=ALU.mult))
        smi = sb(st, "s5smi", [128, 8], mybir.dt.int32)
        range_reduce(k, sl(T1), sl(TH), 0.0, smi[:], sl(T2), [par, dsm, smi], [dsm, smi])
        ac(lambda e: e.activation(out=sl(SN), in_=sl(T1), func=AF.Sin))
        range_reduce(k, sl(T1), sl(TH), 0.5 * PI, smi[:], sl(T2), [par, dsm, smi], [dsm, smi])
        ac(lambda e: e.activation(out=sl(CS), in_=sl(T1), func=AF.Sin))
        ac(lambda e: e.activation(out=sl(EM), in_=sl(A_), func=AF.Exp))
        dv(lambda e: e.tensor_tensor(out=sl(LBR), in0=sl(EM), in1=sl(CS), op=ALU.mult))
        dv(lambda e: e.tensor_tensor(out=sl(LBI), in0=sl(EM), in1=sl(SN), op=ALU.mult))
        dv(lambda e: e.tensor_tensor(out=sl(DEN), in0=lre, in1=lre, op=ALU.mult))
        dv(lambda e: e.tensor_tensor(out=sl(T1), in0=lim, in1=lim, op=ALU.mult))
        dv(lambda e: e.tensor_tensor(out=sl(DEN), in0=sl(DEN), in1=sl(T1), op=ALU.add))
        dv(lambda e: e.reciprocal(out=sl(DEN), in_=sl(DEN)))
        dv(lambda e: e.tensor_scalar(out=sl(T2), in0=sl(LBR), scalar1=-1.0, scalar2=None, op0=ALU.add))
        dv(lambda e: e.tensor_tensor(out=sl(FRE), in0=sl(T2), in1=lre, op=ALU.mult))
        dv(lambda e: e.tensor_tensor(out=sl(T1), in0=sl(LBI), in1=lim, op=ALU.mult))
        dv(lambda e: e.tensor_tensor(out=sl(FRE), in0=sl(FRE), in1=sl(T1), op=ALU.add))
        dv(lambda e: e.tensor_tensor(out=sl(FRE), in0=sl(FRE), in1=sl(DEN), op=ALU.mult))
        dv(lambda e: e.tensor_tensor(out=sl(FIM), in0=sl(LBI), in1=lre, op=ALU.mult))
        dv(lambda e: e.tensor_tensor(out=sl(T1), in0=sl(T2), in1=lim, op=ALU.mult))
        dv(lambda e: e.tensor_tensor(out=sl(FIM), in0=sl(FIM), in1=sl(T1), op=ALU.subtract))
        dv(lambda e: e.tensor_tensor(out=sl(FIM), in0=sl(FIM), in1=sl(DEN), op=ALU.mult))
        dv(lambda e: e.tensor_scalar(out=sl(NFRE), in0=sl(FRE), scalar1=-1.0, scalar2=None, op0=ALU.mult))
        range_reduce(k, sl(THR), sl(TH), 0.0, smi[:], sl(T1), [par, dsm, smi], [dsm, smi])
        idx = sb(st, "s5idx", [128, 129], F32)
        I("dve", lambda e: e.memset(idx[:], 1.0), wr=[idx])
        I("dve", lambda e: e.tensor_tensor_scan(out=idx[:], data0=idx[:], data1=idx[:], initial=-1.0,
                                                op0=ALU.mult, op1=ALU.add), rd=[idx], wr=[idx])
        tabr = sb(st, "s5tabr", [128, 8, 129], F32)
        tabi = sb(st, "s5tabi", [128, 8, 129], F32)
        tifr = sb(st, "s5tifr", [128, 8, 128], F32)
        tifi = sb(st, "s5tifi", [128, 8, 128], F32)
        ntl = sb(st, "s5ntl", [128, 8], F32)
        w1 = sb(st, "s5w1", [128, 129], F32)
        w2 = sb(st, "s5w2", [128, 129], F32)
        w3 = sb(st, "s5w3", [128, 129], F32)
        w4 = sb(st, "s5w4", [128, 129], F32)
        w5 = sb(st, "s5w5", [128, 129], F32)
        wi = sb(st, "s5wi", [128, 129], mybir.dt.int32)
        tabs = [tabr, tabi, tifr, tifi]
        wk = [w1, w2, w3, w4, w5, idx]
        for rt in range(8):
            def col(i):
                return sm[:, i, rt:rt + 1]
            def dvw(fn):
                I("dve", fn, rd=wk + [dsm], wr=wk[:5] + tabs)
            def acw(fn):
                I("act", fn, rd=wk + [dsm], wr=wk[:5] + tabs)
            def plw(fn):
                I("pool", fn, rd=wk + [dsm], wr=wk[:5] + tabs)
            dvw(lambda e: e.tensor_scalar(out=w1[:], in0=idx[:], scalar1=col(THR), scalar2=None, op0=ALU.mult))
            range_reduce(k, w2[:], w1[:], 0.0, wi[:], w5[:], wk + [dsm, wi], wk[:5] + tabs + [wi])
            acw(lambda e: e.activation(out=w2[:], in_=w2[:], func=AF.Sin))
            range_reduce(k, w3[:], w1[:], 0.5 * PI, wi[:], w5[:], wk + [dsm, wi], wk[:5] + tabs + [wi])
            acw(lambda e: e.activation(out=w3[:], in_=w3[:], func=AF.Sin))
            acw(lambda e: e.activation(out=w4[:], in_=idx[:], func=AF.Exp, scale=col(A_)))
            acw(lambda e: e.activation(out=w5[:], in_=idx[:], func=AF.Exp, scale=col(NA)))
            dvw(lambda e: e.tensor_tensor(out=tabr[:, rt, :], in0=w4[:], in1=w3[:], op=ALU.mult))
            dvw(lambda e: e.tensor_tensor(out=tabi[:, rt, :], in0=w4[:], in1=w2[:], op=ALU.mult))
            dvw(lambda e: e.tensor_scalar(out=w1[:], in0=w3[:], scalar1=col(FRE), scalar2=None, op0=ALU.mult))
            dvw(lambda e: e.scalar_tensor_tensor(out=w1[:], in0=w2[:], scalar=col(FIM), in1=w1[:], op0=ALU.mult, op1=ALU.add))
            dvw(lambda e: e.tensor_tensor(out=tifr[:, rt, :], in0=w1[:, 0:128], in1=w5[:, 0:128], op=ALU.mult))
            dvw(lambda e: e.tensor_scalar(out=w1[:], in0=w3[:], scalar1=col(FIM), scalar2=None, op0=ALU.mult))
            dvw(lambda e: e.scalar_tensor_tensor(out=w1[:], in0=w2[:], scalar=col(NFRE), in1=w1[:], op0=ALU.mult, op1=ALU.add))
            dvw(lambda e: e.tensor_tensor(out=tifi[:, rt, :], in0=w1[:, 0:128], in1=w5[:, 0:128], op=ALU.mult))
        I("dve", lambda e: e.tensor_scalar(out=ntl[:], in0=tabi[:, :, 128], scalar1=-1.0, scalar2=None, op0=ALU.mult),
          rd=tabs, wr=[ntl])
        uf = sb(st, "s5uf", [128, 2, T], F32)
        ub = sb(st, "s5ub", [128, 2, T], BF16)
        S.dma("sp", uf[:], k.uTf[:], rd=[k.uTf], wr=[uf])
        I("pool", lambda e: e.tensor_copy(ub[:], uf[:]), rd=[uf], wr=[ub])
        ygf = sb(st, "s5ygf", [128, 2, T], F32)
        ygb = sb(st, "s5ygb", [128, 2, T], BF16)
        q0 = sb(st, "s5q0", [128, 8, 2], F32)
        I("dve", lambda e: e.memset(q0[:], 0.0), wr=[q0])
        rawp = Ring([ps(st, "s5rp%d" % i, [128, 512], F32) for i in range(4)])
        yp = Ring([ps(st, "s5yp%d" % i, [128, 512], F32) for i in range(2)])
        rr = Ring([sb(st, "s5rr%d" % i, [128, 2, 512], F32) for i in range(2)])
        zz = Ring([sb(st, "s5zz%d" % i, [128, 2, 512], F32) for i in range(2)])
        mmr = Ring([sb(st, "s5mm%d" % i, [128, 4, 512], F32) for i in range(2)])
        qq = Ring([sb(st, "s5qq%d" % i, [128, 2, 512], F32) for i in range(2)])
        xb = Ring([sb(st, "s5xb%d" % i, [128, 2, 512], BF16) for i in range(3)])
        ew = Ring([sb(st, "s5ew%d" % i, [128, 3, 512], F32) for i in range(2)])
        tq = sb(st, "s5tq", [128, 2], F32)

        def b4(tab, rt):
            return tab[:, rt:rt + 1, 0:128].broadcast_to([128, 4, 128])

        def v4(ap):
            return ap.rearrange("p (a b) -> p a b", a=4)
        for oc in range(2):
            for tt in range(NT):
                tsl = slice(tt * 512, (tt + 1) * 512)
                ypt = yp.next()
                for r4 in range(4):
                    rt = oc * 4 + r4
                    pr, pi_ = rawp.next(), rawp.next()
                    mm(k, pr[:], Bb[:, rt, 0, :], ub[:, oc, tsl], True, True, rd=[Bb, ub], wr=[pr])
                    mm(k, pi_[:], Bb[:, rt, 1, :], ub[:, oc, tsl], True, True, rd=[Bb, ub], wr=[pi_])
                    r_ = rr.next()
                    I("act", lambda e: e.activation(out=r_[:, 0, :], in_=pr[:], func=AF.Copy), rd=[pr], wr=[r_])
                    I("act", lambda e: e.activation(out=r_[:, 1, :], in_=pi_[:], func=AF.Copy), rd=[pi_], wr=[r_])
                    z_, m_ = zz.next(), mmr.next()
                    I("dve", lambda e: e.tensor_tensor(out=v4(m_[:, 0, :]), in0=v4(r_[:, 0, :]), in1=b4(tifr, rt), op=ALU.mult),
                      rd=[r_] + tabs, wr=[m_])
                    I("dve", lambda e: e.tensor_tensor(out=v4(m_[:, 1, :]), in0=v4(r_[:, 1, :]), in1=b4(tifi, rt), op=ALU.mult),
                      rd=[r_] + tabs, wr=[m_])
                    I("pool", lambda e: e.tensor_tensor(out=v4(m_[:, 2, :]), in0=v4(r_[:, 0, :]), in1=b4(tifi, rt), op=ALU.mult),
                      rd=[r_] + tabs, wr=[m_])
                    I("pool", lambda e: e.tensor_tensor(out=v4(m_[:, 3, :]), in0=v4(r_[:, 1, :]), in1=b4(tifr, rt), op=ALU.mult),
                      rd=[r_] + tabs, wr=[m_])
                    I("dve", lambda e: e.tensor_tensor(out=z_[:, 0, :], in0=m_[:, 0, :], in1=m_[:, 1, :], op=ALU.subtract),
                      rd=[m_], wr=[z_])
                    I("pool", lambda e: e.tensor_tensor(out=z_[:, 1, :], in0=m_[:, 2, :], in1=m_[:, 3, :], op=ALU.add),
                      rd=[m_], wr=[z_])
                    q_ = qq.next()
                    for sc in range(4):
                        csl = slice(sc * 128, (sc + 1) * 128)
                        for ri in range(2):
                            I("dve", lambda e: e.tensor_tensor_scan(out=q_[:, ri, csl], data0=k.ones[:, 0:128], data1=z_[:, ri, csl],
                                                                    initial=q0[:, rt, ri:ri + 1], op0=ALU.mult, op1=ALU.add),
                              rd=[z_, q0, k.ones], wr=[q_])
                        qe_r = q_[:, 0, sc * 128 + 127:sc * 128 + 128]
                        qe_i = q_[:, 1, sc * 128 + 127:sc * 128 + 128]
                        lr_ = tabr[:, rt, 128:129]
                        li_ = tabi[:, rt, 128:129]
                        I("dve", lambda e: e.tensor_scalar(out=tq[:, 0:1], in0=qe_r, scalar1=lr_, scalar2=None, op0=ALU.mult),
                          rd=[q_] + tabs, wr=[tq])
                        I("dve", lambda e: e.tensor_scalar(out=tq[:, 1:2], in0=qe_i, scalar1=lr_, scalar2=None, op0=ALU.mult),
                          rd=[q_] + tabs, wr=[tq])
                        I("dve", lambda e: e.scalar_tensor_tensor(out=q0[:, rt, 0:1], in0=qe_i, scalar=ntl[:, rt:rt + 1], in1=tq[:, 0:1],
                                                                  op0=ALU.mult, op1=ALU.add), rd=[q_, ntl, tq], wr=[q0])
                        I("dve", lambda e: e.scalar_tensor_tensor(out=q0[:, rt, 1:2], in0=qe_r, scalar=li_, in1=tq[:, 1:2],
                                                                  op0=ALU.mult, op1=ALU.add), rd=[q_, tq] + tabs, wr=[q0])
                    m2 = mmr.next()
                    x_ = xb.next()
                    I("dve", lambda e: e.tensor_tensor(out=v4(m2[:, 0, :]), in0=v4(q_[:, 0, :]), in1=b4(tabr, rt), op=ALU.mult),
                      rd=[q_] + tabs, wr=[m2])
                    I("dve", lambda e: e.tensor_tensor(out=v4(m2[:, 1, :]), in0=v4(q_[:, 1, :]), in1=b4(tabi, rt), op=ALU.mult),
                      rd=[q_] + tabs, wr=[m2])
                    I("pool", lambda e: e.tensor_tensor(out=v4(m2[:, 2, :]), in0=v4(q_[:, 0, :]), in1=b4(tabi, rt), op=ALU.mult),
                      rd=[q_] + tabs, wr=[m2])
                    I("pool", lambda e: e.tensor_tensor(out=v4(m2[:, 3, :]), in0=v4(q_[:, 1, :]), in1=b4(tabr, rt), op=ALU.mult),
                      rd=[q_] + tabs, wr=[m2])
                    I("dve", lambda e: e.tensor_tensor(out=x_[:, 0, :], in0=m2[:, 0, :], in1=m2[:, 1, :], op=ALU.subtract),
                      rd=[m2], wr=[x_])
                    I("pool", lambda e: e.tensor_tensor(out=x_[:, 1, :], in0=m2[:, 2, :], in1=m2[:, 3, :], op=ALU.add),
                      rd=[m2], wr=[x_])
                    mm(k, ypt[:], Cb[:, rt, 0, :], x_[:, 0, :], r4 == 0, False, rd=[Cb, x_], wr=[ypt])
                    mm(k, ypt[:], Cb[:, rt, 1, :], x_[:, 1, :], False, r4 == 3, rd=[Cb, x_], wr=[ypt])
                e_ = ew.next()
                I("dve", lambda e: e.scalar_tensor_tensor(out=e_[:, 0, :], in0=uf[:, oc, tsl], scalar=Dc[:, oc:oc + 1], in1=ypt[:],
                                                          op0=ALU.mult, op1=ALU.add), rd=[uf, Dc, ypt], wr=[e_])
                I("pool", lambda e: e.tensor_tensor(out=e_[:, 1, :], in0=e_[:, 0, :], in1=e_[:, 0, :], op=ALU.mult), rd=[e_], wr=[e_])
                I("pool", lambda e: e.tensor_scalar(out=e_[:, 1, :], in0=e_[:, 1, :], scalar1=0.044715, scalar2=1.0,
                                                    op0=ALU.mult, op1=ALU.add), rd=[e_], wr=[e_])
                I("pool", lambda e: e.tensor_tensor(out=e_[:, 1, :], in0=e_[:, 1, :], in1=e_[:, 0, :], op=ALU.mult), rd=[e_], wr=[e_])
                I("act", lambda e: e.activation(out=e_[:, 2, :], in_=e_[:, 1, :], func=AF.Sigmoid, scale=1.5957691216057308),
                  rd=[e_], wr=[e_])
                I("dve", lambda e: e.tensor_tensor(out=ygf[:, oc, tsl], in0=e_[:, 0, :], in1=e_[:, 2, :], op=ALU.mult),
                  rd=[e_], wr=[ygf])
                I("pool", lambda e: e.tensor_copy(ygb[:, oc, tsl], ygf[:, oc, tsl]), rd=[ygf], wr=[ygb])
        ob = Ring([sb(st, "s5ob%d" % i, [128, 512], BF16) for i in range(2)])
        for oc in range(2):
            for tt in range(NT):
                tsl = slice(tt * 512, (tt + 1) * 512)
                pt = yp.next()
                for kc in range(2):
                    mm(k, pt[:], gwb[:, kc, oc * 128:(oc + 1) * 128], ygb[:, kc, tsl], kc == 0, kc == 1, rd=[gwb, ygb], wr=[pt])
                e_ = ew.next()
                I("act", lambda e: e.activation(out=e_[:, 0, :], in_=pt[:], func=AF.Sigmoid, bias=gb[:, oc:oc + 1], scale=1.0),
                  rd=[pt, gb], wr=[e_])
                o_ = ob.next()
                I("dve", lambda e: e.tensor_tensor(out=o_[:], in0=e_[:, 0, :], in1=ygf[:, oc, tsl], op=ALU.mult),
                  rd=[e_, ygf], wr=[o_])
                S.dma("sp", k.ycat[0, :, oc, tsl], o_[:], rd=[o_], wr=[k.ycat])


def attn_phase(k, l, kind):
    S, I = k.S, k.S.I
    fox = (kind == 0)
    VW = 65 if fox else 64
    with phase(k) as st:
        sb, ps = k.sb, k.ps
        qT = sb(st, "atq", [128, 2, T], BF16)
        kT = sb(st, "atk", [128, 2, T], BF16)
        S.dma("sp", qT[:], k.qk[2 * kind], rd=[k.qk], wr=[qT])
        S.dma("sp", kT[:], k.qk[2 * kind + 1], rd=[k.qk], wr=[kT])
        V = sb(st, "atv", [128, NB, 4, VW], BF16)
        if fox:
            I("pool", lambda e: e.memset(V[:, :, :, 64:65], 1.0), wr=[V])
        S.dma("sp", V[:, :, :, 0:64], k.vv[kind].rearrange("p b (h d) -> p b h d", h=4), rd=[k.vv], wr=[V])
        yT = sb(st, "atyT", [128, 2, T], BF16)
        ytok = sb(st, "atytok", [128, NB, 256], BF16)
        A = Ring([sb(st, "atA%d" % i, [128, T], F32) for i in range(2)])
        P = Ring([sb(st, "atP%d" % i, [128, T], BF16) for i in range(2)])
        PT = Ring([sb(st, "atPT%d" % i, [128, 4, 128], BF16) for i in range(3)])
        zp = Ring([ps(st, "atzp%d" % i, [128, 512], F32) for i in range(2)])
        tp = Ring([ps(st, "attp%d" % i, [128, 4, 128], BF16) for i in range(2)])
        op = Ring([ps(st, "atop%d" % i, [128, VW], F32) for i in range(2)])
        xp = ps(st, "atxp", [128, 512], F32)
        nb = Ring([sb(st, "atnb%d" % i, [128, 2], F32) for i in range(3)])
        if fox:
            negc = sb(st, "atnegc", [4, T], F32)
            S.dma("sp", negc[:], k.negc[:], rd=[k.negc], wr=[negc])
            crow = sb(st, "atcrow", [128, T], F32)
        else:
            spb = Ring([sb(st, "atsp%d" % i, [128, 512], F32) for i in range(2)])
            Fb = Ring([sb(st, "atF%d" % i, [128, 512], F32) for i in range(2)])
            t1b = Ring([sb(st, "att1%d" % i, [128, 512], F32) for i in range(2)])
        items = [(h, qb) for h in range(4) for qb in range(NB)]
        state = {}

        def stage1(h, qb):
            hc, po = h // 2, (h % 2) * 64
            if fox and qb == 0:
                for tt in range(NT):
                    mm(k, xp[:], k.sel4[:, h, :], negc[:, tt * 512:(tt + 1) * 512], True, True, rd=[k.sel4, negc], wr=[xp])
                    I("act", lambda e: e.activation(out=crow[:, tt * 512:(tt + 1) * 512], in_=xp[:], func=AF.Copy),
                      rd=[xp], wr=[crow])
            nk = qb + 1
            a_ = A.next()
            nb_ = nb.next()
            carry = None
            for kt in range((nk + 3) // 4):
                w = min(4, nk - 4 * kt) * 128
                ks = slice(kt * 512, kt * 512 + w)
                last = (kt == (nk + 3) // 4 - 1)
                z = zp.next()
                mm(k, z[:, 0:w], qT[po:po + 64, hc, qb * 128:(qb + 1) * 128], kT[po:po + 64, hc, ks], True, True,
                   rd=[qT, kT], wr=[z])
                if fox:
                    I("dve", lambda e: e.tensor_tensor(out=a_[:, ks], in0=z[:, 0:w], in1=crow[:, ks], op=ALU.add),
                      rd=[z, crow], wr=[a_])
                else:
                    sp_ = spb.next()
                    I("act", lambda e: e.activation(out=sp_[:, 0:w], in_=z[:, 0:w], func=AF.Exp), rd=[z], wr=[sp_])
                    I("act", lambda e: e.activation(out=sp_[:, 0:w], in_=sp_[:, 0:w], func=AF.Ln, bias=k.one1[:, 0:1], scale=1.0),
                      rd=[sp_, k.one1], wr=[sp_])
                    if last:
                        I("pool", lambda e: e.tensor_tensor(out=sp_[:, w - 128:w], in0=sp_[:, w - 128:w], in1=k.m01[:], op=ALU.mult),
                          rd=[sp_, k.m01], wr=[sp_])
                    f_ = Fb.next()
                    init = 0.0 if carry is None else carry[0][:, carry[1] - 1:carry[1]]
                    rdl = [sp_, k.ones] + ([carry[0]] if carry is not None else [])
                    I("dve", lambda e: e.tensor_tensor_scan(out=f_[:, 0:w], data0=k.ones[:, 0:w], data1=sp_[:, 0:w], initial=init,
                                                            op0=ALU.mult, op1=ALU.add), rd=rdl, wr=[f_])
                    carry = (f_, w)
                    t_ = t1b.next()
                    I("dve", lambda e: e.tensor_tensor(out=t_[:, 0:w], in0=z[:, 0:w], in1=sp_[:, 0:w], op=ALU.subtract),
                      rd=[z, sp_], wr=[t_])
                    I("pool", lambda e: e.tensor_tensor(out=a_[:, ks], in0=t_[:, 0:w], in1=f_[:, 0:w], op=ALU.add),
                      rd=[t_, f_], wr=[a_])
            dsl = slice(qb * 128, (qb + 1) * 128)
            I("pool", lambda e: e.tensor_tensor(out=a_[:, dsl], in0=a_[:, dsl], in1=(k.nmi if fox else k.nms)[:], op=ALU.add),
              rd=[a_, k.nmi, k.nms], wr=[a_])
            if fox:
                I("dve", lambda e: e.reduce_max(out=nb_[:, 0:1], in_=a_[:, 0:nk * 128], axis=AX.X), rd=[a_], wr=[nb_])
                I("dve", lambda e: e.tensor_scalar(out=nb_[:, 1:2], in0=nb_[:, 0:1], scalar1=-1.0, scalar2=None, op0=ALU.mult),
                  rd=[nb_], wr=[nb_])
            else:
                I("dve", lambda e: e.tensor_scalar(out=nb_[:, 1:2], in0=carry[0][:, carry[1] - 1:carry[1]], scalar1=-1.0, scalar2=None,
                                                   op0=ALU.mult), rd=[carry[0]], wr=[nb_])
            state[(h, qb)] = (a_, nb_)

        def stage2(h, qb):
            a_, nb_ = state.pop((h, qb))
            nk = qb + 1
            p_ = P.next()
            I("act", lambda e: e.activation(out=p_[:, 0:nk * 128], in_=a_[:, 0:nk * 128], func=AF.Exp, bias=nb_[:, 1:2], scale=1.0),
              rd=[a_, nb_], wr=[p_])
            o_ = op.next()
            for kt in range((nk + 3) // 4):
                nbk = min(4, nk - 4 * kt)
                t_ = tp.next()
                for j in range(nbk):
                    kb = kt * 4 + j
                    tr(k, t_[:, j, :], p_[:, kb * 128:(kb + 1) * 128], k.identb[:], rd=[p_, k.identb], wr=[t_])
                pt_ = PT.next()
                if kt % 2 == 0:
                    I("act", lambda e: e.activation(out=pt_[:, 0:nbk, :], in_=t_[:, 0:nbk, :], func=AF.Copy), rd=[t_], wr=[pt_])
                else:
                    I("dve", lambda e: e.tensor_copy(pt_[:, 0:nbk, :], t_[:, 0:nbk, :]), rd=[t_], wr=[pt_])
                for j in range(nbk):
                    kb = kt * 4 + j
                    mm(k, o_[:], pt_[:, j, :], V[:, kb, h, :], kb == 0, kb == nk - 1, rd=[pt_, V], wr=[o_])
            if fox:
                I("dve", lambda e: e.reciprocal(out=nb_[:, 0:1], in_=o_[:, 64:65]), rd=[o_], wr=[nb_])
                I("act", lambda e: e.activation(out=ytok[:, qb, h * 64:(h + 1) * 64], in_=o_[:, 0:64], func=AF.Copy,
                                                scale=nb_[:, 0:1]), rd=[o_, nb_], wr=[ytok])
            else:
                I("act", lambda e: e.activation(out=ytok[:, qb, h * 64:(h + 1) * 64], in_=o_[:, 0:64], func=AF.Copy),
                  rd=[o_], wr=[ytok])

        for i, it in enumerate(items):
            stage1(*it)
            if i > 0:
                stage2(*items[i - 1])
        stage2(*items[-1])
        for qb in range(NB):
            t_ = tp.next()
            for c in range(2):
                tr(k, t_[:, c, :], ytok[:, qb, c * 128:(c + 1) * 128], k.identb[:], rd=[ytok, k.identb], wr=[t_])
            I("act", lambda e: e.activation(out=yT[:, :, qb * 128:(qb + 1) * 128], in_=t_[:, 0:2, :], func=AF.Copy),
              rd=[t_], wr=[yT])
        S.dma("sp", k.ycat[1 if fox else 3], yT[:], rd=[yT], wr=[k.ycat])


def rwkv_prep(k, l, W):
    S, I = k.S, k.S.I
    with phase(k) as st:
        sb, ps = k.sb, k.ps
        rvec = sb(st, "rpvec", [128, 2, 7], F32)
        S.dma("sp", rvec[:], W["rvec"], wr=[rvec])
        nw0 = sb(st, "rpnw0", [128, 2], F32)
        I("dve", lambda e: e.tensor_scalar(out=nw0[:], in0=rvec[:, :, 0], scalar1=-1.0, scalar2=None, op0=ALU.mult),
          rd=[rvec], wr=[nw0])
        wa2b = sb(st, "rpwa2b", [128, 256], BF16)
        g2b = sb(st, "rpg2b", [128, 256], BF16)
        tw = sb(st, "rptw", [128, T], BF16)
        gs = sb(st, "rpgs", [128, T], BF16)
        with phase(k) as st2:
            wa2f = sb(st2, "rpwa2f", [128, 256], F32)
            g2f = sb(st2, "rpg2f", [128, 256], F32)
            x6 = sb(st2, "rpx6", [128, T], F32)
            x7 = sb(st2, "rpx7", [128, T], F32)
            S.dma("sp", wa2f[:], W["wa2"], wr=[wa2f])
            S.dma("sp", g2f[:], W["g2"], wr=[g2f])
            S.dma("sp", x6[:], k.rwxs[6], rd=[k.rwxs], wr=[x6])
            S.dma("sp", x7[:], k.rwxs[7], rd=[k.rwxs], wr=[x7])
            I("pool", lambda e: e.tensor_copy(wa2b[:], wa2f[:]), rd=[wa2f], wr=[wa2b])
            I("pool", lambda e: e.tensor_copy(g2b[:], g2f[:]), rd=[g2f], wr=[g2b])
            I("act", lambda e: e.activation(out=tw[0:64, :], in_=x6[0:64, :], func=AF.Tanh), rd=[x6], wr=[tw])
            I("pool", lambda e: e.tensor_copy(tw[64:128, :], x6[64:128, :]), rd=[x6], wr=[tw])
            I("act", lambda e: e.activation(out=gs[:], in_=x7[:], func=AF.Sigmoid), rd=[x7], wr=[gs])
        for oc in range(2):
            S.dma("sp", k.sc5[oc, :, 4, :], k.rwxs[oc], rd=[k.rwxs], wr=[k.sc5])
        ld = Ring([sb(st, "rpld%d" % i, [128, 3, 512], F32) for i in range(2)])
        o5 = Ring([sb(st, "rpo5%d" % i, [128, 4, 512], F32) for i in range(2)])
        wk = Ring([sb(st, "rpwk%d" % i, [128, 6, 512], F32) for i in range(2)])
        so = Ring([sb(st, "rpso%d" % i, [128, 3, 512], F32) for i in range(2)])
        pp = Ring([ps(st, "rppp%d" % i, [128, 512], F32) for i in range(6)])
        for oc in range(2):
            osl = slice(oc * 128, (oc + 1) * 128)
            for tt in range(NT):
                tsl = slice(tt * 512, (tt + 1) * 512)
                x_ = ld.next()
                for i, j in enumerate((oc, 2 + oc, 4 + oc)):
                    S.dma("sp", x_[:, i, :], k.rwxs[j, :, tsl], rd=[k.rwxs], wr=[x_])
                r_t, k_t, v_t = x_[:, 0, :], x_[:, 1, :], x_[:, 2, :]
                o_ = o5.next()
                w_ = wk.next()
                s_ = so.next()
                pw = pp.next()
                mm(k, pw[:], wa2b[0:64, osl], tw[0:64, tsl], True, True, rd=[wa2b, tw], wr=[pw])
                I("act", lambda e: e.activation(out=w_[:, 0, :], in_=pw[:], func=AF.Exp, bias=nw0[:, oc:oc + 1], scale=-1.0),
                  rd=[pw, nw0], wr=[w_])
                I("act", lambda e: e.activation(out=w_[:, 0, :], in_=w_[:, 0, :], func=AF.Ln, bias=k.one1[:, 0:1], scale=1.0),
                  rd=[w_, k.one1], wr=[w_])
                I("act", lambda e: e.activation(out=w_[:, 0, :], in_=w_[:, 0, :], func=AF.Exp, bias=k.mhalf[:, 0:1], scale=-1.0),
                  rd=[w_, k.mhalf], wr=[w_])
                I("act", lambda e: e.activation(out=o_[:, 0, :], in_=w_[:, 0, :], func=AF.Exp, scale=-1.0), rd=[w_], wr=[o_])
                pa = pp.next()
                mm(k, pa[:], wa2b[64:128, osl], tw[64:128, tsl], True, True, rd=[wa2b, tw], wr=[pa])
                I("act", lambda e: e.activation(out=w_[:, 1, :], in_=pa[:], func=AF.Sigmoid, bias=rvec[:, oc, 1:2], scale=1.0),
                  rd=[pa, rvec], wr=[w_])
                a_t = w_[:, 1, :]
                I("dve", lambda e: e.tensor_scalar(out=w_[:, 2, :], in0=k_t, scalar1=rvec[:, oc, 2:3], scalar2=None, op0=ALU.mult),
                  rd=[x_, rvec], wr=[w_])
                I("pool", lambda e: e.tensor_tensor(out=w_[:, 3, :], in0=w_[:, 2, :], in1=w_[:, 2, :], op=ALU.mult), rd=[w_], wr=[w_])
                pn = pp.next()
                mm(k, pn[:], k.blk64[:], w_[:, 3, :], True, True, rd=[k.blk64, w_], wr=[pn])
                I("act", lambda e: e.activation(out=w_[:, 3, :], in_=pn[:], func=AF.Sqrt), rd=[pn], wr=[w_])
                I("dve", lambda e: e.tensor_scalar(out=w_[:, 3, :], in0=w_[:, 3, :], scalar1=1e-12, scalar2=None, op0=ALU.max),
                  rd=[w_], wr=[w_])
                I("dve", lambda e: e.reciprocal(out=w_[:, 3, :], in_=w_[:, 3, :]), rd=[w_], wr=[w_])
                I("dve", lambda e: e.tensor_tensor(out=w_[:, 2, :], in0=w_[:, 2, :], in1=w_[:, 3, :], op=ALU.mult), rd=[w_], wr=[w_])
                I("pool", lambda e: e.tensor_scalar(out=o_[:, 3, :], in0=w_[:, 2, :], scalar1=-1.0, scalar2=None, op0=ALU.mult),
                  rd=[w_], wr=[o_])
                I("pool", lambda e: e.tensor_tensor(out=o_[:, 1, :], in0=w_[:, 2, :], in1=a_t, op=ALU.mult), rd=[w_], wr=[o_])
                I("dve", lambda e: e.tensor_scalar(out=w_[:, 4, :], in0=a_t, scalar1=1.0, scalar2=rvec[:, oc, 3:4],
                                                   op0=ALU.subtract, op1=ALU.mult), rd=[w_, rvec], wr=[w_])
                I("dve", lambda e: e.scalar_tensor_tensor(out=o_[:, 2, :], in0=w_[:, 4, :], scalar=1.0, in1=k_t,
                                                          op0=ALU.add, op1=ALU.mult), rd=[w_, x_], wr=[o_])
                S.dma("sp", k.sc5[oc, :, 0:4, tsl], o_[:], rd=[o_], wr=[k.sc5])
                I("pool", lambda e: e.tensor_tensor(out=w_[:, 5, :], in0=r_t, in1=o_[:, 2, :], op=ALU.mult), rd=[x_, o_], wr=[w_])
                I("pool", lambda e: e.tensor_scalar(out=w_[:, 5, :], in0=w_[:, 5, :], scalar1=rvec[:, oc, 4:5], scalar2=None,
                                                    op0=ALU.mult), rd=[w_, rvec], wr=[w_])
                pb = pp.next()
                mm(k, pb[:], k.blk64[:], w_[:, 5, :], True, True, rd=[k.blk64, w_], wr=[pb])
                I("dve", lambda e: e.tensor_tensor(out=s_[:, 0, :], in0=pb[:], in1=v_t, op=ALU.mult), rd=[pb, x_], wr=[s_])
                S.dma("sp", k.bon[oc, :, tsl], s_[:, 0, :], rd=[s_], wr=[k.bon])
                pg = pp.next()
                mm(k, pg[:], g2b[:, osl], gs[:, tsl], True, True, rd=[g2b, gs], wr=[pg])
                I("act", lambda e: e.activation(out=s_[:, 1, :], in_=pg[:], func=AF.Copy), rd=[pg], wr=[s_])
                S.dma("sp", k.gg[oc, :, tsl], s_[:, 1, :], rd=[s_], wr=[k.gg])
                pv = pp.next()
                for b in range(4):
                    tr(k, pv[:, b * 128:(b + 1) * 128], x_[:, 2, b * 128:(b + 1) * 128], k.identf[:], rd=[x_, k.identf], wr=[pv])
                I("act", lambda e: e.activation(out=s_[:, 2, :], in_=pv[:], func=AF.Copy), rd=[pv], wr=[s_])
                S.dma("sp", k.vtok[tsl, oc, :].rearrange("(b p) c -> p b c", p=128),
                      s_[:, 2, :].rearrange("p (b c) -> p b c", b=4), rd=[s_], wr=[k.vtok])


def rwkv_rec(k, l, W, nsteps=T):
    S, I = k.S, k.S.I
    VC = 16
    with phase(k) as st:
        sb, ps = k.sb, k.ps
        rvec = sb(st, "rrvec", [128, 2, 7], F32)
        S.dma("sp", rvec[:], W["rvec"], wr=[rvec])
        Sx = [Ring([sb(st, "rrS%d_%d" % (hp, i), [128, 128], F32) for i in range(3)]) for hp in range(2)]
        S1 = [Ring([sb(st, "rrS1%d_%d" % (hp, i), [128, 128], F32) for i in range(2)]) for hp in range(2)]
        KK = [Ring([sb(st, "rrKK%d_%d" % (hp, i), [128, 128], F32) for i in range(3)]) for hp in range(2)]
        S2 = [Ring([sb(st, "rrS2%d_%d" % (hp, i), [128, 128], F32) for i in range(2)]) for hp in range(2)]
        sap = [Ring([ps(st, "rrsa%d_%d" % (hp, i), [128, 128], F32) for i in range(2)]) for hp in range(2)]
        yp = [ps(st, "rryp%d" % hp, [128, 512], F32) for hp in range(2)]
        c5 = [Ring([sb(st, "rrc5%d_%d" % (hp, i), [128, 5, 512], F32) for i in range(2)]) for hp in range(2)]
        vr = [Ring([sb(st, "rrvr%d_%d" % (hp, i), [128, VC, 128], F32) for i in range(2)]) for hp in range(2)]
        ysb = [sb(st, "rry%d" % hp, [128, T], F32) for hp in range(2)]
        for hp in range(2):
            for b in vr[hp].bufs:
                I("pool", lambda e: e.memset(b[:], 0.0), wr=[b])
            b0 = Sx[hp].bufs[0]
            I("dve", lambda e: e.memset(b0[:], 0.0), wr=[b0])
            if nsteps < T:
                I("pool", lambda e: e.memset(ysb[hp][:], 0.0), wr=[ysb[hp]])
        cur = [Sx[0].next(), Sx[1].next()]
        c5c = [None, None]
        vrc = [None, None]
        for t in range(nsteps):
            tl = t % 512
            for hp in range(2):
                if tl == 0:
                    c_ = c5[hp].next()
                    S.dma("sp", c_[:], k.sc5[hp, :, :, t:t + 512], rd=[k.sc5], wr=[c_])
                    c5c[hp] = c_
                if t % VC == 0:
                    v_ = vr[hp].next()
                    for hh in range(2):
                        S.dma("sp", v_[hh * 64:(hh + 1) * 64, :, hh * 64:(hh + 1) * 64],
                              k.vtok[t:t + VC, hp, hh * 64:(hh + 1) * 64].partition_broadcast(64), rd=[k.vtok], wr=[v_])
                    vrc[hp] = v_
                c_, v_ = c5c[hp], vrc[hp]
                old = cur[hp]
                new = Sx[hp].next()
                kk_ = KK[hp].next()
                I("act", lambda e: e.activation(out=kk_[:], in_=k.blk64[:], func=AF.Copy, scale=c_[:, 3, tl:tl + 1]),
                  rd=[k.blk64, c_], wr=[kk_])
                sa = sap[hp].next()
                mm(k, sa[:], kk_[:], old[:], True, True, rd=[kk_, old], wr=[sa])
                s1 = S1[hp].next()
                s2 = S2[hp].next()
                I("act", lambda e: e.activation(out=s1[:], in_=old[:], func=AF.Copy, scale=c_[:, 0, tl:tl + 1]),
                  rd=[old, c_], wr=[s1])
                I("pool", lambda e: e.tensor_scalar(out=s2[:], in0=v_[:, t % VC, :], scalar1=c_[:, 2, tl:tl + 1], scalar2=None, op0=ALU.mult),
                  rd=[v_, c_], wr=[s2])
                I("pool", lambda e: e.tensor_tensor(out=s1[:], in0=s1[:], in1=s2[:], op=ALU.add), rd=[s1, s2], wr=[s1])
                I("dve", lambda e: e.scalar_tensor_tensor(out=new[:], in0=sa[:], scalar=c_[:, 1, tl:tl + 1], in1=s1[:],
                                                          op0=ALU.mult, op1=ALU.add), rd=[sa, c_, s1], wr=[new])
                mm(k, yp[hp][:, tl:tl + 1], new[:], c_[:, 4, tl:tl + 1], True, True, rd=[new, c_], wr=[yp[hp]])
                cur[hp] = new
                if tl == 511 or t == nsteps - 1:
                    t0 = t - tl
                    ypb = yp[hp]
                    I("act", lambda e: e.activation(out=ysb[hp][:, t0:t0 + tl + 1], in_=ypb[:, 0:tl + 1], func=AF.Copy),
                      rd=[ypb], wr=[ysb[hp]])
        ld = Ring([sb(st, "rrld%d" % i, [128, 2, 512], F32) for i in range(2)])
        wk = Ring([sb(st, "rrwk%d" % i, [128, 3, 512], F32) for i in range(2)])
        ob = Ring([sb(st, "rrob%d" % i, [128, 512], BF16) for i in range(2)])
        for hp in range(2):
            for tt in range(NT):
                tsl = slice(tt * 512, (tt + 1) * 512)
                x_ = ld.next()
                S.dma("sp", x_[:, 0, :], k.bon[hp, :, tsl], rd=[k.bon], wr=[x_])
                S.dma("sp", x_[:, 1, :], k.gg[hp, :, tsl], rd=[k.gg], wr=[x_])
                w_ = wk.next()
                pm = sap[0].next()
                pv_ = sap[1].next()
                pmt = yp[0]
                mm(k, pmt[:], k.blk64[:], ysb[hp][:, tsl], True, True, rd=[k.blk64, ysb[hp]], wr=[pmt])
                I("dve", lambda e: e.scalar_tensor_tensor(out=w_[:, 0, :], in0=pmt[:], scalar=-1.0 / 64.0, in1=ysb[hp][:, tsl],
                                                          op0=ALU.mult, op1=ALU.add), rd=[pmt, ysb[hp]], wr=[w_])
                I("act", lambda e: e.activation(out=w_[:, 1, :], in_=w_[:, 0, :], func=AF.Square), rd=[w_], wr=[w_])
                pvt = yp[1]
                mm(k, pvt[:], k.blk64[:], w_[:, 1, :], True, True, rd=[k.blk64, w_], wr=[pvt])
                I("act", lambda e: e.activation(out=w_[:, 1, :], in_=pvt[:], func=AF.Ln, bias=k.epsgn[:, 0:1], scale=1.0 / 64.0),
                  rd=[pvt, k.epsgn], wr=[w_])
                I("act", lambda e: e.activation(out=w_[:, 1, :], in_=w_[:, 1, :], func=AF.Exp, scale=-0.5), rd=[w_], wr=[w_])
                I("dve", lambda e: e.tensor_tensor(out=w_[:, 0, :], in0=w_[:, 0, :], in1=w_[:, 1, :], op=ALU.mult), rd=[w_], wr=[w_])
                I("dve", lambda e: e.tensor_scalar(out=w_[:, 0, :], in0=w_[:, 0, :], scalar1=rvec[:, hp, 5:6], scalar2=rvec[:, hp, 6:7],
                                                   op0=ALU.mult, op1=ALU.add), rd=[w_, rvec], wr=[w_])
                I("pool", lambda e: e.tensor_tensor(out=w_[:, 0, :], in0=w_[:, 0, :], in1=x_[:, 0, :], op=ALU.add), rd=[w_, x_], wr=[w_])
                o_ = ob.next()
                I("pool", lambda e: e.tensor_tensor(out=o_[:], in0=w_[:, 0, :], in1=x_[:, 1, :], op=ALU.mult), rd=[w_, x_], wr=[o_])
                S.dma("sp", k.ycat[2, :, hp, tsl], o_[:], rd=[o_], wr=[k.ycat])


def precast_mlp(k, l, W):
    with phase(k) as st:
        cast_weight(k, st, W["w1"], k.w1s, k.w1s, 1024, 4096, "w1", to_dram=True)
    with phase(k) as st:
        cast_weight(k, st, W["w2m"], k.w2s, k.w2s, 4096, 1024, "w2", to_dram=True)


def merge_phase(k, l, W, h1T):
    S, I = k.S, k.S.I
    with phase(k) as st:
        sb, ps = k.sb, k.ps
        wbr = sb(st, "mgwbr", [128, 8, 1024], BF16)
        wout = sb(st, "mgwout", [128, 8, 1024], BF16)
        with phase(k) as st2:
            cast_weight(k, st2, W["wbr"], wbr, wbr, 1024, 1024, "br")
        with phase(k) as st2:
            cast_weight(k, st2, W["wout"], wout, wout, 1024, 1024, "wo")
        ln = sb(st, "mgln", [128, 8, 2], F32)
        S.dma("sp", ln[:], W["ln1"], wr=[ln])
        yt = Ring([sb(st, "mgyt%d" % i, [128, 4, 2, 512], BF16) for i in range(2)])
        gt = Ring([sb(st, "mggt%d" % i, [128, 4, 512], BF16) for i in range(3)])
        acc = Ring([sb(st, "mgacc%d" % i, [128, 2, 512], F32) for i in range(2)])
        mg = Ring([sb(st, "mgmg%d" % i, [128, 8, 512], BF16) for i in range(1)])
        hr = Ring([sb(st, "mghr%d" % i, [128, 8, 512], F32) for i in range(1)])
        z = Ring([sb(st, "mgz%d" % i, [128, 8, 512], F32) for i in range(1)])
        sq = sb(st, "mgsq", [128, 8, 512], F32)
        msb = sb(st, "mgmsb", [128, 512], F32)
        up = Ring([ps(st, "mgup%d" % i, [128, 512], F32) for i in range(4)])
        mps = ps(st, "mgmps", [128, 512], F32)
        vps = ps(st, "mgvps", [128, 512], F32)
        gview = k.gsc.t.rearrange("(n m) p t -> m p n t", n=4)
        for tt in range(NT):
            tsl = slice(tt * 512, (tt + 1) * 512)
            y_ = yt.next()
            for n in range(4):
                S.dma("sp", y_[:, n, :, :], k.ycat[n, :, :, tsl], rd=[k.ycat], wr=[y_])
            h_ = hr.next()
            S.dma("sp", h_[:], k.hres[:, :, tsl], rd=[k.hres], wr=[h_])
            m_ = mg.next()
            for mc in range(8):
                msl = slice(mc * 128, (mc + 1) * 128)
                g_ = gt.next()
                S.dma("sp", g_[:], gview[mc][:, :, tsl], rd=[k.gsc], wr=[g_])
                a_ = acc.next()
                for n in range(4):
                    u_ = up.next()
                    for kc in range(2):
                        mm(k, u_[:], wbr[:, n * 2 + kc, msl], y_[:, n, kc, :], kc == 0, kc == 1, rd=[wbr, y_], wr=[u_])
                    if n == 0:
                        I("dve", lambda e: e.tensor_tensor(out=a_[:, 0, :], in0=u_[:], in1=g_[:, 0, :], op=ALU.mult),
                          rd=[u_, g_], wr=[a_])
                    else:
                        I("dve", lambda e: e.tensor_tensor(out=a_[:, 1, :], in0=u_[:], in1=g_[:, n, :], op=ALU.mult),
                          rd=[u_, g_], wr=[a_])
                        if n < 3:
                            I("pool", lambda e: e.tensor_tensor(out=a_[:, 0, :], in0=a_[:, 0, :], in1=a_[:, 1, :], op=ALU.add),
                              rd=[a_], wr=[a_])
                        else:
                            I("pool", lambda e: e.tensor_tensor(out=m_[:, mc, :], in0=a_[:, 0, :], in1=a_[:, 1, :], op=ALU.add),
                              rd=[a_], wr=[m_])
            z_ = z.next()
            for mc in range(8):
                msl = slice(mc * 128, (mc + 1) * 128)
                u_ = up.next()
                for kc in range(8):
                    mm(k, u_[:], wout[:, kc, msl], m_[:, kc, :], kc == 0, kc == 7, rd=[wout, m_], wr=[u_])
                I("dve", lambda e: e.scalar_tensor_tensor(out=z_[:, mc, :], in0=h_[:, mc, :], scalar=ALPHA, in1=u_[:],
                                                          op0=ALU.mult, op1=ALU.add), rd=[h_, u_], wr=[z_])

            def outs(c, dn, g_ap, b_ap):
                I("pool", lambda e: e.tensor_scalar(out=dn, in0=dn, scalar1=g_ap, scalar2=b_ap, op0=ALU.mult, op1=ALU.add),
                  rd=[z_, ln], wr=[z_])
            layer_norm_tile(k, (mps, msb, sq, vps), z_, ln, LN_EPS, outs)
            I("act", lambda e: e.activation(out=h1T[:, :, tsl], in_=z_[:], func=AF.Copy), rd=[z_], wr=[h1T])
            S.dma("sp", k.h1res[:, :, tsl], z_[:], rd=[z_], wr=[k.h1res])


def mlp_phase(k, l, W, h1T, p_l, out, last):
    S, I = k.S, k.S.I
    with phase(k) as st:
        sb, ps = k.sb, k.ps
        pgw = sb(st, "mlpgw", [128, 8, 1024], BF16)
        plw = sb(st, "mlplw", [128, 2, 1024], BF16)
        with phase(k) as st2:
            cast_weight(k, st2, W["pgw"], pgw, pgw, 1024, 1024, "pg")
        with phase(k) as st2:
            cast_weight(k, st2, W["plew"], plw, plw, 256, 1024, "pl")
        with phase(k) as st2:
            pin = Ring([sb(st2, "mlpin%d" % i, [128, 256], F32) for i in range(2)])
            tps = Ring([ps(st2, "mltps%d" % i, [128, 2, 128], F32) for i in range(2)])
            pT = sb(st2, "mlpT", [128, 2, T], BF16)
            for blk in range(NB):
                pi_ = pin.next()
                S.dma("sp", pi_[:], p_l[blk * 128:(blk + 1) * 128, :], wr=[pi_])
                t_ = tps.next()
                for c in range(2):
                    tr(k, t_[:, c, :], pi_[:, c * 128:(c + 1) * 128], k.identf[:], rd=[pi_, k.identf], wr=[t_])
                I("act", lambda e: e.activation(out=pT[:, :, blk * 128:(blk + 1) * 128], in_=t_[:], func=AF.Copy), rd=[t_], wr=[pT])
            S.dma("sp", k.pTs[:], pT[:], rd=[pT], wr=[k.pTs])
        ln = sb(st, "mlln", [128, 8, 2], F32)
        S.dma("sp", ln[:], W["ln2"], wr=[ln])
        w1r = Ring([sb(st, "mlw1%d" % i, [128, 8, 512], BF16) for i in range(2)])
        w2r = Ring([sb(st, "mlw2%d" % i, [128, 4, 128], BF16) for i in range(4)])
        pTr = Ring([sb(st, "mlpT%d" % i, [128, 2, 512], BF16) for i in range(2)])
        a = sb(st, "mla", [128, 32, 512], BF16)
        rl = Ring([sb(st, "mlrl%d" % i, [128, 512], F32) for i in range(2)])
        hr = sb(st, "mlhr", [128, 8, 512], F32)
        z = sb(st, "mlz", [128, 8, 512], F32)
        msb = sb(st, "mlmsb", [128, 512], F32)
        sg = Ring([sb(st, "mlsg%d" % i, [128, 2, 512], F32) for i in range(2)])
        up = Ring([ps(st, "mlup%d" % i, [128, 512], F32) for i in range(3)])
        gp = Ring([ps(st, "mlgp%d" % i, [128, 512], F32) for i in range(2)])
        mps = ps(st, "mlmps", [128, 512], F32)
        vps = ps(st, "mlvps", [128, 512], F32)
        tpo = ps(st, "mltpo", [128, 4, 128], F32)
        ot = Ring([sb(st, "mlot%d" % i, [128, D], F32) for i in range(2)]) if last else None
        hb = Ring([sb(st, "mlhb%d" % i, [128, 8, 512], BF16) for i in range(1)]) if not last else None
        for tt in range(NT):
            tsl = slice(tt * 512, (tt + 1) * 512)
            S.dma("sp", hr[:], k.h1res[:, :, tsl], rd=[k.h1res], wr=[hr])
            pT = pTr.next()
            S.dma("sp", pT[:], k.pTs[:, :, tsl], rd=[k.pTs], wr=[pT])
            for fg in range(8):
                w1_ = w1r.next()
                S.dma("sp", w1_[:], k.w1s[:, :, fg * 512:(fg + 1) * 512], rd=[k.w1s], wr=[w1_])
                for f8 in range(4):
                    fc = fg * 4 + f8
                    u_ = up.next()
                    for kc in range(8):
                        mm(k, u_[:], w1_[:, kc, f8 * 128:(f8 + 1) * 128], h1T[:, kc, tsl], kc == 0, kc == 7, rd=[w1_, h1T], wr=[u_])
                    r_ = rl.next()
                    I("act", lambda e: e.activation(out=r_[:], in_=u_[:], func=AF.Relu), rd=[u_], wr=[r_])
                    I("pool", lambda e: e.tensor_tensor(out=a[:, fc, :], in0=r_[:], in1=r_[:], op=ALU.mult), rd=[r_], wr=[a])
            for mc in range(8):
                msl = slice(mc * 128, (mc + 1) * 128)
                s_ = sg.next()
                g_ = gp.next()
                for kc in range(8):
                    mm(k, g_[:], pgw[:, kc, msl], h1T[:, kc, tsl], kc == 0, kc == 7, rd=[pgw, h1T], wr=[g_])
                I("act", lambda e: e.activation(out=s_[:, 0, :], in_=g_[:], func=AF.Sigmoid), rd=[g_], wr=[s_])
                g2_ = gp.next()
                for kc in range(2):
                    mm(k, g2_[:], plw[:, kc, msl], pT[:, kc, :], kc == 0, kc == 1, rd=[plw, pT], wr=[g2_])
                I("dve", lambda e: e.tensor_tensor(out=s_[:, 1, :], in0=g2_[:], in1=s_[:, 0, :], op=ALU.mult), rd=[g2_, s_], wr=[s_])
                u_ = up.next()
                for fg in range(8):
                    w2_ = w2r.next()
                    S.dma("sp", w2_[:, :, 0:128], k.w2s[:, fg * 4:(fg + 1) * 4, msl], rd=[k.w2s], wr=[w2_])
                    for f4 in range(4):
                        fc = fg * 4 + f4
                        mm(k, u_[:], w2_[:, f4, 0:128], a[:, fc, :], fc == 0, fc == 31, rd=[w2_, a], wr=[u_])
                I("dve", lambda e: e.scalar_tensor_tensor(out=z[:, mc, :], in0=hr[:, mc, :], scalar=ALPHA, in1=u_[:],
                                                          op0=ALU.mult, op1=ALU.add), rd=[hr, u_], wr=[z])
                I("pool", lambda e: e.tensor_tensor(out=z[:, mc, :], in0=z[:, mc, :], in1=s_[:, 1, :], op=ALU.add), rd=[z, s_], wr=[z])

            def outs(c, dn, g_ap, b_ap):
                I("pool", lambda e: e.tensor_scalar(out=dn, in0=dn, scalar1=g_ap, scalar2=b_ap, op0=ALU.mult, op1=ALU.add),
                  rd=[z, ln], wr=[z])
            layer_norm_tile(k, (mps, msb, hr, vps), z, ln, LN_EPS, outs)
            if not last:
                hb_ = hb.next()
                I("act", lambda e: e.activation(out=hb_[:], in_=z[:], func=AF.Copy), rd=[z], wr=[hb_])
                S.dma("sp", k.hTs[:, :, tsl], hb_[:], rd=[hb_], wr=[k.hTs])
                S.dma("sp", k.hres[:, :, tsl], z[:], rd=[z], wr=[k.hres])
            else:
                for b in range(4):
                    o_ = ot.next()
                    for half in range(2):
                        for j in range(4):
                            c = half * 4 + j
                            tr(k, tpo[:, j, :], z[:, c, b * 128:(b + 1) * 128], k.identf[:], rd=[z, k.identf], wr=[tpo])
                        I("act", lambda e: e.activation(out=o_[:, half * 512:(half + 1) * 512],
                                                        in_=tpo[:].rearrange("p a b -> p (a b)"), func=AF.Copy), rd=[tpo], wr=[o_])
                    r0 = tt * 512 + b * 128
                    S.dma("sp", out[r0:r0 + 128, :], o_[:], rd=[o_], wr=[Dep()])


_CACHE = {}


def kernel(**inputs):
    if "nc" not in _CACHE:
        _CACHE["nc"] = build_program(depth=DEPTH)[0]
    nc = _CACHE["nc"]
    shared = {}
    for l in range(DEPTH):
        for n, v in prep_layer(inputs, l).items():
            shared["%s_%d" % (n, l)] = v
    x = np.asarray(inputs["x"], np.float32)
    p = np.asarray(inputs["p"], np.float32)
    in_maps = []
    for b in range(8):
        m = dict(shared)
        m["x"] = np.ascontiguousarray(x[b])
        m["p"] = np.ascontiguousarray(p[:, b])
        in_maps.append(m)
    res = run_bass_kernel_spmd(nc, in_maps, core_ids=list(range(8)))
    return np.stack([np.asarray(r["out"], np.float32) for r in res.results], axis=0)
```

```python
import math
import contextlib
import numpy as np
import concourse.bass as bass
import concourse.mybir as mybir
from concourse.bass_utils import run_bass_kernel_spmd

F32 = mybir.dt.float32
BF16 = mybir.dt.bfloat16
AF = mybir.ActivationFunctionType
ALU = mybir.AluOpType
AX = mybir.AxisListType

T = 4096
D = 1024
NT = 8
NB = 32
DEPTH = 2
ALPHA = (2 * DEPTH) ** 0.25
LN_EPS = 1e-5
GN_EPS = 64e-5
PI = math.pi


class Dep:
    __slots__ = ("w", "r", "excl")

    def __init__(self):
        self.w = {}
        self.r = {}
        self.excl = False


class Buf:
    def __init__(self, t, excl=False):
        self.t = t
        self.dep = Dep()
        self.dep.excl = excl

    def __getitem__(self, k):
        return self.t[k]


class Ring:
    def __init__(self, bufs):
        self.bufs = bufs
        self.i = 0

    def next(self):
        b = self.bufs[self.i % len(self.bufs)]
        self.i += 1
        return b


class Sched:
    EPOCH = 30000
    NDMA = 6
    NEPOCH = 8
    qmap = {"pool": "sp"}

    def __init__(self, nc):
        self.nc = nc
        self.eng = {"pe": nc.tensor, "act": nc.scalar, "dve": nc.vector,
                    "pool": nc.gpsimd, "sp": nc.sync}
        self.sem = {}
        self.cnt = {}
        self.seen = {e: {} for e in self.eng}
        self.nsem = 0
        self.ninst = 0
        self.allsems = []
        self.epool = {e: [self._alloc("e_%s_%d" % (e, i)) for i in range(self.NEPOCH)] for e in ("pe", "act", "dve", "pool")}
        self.dq = {}
        for q in ("sp",):
            self.dq[q] = {"sems": [self._alloc("dq_%s_%d" % (q, i)) for i in range(self.NDMA)], "n": 0}
        for e in ("pe", "act", "dve", "pool"):
            self._new_epoch(e)

    def _alloc(self, name):
        self.nsem += 1
        h = self.nc.alloc_semaphore(name)
        self.allsems.append(h)
        return h

    def _new_epoch(self, e):
        self.sem[e] = self.epool[e].pop(0)
        self.cnt[e] = 0

    def _wait(self, eng, sem, val):
        key = id(sem)
        if self.seen[eng].get(key, 0) >= val:
            return
        self.eng[eng].wait_ge(sem, val)
        self.seen[eng][key] = val

    def _gather(self, eng, rd, wr, same_ok=True):
        need = {}

        def add(p):
            pe, sem, val = p
            if pe == eng and (same_ok or eng == "pe"):
                return
            k = id(sem)
            if k not in need or need[k][1] < val:
                need[k] = (sem, val)
        for d in rd:
            for p in d.w.values():
                add(p)
        for d in wr:
            for p in d.w.values():
                add(p)
            for p in d.r.values():
                add(p)
        for sem, val in need.values():
            self._wait(eng, sem, val)

    @staticmethod
    def _deps(lst):
        return [x.dep if isinstance(x, Buf) else x for x in lst]

    @staticmethod
    def _merge(dct, tok):
        k = id(tok[1])
        if k not in dct or dct[k][2] < tok[2]:
            dct[k] = tok

    def I(self, eng, fn, rd=(), wr=()):
        rd = self._deps(rd)
        wr = self._deps(wr)
        ex = [d for d in rd if d.excl and d not in wr]
        if ex:
            rd = [d for d in rd if not d.excl]
            wr = list(wr) + ex
        self._gather(eng, rd, wr, same_ok=False)
        if self.cnt[eng] >= self.EPOCH:
            self._new_epoch(eng)
        ins = fn(self.eng[eng])
        self.ninst += 1
        self.cnt[eng] += 1
        ins.then_inc(self.sem[eng], 1)
        tok = (eng, self.sem[eng], self.cnt[eng])
        for d in rd:
            self._merge(d.r, tok)
        for d in wr:
            d.w = {id(tok[1]): tok}
            d.r = {}
        return ins

    def dma(self, q, out, in_, rd=(), wr=()):
        rd = self._deps(rd)
        wr = self._deps(wr)
        q = self.qmap.get(q, q)
        dq = self.dq[q]
        j = dq["n"]
        dq["n"] += 1
        sem = dq["sems"][j % self.NDMA]
        val = 16 * (j // self.NDMA + 1)
        if val > 16:
            self._wait(q, sem, val - 16)
        self._gather(q, rd, wr, same_ok=False)
        ins = self.eng[q].dma_start(out=out, in_=in_)
        self.ninst += 1
        ins.then_inc(sem, 16)
        tok = ("dma", sem, val)
        for d in rd:
            self._merge(d.r, tok)
        for d in wr:
            self._merge(d.w, tok)
            d.r = {}
        return ins

    def barrier(self, force=False):
        for w in ("pe", "act", "dve", "pool", "sp"):
            for q, dq in self.dq.items():
                n = dq["n"]
                for i in range(self.NDMA):
                    cnt_i = len(range(i, n, self.NDMA))
                    if cnt_i > 0:
                        self._wait(w, dq["sems"][i], 16 * cnt_i)
            for e in ("pe", "act", "dve", "pool"):
                if e != w and self.cnt[e] > 0:
                    self._wait(w, self.sem[e], self.cnt[e])

    def finish(self):
        for q, dq in self.dq.items():
            n = dq["n"]
            for i in range(self.NDMA):
                cnt_i = len(range(i, n, self.NDMA))
                if cnt_i > 0:
                    self._wait("sp", dq["sems"][i], 16 * cnt_i)
        for e in ("pe", "act", "dve", "pool"):
            if self.cnt[e] > 0:
                self._wait("sp", self.sem[e], self.cnt[e])


def _chunkcol(v):
    v = np.asarray(v, np.float32)
    return np.ascontiguousarray(v.reshape(-1, 128).T)


def prep_layer(inp, l):
    f = np.float32
    w_in = np.asarray(inp["w_in"][l], f)
    o = {}
    o["winF"] = np.ascontiguousarray(np.concatenate(
        [w_in[:, 0:256], w_in[:, 256:512], w_in[:, 512:768], w_in[:, 1028:2052],
         w_in[:, 2052:2308], w_in[:, 2308:2564], w_in[:, 2820:6916]], axis=1))
    o["winT"] = np.ascontiguousarray(np.concatenate([w_in[:, 768:1024], w_in[:, 2564:2820]], axis=1))
    o["wff"] = np.ascontiguousarray(w_in[:, 1024:1028])
    lre = np.asarray(inp["s5_lambda_re"][l], f)
    lim = np.asarray(inp["s5_lambda_im"][l], f)
    ldt = np.repeat(np.asarray(inp["s5_log_dt"][l], f)[:, None], 64, axis=1)
    def gp(a):
        return a.reshape(8, 2, 64).transpose(1, 2, 0).reshape(128, 8)
    o["s5par"] = np.ascontiguousarray(np.stack([gp(lre), gp(lim), gp(ldt)], axis=2))
    b_re = np.asarray(inp["s5_b_re"][l], f)
    b_im = np.asarray(inp["s5_b_im"][l], f)
    c_re = np.asarray(inp["s5_c_re"][l], f)
    c_im = np.asarray(inp["s5_c_im"][l], f)
    Bw = np.zeros((8, 128, 2, 128), f)
    Cw = np.zeros((8, 128, 2, 128), f)
    for g in range(16):
        rt = g // 2
        k0 = (g % 8) * 16
        m0 = (g % 2) * 64
        Bw[rt, k0:k0 + 16, 0, m0:m0 + 64] = b_re[g].T
        Bw[rt, k0:k0 + 16, 1, m0:m0 + 64] = b_im[g].T
        Cw[rt, m0:m0 + 64, 0, k0:k0 + 16] = c_re[g].T
        Cw[rt, m0:m0 + 64, 1, k0:k0 + 16] = c_im[g].T
    o["s5B"] = np.ascontiguousarray(Bw.transpose(1, 0, 2, 3))
    o["s5C"] = np.ascontiguousarray(Cw.transpose(1, 0, 2, 3))
    o["s5D"] = _chunkcol(np.asarray(inp["s5_d"][l], f).reshape(-1))
    o["gluw"] = np.asarray(inp["s5_glu_w"][l], f)
    o["glub"] = _chunkcol(inp["s5_glu_b"][l])
    o["foxb"] = np.asarray(inp["fox_f_bias"][l], f).reshape(4, 1)
    o["mu"] = _chunkcol(inp["rwkv_mu"][l])
    vec = [inp["rwkv_w0"][l], inp["rwkv_a0"][l], inp["rwkv_k_k"][l], inp["rwkv_k_a"][l],
           np.asarray(inp["rwkv_r_k"][l]).reshape(-1), inp["rwkv_lnx_g"][l], inp["rwkv_lnx_b"][l]]
    o["rvec"] = np.ascontiguousarray(np.stack([_chunkcol(v) for v in vec], axis=2))
    o["wa2"] = np.ascontiguousarray(np.concatenate([np.asarray(inp["rwkv_w2"][l], f),
                                                     np.asarray(inp["rwkv_a2"][l], f)], axis=0))
    o["g2"] = np.asarray(inp["rwkv_g2"][l], f)
    o["wbr"] = np.ascontiguousarray(np.asarray(inp["w_branch"][l], f).reshape(1024, 1024))
    o["wout"] = np.asarray(inp["w_out"][l], f)
    o["ln1"] = np.ascontiguousarray(np.stack([_chunkcol(inp["ln1_g"][l]), _chunkcol(inp["ln1_b"][l])], axis=2))
    o["ln2"] = np.ascontiguousarray(np.stack([_chunkcol(inp["ln2_g"][l]), _chunkcol(inp["ln2_b"][l])], axis=2))
    o["w1"] = np.asarray(inp["mlp_w1"][l], f)
    o["w2m"] = np.asarray(inp["mlp_w2"][l], f)
    o["plew"] = np.asarray(inp["ple_w"][l], f)
    o["pgw"] = np.asarray(inp["ple_gate_w"][l], f)
    return {k: np.ascontiguousarray(v, dtype=f) for k, v in o.items()}


LAYER_SHAPES = {
    "winF": [1024, 6400], "winT": [1024, 512], "wff": [1024, 4], "s5par": [128, 8, 3],
    "s5B": [128, 8, 2, 128], "s5C": [128, 8, 2, 128], "s5D": [128, 2], "gluw": [256, 256], "glub": [128, 2],
    "foxb": [4, 1], "mu": [128, 8], "rvec": [128, 2, 7], "wa2": [128, 256], "g2": [128, 256],
    "wbr": [1024, 1024], "wout": [1024, 1024], "ln1": [128, 8, 2], "ln2": [128, 8, 2],
    "w1": [1024, 4096], "w2m": [4096, 1024], "plew": [256, 1024], "pgw": [1024, 1024],
}


class K:
    pass


@contextlib.contextmanager
def phase(k):
    with contextlib.ExitStack() as st:
        yield st
        k.S.barrier()


def build_program(depth=DEPTH, dbg=(), stop_after=None, only=None, nsteps=T):
    nc = bass.Bass("TRN2", target_bir_lowering=False)
    S = Sched(nc)
    k = K()
    k.nc, k.S = nc, S
    k.dbg = {}
    x_in = nc.dram_tensor("x", [T, D], F32, kind="ExternalInput").ap()
    p_in = nc.dram_tensor("p", [DEPTH, T, 256], F32, kind="ExternalInput").ap()
    out = nc.dram_tensor("out", [T, D], F32, kind="ExternalOutput").ap()
    W = []
    for l in range(DEPTH):
        W.append({n: nc.dram_tensor("%s_%d" % (n, l), s, F32, kind="ExternalInput").ap()
                  for n, s in LAYER_SHAPES.items()})
    k.stop_after = stop_after
    k.only = only
    k.nsteps = nsteps

    def scratch(name, shape, dt):
        kind = "ExternalOutput" if name in dbg else "Internal"
        import os
        if os.environ.get("KNOB_TINY") and name != "hres":
            shape = [2, 2]
        return Buf(nc.dram_tensor(name, shape, dt, kind=kind).ap())
    k.hres = scratch("hres", [128, 8, T], F32)
    k.h1res = scratch("h1res", [128, 8, T], F32)
    k.gsc = scratch("gsc", [32, 128, T], BF16)
    k.ycat = scratch("ycat", [4, 128, 2, T], BF16)
    k.rwxs = scratch("rwxs", [8, 128, T], F32)
    k.sc5 = scratch("sc5", [2, 128, 5, T], F32)
    k.bon = scratch("bon", [2, 128, T], F32)
    k.gg = scratch("gg", [2, 128, T], F32)
    k.vtok = scratch("vtok", [T, 2, 128], F32)
    k.uTf = scratch("uTf", [128, 2, T], F32)
    k.qk = scratch("qk", [4, 128, 2, T], BF16)
    k.vv = scratch("vv", [2, 128, NB, 256], BF16)
    k.negc = scratch("negc", [4, T], F32)
    k.hTs = scratch("hTs", [128, 8, T], BF16)
    k.pTs = scratch("pTs", [128, 2, T], BF16)
    k.w1s = scratch("w1s", [128, 8, 4096], BF16)
    k.w2s = scratch("w2s", [128, 32, 1024], BF16)

    es = contextlib.ExitStack()

    uid = [0]

    def sb(stack, name, shape, dt):
        uid[0] += 1
        return Buf(stack.enter_context(nc.sbuf_tensor("%s_u%d" % (name, uid[0]), shape, dt)))

    def ps(stack, name, shape, dt=F32):
        uid[0] += 1
        full = [128, 512] if dt == F32 else [128, 1024]
        t = stack.enter_context(nc.psum_tensor("%s_u%d" % (name, uid[0]), full, dt))
        n = 1
        for d_ in shape[1:]:
            n *= d_
        if len(shape) == 2:
            v = t[0:shape[0], 0:n]
        else:
            assert len(shape) == 3
            v = t[0:shape[0], 0:n].rearrange("p (a b) -> p a b", a=shape[1])
        return Buf(v, excl=True)
    k.sb, k.ps = sb, ps

    with es:
        k.identf = sb(es, "identf", [128, 128], F32)
        k.identb = sb(es, "identb", [128, 128], BF16)
        k.onesm = sb(es, "onesm", [128, 128], F32)
        k.blk64 = sb(es, "blk64", [128, 128], F32)
        k.nmi = sb(es, "nmi", [128, 128], F32)
        k.nms = sb(es, "nms", [128, 128], F32)
        k.m01 = sb(es, "m01", [128, 128], F32)
        k.ones = sb(es, "ones", [128, 512], F32)
        k.sel4 = sb(es, "sel4", [4, 4, 128], F32)
        k.epsln = sb(es, "epsln", [128, 1], F32)
        k.epsgn = sb(es, "epsgn", [128, 1], F32)
        k.one1 = sb(es, "one1", [128, 1], F32)
        k.mhalf = sb(es, "mhalf", [128, 1], F32)
        I = S.I
        I("pool", lambda e: e.memset(k.epsln[:], LN_EPS), wr=[k.epsln])
        I("pool", lambda e: e.memset(k.epsgn[:], GN_EPS), wr=[k.epsgn])
        I("pool", lambda e: e.memset(k.one1[:], 1.0), wr=[k.one1])
        I("pool", lambda e: e.memset(k.mhalf[:], -0.5), wr=[k.mhalf])
        I("pool", lambda e: e.memset(k.identf[:], 0.0), wr=[k.identf])
        I("pool", lambda e: e.affine_select(out=k.identf[:], in_=k.identf[:], compare_op=ALU.not_equal, fill=1.0,
                                            base=0, pattern=[[-1, 128]], channel_multiplier=1),
          rd=[k.identf], wr=[k.identf])
        I("pool", lambda e: e.tensor_copy(k.identb[:], k.identf[:]), rd=[k.identf], wr=[k.identb])
        I("pool", lambda e: e.memset(k.onesm[:], 1.0 / 1024.0), wr=[k.onesm])
        I("pool", lambda e: e.memset(k.ones[:], 1.0), wr=[k.ones])
        I("pool", lambda e: e.memset(k.blk64[:], 0.0), wr=[k.blk64])
        I("pool", lambda e: e.memset(k.blk64[0:64, 0:64], 1.0), wr=[k.blk64])
        I("pool", lambda e: e.memset(k.blk64[64:128, 64:128], 1.0), wr=[k.blk64])
        I("pool", lambda e: e.memset(k.nmi[:], 0.0), wr=[k.nmi])
        I("pool", lambda e: e.affine_select(out=k.nmi[:], in_=k.nmi[:], compare_op=ALU.is_ge, fill=-1e30,
                                            base=0, pattern=[[-1, 128]], channel_multiplier=1), rd=[k.nmi], wr=[k.nmi])
        I("pool", lambda e: e.memset(k.nms[:], 0.0), wr=[k.nms])
        I("pool", lambda e: e.affine_select(out=k.nms[:], in_=k.nms[:], compare_op=ALU.is_gt, fill=-1e30,
                                            base=0, pattern=[[-1, 128]], channel_multiplier=1), rd=[k.nms], wr=[k.nms])
        I("pool", lambda e: e.affine_select(out=k.m01[:], in_=k.ones[:, 0:128], compare_op=ALU.is_gt, fill=0.0,
                                            base=0, pattern=[[-1, 128]], channel_multiplier=1), rd=[k.ones], wr=[k.m01])
        I("pool", lambda e: e.affine_select(out=k.sel4[:], in_=k.ones[0:4, :].rearrange("p (h m) -> p h m", h=4),
                                            compare_op=ALU.is_equal, fill=0.0, base=0, pattern=[[-1, 4], [0, 128]],
                                            channel_multiplier=1), rd=[k.ones], wr=[k.sel4])

        if stop_after != "consts":
            for l in range(depth):
                layer(k, l, W[l], x_in, p_in[l], out, last=(l == depth - 1))
        S.finish()
    k.ninst = S.ninst
    return nc, k


def dbg_dump(k, name, src_buf, src_ap):
    if name in k.dbg:
        k.S.dma("sp", k.dbg[name], src_ap, rd=[src_buf], wr=[Dep()])


def mm(k, out_ap, lhsT, rhs, start, stop, rd, wr):
    return k.S.I("pe", lambda e: e.matmul(out_ap, lhsT, rhs, start=start, stop=stop), rd=rd, wr=wr)


def tr(k, out_ap, in_ap, ident_ap, rd, wr):
    return k.S.I("pe", lambda e: e.transpose(out_ap, in_ap, ident_ap), rd=rd, wr=wr)


def phase0(k, x_in, hT):
    import os
    lvl = int(os.environ.get("KNOB_P0", "9"))
    nblk = int(os.environ.get("KNOB_P0N", str(NB)))
    S, I = k.S, k.S.I
    with phase(k) as st:
        xin = Ring([k.sb(st, "p0x%d" % i, [128, D], F32) for i in range(2)])
        stg = Ring([k.sb(st, "p0s%d" % i, [128, 8, 128], F32) for i in range(2)])
        pss = Ring([k.ps(st, "p0p%d" % i, [128, 4, 128], F32) for i in range(4)])
        for blk in range(nblk):
            xi = xin.next()
            S.dma("sp", xi[:], x_in[blk * 128:(blk + 1) * 128, :], wr=[xi])
            sg = stg.next()
            if lvl < 1:
                continue
            for half in range(2):
                pt = pss.next()
                for j in range(4):
                    c = half * 4 + j
                    tr(k, pt[:, j, :], xi[:, c * 128:(c + 1) * 128], k.identf[:], rd=[xi, k.identf], wr=[pt])
                if lvl >= 2:
                    I("act", lambda e: e.activation(out=hT[:, half * 4:half * 4 + 4, blk * 128:(blk + 1) * 128],
                                                    in_=pt[:], func=AF.Copy), rd=[pt], wr=[hT])
                if lvl >= 3:
                    I("dve", lambda e: e.tensor_copy(sg[:, half * 4:half * 4 + 4, :], pt[:]), rd=[pt], wr=[sg])
            if lvl >= 4:
                S.dma("pool", k.hres[:, :, blk * 128:(blk + 1) * 128], sg[:], rd=[sg], wr=[k.hres])


def cast_weight(k, st, src, dst_buf, dst_ap, K_, N_, tag, to_dram=False):
    S, I = k.S, k.S.I
    kc = K_ // 128
    ncol = max(1, min(N_, 4096 // kc))
    f32r = Ring([k.sb(st, "cw%s_f%d" % (tag, i), [128, kc, ncol], F32) for i in range(2)])
    if to_dram:
        bfr = Ring([k.sb(st, "cw%s_b%d" % (tag, i), [128, kc, ncol], BF16) for i in range(2)])
    srcv = src.rearrange("(c p) n -> p c n", p=128)
    engs = ["pool", "dve"]
    i = 0
    for n0 in range(0, N_, ncol):
        n1 = min(N_, n0 + ncol)
        w = n1 - n0
        fb = f32r.next()
        S.dma("sp", fb[:, :, 0:w], srcv[:, :, n0:n1], wr=[fb])
        if to_dram:
            bb = bfr.next()
            I(engs[i % 2], lambda e: e.tensor_copy(bb[:, :, 0:w], fb[:, :, 0:w]), rd=[fb], wr=[bb])
            S.dma("pool", dst_ap[:, :, n0:n1], bb[:, :, 0:w], rd=[bb], wr=[dst_buf])
        else:
            I(engs[i % 2], lambda e: e.tensor_copy(dst_ap[:, :, n0:n1], fb[:, :, 0:w]), rd=[fb], wr=[dst_buf])
        i += 1


def layer_norm_tile(k, st_bufs, z, gb, eps, outs):
    S, I = k.S, k.S.I
    mps, msb, sq, vps = st_bufs
    for c in range(8):
        mm(k, mps[:], k.onesm[:], z[:, c, :], c == 0, c == 7, rd=[k.onesm, z], wr=[mps])
    I("act", lambda e: e.activation(out=msb[:], in_=mps[:], func=AF.Copy), rd=[mps], wr=[msb])
    I("dve", lambda e: e.tensor_tensor(out=z[:], in0=z[:], in1=msb[:, None, :].broadcast_to([128, 8, 512]),
                                       op=ALU.subtract), rd=[z, msb], wr=[z])
    I("act", lambda e: e.activation(out=sq[:], in_=z[:], func=AF.Square), rd=[z], wr=[sq])
    for c in range(8):
        mm(k, vps[:], k.onesm[:], sq[:, c, :], c == 0, c == 7, rd=[k.onesm, sq], wr=[vps])
    I("act", lambda e: e.activation(out=msb[:], in_=vps[:], func=AF.Ln, bias=k.epsln[:, 0:1], scale=1.0),
      rd=[vps, k.epsln], wr=[msb])
    I("act", lambda e: e.activation(out=msb[:], in_=msb[:], func=AF.Exp, scale=-0.5), rd=[msb], wr=[msb])
    I("dve", lambda e: e.tensor_tensor(out=z[:], in0=z[:], in1=msb[:, None, :].broadcast_to([128, 8, 512]),
                                       op=ALU.mult), rd=[z, msb], wr=[z])
    for c in range(8):
        outs(c, z[:, c, :], gb[:, c, 0:1], gb[:, c, 1:2])


def inproj(k, l, W, hT):
    S, I = k.S, k.S.I
    with phase(k) as st:
        wf = Ring([k.sb(st, "ipwf%d" % i, [128, 8, 128], F32) for i in range(2)])
        wb = Ring([k.sb(st, "ipwb%d" % i, [128, 8, 128], BF16) for i in range(2)])
        pss = Ring([k.ps(st, "ipps%d" % i, [128, 512], F32) for i in range(4)])
        stb = Ring([k.sb(st, "ipsb%d" % i, [128, 512], BF16) for i in range(3)])
        stf = Ring([k.sb(st, "ipsf%d" % i, [128, 512], F32) for i in range(2)])
        raw = Ring([k.sb(st, "ipraw%d" % i, [128, T + 1], F32) for i in range(2)])
        xs = Ring([k.sb(st, "ipxs%d" % i, [128, T], F32) for i in range(1)])
        mu = k.sb(st, "ipmu", [128, 8], F32)
        S.dma("sp", mu[:], W["mu"], wr=[mu])
        for r_ in raw.bufs:
            I("pool", lambda e: e.memset(r_[:, 0:1], 0.0), wr=[r_])
        winF = W["winF"].rearrange("(c p) n -> p c n", p=128)
        ev = 0
        for j in range(50):
            if k.stop_after == "ip_fm%d" % j:
                return
            f_ = wf.next()
            S.dma("sp", f_[:], winF[:, :, j * 128:(j + 1) * 128], wr=[f_])
            b_ = wb.next()
            I("pool", lambda e: e.tensor_copy(b_[:], f_[:]), rd=[f_], wr=[b_])
            if 6 <= j < 14:
                rw = raw.next()
            for tt in range(NT):
                pt = pss.next()
                for c in range(8):
                    mm(k, pt[:], b_[:, c, :], hT[:, c, tt * 512:(tt + 1) * 512], c == 0, c == 7,
                       rd=[b_, hT], wr=[pt])
                eng = "act" if ev % 2 == 0 else "dve"
                ev += 1
                tsl = slice(tt * 512, (tt + 1) * 512)
                if j < 2:
                    sf = stf.next()
                    if eng == "act":
                        I("act", lambda e: e.activation(out=sf[:], in_=pt[:], func=AF.Copy), rd=[pt], wr=[sf])
                    else:
                        I("dve", lambda e: e.tensor_copy(sf[:], pt[:]), rd=[pt], wr=[sf])
                    S.dma("pool", k.uTf[:, j, tsl], sf[:], rd=[sf], wr=[k.uTf])
                elif j < 6 or 14 <= j < 18:
                    which = (j - 2) // 2 if j < 6 else 2 + (j - 14) // 2
                    cc = j % 2
                    scale = 0.125 if which in (0, 2) else 1.0
                    sb_ = stb.next()
                    I("act", lambda e: e.activation(out=sb_[:], in_=pt[:], func=AF.Copy, scale=scale), rd=[pt], wr=[sb_])
                    S.dma("pool", k.qk[which, :, cc, tsl], sb_[:], rd=[sb_], wr=[k.qk])
                elif j < 14:
                    if eng == "act":
                        I("act", lambda e: e.activation(out=rw[:, 1 + tt * 512:1 + (tt + 1) * 512], in_=pt[:], func=AF.Copy),
                          rd=[pt], wr=[rw])
                    else:
                        I("dve", lambda e: e.tensor_copy(rw[:, 1 + tt * 512:1 + (tt + 1) * 512], pt[:]), rd=[pt], wr=[rw])
                else:
                    gc = j - 18
                    sb_ = stb.next()
                    I("act", lambda e: e.activation(out=sb_[:], in_=pt[:], func=AF.Sigmoid), rd=[pt], wr=[sb_])
                    S.dma("pool", k.gsc[gc, :, tsl], sb_[:], rd=[sb_], wr=[k.gsc])
            if 6 <= j < 14:
                jj = j - 6
                x_ = xs.next()
                I("pool", lambda e: e.tensor_tensor(out=x_[:], in0=rw[:, 0:T], in1=rw[:, 1:T + 1], op=ALU.subtract),
                  rd=[rw], wr=[x_])
                I("dve", lambda e: e.scalar_tensor_tensor(out=x_[:], in0=x_[:], scalar=mu[:, jj:jj + 1], in1=rw[:, 1:T + 1],
                                                          op0=ALU.mult, op1=ALU.add), rd=[x_, rw, mu], wr=[x_])
                S.dma("sp", k.rwxs[jj], x_[:], rd=[x_], wr=[k.rwxs])
        if k.stop_after == "ip_fm":
            return
        wtf = k.sb(st, "ipwtf", [128, 8, 512], F32)
        wtb = k.sb(st, "ipwtb", [128, 8, 512], BF16)
        S.dma("sp", wtf[:], W["winT"].rearrange("(c p) n -> p c n", p=128), wr=[wtf])
        I("pool", lambda e: e.tensor_copy(wtb[:], wtf[:]), rd=[wtf], wr=[wtb])
        for blk in range(NB):
            pt = pss.next()
            for c in range(8):
                mm(k, pt[:], hT[:, c, blk * 128:(blk + 1) * 128], wtb[:, c, :], c == 0, c == 7, rd=[wtb, hT], wr=[pt])
            sb_ = stb.next()
            if blk % 2 == 0:
                I("act", lambda e: e.activation(out=sb_[:], in_=pt[:], func=AF.Copy), rd=[pt], wr=[sb_])
            else:
                I("dve", lambda e: e.tensor_copy(sb_[:], pt[:]), rd=[pt], wr=[sb_])
            S.dma("pool", k.vv[:, :, blk, :].rearrange("w p n -> p w n"), sb_[:].rearrange("p (w n) -> p w n", w=2),
                  rd=[sb_], wr=[k.vv])
        if k.stop_after == "ip_tm":
            return
        wff_f = k.sb(st, "ipwff", [128, 8, 4], F32)
        wff_b = k.sb(st, "ipwffb", [128, 8, 4], BF16)
        fb = k.sb(st, "ipfb", [4, 1], F32)
        fl = k.sb(st, "ipfl", [4, T], F32)
        S.dma("sp", wff_f[:], W["wff"].rearrange("(c p) n -> p c n", p=128), wr=[wff_f])
        S.dma("sp", fb[:], W["foxb"], wr=[fb])
        I("pool", lambda e: e.tensor_copy(wff_b[:], wff_f[:]), rd=[wff_f], wr=[wff_b])
        I("dve", lambda e: e.tensor_scalar(out=fb[:], in0=fb[:], scalar1=-1.0, scalar2=None, op0=ALU.mult), rd=[fb], wr=[fb])
        for tt in range(NT):
            pt = pss.next()
            for c in range(8):
                mm(k, pt[0:4, :], wff_b[:, c, :], hT[:, c, tt * 512:(tt + 1) * 512], c == 0, c == 7, rd=[wff_b, hT], wr=[pt])
            I("act", lambda e: e.activation(out=fl[:, tt * 512:(tt + 1) * 512], in_=pt[0:4, :], func=AF.Exp,
                                            bias=fb[:, 0:1], scale=-1.0), rd=[pt, fb], wr=[fl])
        I("act", lambda e: e.activation(out=fl[:], in_=fl[:], func=AF.Ln, bias=k.one1[0:4, 0:1], scale=1.0),
          rd=[fl, k.one1], wr=[fl])
        for tt in range(NT):
            init = 0.0 if tt == 0 else fl[:, tt * 512 - 1:tt * 512]
            I("dve", lambda e: e.tensor_tensor_scan(out=fl[:, tt * 512:(tt + 1) * 512], data0=k.ones[0:4, :],
                                                    data1=fl[:, tt * 512:(tt + 1) * 512], initial=init,
                                                    op0=ALU.mult, op1=ALU.add), rd=[fl, k.ones], wr=[fl])
        S.dma("sp", k.negc[:], fl[:], rd=[fl], wr=[k.negc])


def layer(k, l, W, x_in, p_l, out, last):
    S, I = k.S, k.S.I
    with phase(k) as st:
        hT = k.sb(st, "hT%d" % l, [128, 8, T], BF16)
        if l == 0:
            phase0(k, x_in, hT)
            if k.stop_after == "phase0":
                return
        else:
            S.dma("sp", hT[:], k.hTs[:], rd=[k.hTs], wr=[hT])
        inproj(k, l, W, hT)
    if k.stop_after is not None and k.stop_after.startswith("ip") or k.stop_after == "inproj":
        return
    if k.only in (None, "s5"):
        s5_phase(k, l, W)
    if k.stop_after == "s5":
        return
    if k.only in (None, "fox"):
        attn_phase(k, l, 0)
    if k.only in (None, "sb"):
        attn_phase(k, l, 1)
    if k.stop_after == "attn":
        return
    if k.only in (None, "rwkv"):
        rwkv_prep(k, l, W)
        rwkv_chunked(k, l, W, nsteps=k.nsteps)
    if k.stop_after == "rwkv":
        return
    precast_mlp(k, l, W)
    with phase(k) as st:
        h1T = k.sb(st, "h1T%d" % l, [128, 8, T], BF16)
        merge_phase(k, l, W, h1T)
        if k.stop_after == "merge":
            return
        mlp_phase(k, l, W, h1T, p_l, out, last)


def range_reduce(k, out, in_, shift, qi, qf, rd, wr):
    I = k.S.I
    TWO_PI = 2.0 * PI
    I("dve", lambda e: e.tensor_scalar(out=out, in0=in_, scalar1=float(shift), scalar2=None, op0=ALU.add), rd=rd, wr=wr)
    I("dve", lambda e: e.tensor_scalar(out=qi, in0=out, scalar1=1.0 / TWO_PI, scalar2=0.5, op0=ALU.mult, op1=ALU.add), rd=rd, wr=wr)
    I("dve", lambda e: e.tensor_copy(qf, qi), rd=rd, wr=wr)
    I("dve", lambda e: e.scalar_tensor_tensor(out=out, in0=qf, scalar=-TWO_PI, in1=out, op0=ALU.mult, op1=ALU.add), rd=rd, wr=wr)
    I("dve", lambda e: e.tensor_scalar(out=qf, in0=out, scalar1=-PI, scalar2=None, op0=ALU.is_lt), rd=rd, wr=wr)
    I("dve", lambda e: e.scalar_tensor_tensor(out=out, in0=qf, scalar=TWO_PI, in1=out, op0=ALU.mult, op1=ALU.add), rd=rd, wr=wr)
    I("dve", lambda e: e.tensor_scalar(out=out, in0=out, scalar1=-3.1415925, scalar2=3.1415925, op0=ALU.max, op1=ALU.min), rd=rd, wr=wr)


def s5_phase(k, l, W):
    S, I = k.S, k.S.I
    TWO_PI = 2.0 * PI
    with phase(k) as st:
        sb, ps = k.sb, k.ps
        par = sb(st, "s5par", [128, 8, 3], F32)
        S.dma("sp", par[:], W["s5par"], wr=[par])
        Bb = sb(st, "s5Bb", [128, 8, 2, 128], BF16)
        Cb = sb(st, "s5Cb", [128, 8, 2, 128], BF16)
        gwb = sb(st, "s5gwb", [128, 2, 256], BF16)
        Dc = sb(st, "s5D", [128, 2], F32)
        S.dma("sp", Dc[:], W["s5D"], wr=[Dc])
        with phase(k) as st2:
            Bf = sb(st2, "s5Bf", [128, 8, 2, 128], F32)
            Cf = sb(st2, "s5Cf", [128, 8, 2, 128], F32)
            gwf = sb(st2, "s5gwf", [128, 2, 256], F32)
            S.dma("sp", Bf[:], W["s5B"], wr=[Bf])
            S.dma("sp", Cf[:], W["s5C"], wr=[Cf])
            S.dma("sp", gwf[:], W["gluw"].rearrange("(c p) n -> p c n", p=128), wr=[gwf])
            I("pool", lambda e: e.tensor_copy(Bb[:], Bf[:]), rd=[Bf], wr=[Bb])
            I("pool", lambda e: e.tensor_copy(Cb[:, :, 0, :], Cf[:, :, 0, :]), rd=[Cf], wr=[Cb])
            I("pool", lambda e: e.tensor_scalar(out=Cb[:, :, 1, :], in0=Cf[:, :, 1, :], scalar1=-1.0, scalar2=None, op0=ALU.mult),
              rd=[Cf], wr=[Cb])
            I("pool", lambda e: e.tensor_copy(gwb[:], gwf[:]), rd=[gwf], wr=[gwb])
        gb = sb(st, "s5gb", [128, 2], F32)
        S.dma("sp", gb[:], W["glub"], wr=[gb])
        sm = sb(st, "s5sm", [128, 16, 8], F32)
        def sl(i):
            return sm[:, i, :]
        lre, lim, ldt = par[:, :, 0], par[:, :, 1], par[:, :, 2]
        DT, A_, TH, NA, EM, SN, CS, LBR, LBI, DEN, FRE, FIM, NFRE, T1, T2, THR = range(16)
        dsm = Dep()
        def dv(fn):
            I("dve", fn, rd=[par, dsm], wr=[dsm])
        def ac(fn):
            I("act", fn, rd=[par, dsm], wr=[dsm])
        def pl(fn):
            I("pool", fn, rd=[par, dsm], wr=[dsm])
        ac(lambda e: e.activation(out=sl(DT), in_=ldt, func=AF.Exp))
        dv(lambda e: e.tensor_tensor(out=sl(A_), in0=lre, in1=sl(DT), op=ALU.mult))
        dv(lambda e: e.tensor_tensor(out=sl(TH), in0=lim, in1=sl(DT), op=ALU.mult))
        dv(lambda e: e.tensor_scalar(out=sl(NA), in0=sl(A_), scalar1=-1.0, scalar2=None, op0=ALU.mult))
        smi = sb(st, "s5smi", [128, 8], mybir.dt.int32)
        range_reduce(k, sl(T1), sl(TH), 0.0, smi[:], sl(T2), [par, dsm, smi], [dsm, smi])
        ac(lambda e: e.activation(out=sl(SN), in_=sl(T1), func=AF.Sin))
        range_reduce(k, sl(T1), sl(TH), 0.5 * PI, smi[:], sl(T2), [par, dsm, smi], [dsm, smi])
        ac(lambda e: e.activation(out=sl(CS), in_=sl(T1), func=AF.Sin))
        ac(lambda e: e.activation(out=sl(EM), in_=sl(A_), func=AF.Exp))
        dv(lambda e: e.tensor_tensor(out=sl(LBR), in0=sl(EM), in1=sl(CS), op=ALU.mult))
        dv(lambda e: e.tensor_tensor(out=sl(LBI), in0=sl(EM), in1=sl(SN), op=ALU.mult))
        dv(lambda e: e.tensor_tensor(out=sl(DEN), in0=lre, in1=lre, op=ALU.mult))
        dv(lambda e: e.tensor_tensor(out=sl(T1), in0=lim, in1=lim, op=ALU.mult))
        dv(lambda e: e.tensor_tensor(out=sl(DEN), in0=sl(DEN), in1=sl(T1), op=ALU.add))
        dv(lambda e: e.reciprocal(out=sl(DEN), in_=sl(DEN)))
        dv(lambda e: e.tensor_scalar(out=sl(T2), in0=sl(LBR), scalar1=-1.0, scalar2=None, op0=ALU.add))
        dv(lambda e: e.tensor_tensor(out=sl(FRE), in0=sl(T2), in1=lre, op=ALU.mult))
        dv(lambda e: e.tensor_tensor(out=sl(T1), in0=sl(LBI), in1=lim, op=ALU.mult))
        dv(lambda e: e.tensor_tensor(out=sl(FRE), in0=sl(FRE), in1=sl(T1), op=ALU.add))
        dv(lambda e: e.tensor_tensor(out=sl(FRE), in0=sl(FRE), in1=sl(DEN), op=ALU.mult))
        dv(lambda e: e.tensor_tensor(out=sl(FIM), in0=sl(LBI), in1=lre, op=ALU.mult))
        dv(lambda e: e.tensor_tensor(out=sl(T1), in0=sl(T2), in1=lim, op=ALU.mult))
        dv(lambda e: e.tensor_tensor(out=sl(FIM), in0=sl(FIM), in1=sl(T1), op=ALU.subtract))
        dv(lambda e: e.tensor_tensor(out=sl(FIM), in0=sl(FIM), in1=sl(DEN), op=ALU.mult))
        dv(lambda e: e.tensor_scalar(out=sl(NFRE), in0=sl(FRE), scalar1=-1.0, scalar2=None, op0=ALU.mult))
        range_reduce(k, sl(THR), sl(TH), 0.0, smi[:], sl(T1), [par, dsm, smi], [dsm, smi])
        idx = sb(st, "s5idx", [128, 129], F32)
        I("dve", lambda e: e.memset(idx[:], 1.0), wr=[idx])
        I("dve", lambda e: e.tensor_tensor_scan(out=idx[:], data0=idx[:], data1=idx[:], initial=-1.0,
                                                op0=ALU.mult, op1=ALU.add), rd=[idx], wr=[idx])
        tabr = sb(st, "s5tabr", [128, 8, 129], F32)
        tabi = sb(st, "s5tabi", [128, 8, 129], F32)
        tifr = sb(st, "s5tifr", [128, 8, 128], F32)
        tifi = sb(st, "s5tifi", [128, 8, 128], F32)
        ntl = sb(st, "s5ntl", [128, 8], F32)
        w1 = sb(st, "s5w1", [128, 129], F32)
        w2 = sb(st, "s5w2", [128, 129], F32)
        w3 = sb(st, "s5w3", [128, 129], F32)
        w4 = sb(st, "s5w4", [128, 129], F32)
        w5 = sb(st, "s5w5", [128, 129], F32)
        wi = sb(st, "s5wi", [128, 129], mybir.dt.int32)
        tabs = [tabr, tabi, tifr, tifi]
        wk = [w1, w2, w3, w4, w5, idx]
        for rt in range(8):
            def col(i):
                return sm[:, i, rt:rt + 1]
            def dvw(fn):
                I("dve", fn, rd=wk + [dsm], wr=wk[:5] + tabs)
            def acw(fn):
                I("act", fn, rd=wk + [dsm], wr=wk[:5] + tabs)
            def plw(fn):
                I("pool", fn, rd=wk + [dsm], wr=wk[:5] + tabs)
            dvw(lambda e: e.tensor_scalar(out=w1[:], in0=idx[:], scalar1=col(THR), scalar2=None, op0=ALU.mult))
            range_reduce(k, w2[:], w1[:], 0.0, wi[:], w5[:], wk + [dsm, wi], wk[:5] + tabs + [wi])
            acw(lambda e: e.activation(out=w2[:], in_=w2[:], func=AF.Sin))
            range_reduce(k, w3[:], w1[:], 0.5 * PI, wi[:], w5[:], wk + [dsm, wi], wk[:5] + tabs + [wi])
            acw(lambda e: e.activation(out=w3[:], in_=w3[:], func=AF.Sin))
            acw(lambda e: e.activation(out=w4[:], in_=idx[:], func=AF.Exp, scale=col(A_)))
            acw(lambda e: e.activation(out=w5[:], in_=idx[:], func=AF.Exp, scale=col(NA)))
            dvw(lambda e: e.tensor_tensor(out=tabr[:, rt, :], in0=w4[:], in1=w3[:], op=ALU.mult))
            dvw(lambda e: e.tensor_tensor(out=tabi[:, rt, :], in0=w4[:], in1=w2[:], op=ALU.mult))
            dvw(lambda e: e.tensor_scalar(out=w1[:], in0=w3[:], scalar1=col(FRE), scalar2=None, op0=ALU.mult))
            dvw(lambda e: e.scalar_tensor_tensor(out=w1[:], in0=w2[:], scalar=col(FIM), in1=w1[:], op0=ALU.mult, op1=ALU.add))
            dvw(lambda e: e.tensor_tensor(out=tifr[:, rt, :], in0=w1[:, 0:128], in1=w5[:, 0:128], op=ALU.mult))
            dvw(lambda e: e.tensor_scalar(out=w1[:], in0=w3[:], scalar1=col(FIM), scalar2=None, op0=ALU.mult))
            dvw(lambda e: e.scalar_tensor_tensor(out=w1[:], in0=w2[:], scalar=col(NFRE), in1=w1[:], op0=ALU.mult, op1=ALU.add))
            dvw(lambda e: e.tensor_tensor(out=tifi[:, rt, :], in0=w1[:, 0:128], in1=w5[:, 0:128], op=ALU.mult))
        I("dve", lambda e: e.tensor_scalar(out=ntl[:], in0=tabi[:, :, 128], scalar1=-1.0, scalar2=None, op0=ALU.mult),
          rd=tabs, wr=[ntl])
        uf = sb(st, "s5uf", [128, 2, T], F32)
        ub = sb(st, "s5ub", [128, 2, T], BF16)
        S.dma("sp", uf[:], k.uTf[:], rd=[k.uTf], wr=[uf])
        I("pool", lambda e: e.tensor_copy(ub[:], uf[:]), rd=[uf], wr=[ub])
        ygf = sb(st, "s5ygf", [128, 2, T], F32)
        ygb = sb(st, "s5ygb", [128, 2, T], BF16)
        q0 = sb(st, "s5q0", [128, 8, 2], F32)
        I("dve", lambda e: e.memset(q0[:], 0.0), wr=[q0])
        rawp = Ring([ps(st, "s5rp%d" % i, [128, 512], F32) for i in range(4)])
        yp = Ring([ps(st, "s5yp%d" % i, [128, 512], F32) for i in range(2)])
        rr = Ring([sb(st, "s5rr%d" % i, [128, 2, 512], F32) for i in range(2)])
        zz = Ring([sb(st, "s5zz%d" % i, [128, 2, 512], F32) for i in range(2)])
        mmr = Ring([sb(st, "s5mm%d" % i, [128, 4, 512], F32) for i in range(2)])
        qq = Ring([sb(st, "s5qq%d" % i, [128, 2, 512], F32) for i in range(2)])
        xb = Ring([sb(st, "s5xb%d" % i, [128, 2, 512], BF16) for i in range(3)])
        ew = Ring([sb(st, "s5ew%d" % i, [128, 3, 512], F32) for i in range(2)])
        tq = sb(st, "s5tq", [128, 2], F32)

        def b4(tab, rt):
            return tab[:, rt:rt + 1, 0:128].broadcast_to([128, 4, 128])

        def v4(ap):
            return ap.rearrange("p (a b) -> p a b", a=4)
        for oc in range(2):
            for tt in range(NT):
                tsl = slice(tt * 512, (tt + 1) * 512)
                ypt = yp.next()
                for r4 in range(4):
                    rt = oc * 4 + r4
                    pr, pi_ = rawp.next(), rawp.next()
                    mm(k, pr[:], Bb[:, rt, 0, :], ub[:, oc, tsl], True, True, rd=[Bb, ub], wr=[pr])
                    mm(k, pi_[:], Bb[:, rt, 1, :], ub[:, oc, tsl], True, True, rd=[Bb, ub], wr=[pi_])
                    r_ = rr.next()
                    I("act", lambda e: e.activation(out=r_[:, 0, :], in_=pr[:], func=AF.Copy), rd=[pr], wr=[r_])
                    I("act", lambda e: e.activation(out=r_[:, 1, :], in_=pi_[:], func=AF.Copy), rd=[pi_], wr=[r_])
                    z_, m_ = zz.next(), mmr.next()
                    I("dve", lambda e: e.tensor_tensor(out=v4(m_[:, 0, :]), in0=v4(r_[:, 0, :]), in1=b4(tifr, rt), op=ALU.mult),
                      rd=[r_] + tabs, wr=[m_])
                    I("dve", lambda e: e.tensor_tensor(out=v4(m_[:, 1, :]), in0=v4(r_[:, 1, :]), in1=b4(tifi, rt), op=ALU.mult),
                      rd=[r_] + tabs, wr=[m_])
                    I("pool", lambda e: e.tensor_tensor(out=v4(m_[:, 2, :]), in0=v4(r_[:, 0, :]), in1=b4(tifi, rt), op=ALU.mult),
                      rd=[r_] + tabs, wr=[m_])
                    I("pool", lambda e: e.tensor_tensor(out=v4(m_[:, 3, :]), in0=v4(r_[:, 1, :]), in1=b4(tifr, rt), op=ALU.mult),
                      rd=[r_] + tabs, wr=[m_])
                    I("dve", lambda e: e.tensor_tensor(out=z_[:, 0, :], in0=m_[:, 0, :], in1=m_[:, 1, :], op=ALU.subtract),
                      rd=[m_], wr=[z_])
                    I("pool", lambda e: e.tensor_tensor(out=z_[:, 1, :], in0=m_[:, 2, :], in1=m_[:, 3, :], op=ALU.add),
                      rd=[m_], wr=[z_])
                    q_ = qq.next()
                    for sc in range(4):
                        csl = slice(sc * 128, (sc + 1) * 128)
                        for ri in range(2):
                            I("dve", lambda e: e.tensor_tensor_scan(out=q_[:, ri, csl], data0=k.ones[:, 0:128], data1=z_[:, ri, csl],
                                                                    initial=q0[:, rt, ri:ri + 1], op0=ALU.mult, op1=ALU.add),
                              rd=[z_, q0, k.ones], wr=[q_])
                        qe_r = q_[:, 0, sc * 128 + 127:sc * 128 + 128]
                        qe_i = q_[:, 1, sc * 128 + 127:sc * 128 + 128]
                        lr_ = tabr[:, rt, 128:129]
                        li_ = tabi[:, rt, 128:129]
                        I("dve", lambda e: e.tensor_scalar(out=tq[:, 0:1], in0=qe_r, scalar1=lr_, scalar2=None, op0=ALU.mult),
                          rd=[q_] + tabs, wr=[tq])
                        I("dve", lambda e: e.tensor_scalar(out=tq[:, 1:2], in0=qe_i, scalar1=lr_, scalar2=None, op0=ALU.mult),
                          rd=[q_] + tabs, wr=[tq])
                        I("dve", lambda e: e.scalar_tensor_tensor(out=q0[:, rt, 0:1], in0=qe_i, scalar=ntl[:, rt:rt + 1], in1=tq[:, 0:1],
                                                                  op0=ALU.mult, op1=ALU.add), rd=[q_, ntl, tq], wr=[q0])
                        I("dve", lambda e: e.scalar_tensor_tensor(out=q0[:, rt, 1:2], in0=qe_r, scalar=li_, in1=tq[:, 1:2],
                                                                  op0=ALU.mult, op1=ALU.add), rd=[q_, tq] + tabs, wr=[q0])
                    m2 = mmr.next()
                    x_ = xb.next()
                    I("dve", lambda e: e.tensor_tensor(out=v4(m2[:, 0, :]), in0=v4(q_[:, 0, :]), in1=b4(tabr, rt), op=ALU.mult),
                      rd=[q_] + tabs, wr=[m2])
                    I("dve", lambda e: e.tensor_tensor(out=v4(m2[:, 1, :]), in0=v4(q_[:, 1, :]), in1=b4(tabi, rt), op=ALU.mult),
                      rd=[q_] + tabs, wr=[m2])
                    I("pool", lambda e: e.tensor_tensor(out=v4(m2[:, 2, :]), in0=v4(q_[:, 0, :]), in1=b4(tabi, rt), op=ALU.mult),
                      rd=[q_] + tabs, wr=[m2])
                    I("pool", lambda e: e.tensor_tensor(out=v4(m2[:, 3, :]), in0=v4(q_[:, 1, :]), in1=b4(tabr, rt), op=ALU.mult),
                      rd=[q_] + tabs, wr=[m2])
                    I("dve", lambda e: e.tensor_tensor(out=x_[:, 0, :], in0=m2[:, 0, :], in1=m2[:, 1, :], op=ALU.subtract),
                      rd=[m2], wr=[x_])
                    I("pool", lambda e: e.tensor_tensor(out=x_[:, 1, :], in0=m2[:, 2, :], in1=m2[:, 3, :], op=ALU.add),
                      rd=[m2], wr=[x_])
                    mm(k, ypt[:], Cb[:, rt, 0, :], x_[:, 0, :], r4 == 0, False, rd=[Cb, x_], wr=[ypt])
                    mm(k, ypt[:], Cb[:, rt, 1, :], x_[:, 1, :], False, r4 == 3, rd=[Cb, x_], wr=[ypt])
                e_ = ew.next()
                I("dve", lambda e: e.scalar_tensor_tensor(out=e_[:, 0, :], in0=uf[:, oc, tsl], scalar=Dc[:, oc:oc + 1], in1=ypt[:],
                                                          op0=ALU.mult, op1=ALU.add), rd=[uf, Dc, ypt], wr=[e_])
                I("pool", lambda e: e.tensor_tensor(out=e_[:, 1, :], in0=e_[:, 0, :], in1=e_[:, 0, :], op=ALU.mult), rd=[e_], wr=[e_])
                I("pool", lambda e: e.tensor_scalar(out=e_[:, 1, :], in0=e_[:, 1, :], scalar1=0.044715, scalar2=1.0,
                                                    op0=ALU.mult, op1=ALU.add), rd=[e_], wr=[e_])
                I("pool", lambda e: e.tensor_tensor(out=e_[:, 1, :], in0=e_[:, 1, :], in1=e_[:, 0, :], op=ALU.mult), rd=[e_], wr=[e_])
                I("act", lambda e: e.activation(out=e_[:, 2, :], in_=e_[:, 1, :], func=AF.Sigmoid, scale=1.5957691216057308),
                  rd=[e_], wr=[e_])
                I("dve", lambda e: e.tensor_tensor(out=ygf[:, oc, tsl], in0=e_[:, 0, :], in1=e_[:, 2, :], op=ALU.mult),
                  rd=[e_], wr=[ygf])
                I("pool", lambda e: e.tensor_copy(ygb[:, oc, tsl], ygf[:, oc, tsl]), rd=[ygf], wr=[ygb])
        ob = Ring([sb(st, "s5ob%d" % i, [128, 512], BF16) for i in range(2)])
        for oc in range(2):
            for tt in range(NT):
                tsl = slice(tt * 512, (tt + 1) * 512)
                pt = yp.next()
                for kc in range(2):
                    mm(k, pt[:], gwb[:, kc, oc * 128:(oc + 1) * 128], ygb[:, kc, tsl], kc == 0, kc == 1, rd=[gwb, ygb], wr=[pt])
                e_ = ew.next()
                I("act", lambda e: e.activation(out=e_[:, 0, :], in_=pt[:], func=AF.Sigmoid, bias=gb[:, oc:oc + 1], scale=1.0),
                  rd=[pt, gb], wr=[e_])
                o_ = ob.next()
                I("dve", lambda e: e.tensor_tensor(out=o_[:], in0=e_[:, 0, :], in1=ygf[:, oc, tsl], op=ALU.mult),
                  rd=[e_, ygf], wr=[o_])
                S.dma("sp", k.ycat[0, :, oc, tsl], o_[:], rd=[o_], wr=[k.ycat])


def attn_phase(k, l, kind):
    S, I = k.S, k.S.I
    fox = (kind == 0)
    VW = 65 if fox else 64
    with phase(k) as st:
        sb, ps = k.sb, k.ps
        qT = sb(st, "atq", [128, 2, T], BF16)
        kT = sb(st, "atk", [128, 2, T], BF16)
        S.dma("sp", qT[:], k.qk[2 * kind], rd=[k.qk], wr=[qT])
        S.dma("sp", kT[:], k.qk[2 * kind + 1], rd=[k.qk], wr=[kT])
        V = sb(st, "atv", [128, NB, 4, VW], BF16)
        if fox:
            I("pool", lambda e: e.memset(V[:, :, :, 64:65], 1.0), wr=[V])
        S.dma("sp", V[:, :, :, 0:64], k.vv[kind].rearrange("p b (h d) -> p b h d", h=4), rd=[k.vv], wr=[V])
        yT = sb(st, "atyT", [128, 2, T], BF16)
        ytok = sb(st, "atytok", [128, NB, 256], BF16)
        A = Ring([sb(st, "atA%d" % i, [128, T], F32) for i in range(2)])
        P = Ring([sb(st, "atP%d" % i, [128, T], BF16) for i in range(2)])
        PT = Ring([sb(st, "atPT%d" % i, [128, 4, 128], BF16) for i in range(3)])
        zp = Ring([ps(st, "atzp%d" % i, [128, 512], F32) for i in range(2)])
        tp = Ring([ps(st, "attp%d" % i, [128, 4, 128], BF16) for i in range(2)])
        op = Ring([ps(st, "atop%d" % i, [128, VW], F32) for i in range(2)])
        xp = ps(st, "atxp", [128, 512], F32)
        nb = Ring([sb(st, "atnb%d" % i, [128, 2], F32) for i in range(3)])
        if fox:
            negc = sb(st, "atnegc", [4, T], F32)
            S.dma("sp", negc[:], k.negc[:], rd=[k.negc], wr=[negc])
            crow = sb(st, "atcrow", [128, T], F32)
        else:
            spb = Ring([sb(st, "atsp%d" % i, [128, 512], F32) for i in range(2)])
            Fb = Ring([sb(st, "atF%d" % i, [128, 512], F32) for i in range(2)])
            t1b = Ring([sb(st, "att1%d" % i, [128, 512], F32) for i in range(2)])
        items = [(h, qb) for h in range(4) for qb in range(NB)]
        state = {}

        def stage1(h, qb):
            hc, po = h // 2, (h % 2) * 64
            if fox and qb == 0:
                for tt in range(NT):
                    mm(k, xp[:], k.sel4[:, h, :], negc[:, tt * 512:(tt + 1) * 512], True, True, rd=[k.sel4, negc], wr=[xp])
                    I("act", lambda e: e.activation(out=crow[:, tt * 512:(tt + 1) * 512], in_=xp[:], func=AF.Copy),
                      rd=[xp], wr=[crow])
            nk = qb + 1
            a_ = A.next()
            nb_ = nb.next()
            carry = None
            for kt in range((nk + 3) // 4):
                w = min(4, nk - 4 * kt) * 128
                ks = slice(kt * 512, kt * 512 + w)
                last = (kt == (nk + 3) // 4 - 1)
                z = zp.next()
                mm(k, z[:, 0:w], qT[po:po + 64, hc, qb * 128:(qb + 1) * 128], kT[po:po + 64, hc, ks], True, True,
                   rd=[qT, kT], wr=[z])
                if fox:
                    I("dve", lambda e: e.tensor_tensor(out=a_[:, ks], in0=z[:, 0:w], in1=crow[:, ks], op=ALU.add),
                      rd=[z, crow], wr=[a_])
                else:
                    sp_ = spb.next()
                    I("act", lambda e: e.activation(out=sp_[:, 0:w], in_=z[:, 0:w], func=AF.Exp), rd=[z], wr=[sp_])
                    I("act", lambda e: e.activation(out=sp_[:, 0:w], in_=sp_[:, 0:w], func=AF.Ln, bias=k.one1[:, 0:1], scale=1.0),
                      rd=[sp_, k.one1], wr=[sp_])
                    if last:
                        I("pool", lambda e: e.tensor_tensor(out=sp_[:, w - 128:w], in0=sp_[:, w - 128:w], in1=k.m01[:], op=ALU.mult),
                          rd=[sp_, k.m01], wr=[sp_])
                    f_ = Fb.next()
                    init = 0.0 if carry is None else carry[0][:, carry[1] - 1:carry[1]]
                    rdl = [sp_, k.ones] + ([carry[0]] if carry is not None else [])
                    I("dve", lambda e: e.tensor_tensor_scan(out=f_[:, 0:w], data0=k.ones[:, 0:w], data1=sp_[:, 0:w], initial=init,
                                                            op0=ALU.mult, op1=ALU.add), rd=rdl, wr=[f_])
                    carry = (f_, w)
                    t_ = t1b.next()
                    I("dve", lambda e: e.tensor_tensor(out=t_[:, 0:w], in0=z[:, 0:w], in1=sp_[:, 0:w], op=ALU.subtract),
                      rd=[z, sp_], wr=[t_])
                    I("pool", lambda e: e.tensor_tensor(out=a_[:, ks], in0=t_[:, 0:w], in1=f_[:, 0:w], op=ALU.add),
                      rd=[t_, f_], wr=[a_])
            dsl = slice(qb * 128, (qb + 1) * 128)
            I("pool", lambda e: e.tensor_tensor(out=a_[:, dsl], in0=a_[:, dsl], in1=(k.nmi if fox else k.nms)[:], op=ALU.add),
              rd=[a_, k.nmi, k.nms], wr=[a_])
            if fox:
                I("dve", lambda e: e.reduce_max(out=nb_[:, 0:1], in_=a_[:, 0:nk * 128], axis=AX.X), rd=[a_], wr=[nb_])
                I("dve", lambda e: e.tensor_scalar(out=nb_[:, 1:2], in0=nb_[:, 0:1], scalar1=-1.0, scalar2=None, op0=ALU.mult),
                  rd=[nb_], wr=[nb_])
            else:
                I("dve", lambda e: e.tensor_scalar(out=nb_[:, 1:2], in0=carry[0][:, carry[1] - 1:carry[1]], scalar1=-1.0, scalar2=None,
                                                   op0=ALU.mult), rd=[carry[0]], wr=[nb_])
            state[(h, qb)] = (a_, nb_)

        def stage2(h, qb):
            a_, nb_ = state.pop((h, qb))
            nk = qb + 1
            p_ = P.next()
            I("act", lambda e: e.activation(out=p_[:, 0:nk * 128], in_=a_[:, 0:nk * 128], func=AF.Exp, bias=nb_[:, 1:2], scale=1.0),
              rd=[a_, nb_], wr=[p_])
            o_ = op.next()
            for kt in range((nk + 3) // 4):
                nbk = min(4, nk - 4 * kt)
                t_ = tp.next()
                for j in range(nbk):
                    kb = kt * 4 + j
                    tr(k, t_[:, j, :], p_[:, kb * 128:(kb + 1) * 128], k.identb[:], rd=[p_, k.identb], wr=[t_])
                pt_ = PT.next()
                if kt % 2 == 0:
                    I("act", lambda e: e.activation(out=pt_[:, 0:nbk, :], in_=t_[:, 0:nbk, :], func=AF.Copy), rd=[t_], wr=[pt_])
                else:
                    I("dve", lambda e: e.tensor_copy(pt_[:, 0:nbk, :], t_[:, 0:nbk, :]), rd=[t_], wr=[pt_])
                for j in range(nbk):
                    kb = kt * 4 + j
                    mm(k, o_[:], pt_[:, j, :], V[:, kb, h, :], kb == 0, kb == nk - 1, rd=[pt_, V], wr=[o_])
            if fox:
                I("dve", lambda e: e.reciprocal(out=nb_[:, 0:1], in_=o_[:, 64:65]), rd=[o_], wr=[nb_])
                I("act", lambda e: e.activation(out=ytok[:, qb, h * 64:(h + 1) * 64], in_=o_[:, 0:64], func=AF.Copy,
                                                scale=nb_[:, 0:1]), rd=[o_, nb_], wr=[ytok])
            else:
                I("act", lambda e: e.activation(out=ytok[:, qb, h * 64:(h + 1) * 64], in_=o_[:, 0:64], func=AF.Copy),
                  rd=[o_], wr=[ytok])

        for i, it in enumerate(items):
            stage1(*it)
            if i > 0:
                stage2(*items[i - 1])
        stage2(*items[-1])
        for qb in range(NB):
            t_ = tp.next()
            for c in range(2):
                tr(k, t_[:, c, :], ytok[:, qb, c * 128:(c + 1) * 128], k.identb[:], rd=[ytok, k.identb], wr=[t_])
            I("act", lambda e: e.activation(out=yT[:, :, qb * 128:(qb + 1) * 128], in_=t_[:, 0:2, :], func=AF.Copy),
              rd=[t_], wr=[yT])
        S.dma("sp", k.ycat[1 if fox else 3], yT[:], rd=[yT], wr=[k.ycat])


def rwkv_prep(k, l, W):
    S, I = k.S, k.S.I
    with phase(k) as st:
        sb, ps = k.sb, k.ps
        rvec = sb(st, "rpvec", [128, 2, 7], F32)
        S.dma("sp", rvec[:], W["rvec"], wr=[rvec])
        nw0 = sb(st, "rpnw0", [128, 2], F32)
        I("dve", lambda e: e.tensor_scalar(out=nw0[:], in0=rvec[:, :, 0], scalar1=-1.0, scalar2=None, op0=ALU.mult),
          rd=[rvec], wr=[nw0])
        wa2b = sb(st, "rpwa2b", [128, 256], BF16)
        g2b = sb(st, "rpg2b", [128, 256], BF16)
        tw = sb(st, "rptw", [128, T], BF16)
        gs = sb(st, "rpgs", [128, T], BF16)
        with phase(k) as st2:
            wa2f = sb(st2, "rpwa2f", [128, 256], F32)
            g2f = sb(st2, "rpg2f", [128, 256], F32)
            x6 = sb(st2, "rpx6", [128, T], F32)
            x7 = sb(st2, "rpx7", [128, T], F32)
            S.dma("sp", wa2f[:], W["wa2"], wr=[wa2f])
            S.dma("sp", g2f[:], W["g2"], wr=[g2f])
            S.dma("sp", x6[:], k.rwxs[6], rd=[k.rwxs], wr=[x6])
            S.dma("sp", x7[:], k.rwxs[7], rd=[k.rwxs], wr=[x7])
            I("pool", lambda e: e.tensor_copy(wa2b[:], wa2f[:]), rd=[wa2f], wr=[wa2b])
            I("pool", lambda e: e.tensor_copy(g2b[:], g2f[:]), rd=[g2f], wr=[g2b])
            I("act", lambda e: e.activation(out=tw[0:64, :], in_=x6[0:64, :], func=AF.Tanh), rd=[x6], wr=[tw])
            I("pool", lambda e: e.tensor_copy(tw[64:128, :], x6[64:128, :]), rd=[x6], wr=[tw])
            I("act", lambda e: e.activation(out=gs[:], in_=x7[:], func=AF.Sigmoid), rd=[x7], wr=[gs])
        for oc in range(2):
            S.dma("sp", k.sc5[oc, :, 4, :], k.rwxs[oc], rd=[k.rwxs], wr=[k.sc5])
        ld = Ring([sb(st, "rpld%d" % i, [128, 3, 512], F32) for i in range(2)])
        o5 = Ring([sb(st, "rpo5%d" % i, [128, 4, 512], F32) for i in range(2)])
        wk = Ring([sb(st, "rpwk%d" % i, [128, 6, 512], F32) for i in range(2)])
        so = Ring([sb(st, "rpso%d" % i, [128, 3, 512], F32) for i in range(2)])
        pp = Ring([ps(st, "rppp%d" % i, [128, 512], F32) for i in range(6)])
        for oc in range(2):
            osl = slice(oc * 128, (oc + 1) * 128)
            for tt in range(NT):
                tsl = slice(tt * 512, (tt + 1) * 512)
                x_ = ld.next()
                for i, j in enumerate((oc, 2 + oc, 4 + oc)):
                    S.dma("sp", x_[:, i, :], k.rwxs[j, :, tsl], rd=[k.rwxs], wr=[x_])
                r_t, k_t, v_t = x_[:, 0, :], x_[:, 1, :], x_[:, 2, :]
                o_ = o5.next()
                w_ = wk.next()
                s_ = so.next()
                pw = pp.next()
                mm(k, pw[:], wa2b[0:64, osl], tw[0:64, tsl], True, True, rd=[wa2b, tw], wr=[pw])
                I("act", lambda e: e.activation(out=w_[:, 0, :], in_=pw[:], func=AF.Exp, bias=nw0[:, oc:oc + 1], scale=-1.0),
                  rd=[pw, nw0], wr=[w_])
                I("act", lambda e: e.activation(out=w_[:, 0, :], in_=w_[:, 0, :], func=AF.Ln, bias=k.one1[:, 0:1], scale=1.0),
                  rd=[w_, k.one1], wr=[w_])
                I("act", lambda e: e.activation(out=w_[:, 0, :], in_=w_[:, 0, :], func=AF.Exp, bias=k.mhalf[:, 0:1], scale=-1.0),
                  rd=[w_, k.mhalf], wr=[w_])
                I("pool", lambda e: e.tensor_scalar(out=o_[:, 0, :], in0=w_[:, 0, :], scalar1=-1.0, scalar2=None, op0=ALU.mult),
                  rd=[w_], wr=[o_])
                pa = pp.next()
                mm(k, pa[:], wa2b[64:128, osl], tw[64:128, tsl], True, True, rd=[wa2b, tw], wr=[pa])
                I("act", lambda e: e.activation(out=w_[:, 1, :], in_=pa[:], func=AF.Sigmoid, bias=rvec[:, oc, 1:2], scale=1.0),
                  rd=[pa, rvec], wr=[w_])
                a_t = w_[:, 1, :]
                I("dve", lambda e: e.tensor_scalar(out=w_[:, 2, :], in0=k_t, scalar1=rvec[:, oc, 2:3], scalar2=None, op0=ALU.mult),
                  rd=[x_, rvec], wr=[w_])
                I("pool", lambda e: e.tensor_tensor(out=w_[:, 3, :], in0=w_[:, 2, :], in1=w_[:, 2, :], op=ALU.mult), rd=[w_], wr=[w_])
                pn = pp.next()
                mm(k, pn[:], k.blk64[:], w_[:, 3, :], True, True, rd=[k.blk64, w_], wr=[pn])
                I("act", lambda e: e.activation(out=w_[:, 3, :], in_=pn[:], func=AF.Sqrt), rd=[pn], wr=[w_])
                I("dve", lambda e: e.tensor_scalar(out=w_[:, 3, :], in0=w_[:, 3, :], scalar1=1e-12, scalar2=None, op0=ALU.max),
                  rd=[w_], wr=[w_])
                I("dve", lambda e: e.reciprocal(out=w_[:, 3, :], in_=w_[:, 3, :]), rd=[w_], wr=[w_])
                I("dve", lambda e: e.tensor_tensor(out=w_[:, 2, :], in0=w_[:, 2, :], in1=w_[:, 3, :], op=ALU.mult), rd=[w_], wr=[w_])
                I("pool", lambda e: e.tensor_scalar(out=o_[:, 3, :], in0=w_[:, 2, :], scalar1=-1.0, scalar2=None, op0=ALU.mult),
                  rd=[w_], wr=[o_])
                I("pool", lambda e: e.tensor_tensor(out=o_[:, 1, :], in0=w_[:, 2, :], in1=a_t, op=ALU.mult), rd=[w_], wr=[o_])
                I("dve", lambda e: e.tensor_scalar(out=w_[:, 4, :], in0=a_t, scalar1=1.0, scalar2=rvec[:, oc, 3:4],
                                                   op0=ALU.subtract, op1=ALU.mult), rd=[w_, rvec], wr=[w_])
                I("dve", lambda e: e.scalar_tensor_tensor(out=o_[:, 2, :], in0=w_[:, 4, :], scalar=1.0, in1=k_t,
                                                          op0=ALU.add, op1=ALU.mult), rd=[w_, x_], wr=[o_])
                S.dma("sp", k.sc5[oc, :, 0:4, tsl], o_[:], rd=[o_], wr=[k.sc5])
                I("pool", lambda e: e.tensor_tensor(out=w_[:, 5, :], in0=r_t, in1=o_[:, 2, :], op=ALU.mult), rd=[x_, o_], wr=[w_])
                I("pool", lambda e: e.tensor_scalar(out=w_[:, 5, :], in0=w_[:, 5, :], scalar1=rvec[:, oc, 4:5], scalar2=None,
                                                    op0=ALU.mult), rd=[w_, rvec], wr=[w_])
                pb = pp.next()
                mm(k, pb[:], k.blk64[:], w_[:, 5, :], True, True, rd=[k.blk64, w_], wr=[pb])
                I("dve", lambda e: e.tensor_tensor(out=s_[:, 0, :], in0=pb[:], in1=v_t, op=ALU.mult), rd=[pb, x_], wr=[s_])
                S.dma("sp", k.bon[oc, :, tsl], s_[:, 0, :], rd=[s_], wr=[k.bon])
                pg = pp.next()
                mm(k, pg[:], g2b[:, osl], gs[:, tsl], True, True, rd=[g2b, gs], wr=[pg])
                I("act", lambda e: e.activation(out=s_[:, 1, :], in_=pg[:], func=AF.Copy), rd=[pg], wr=[s_])
                S.dma("sp", k.gg[oc, :, tsl], s_[:, 1, :], rd=[s_], wr=[k.gg])
                pv = pp.next()
                for b in range(4):
                    tr(k, pv[:, b * 128:(b + 1) * 128], x_[:, 2, b * 128:(b + 1) * 128], k.identf[:], rd=[x_, k.identf], wr=[pv])
                I("act", lambda e: e.activation(out=s_[:, 2, :], in_=pv[:], func=AF.Copy), rd=[pv], wr=[s_])
                S.dma("sp", k.vtok[tsl, oc, :].rearrange("(b p) c -> p b c", p=128),
                      s_[:, 2, :].rearrange("p (b c) -> p b c", b=4), rd=[s_], wr=[k.vtok])


def rwkv_rec(k, l, W, nsteps=T):
    S, I = k.S, k.S.I
    VC = 16
    with phase(k) as st:
        sb, ps = k.sb, k.ps
        rvec = sb(st, "rrvec", [128, 2, 7], F32)
        S.dma("sp", rvec[:], W["rvec"], wr=[rvec])
        Sx = [Ring([sb(st, "rrS%d_%d" % (hp, i), [128, 128], F32) for i in range(3)]) for hp in range(2)]
        S1 = [Ring([sb(st, "rrS1%d_%d" % (hp, i), [128, 128], F32) for i in range(2)]) for hp in range(2)]
        KK = [Ring([sb(st, "rrKK%d_%d" % (hp, i), [128, 128], F32) for i in range(3)]) for hp in range(2)]
        S2 = [Ring([sb(st, "rrS2%d_%d" % (hp, i), [128, 128], F32) for i in range(2)]) for hp in range(2)]
        sap = [Ring([ps(st, "rrsa%d_%d" % (hp, i), [128, 128], F32) for i in range(2)]) for hp in range(2)]
        yp = [ps(st, "rryp%d" % hp, [128, 512], F32) for hp in range(2)]
        c5 = [Ring([sb(st, "rrc5%d_%d" % (hp, i), [128, 5, 512], F32) for i in range(2)]) for hp in range(2)]
        vr = [Ring([sb(st, "rrvr%d_%d" % (hp, i), [128, VC, 128], F32) for i in range(2)]) for hp in range(2)]
        ysb = [sb(st, "rry%d" % hp, [128, T], F32) for hp in range(2)]
        for hp in range(2):
            for b in vr[hp].bufs:
                I("pool", lambda e: e.memset(b[:], 0.0), wr=[b])
            b0 = Sx[hp].bufs[0]
            I("dve", lambda e: e.memset(b0[:], 0.0), wr=[b0])
            if nsteps < T:
                I("pool", lambda e: e.memset(ysb[hp][:], 0.0), wr=[ysb[hp]])
        cur = [Sx[0].next(), Sx[1].next()]
        c5c = [None, None]
        vrc = [None, None]
        for t in range(nsteps):
            tl = t % 512
            for hp in range(2):
                if tl == 0:
                    c_ = c5[hp].next()
                    S.dma("sp", c_[:], k.sc5[hp, :, :, t:t + 512], rd=[k.sc5], wr=[c_])
                    c5c[hp] = c_
                if t % VC == 0:
                    v_ = vr[hp].next()
                    for hh in range(2):
                        S.dma("sp", v_[hh * 64:(hh + 1) * 64, :, hh * 64:(hh + 1) * 64],
                              k.vtok[t:t + VC, hp, hh * 64:(hh + 1) * 64].partition_broadcast(64), rd=[k.vtok], wr=[v_])
                    vrc[hp] = v_
                c_, v_ = c5c[hp], vrc[hp]
                old = cur[hp]
                new = Sx[hp].next()
                kk_ = KK[hp].next()
                I("act", lambda e: e.activation(out=kk_[:], in_=k.blk64[:], func=AF.Copy, scale=c_[:, 3, tl:tl + 1]),
                  rd=[k.blk64, c_], wr=[kk_])
                sa = sap[hp].next()
                mm(k, sa[:], kk_[:], old[:], True, True, rd=[kk_, old], wr=[sa])
                s1 = S1[hp].next()
                s2 = S2[hp].next()
                I("act", lambda e: e.activation(out=s1[:], in_=old[:], func=AF.Copy, scale=c_[:, 0, tl:tl + 1]),
                  rd=[old, c_], wr=[s1])
                I("pool", lambda e: e.tensor_scalar(out=s2[:], in0=v_[:, t % VC, :], scalar1=c_[:, 2, tl:tl + 1], scalar2=None, op0=ALU.mult),
                  rd=[v_, c_], wr=[s2])
                I("pool", lambda e: e.tensor_tensor(out=s1[:], in0=s1[:], in1=s2[:], op=ALU.add), rd=[s1, s2], wr=[s1])
                I("dve", lambda e: e.scalar_tensor_tensor(out=new[:], in0=sa[:], scalar=c_[:, 1, tl:tl + 1], in1=s1[:],
                                                          op0=ALU.mult, op1=ALU.add), rd=[sa, c_, s1], wr=[new])
                mm(k, yp[hp][:, tl:tl + 1], new[:], c_[:, 4, tl:tl + 1], True, True, rd=[new, c_], wr=[yp[hp]])
                cur[hp] = new
                if tl == 511 or t == nsteps - 1:
                    t0 = t - tl
                    ypb = yp[hp]
                    I("act", lambda e: e.activation(out=ysb[hp][:, t0:t0 + tl + 1], in_=ypb[:, 0:tl + 1], func=AF.Copy),
                      rd=[ypb], wr=[ysb[hp]])
        ld = Ring([sb(st, "rrld%d" % i, [128, 2, 512], F32) for i in range(2)])
        wk = Ring([sb(st, "rrwk%d" % i, [128, 3, 512], F32) for i in range(2)])
        ob = Ring([sb(st, "rrob%d" % i, [128, 512], BF16) for i in range(2)])
        for hp in range(2):
            for tt in range(NT):
                tsl = slice(tt * 512, (tt + 1) * 512)
                x_ = ld.next()
                S.dma("sp", x_[:, 0, :], k.bon[hp, :, tsl], rd=[k.bon], wr=[x_])
                S.dma("sp", x_[:, 1, :], k.gg[hp, :, tsl], rd=[k.gg], wr=[x_])
                w_ = wk.next()
                pm = sap[0].next()
                pv_ = sap[1].next()
                pmt = yp[0]
                mm(k, pmt[:], k.blk64[:], ysb[hp][:, tsl], True, True, rd=[k.blk64, ysb[hp]], wr=[pmt])
                I("dve", lambda e: e.scalar_tensor_tensor(out=w_[:, 0, :], in0=pmt[:], scalar=-1.0 / 64.0, in1=ysb[hp][:, tsl],
                                                          op0=ALU.mult, op1=ALU.add), rd=[pmt, ysb[hp]], wr=[w_])
                I("act", lambda e: e.activation(out=w_[:, 1, :], in_=w_[:, 0, :], func=AF.Square), rd=[w_], wr=[w_])
                pvt = yp[1]
                mm(k, pvt[:], k.blk64[:], w_[:, 1, :], True, True, rd=[k.blk64, w_], wr=[pvt])
                I("act", lambda e: e.activation(out=w_[:, 1, :], in_=pvt[:], func=AF.Ln, bias=k.epsgn[:, 0:1], scale=1.0 / 64.0),
                  rd=[pvt, k.epsgn], wr=[w_])
                I("act", lambda e: e.activation(out=w_[:, 1, :], in_=w_[:, 1, :], func=AF.Exp, scale=-0.5), rd=[w_], wr=[w_])
                I("dve", lambda e: e.tensor_tensor(out=w_[:, 0, :], in0=w_[:, 0, :], in1=w_[:, 1, :], op=ALU.mult), rd=[w_], wr=[w_])
                I("dve", lambda e: e.tensor_scalar(out=w_[:, 0, :], in0=w_[:, 0, :], scalar1=rvec[:, hp, 5:6], scalar2=rvec[:, hp, 6:7],
                                                   op0=ALU.mult, op1=ALU.add), rd=[w_, rvec], wr=[w_])
                I("pool", lambda e: e.tensor_tensor(out=w_[:, 0, :], in0=w_[:, 0, :], in1=x_[:, 0, :], op=ALU.add), rd=[w_, x_], wr=[w_])
                o_ = ob.next()
                I("pool", lambda e: e.tensor_tensor(out=o_[:], in0=w_[:, 0, :], in1=x_[:, 1, :], op=ALU.mult), rd=[w_, x_], wr=[o_])
                S.dma("sp", k.ycat[2, :, hp, tsl], o_[:], rd=[o_], wr=[k.ycat])


def rwkv_chunked(k, l, W, nsteps=T):
    S, I = k.S, k.S.I
    C = 64
    TA = 256
    NCH = nsteps // C
    with phase(k) as st:
        sb, ps = k.sb, k.ps
        rvec = sb(st, "rcvec", [128, 2, 7], F32)
        S.dma("sp", rvec[:], W["rvec"], wr=[rvec])
        ysb = [sb(st, "rcy%d" % hp, [128, T], F32) for hp in range(2)]
        if nsteps < T:
            for hp in range(2):
                I("pool", lambda e: e.memset(ysb[hp][:], 0.0), wr=[ysb[hp]])
        _rwkv_chunked_core(k, ysb, nsteps)
        _rwkv_post(k, ysb, rvec)


def _rwkv_chunked_core(k, ysb, nsteps):
    S, I = k.S, k.S.I
    C = 64
    TA = 256
    NCH = nsteps // C
    with phase(k) as st:
        sb, ps = k.sb, k.ps
        msu = sb(st, "rcmsu", [128, 128], F32)
        msl = sb(st, "rcmsl", [128, 128], F32)
        mui = sb(st, "rcmui", [128, 128], F32)
        rmask = sb(st, "rcrmask", [128, TA], F32)
        I("pool", lambda e: e.affine_select(out=msu[:], in_=k.blk64[:], compare_op=ALU.is_gt, fill=0.0, base=0,
                                            pattern=[[1, 128]], channel_multiplier=-1), rd=[k.blk64], wr=[msu])
        I("pool", lambda e: e.affine_select(out=msl[:], in_=k.blk64[:], compare_op=ALU.is_gt, fill=0.0, base=0,
                                            pattern=[[-1, 128]], channel_multiplier=1), rd=[k.blk64], wr=[msl])
        I("pool", lambda e: e.affine_select(out=mui[:], in_=k.blk64[:], compare_op=ALU.is_ge, fill=0.0, base=0,
                                            pattern=[[1, 128]], channel_multiplier=-1), rd=[k.blk64], wr=[mui])
        I("pool", lambda e: e.memset(rmask[:], 1.0), wr=[rmask])
        I("pool", lambda e: e.memset(rmask[:].rearrange("p (c t) -> p c t", t=C)[:, :, 0:1], 0.0), wr=[rmask])
        c5 = [Ring([sb(st, "rcc5%d_%d" % (hp, i), [128, 5, TA], F32) for i in range(2)]) for hp in range(2)]
        wkA = [Ring([sb(st, "rcwa%d_%d" % (hp, i), [128, 4, TA], F32) for i in range(2)]) for hp in range(2)]
        eLr = [Ring([sb(st, "rceL%d_%d" % (hp, i), [128, TA], F32) for i in range(3)]) for hp in range(2)]
        pad = [Ring([sb(st, "rcpad%d_%d" % (hp, i), [128, 4, TA // C, 128], F32) for i in range(3)]) for hp in range(2)]
        for hp in range(2):
            for b in pad[hp].bufs:
                I("pool", lambda e: e.memset(b[:], 0.0), wr=[b])
        NPER = 12
        pers = Ring([sb(st, "rcper%d" % i, [128, 6, 128], F32) for i in range(NPER)])
        tmp = Ring([sb(st, "rctmp%d" % i, [128, 8, 128], F32) for i in range(4)])
        Vp = Ring([sb(st, "rcvp%d" % i, [128, 128], F32) for i in range(8)])
        for b in Vp.bufs:
            I("pool", lambda e: e.memset(b[:], 0.0), wr=[b])
        Sx = [Ring([sb(st, "rcS%d_%d" % (hp, i), [128, 128], F32) for i in range(3)]) for hp in range(2)]
        cw = [Ring([sb(st, "rccw%d_%d" % (hp, i), [128, 3, 128], F32) for i in range(2)]) for hp in range(2)]
        pbB = Ring([ps(st, "rcpb%d" % i, [128, 4, 128], F32) for i in range(4)])
        pbA = [ps(st, "rcpa%d" % hp, [128, 3, 128], F32) for hp in range(2)]
        pbY = [ps(st, "rcpy%d" % hp, [128, 128], F32) for hp in range(2)]
        cur = []
        for hp in range(2):
            b0 = Sx[hp].next()
            I("dve", lambda e: e.memset(b0[:], 0.0), wr=[b0])
            cur.append(b0)

        tiles = {}

        def stageA(hp, ti):
            t0 = ti * TA
            c_ = c5[hp].next()
            S.dma("sp", c_[:], k.sc5[hp, :, :, t0:t0 + TA], rd=[k.sc5], wr=[c_])
            w_ = wkA[hp].next()
            e_ = eLr[hp].next()
            p_ = pad[hp].next()
            I("dve", lambda e: e.tensor_tensor_scan(out=w_[:, 0, :], data0=rmask[:], data1=c_[:, 0, :], initial=0.0,
                                                    op0=ALU.mult, op1=ALU.add), rd=[c_, rmask], wr=[w_])
            I("pool", lambda e: e.tensor_tensor(out=w_[:, 1, :], in0=w_[:, 0, :], in1=c_[:, 0, :], op=ALU.subtract), rd=[w_, c_], wr=[w_])
            I("act", lambda e: e.activation(out=e_[:], in_=w_[:, 0, :], func=AF.Exp), rd=[w_], wr=[e_])
            I("act", lambda e: e.activation(out=w_[:, 2, :], in_=w_[:, 0, :], func=AF.Exp, scale=-1.0), rd=[w_], wr=[w_])
            I("act", lambda e: e.activation(out=w_[:, 3, :], in_=w_[:, 1, :], func=AF.Exp), rd=[w_], wr=[w_])

            def v3(ap):
                return ap.rearrange("p (c t) -> p c t", t=C)
            for hh in range(2):
                rs = slice(hh * 64, hh * 64 + 64)
                specs = [(0, c_[rs, 3, :], w_[rs, 3, :]),
                         (1, c_[rs, 4, :], e_[rs, :]),
                         (2, c_[rs, 1, :], w_[rs, 2, :]),
                         (3, c_[rs, 2, :], w_[rs, 2, :])]
                for qi, a_, b_ in specs:
                    eng = "dve" if qi % 2 == 0 else "pool"
                    I(eng, lambda e: e.tensor_tensor(out=p_[rs, qi, :, rs], in0=v3(a_), in1=v3(b_), op=ALU.mult),
                      rd=[c_, w_, e_], wr=[p_])
            tiles[(hp, ti)] = (c_, e_, p_)

        inst = {}

        def stageB(group):
            ctx = []
            for (ch, hp) in group:
                ti, ci = divmod(ch, TA // C)
                c_, e_, p_ = tiles[(hp, ti)]
                Ap, Rp, Bp, Kp = (p_[:, q, ci, :] for q in range(4))
                pr = pers.next()
                tm = tmp.next()
                pb = pbB.next()
                inst[(ch, hp)] = pr
                ctx.append((ch, hp, p_, Ap, Rp, Bp, Kp, pr, tm, pb))
            for (ch, hp, p_, Ap, Rp, Bp, Kp, pr, tm, pb) in ctx:
                mm(k, pb[:, 0, :], Bp, Ap, True, True, rd=[p_], wr=[pb])
                mm(k, pb[:, 1, :], Ap, Bp, True, True, rd=[p_], wr=[pb])
                mm(k, pb[:, 2, :], Kp, Ap, True, True, rd=[p_], wr=[pb])
                mm(k, pb[:, 3, :], Bp, Rp, True, True, rd=[p_], wr=[pb])
            for (ch, hp, p_, Ap, Rp, Bp, Kp, pr, tm, pb) in ctx:
                I("dve", lambda e: e.tensor_tensor(out=tm[:, 0, :], in0=pb[:, 0, :], in1=msu[:], op=ALU.mult), rd=[pb, msu], wr=[tm])
                I("dve", lambda e: e.tensor_tensor(out=tm[:, 1, :], in0=pb[:, 1, :], in1=msl[:], op=ALU.mult), rd=[pb, msl], wr=[tm])
                I("dve", lambda e: e.tensor_tensor(out=pr[:, 1, :], in0=pb[:, 2, :], in1=msu[:], op=ALU.mult), rd=[pb, msu], wr=[pr])
                I("dve", lambda e: e.tensor_tensor(out=pr[:, 2, :], in0=pb[:, 3, :], in1=mui[:], op=ALU.mult), rd=[pb, mui], wr=[pr])
                I("pool", lambda e: e.tensor_tensor(out=tm[:, 4, :], in0=tm[:, 0, :], in1=k.identf[:], op=ALU.add), rd=[tm, k.identf], wr=[tm])
                I("pool", lambda e: e.tensor_tensor(out=tm[:, 5, :], in0=tm[:, 1, :], in1=k.identf[:], op=ALU.add), rd=[tm, k.identf], wr=[tm])
            for (ch, hp, p_, Ap, Rp, Bp, Kp, pr, tm, pb) in ctx:
                mm(k, pb[:, 0, :], Kp, Rp, True, True, rd=[p_], wr=[pb])
                tr(k, pb[:, 1, :], Bp, k.identf[:], rd=[p_, k.identf], wr=[pb])
                tr(k, pb[:, 2, :], Kp, k.identf[:], rd=[p_, k.identf], wr=[pb])
            for (ch, hp, p_, Ap, Rp, Bp, Kp, pr, tm, pb) in ctx:
                I("dve", lambda e: e.tensor_tensor(out=pr[:, 3, :], in0=pb[:, 0, :], in1=mui[:], op=ALU.mult), rd=[pb, mui], wr=[pr])
                I("act", lambda e: e.activation(out=pr[:, 4:6, :], in_=pb[:, 1:3, :], func=AF.Copy), rd=[pb], wr=[pr])
            for j in range(1, 6):
                po, pn = (0, 2) if j % 2 == 1 else (2, 0)
                qo, qn = (1, 3) if j % 2 == 1 else (3, 1)
                go, gn = (4, 6) if j % 2 == 1 else (6, 4)
                ho, hn = (5, 7) if j % 2 == 1 else (7, 5)
                for (ch, hp, p_, Ap, Rp, Bp, Kp, pr, tm, pb) in ctx:
                    mm(k, pb[:, 0, :], tm[:, qo, :], tm[:, po, :], True, True, rd=[tm], wr=[pb])
                    if j <= 4:
                        mm(k, pb[:, 1, :], tm[:, po, :], tm[:, qo, :], True, True, rd=[tm], wr=[pb])
                for (ch, hp, p_, Ap, Rp, Bp, Kp, pr, tm, pb) in ctx:
                    if j <= 4:
                        I("act", lambda e: e.activation(out=tm[:, pn:pn + 2, :] if qn == pn + 1 else tm[:, pn, :],
                                                        in_=pb[:, 0:2, :] if qn == pn + 1 else pb[:, 0, :], func=AF.Copy), rd=[pb], wr=[tm])
                        if qn != pn + 1:
                            I("act", lambda e: e.activation(out=tm[:, qn, :], in_=pb[:, 1, :], func=AF.Copy), rd=[pb], wr=[tm])
                    else:
                        I("act", lambda e: e.activation(out=tm[:, pn, :], in_=pb[:, 0, :], func=AF.Copy), rd=[pb], wr=[tm])
                for (ch, hp, p_, Ap, Rp, Bp, Kp, pr, tm, pb) in ctx:
                    mm(k, pb[:, 2, :], tm[:, ho, :], tm[:, pn, :], True, True, rd=[tm], wr=[pb])
                    if j <= 4:
                        mm(k, pb[:, 3, :], tm[:, pn, :], tm[:, ho, :], True, True, rd=[tm], wr=[pb])
                for (ch, hp, p_, Ap, Rp, Bp, Kp, pr, tm, pb) in ctx:
                    gout = pr[:, 0, :] if j == 5 else tm[:, gn, :]
                    I("dve", lambda e: e.tensor_tensor(out=gout, in0=pb[:, 2, :], in1=tm[:, go, :], op=ALU.add), rd=[pb, tm], wr=[tm, pr])
                    if j <= 4:
                        I("dve", lambda e: e.tensor_tensor(out=tm[:, hn, :], in0=pb[:, 3, :], in1=tm[:, ho, :], op=ALU.add), rd=[pb, tm], wr=[tm])

        def stageC(ch, hp):
            ti, ci = divmod(ch, TA // C)
            c_, e_, p_ = tiles[(hp, ti)]
            Ap, Rp = p_[:, 0, ci, :], p_[:, 1, ci, :]
            pr = inst.pop((ch, hp))
            t0 = ch * C
            vp = Vp.next()
            for hh in range(2):
                rs = slice(hh * 64, hh * 64 + 64)
                S.dma("sp", vp[rs, rs], k.vtok[t0:t0 + C, hp, rs], rd=[k.vtok], wr=[vp])
            old = cur[hp]
            new = Sx[hp].next()
            w_ = cw[hp].next()
            pa, py = pbA[hp], pbY[hp]
            elc = e_[:, ci * C + C - 1:ci * C + C]
            mm(k, pa[:, 0, :], Ap, old[:], True, False, rd=[p_, old], wr=[pa])
            mm(k, pa[:, 0, :], pr[:, 1, :], vp[:], False, True, rd=[pr, vp], wr=[pa])
            I("act", lambda e: e.activation(out=w_[:, 0, :], in_=pa[:, 0, :], func=AF.Copy), rd=[pa], wr=[w_])
            I("pool", lambda e: e.tensor_scalar(out=w_[:, 2, :], in0=old[:], scalar1=elc, scalar2=None, op0=ALU.mult), rd=[old, e_], wr=[w_])
            mm(k, pa[:, 1, :], pr[:, 0, :], w_[:, 0, :], True, True, rd=[pr, w_], wr=[pa])
            I("dve", lambda e: e.tensor_copy(w_[:, 1, :], pa[:, 1, :]), rd=[pa], wr=[w_])
            mm(k, pa[:, 2, :], pr[:, 4, :], w_[:, 1, :], True, False, rd=[pr, w_], wr=[pa])
            mm(k, pa[:, 2, :], pr[:, 5, :], vp[:], False, True, rd=[pr, vp], wr=[pa])
            I("dve", lambda e: e.scalar_tensor_tensor(out=new[:], in0=pa[:, 2, :], scalar=elc, in1=w_[:, 2, :],
                                                      op0=ALU.mult, op1=ALU.add), rd=[pa, e_, w_], wr=[new])
            mm(k, py[:], old[:], Rp, True, False, rd=[old, p_], wr=[py])
            mm(k, py[:], w_[:, 1, :], pr[:, 2, :], False, False, rd=[w_, pr], wr=[py])
            mm(k, py[:], vp[:], pr[:, 3, :], False, True, rd=[vp, pr], wr=[py])
            for hh in range(2):
                rs = slice(hh * 64, hh * 64 + 64)
                I("act", lambda e: e.activation(out=ysb[hp][rs, t0:t0 + C], in_=py[rs, rs], func=AF.Copy), rd=[py], wr=[ysb[hp]])
            cur[hp] = new

        GC = 2
        ngroups = NCH // GC
        def group(g):
            return [(g * GC + c, hp) for c in range(GC) for hp in range(2)]
        def ensureA(g):
            for (ch, hp) in group(g):
                ti = ch // (TA // C)
                if (hp, ti) not in tiles:
                    stageA(hp, ti)
        ensureA(0)
        stageB(group(0))
        for g in range(ngroups):
            if g + 1 < ngroups:
                ensureA(g + 1)
                stageB(group(g + 1))
            for (ch, hp) in group(g):
                stageC(ch, hp)


def _rwkv_post(k, ysb, rvec):
    S, I = k.S, k.S.I
    with phase(k) as st:
        sb, ps = k.sb, k.ps
        pbB = Ring([ps(st, "rcpq%d" % i, [128, 4, 128], F32) for i in range(2)])
        ld = Ring([sb(st, "rcld%d" % i, [128, 2, 512], F32) for i in range(2)])
        wk = Ring([sb(st, "rcwk%d" % i, [128, 3, 512], F32) for i in range(2)])
        ob = Ring([sb(st, "rcob%d" % i, [128, 512], BF16) for i in range(2)])
        pmt, pvt = pbB.next(), pbB.next()
        pmt = Buf(pmt.t.rearrange("p a b -> p (a b)"), excl=True)
        pvt = Buf(pvt.t.rearrange("p a b -> p (a b)"), excl=True)
        for hp in range(2):
            for tt in range(NT):
                tsl = slice(tt * 512, (tt + 1) * 512)
                x_ = ld.next()
                S.dma("sp", x_[:, 0, :], k.bon[hp, :, tsl], rd=[k.bon], wr=[x_])
                S.dma("sp", x_[:, 1, :], k.gg[hp, :, tsl], rd=[k.gg], wr=[x_])
                w_ = wk.next()
                mm(k, pmt[:], k.blk64[:], ysb[hp][:, tsl], True, True, rd=[k.blk64, ysb[hp]], wr=[pmt])
                I("dve", lambda e: e.scalar_tensor_tensor(out=w_[:, 0, :], in0=pmt[:], scalar=-1.0 / 64.0, in1=ysb[hp][:, tsl],
                                                          op0=ALU.mult, op1=ALU.add), rd=[pmt, ysb[hp]], wr=[w_])
                I("act", lambda e: e.activation(out=w_[:, 1, :], in_=w_[:, 0, :], func=AF.Square), rd=[w_], wr=[w_])
                mm(k, pvt[:], k.blk64[:], w_[:, 1, :], True, True, rd=[k.blk64, w_], wr=[pvt])
                I("act", lambda e: e.activation(out=w_[:, 1, :], in_=pvt[:], func=AF.Ln, bias=k.epsgn[:, 0:1], scale=1.0 / 64.0),
                  rd=[pvt, k.epsgn], wr=[w_])
                I("act", lambda e: e.activation(out=w_[:, 1, :], in_=w_[:, 1, :], func=AF.Exp, scale=-0.5), rd=[w_], wr=[w_])
                I("dve", lambda e: e.tensor_tensor(out=w_[:, 0, :], in0=w_[:, 0, :], in1=w_[:, 1, :], op=ALU.mult), rd=[w_], wr=[w_])
                I("dve", lambda e: e.tensor_scalar(out=w_[:, 0, :], in0=w_[:, 0, :], scalar1=rvec[:, hp, 5:6], scalar2=rvec[:, hp, 6:7],
                                                   op0=ALU.mult, op1=ALU.add), rd=[w_, rvec], wr=[w_])
                I("pool", lambda e: e.tensor_tensor(out=w_[:, 0, :], in0=w_[:, 0, :], in1=x_[:, 0, :], op=ALU.add), rd=[w_, x_], wr=[w_])
                o_ = ob.next()
                I("pool", lambda e: e.tensor_tensor(out=o_[:], in0=w_[:, 0, :], in1=x_[:, 1, :], op=ALU.mult), rd=[w_, x_], wr=[o_])
                S.dma("sp", k.ycat[2, :, hp, tsl], o_[:], rd=[o_], wr=[k.ycat])


def precast_mlp(k, l, W):
    with phase(k) as st:
        cast_weight(k, st, W["w1"], k.w1s, k.w1s, 1024, 4096, "w1", to_dram=True)
    with phase(k) as st:
        cast_weight(k, st, W["w2m"], k.w2s, k.w2s, 4096, 1024, "w2", to_dram=True)


def merge_phase(k, l, W, h1T):
    S, I = k.S, k.S.I
    with phase(k) as st:
        sb, ps = k.sb, k.ps
        wbr = sb(st, "mgwbr", [128, 8, 1024], BF16)
        wout = sb(st, "mgwout", [128, 8, 1024], BF16)
        with phase(k) as st2:
            cast_weight(k, st2, W["wbr"], wbr, wbr, 1024, 1024, "br")
        with phase(k) as st2:
            cast_weight(k, st2, W["wout"], wout, wout, 1024, 1024, "wo")
        ln = sb(st, "mgln", [128, 8, 2], F32)
        S.dma("sp", ln[:], W["ln1"], wr=[ln])
        yt = Ring([sb(st, "mgyt%d" % i, [128, 4, 2, 512], BF16) for i in range(2)])
        gt = Ring([sb(st, "mggt%d" % i, [128, 4, 512], BF16) for i in range(3)])
        acc = Ring([sb(st, "mgacc%d" % i, [128, 2, 512], F32) for i in range(2)])
        mg = Ring([sb(st, "mgmg%d" % i, [128, 8, 512], BF16) for i in range(1)])
        hr = Ring([sb(st, "mghr%d" % i, [128, 8, 512], F32) for i in range(1)])
        z = Ring([sb(st, "mgz%d" % i, [128, 8, 512], F32) for i in range(1)])
        sq = sb(st, "mgsq", [128, 8, 512], F32)
        msb = sb(st, "mgmsb", [128, 512], F32)
        up = Ring([ps(st, "mgup%d" % i, [128, 512], F32) for i in range(4)])
        mps = ps(st, "mgmps", [128, 512], F32)
        vps = ps(st, "mgvps", [128, 512], F32)
        gview = k.gsc.t.rearrange("(n m) p t -> m p n t", n=4)
        for tt in range(NT):
            tsl = slice(tt * 512, (tt + 1) * 512)
            y_ = yt.next()
            for n in range(4):
                S.dma("sp", y_[:, n, :, :], k.ycat[n, :, :, tsl], rd=[k.ycat], wr=[y_])
            h_ = hr.next()
            S.dma("sp", h_[:], k.hres[:, :, tsl], rd=[k.hres], wr=[h_])
            m_ = mg.next()
            for mc in range(8):
                msl = slice(mc * 128, (mc + 1) * 128)
                g_ = gt.next()
                S.dma("sp", g_[:], gview[mc][:, :, tsl], rd=[k.gsc], wr=[g_])
                a_ = acc.next()
                for n in range(4):
                    u_ = up.next()
                    for kc in range(2):
                        mm(k, u_[:], wbr[:, n * 2 + kc, msl], y_[:, n, kc, :], kc == 0, kc == 1, rd=[wbr, y_], wr=[u_])
                    if n == 0:
                        I("dve", lambda e: e.tensor_tensor(out=a_[:, 0, :], in0=u_[:], in1=g_[:, 0, :], op=ALU.mult),
                          rd=[u_, g_], wr=[a_])
                    else:
                        I("dve", lambda e: e.tensor_tensor(out=a_[:, 1, :], in0=u_[:], in1=g_[:, n, :], op=ALU.mult),
                          rd=[u_, g_], wr=[a_])
                        if n < 3:
                            I("pool", lambda e: e.tensor_tensor(out=a_[:, 0, :], in0=a_[:, 0, :], in1=a_[:, 1, :], op=ALU.add),
                              rd=[a_], wr=[a_])
                        else:
                            I("pool", lambda e: e.tensor_tensor(out=m_[:, mc, :], in0=a_[:, 0, :], in1=a_[:, 1, :], op=ALU.add),
                              rd=[a_], wr=[m_])
            z_ = z.next()
            for mc in range(8):
                msl = slice(mc * 128, (mc + 1) * 128)
                u_ = up.next()
                for kc in range(8):
                    mm(k, u_[:], wout[:, kc, msl], m_[:, kc, :], kc == 0, kc == 7, rd=[wout, m_], wr=[u_])
                I("dve", lambda e: e.scalar_tensor_tensor(out=z_[:, mc, :], in0=h_[:, mc, :], scalar=ALPHA, in1=u_[:],
                                                          op0=ALU.mult, op1=ALU.add), rd=[h_, u_], wr=[z_])

            def outs(c, dn, g_ap, b_ap):
                I("pool", lambda e: e.tensor_scalar(out=dn, in0=dn, scalar1=g_ap, scalar2=b_ap, op0=ALU.mult, op1=ALU.add),
                  rd=[z_, ln], wr=[z_])
            layer_norm_tile(k, (mps, msb, sq, vps), z_, ln, LN_EPS, outs)
            I("act", lambda e: e.activation(out=h1T[:, :, tsl], in_=z_[:], func=AF.Copy), rd=[z_], wr=[h1T])
            S.dma("sp", k.h1res[:, :, tsl], z_[:], rd=[z_], wr=[k.h1res])


def mlp_phase(k, l, W, h1T, p_l, out, last):
    S, I = k.S, k.S.I
    with phase(k) as st:
        sb, ps = k.sb, k.ps
        pgw = sb(st, "mlpgw", [128, 8, 1024], BF16)
        plw = sb(st, "mlplw", [128, 2, 1024], BF16)
        with phase(k) as st2:
            cast_weight(k, st2, W["pgw"], pgw, pgw, 1024, 1024, "pg")
        with phase(k) as st2:
            cast_weight(k, st2, W["plew"], plw, plw, 256, 1024, "pl")
        with phase(k) as st2:
            pin = Ring([sb(st2, "mlpin%d" % i, [128, 256], F32) for i in range(2)])
            tps = Ring([ps(st2, "mltps%d" % i, [128, 2, 128], F32) for i in range(2)])
            pT = sb(st2, "mlpT", [128, 2, T], BF16)
            for blk in range(NB):
                pi_ = pin.next()
                S.dma("sp", pi_[:], p_l[blk * 128:(blk + 1) * 128, :], wr=[pi_])
                t_ = tps.next()
                for c in range(2):
                    tr(k, t_[:, c, :], pi_[:, c * 128:(c + 1) * 128], k.identf[:], rd=[pi_, k.identf], wr=[t_])
                I("act", lambda e: e.activation(out=pT[:, :, blk * 128:(blk + 1) * 128], in_=t_[:], func=AF.Copy), rd=[t_], wr=[pT])
            S.dma("sp", k.pTs[:], pT[:], rd=[pT], wr=[k.pTs])
        ln = sb(st, "mlln", [128, 8, 2], F32)
        S.dma("sp", ln[:], W["ln2"], wr=[ln])
        w1r = Ring([sb(st, "mlw1%d" % i, [128, 8, 512], BF16) for i in range(2)])
        w2r = Ring([sb(st, "mlw2%d" % i, [128, 4, 128], BF16) for i in range(4)])
        pTr = Ring([sb(st, "mlpT%d" % i, [128, 2, 512], BF16) for i in range(2)])
        a = sb(st, "mla", [128, 32, 512], BF16)
        rl = Ring([sb(st, "mlrl%d" % i, [128, 512], F32) for i in range(2)])
        hr = sb(st, "mlhr", [128, 8, 512], F32)
        z = sb(st, "mlz", [128, 8, 512], F32)
        msb = sb(st, "mlmsb", [128, 512], F32)
        sg = Ring([sb(st, "mlsg%d" % i, [128, 2, 512], F32) for i in range(2)])
        up = Ring([ps(st, "mlup%d" % i, [128, 512], F32) for i in range(3)])
        gp = Ring([ps(st, "mlgp%d" % i, [128, 512], F32) for i in range(2)])
        mps = ps(st, "mlmps", [128, 512], F32)
        vps = ps(st, "mlvps", [128, 512], F32)
        tpo = ps(st, "mltpo", [128, 4, 128], F32)
        ot = Ring([sb(st, "mlot%d" % i, [128, D], F32) for i in range(2)]) if last else None
        hb = Ring([sb(st, "mlhb%d" % i, [128, 8, 512], BF16) for i in range(1)]) if not last else None
        for tt in range(NT):
            tsl = slice(tt * 512, (tt + 1) * 512)
            S.dma("sp", hr[:], k.h1res[:, :, tsl], rd=[k.h1res], wr=[hr])
            pT = pTr.next()
            S.dma("sp", pT[:], k.pTs[:, :, tsl], rd=[k.pTs], wr=[pT])
            for fg in range(8):
                w1_ = w1r.next()
                S.dma("sp", w1_[:], k.w1s[:, :, fg * 512:(fg + 1) * 512], rd=[k.w1s], wr=[w1_])
                for f8 in range(4):
                    fc = fg * 4 + f8
                    u_ = up.next()
                    for kc in range(8):
                        mm(k, u_[:], w1_[:, kc, f8 * 128:(f8 + 1) * 128], h1T[:, kc, tsl], kc == 0, kc == 7, rd=[w1_, h1T], wr=[u_])
                    r_ = rl.next()
                    I("act", lambda e: e.activation(out=r_[:], in_=u_[:], func=AF.Relu), rd=[u_], wr=[r_])
                    I("pool", lambda e: e.tensor_tensor(out=a[:, fc, :], in0=r_[:], in1=r_[:], op=ALU.mult), rd=[r_], wr=[a])
            for mc in range(8):
                msl = slice(mc * 128, (mc + 1) * 128)
                s_ = sg.next()
                g_ = gp.next()
                for kc in range(8):
                    mm(k, g_[:], pgw[:, kc, msl], h1T[:, kc, tsl], kc == 0, kc == 7, rd=[pgw, h1T], wr=[g_])
                I("act", lambda e: e.activation(out=s_[:, 0, :], in_=g_[:], func=AF.Sigmoid), rd=[g_], wr=[s_])
                g2_ = gp.next()
                for kc in range(2):
                    mm(k, g2_[:], plw[:, kc, msl], pT[:, kc, :], kc == 0, kc == 1, rd=[plw, pT], wr=[g2_])
                I("dve", lambda e: e.tensor_tensor(out=s_[:, 1, :], in0=g2_[:], in1=s_[:, 0, :], op=ALU.mult), rd=[g2_, s_], wr=[s_])
                u_ = up.next()
                for fg in range(8):
                    w2_ = w2r.next()
                    S.dma("sp", w2_[:, :, 0:128], k.w2s[:, fg * 4:(fg + 1) * 4, msl], rd=[k.w2s], wr=[w2_])
                    for f4 in range(4):
                        fc = fg * 4 + f4
                        mm(k, u_[:], w2_[:, f4, 0:128], a[:, fc, :], fc == 0, fc == 31, rd=[w2_, a], wr=[u_])
                I("dve", lambda e: e.scalar_tensor_tensor(out=z[:, mc, :], in0=hr[:, mc, :], scalar=ALPHA, in1=u_[:],
                                                          op0=ALU.mult, op1=ALU.add), rd=[hr, u_], wr=[z])
                I("pool", lambda e: e.tensor_tensor(out=z[:, mc, :], in0=z[:, mc, :], in1=s_[:, 1, :], op=ALU.add), rd=[z, s_], wr=[z])

            def outs(c, dn, g_ap, b_ap):
                I("pool", lambda e: e.tensor_scalar(out=dn, in0=dn, scalar1=g_ap, scalar2=b_ap, op0=ALU.mult, op1=ALU.add),
                  rd=[z, ln], wr=[z])
            layer_norm_tile(k, (mps, msb, hr, vps), z, ln, LN_EPS, outs)
            if not last:
                hb_ = hb.next()
                I("act", lambda e: e.activation(out=hb_[:], in_=z[:], func=AF.Copy), rd=[z], wr=[hb_])
                S.dma("sp", k.hTs[:, :, tsl], hb_[:], rd=[hb_], wr=[k.hTs])
                S.dma("sp", k.hres[:, :, tsl], z[:], rd=[z], wr=[k.hres])
            else:
                for b in range(4):
                    o_ = ot.next()
                    for half in range(2):
                        for j in range(4):
                            c = half * 4 + j
                            tr(k, tpo[:, j, :], z[:, c, b * 128:(b + 1) * 128], k.identf[:], rd=[z, k.identf], wr=[tpo])
                        I("act", lambda e: e.activation(out=o_[:, half * 512:(half + 1) * 512],
                                                        in_=tpo[:].rearrange("p a b -> p (a b)"), func=AF.Copy), rd=[tpo], wr=[o_])
                    r0 = tt * 512 + b * 128
                    S.dma("sp", out[r0:r0 + 128, :], o_[:], rd=[o_], wr=[Dep()])


_CACHE = {}


def kernel(**inputs):
    if "nc" not in _CACHE:
        _CACHE["nc"] = build_program(depth=DEPTH)[0]
    nc = _CACHE["nc"]
    shared = {}
    for l in range(DEPTH):
        for n, v in prep_layer(inputs, l).items():
            shared["%s_%d" % (n, l)] = v
    x = np.asarray(inputs["x"], np.float32)
    p = np.asarray(inputs["p"], np.float32)
    in_maps = []
    for b in range(8):
        m = dict(shared)
        m["x"] = np.ascontiguousarray(x[b])
        m["p"] = np.ascontiguousarray(p[:, b])
        in_maps.append(m)
    res = run_bass_kernel_spmd(nc, in_maps, core_ids=list(range(8)))
    return np.stack([np.asarray(r["out"], np.float32) for r in res.results], axis=0)
```

```python
import math
import contextlib
import numpy as np
import concourse.bass as bass
import concourse.mybir as mybir
from concourse.bass_utils import run_bass_kernel_spmd

F32 = mybir.dt.float32
BF16 = mybir.dt.bfloat16
AF = mybir.ActivationFunctionType
ALU = mybir.AluOpType
AX = mybir.AxisListType

T = 4096
D = 1024
NT = 8
NB = 32
DEPTH = 2
ALPHA = (2 * DEPTH) ** 0.25
LN_EPS = 1e-5
GN_EPS = 64e-5
PI = math.pi


class Dep:
    __slots__ = ("w", "r", "excl")

    def __init__(self):
        self.w = {}
        self.r = {}
        self.excl = False


class Buf:
    def __init__(self, t, excl=False):
        self.t = t
        self.dep = Dep()
        self.dep.excl = excl

    def __getitem__(self, k):
        return self.t[k]


class Ring:
    def __init__(self, bufs):
        self.bufs = bufs
        self.i = 0

    def next(self):
        b = self.bufs[self.i % len(self.bufs)]
        self.i += 1
        return b


class Sched:
    EPOCH = 30000
    NDMA = 6
    NEPOCH = 8
    import os as _os
    qmap = {} if _os.environ.get("KNOB_POOLQ") else {"pool": "sp"}

    def __init__(self, nc):
        self.nc = nc
        self.eng = {"pe": nc.tensor, "act": nc.scalar, "dve": nc.vector,
                    "pool": nc.gpsimd, "sp": nc.sync}
        self.sem = {}
        self.cnt = {}
        self.seen = {e: {} for e in self.eng}
        self.nsem = 0
        self.ninst = 0
        self.allsems = []
        self.epool = {e: [self._alloc("e_%s_%d" % (e, i)) for i in range(self.NEPOCH)] for e in ("pe", "act", "dve", "pool")}
        self.dq = {}
        for q in ("sp", "pool"):
            self.dq[q] = {"sems": [self._alloc("dq_%s_%d" % (q, i)) for i in range(self.NDMA)], "n": 0}
        for e in ("pe", "act", "dve", "pool"):
            self._new_epoch(e)

    def _alloc(self, name):
        self.nsem += 1
        h = self.nc.alloc_semaphore(name)
        self.allsems.append(h)
        return h

    def _new_epoch(self, e):
        self.sem[e] = self.epool[e].pop(0)
        self.cnt[e] = 0

    def _wait(self, eng, sem, val):
        key = id(sem)
        if self.seen[eng].get(key, 0) >= val:
            return
        self.eng[eng].wait_ge(sem, val)
        self.seen[eng][key] = val

    def _gather(self, eng, rd, wr, same_ok=True):
        need = {}

        def add(p):
            pe, sem, val = p
            if pe == eng and (same_ok or eng == "pe"):
                return
            k = id(sem)
            if k not in need or need[k][1] < val:
                need[k] = (sem, val)
        for d in rd:
            for p in d.w.values():
                add(p)
        for d in wr:
            for p in d.w.values():
                add(p)
            for p in d.r.values():
                add(p)
        for sem, val in need.values():
            self._wait(eng, sem, val)

    @staticmethod
    def _deps(lst):
        return [x.dep if isinstance(x, Buf) else x for x in lst]

    @staticmethod
    def _merge(dct, tok):
        k = id(tok[1])
        if k not in dct or dct[k][2] < tok[2]:
            dct[k] = tok

    def I(self, eng, fn, rd=(), wr=()):
        rd = self._deps(rd)
        wr = self._deps(wr)
        ex = [d for d in rd if d.excl and d not in wr]
        if ex:
            rd = [d for d in rd if not d.excl]
            wr = list(wr) + ex
        self._gather(eng, rd, wr, same_ok=False)
        if self.cnt[eng] >= self.EPOCH:
            self._new_epoch(eng)
        ins = fn(self.eng[eng])
        self.ninst += 1
        self.cnt[eng] += 1
        ins.then_inc(self.sem[eng], 1)
        tok = (eng, self.sem[eng], self.cnt[eng])
        for d in rd:
            self._merge(d.r, tok)
        for d in wr:
            d.w = {id(tok[1]): tok}
            d.r = {}
        return ins

    def dma(self, q, out, in_, rd=(), wr=()):
        rd = self._deps(rd)
        wr = self._deps(wr)
        q = self.qmap.get(q, q)
        dq = self.dq[q]
        j = dq["n"]
        dq["n"] += 1
        sem = dq["sems"][j % self.NDMA]
        val = 16 * (j // self.NDMA + 1)
        if val > 16:
            self._wait(q, sem, val - 16)
        self._gather(q, rd, wr, same_ok=False)
        ins = self.eng[q].dma_start(out=out, in_=in_)
        self.ninst += 1
        ins.then_inc(sem, 16)
        tok = ("dma", sem, val)
        for d in rd:
            self._merge(d.r, tok)
        for d in wr:
            self._merge(d.w, tok)
            d.r = {}
        return ins

    def barrier(self, force=False):
        for w in ("pe", "act", "dve", "pool", "sp"):
            for q, dq in self.dq.items():
                n = dq["n"]
                for i in range(self.NDMA):
                    cnt_i = len(range(i, n, self.NDMA))
                    if cnt_i > 0:
                        self._wait(w, dq["sems"][i], 16 * cnt_i)
            for e in ("pe", "act", "dve", "pool"):
                if e != w and self.cnt[e] > 0:
                    self._wait(w, self.sem[e], self.cnt[e])

    def finish(self):
        for q, dq in self.dq.items():
            n = dq["n"]
            for i in range(self.NDMA):
                cnt_i = len(range(i, n, self.NDMA))
                if cnt_i > 0:
                    self._wait("sp", dq["sems"][i], 16 * cnt_i)
        for e in ("pe", "act", "dve", "pool"):
            if self.cnt[e] > 0:
                self._wait("sp", self.sem[e], self.cnt[e])


def _chunkcol(v):
    v = np.asarray(v, np.float32)
    return np.ascontiguousarray(v.reshape(-1, 128).T)


def prep_layer(inp, l):
    f = np.float32
    w_in = np.asarray(inp["w_in"][l], f)
    o = {}
    o["winF"] = np.ascontiguousarray(np.concatenate(
        [w_in[:, 0:256], w_in[:, 256:512], w_in[:, 512:768], w_in[:, 1028:2052],
         w_in[:, 2052:2308], w_in[:, 2308:2564], w_in[:, 2820:6916]], axis=1))
    o["winT"] = np.ascontiguousarray(np.concatenate([w_in[:, 768:1024], w_in[:, 2564:2820]], axis=1))
    o["wff"] = np.ascontiguousarray(w_in[:, 1024:1028])
    lre = np.asarray(inp["s5_lambda_re"][l], f)
    lim = np.asarray(inp["s5_lambda_im"][l], f)
    ldt = np.repeat(np.asarray(inp["s5_log_dt"][l], f)[:, None], 64, axis=1)
    def gp(a):
        return a.reshape(8, 2, 64).transpose(1, 2, 0).reshape(128, 8)
    o["s5par"] = np.ascontiguousarray(np.stack([gp(lre), gp(lim), gp(ldt)], axis=2))
    b_re = np.asarray(inp["s5_b_re"][l], f)
    b_im = np.asarray(inp["s5_b_im"][l], f)
    c_re = np.asarray(inp["s5_c_re"][l], f)
    c_im = np.asarray(inp["s5_c_im"][l], f)
    Bw = np.zeros((8, 128, 2, 128), f)
    Cw = np.zeros((8, 128, 2, 128), f)
    for g in range(16):
        rt = g // 2
        k0 = (g % 8) * 16
        m0 = (g % 2) * 64
        Bw[rt, k0:k0 + 16, 0, m0:m0 + 64] = b_re[g].T
        Bw[rt, k0:k0 + 16, 1, m0:m0 + 64] = b_im[g].T
        Cw[rt, m0:m0 + 64, 0, k0:k0 + 16] = c_re[g].T
        Cw[rt, m0:m0 + 64, 1, k0:k0 + 16] = c_im[g].T
    o["s5B"] = np.ascontiguousarray(Bw.transpose(1, 0, 2, 3))
    o["s5C"] = np.ascontiguousarray(Cw.transpose(1, 0, 2, 3))
    o["s5D"] = _chunkcol(np.asarray(inp["s5_d"][l], f).reshape(-1))
    o["gluw"] = np.asarray(inp["s5_glu_w"][l], f)
    o["glub"] = _chunkcol(inp["s5_glu_b"][l])
    o["foxb"] = np.asarray(inp["fox_f_bias"][l], f).reshape(4, 1)
    o["mu"] = _chunkcol(inp["rwkv_mu"][l])
    vec = [inp["rwkv_w0"][l], inp["rwkv_a0"][l], inp["rwkv_k_k"][l], inp["rwkv_k_a"][l],
           np.asarray(inp["rwkv_r_k"][l]).reshape(-1), inp["rwkv_lnx_g"][l], inp["rwkv_lnx_b"][l]]
    o["rvec"] = np.ascontiguousarray(np.stack([_chunkcol(v) for v in vec], axis=2))
    o["wa2"] = np.ascontiguousarray(np.concatenate([np.asarray(inp["rwkv_w2"][l], f),
                                                     np.asarray(inp["rwkv_a2"][l], f)], axis=0))
    o["g2"] = np.asarray(inp["rwkv_g2"][l], f)
    o["wbr"] = np.ascontiguousarray(np.asarray(inp["w_branch"][l], f).reshape(1024, 1024))
    o["wout"] = np.asarray(inp["w_out"][l], f)
    o["ln1"] = np.ascontiguousarray(np.stack([_chunkcol(inp["ln1_g"][l]), _chunkcol(inp["ln1_b"][l])], axis=2))
    o["ln2"] = np.ascontiguousarray(np.stack([_chunkcol(inp["ln2_g"][l]), _chunkcol(inp["ln2_b"][l])], axis=2))
    o["w1"] = np.asarray(inp["mlp_w1"][l], f)
    o["w2m"] = np.asarray(inp["mlp_w2"][l], f)
    o["plew"] = np.asarray(inp["ple_w"][l], f)
    o["pgw"] = np.asarray(inp["ple_gate_w"][l], f)
    return {k: np.ascontiguousarray(v, dtype=f) for k, v in o.items()}


LAYER_SHAPES = {
    "winF": [1024, 6400], "winT": [1024, 512], "wff": [1024, 4], "s5par": [128, 8, 3],
    "s5B": [128, 8, 2, 128], "s5C": [128, 8, 2, 128], "s5D": [128, 2], "gluw": [256, 256], "glub": [128, 2],
    "foxb": [4, 1], "mu": [128, 8], "rvec": [128, 2, 7], "wa2": [128, 256], "g2": [128, 256],
    "wbr": [1024, 1024], "wout": [1024, 1024], "ln1": [128, 8, 2], "ln2": [128, 8, 2],
    "w1": [1024, 4096], "w2m": [4096, 1024], "plew": [256, 1024], "pgw": [1024, 1024],
}


class K:
    pass


@contextlib.contextmanager
def phase(k):
    with contextlib.ExitStack() as st:
        yield st
        k.S.barrier()


def build_program(depth=DEPTH, dbg=(), stop_after=None, only=None, nsteps=T):
    nc = bass.Bass("TRN2", target_bir_lowering=False)
    S = Sched(nc)
    k = K()
    k.nc, k.S = nc, S
    k.dbg = {}
    x_in = nc.dram_tensor("x", [T, D], F32, kind="ExternalInput").ap()
    p_in = nc.dram_tensor("p", [DEPTH, T, 256], F32, kind="ExternalInput").ap()
    out = nc.dram_tensor("out", [T, D], F32, kind="ExternalOutput").ap()
    W = []
    for l in range(DEPTH):
        W.append({n: nc.dram_tensor("%s_%d" % (n, l), s, F32, kind="ExternalInput").ap()
                  for n, s in LAYER_SHAPES.items()})
    k.stop_after = stop_after
    k.only = only
    k.nsteps = nsteps

    def scratch(name, shape, dt):
        kind = "ExternalOutput" if name in dbg else "Internal"
        import os
        if os.environ.get("KNOB_TINY") and name != "hres":
            shape = [2, 2]
        return Buf(nc.dram_tensor(name, shape, dt, kind=kind).ap())
    k.hres = scratch("hres", [128, 8, T], F32)
    k.h1res = scratch("h1res", [128, 8, T], F32)
    k.gsc = scratch("gsc", [32, 128, T], BF16)
    k.ycat = scratch("ycat", [4, 128, 2, T], BF16)
    k.rwxs = scratch("rwxs", [8, 128, T], F32)
    k.sc5 = scratch("sc5", [2, 128, 5, T], F32)
    k.bon = scratch("bon", [2, 128, T], F32)
    k.gg = scratch("gg", [2, 128, T], F32)
    k.vtok = scratch("vtok", [T, 2, 128], F32)
    k.uTf = scratch("uTf", [128, 2, T], F32)
    k.qk = scratch("qk", [4, 128, 2, T], BF16)
    k.vv = scratch("vv", [2, 128, NB, 256], BF16)
    k.negc = scratch("negc", [4, T], F32)
    k.hTs = scratch("hTs", [128, 8, T], BF16)
    k.pTs = scratch("pTs", [128, 2, T], BF16)
    k.w1s = scratch("w1s", [128, 8, 4096], BF16)
    k.w2s = scratch("w2s", [8, 128, 32, 128], BF16)

    es = contextlib.ExitStack()

    uid = [0]

    def sb(stack, name, shape, dt):
        uid[0] += 1
        return Buf(stack.enter_context(nc.sbuf_tensor("%s_u%d" % (name, uid[0]), shape, dt)))

    def ps(stack, name, shape, dt=F32):
        uid[0] += 1
        full = [128, 512] if dt == F32 else [128, 1024]
        t = stack.enter_context(nc.psum_tensor("%s_u%d" % (name, uid[0]), full, dt))
        n = 1
        for d_ in shape[1:]:
            n *= d_
        if len(shape) == 2:
            v = t[0:shape[0], 0:n]
        else:
            assert len(shape) == 3
            v = t[0:shape[0], 0:n].rearrange("p (a b) -> p a b", a=shape[1])
        return Buf(v, excl=True)
    k.sb, k.ps = sb, ps

    with es:
        k.identf = sb(es, "identf", [128, 128], F32)
        k.identb = sb(es, "identb", [128, 128], BF16)
        k.onesm = sb(es, "onesm", [128, 128], F32)
        k.blk64 = sb(es, "blk64", [128, 128], F32)
        k.nmi = sb(es, "nmi", [128, 128], F32)
        k.nms = sb(es, "nms", [128, 128], F32)
        k.m01 = sb(es, "m01", [128, 128], F32)
        k.ones = sb(es, "ones", [128, 512], F32)
        k.sel4 = sb(es, "sel4", [4, 4, 128], F32)
        k.epsln = sb(es, "epsln", [128, 1], F32)
        k.epsgn = sb(es, "epsgn", [128, 1], F32)
        k.one1 = sb(es, "one1", [128, 1], F32)
        k.mhalf = sb(es, "mhalf", [128, 1], F32)
        I = S.I
        I("pool", lambda e: e.memset(k.epsln[:], LN_EPS), wr=[k.epsln])
        I("pool", lambda e: e.memset(k.epsgn[:], GN_EPS), wr=[k.epsgn])
        I("pool", lambda e: e.memset(k.one1[:], 1.0), wr=[k.one1])
        I("pool", lambda e: e.memset(k.mhalf[:], -0.5), wr=[k.mhalf])
        I("pool", lambda e: e.memset(k.identf[:], 0.0), wr=[k.identf])
        I("pool", lambda e: e.affine_select(out=k.identf[:], in_=k.identf[:], compare_op=ALU.not_equal, fill=1.0,
                                            base=0, pattern=[[-1, 128]], channel_multiplier=1),
          rd=[k.identf], wr=[k.identf])
        I("pool", lambda e: e.tensor_copy(k.identb[:], k.identf[:]), rd=[k.identf], wr=[k.identb])
        I("pool", lambda e: e.memset(k.onesm[:], 1.0 / 1024.0), wr=[k.onesm])
        I("pool", lambda e: e.memset(k.ones[:], 1.0), wr=[k.ones])
        I("pool", lambda e: e.memset(k.blk64[:], 0.0), wr=[k.blk64])
        I("pool", lambda e: e.memset(k.blk64[0:64, 0:64], 1.0), wr=[k.blk64])
        I("pool", lambda e: e.memset(k.blk64[64:128, 64:128], 1.0), wr=[k.blk64])
        I("pool", lambda e: e.memset(k.nmi[:], 0.0), wr=[k.nmi])
        I("pool", lambda e: e.affine_select(out=k.nmi[:], in_=k.nmi[:], compare_op=ALU.is_ge, fill=-1e30,
                                            base=0, pattern=[[-1, 128]], channel_multiplier=1), rd=[k.nmi], wr=[k.nmi])
        I("pool", lambda e: e.memset(k.nms[:], 0.0), wr=[k.nms])
        I("pool", lambda e: e.affine_select(out=k.nms[:], in_=k.nms[:], compare_op=ALU.is_gt, fill=-1e30,
                                            base=0, pattern=[[-1, 128]], channel_multiplier=1), rd=[k.nms], wr=[k.nms])
        I("pool", lambda e: e.affine_select(out=k.m01[:], in_=k.ones[:, 0:128], compare_op=ALU.is_gt, fill=0.0,
                                            base=0, pattern=[[-1, 128]], channel_multiplier=1), rd=[k.ones], wr=[k.m01])
        I("pool", lambda e: e.affine_select(out=k.sel4[:], in_=k.ones[0:4, :].rearrange("p (h m) -> p h m", h=4),
                                            compare_op=ALU.is_equal, fill=0.0, base=0, pattern=[[-1, 4], [0, 128]],
                                            channel_multiplier=1), rd=[k.ones], wr=[k.sel4])

        if stop_after != "consts":
            for l in range(depth):
                layer(k, l, W[l], x_in, p_in[l], out, last=(l == depth - 1))
        S.finish()
    k.ninst = S.ninst
    return nc, k


def dbg_dump(k, name, src_buf, src_ap):
    if name in k.dbg:
        k.S.dma("sp", k.dbg[name], src_ap, rd=[src_buf], wr=[Dep()])


def mm(k, out_ap, lhsT, rhs, start, stop, rd, wr):
    return k.S.I("pe", lambda e: e.matmul(out_ap, lhsT, rhs, start=start, stop=stop), rd=rd, wr=wr)


def tr(k, out_ap, in_ap, ident_ap, rd, wr):
    return k.S.I("pe", lambda e: e.transpose(out_ap, in_ap, ident_ap), rd=rd, wr=wr)


def phase0(k, x_in, hT):
    import os
    lvl = int(os.environ.get("KNOB_P0", "9"))
    nblk = int(os.environ.get("KNOB_P0N", str(NB)))
    S, I = k.S, k.S.I
    with phase(k) as st:
        xin = Ring([k.sb(st, "p0x%d" % i, [128, D], F32) for i in range(2)])
        stg = Ring([k.sb(st, "p0s%d" % i, [128, 8, 128], F32) for i in range(2)])
        pss = Ring([k.ps(st, "p0p%d" % i, [128, 4, 128], F32) for i in range(4)])
        for blk in range(nblk):
            xi = xin.next()
            S.dma("sp", xi[:], x_in[blk * 128:(blk + 1) * 128, :], wr=[xi])
            sg = stg.next()
            if lvl < 1:
                continue
            for half in range(2):
                pt = pss.next()
                for j in range(4):
                    c = half * 4 + j
                    tr(k, pt[:, j, :], xi[:, c * 128:(c + 1) * 128], k.identf[:], rd=[xi, k.identf], wr=[pt])
                if lvl >= 2:
                    I("act", lambda e: e.activation(out=hT[:, half * 4:half * 4 + 4, blk * 128:(blk + 1) * 128],
                                                    in_=pt[:], func=AF.Copy), rd=[pt], wr=[hT])
                if lvl >= 3:
                    I("dve", lambda e: e.tensor_copy(sg[:, half * 4:half * 4 + 4, :], pt[:]), rd=[pt], wr=[sg])
            if lvl >= 4:
                S.dma("pool", k.hres[:, :, blk * 128:(blk + 1) * 128], sg[:], rd=[sg], wr=[k.hres])


def cast_weight(k, st, src, dst_buf, dst_ap, K_, N_, tag, to_dram=False, dst_fn=None):
    S, I = k.S, k.S.I
    kc = K_ // 128
    ncol = max(1, min(N_, 4096 // kc))
    f32r = Ring([k.sb(st, "cw%s_f%d" % (tag, i), [128, kc, ncol], F32) for i in range(2)])
    if to_dram:
        bfr = Ring([k.sb(st, "cw%s_b%d" % (tag, i), [128, kc, ncol], BF16) for i in range(2)])
    srcv = src.rearrange("(c p) n -> p c n", p=128)
    engs = ["pool", "dve"]
    i = 0
    for n0 in range(0, N_, ncol):
        n1 = min(N_, n0 + ncol)
        w = n1 - n0
        fb = f32r.next()
        S.dma("sp", fb[:, :, 0:w], srcv[:, :, n0:n1], wr=[fb])
        if to_dram:
            bb = bfr.next()
            I(engs[i % 2], lambda e: e.tensor_copy(bb[:, :, 0:w], fb[:, :, 0:w]), rd=[fb], wr=[bb])
            dst = dst_fn(n0, n1) if dst_fn is not None else dst_ap[:, :, n0:n1]
            S.dma("pool", dst, bb[:, :, 0:w], rd=[bb], wr=[dst_buf])
        else:
            I(engs[i % 2], lambda e: e.tensor_copy(dst_ap[:, :, n0:n1], fb[:, :, 0:w]), rd=[fb], wr=[dst_buf])
        i += 1


def layer_norm_tile(k, st_bufs, z, gb, eps, outs):
    S, I = k.S, k.S.I
    mps, msb, sq, vps = st_bufs
    for c in range(8):
        mm(k, mps[:], k.onesm[:], z[:, c, :], c == 0, c == 7, rd=[k.onesm, z], wr=[mps])
    I("act", lambda e: e.activation(out=msb[:], in_=mps[:], func=AF.Copy), rd=[mps], wr=[msb])
    I("dve", lambda e: e.tensor_tensor(out=z[:], in0=z[:], in1=msb[:, None, :].broadcast_to([128, 8, 512]),
                                       op=ALU.subtract), rd=[z, msb], wr=[z])
    I("act", lambda e: e.activation(out=sq[:], in_=z[:], func=AF.Square), rd=[z], wr=[sq])
    for c in range(8):
        mm(k, vps[:], k.onesm[:], sq[:, c, :], c == 0, c == 7, rd=[k.onesm, sq], wr=[vps])
    I("act", lambda e: e.activation(out=msb[:], in_=vps[:], func=AF.Ln, bias=k.epsln[:, 0:1], scale=1.0),
      rd=[vps, k.epsln], wr=[msb])
    I("act", lambda e: e.activation(out=msb[:], in_=msb[:], func=AF.Exp, scale=-0.5), rd=[msb], wr=[msb])
    I("dve", lambda e: e.tensor_tensor(out=z[:], in0=z[:], in1=msb[:, None, :].broadcast_to([128, 8, 512]),
                                       op=ALU.mult), rd=[z, msb], wr=[z])
    for c in range(8):
        outs(c, z[:, c, :], gb[:, c, 0:1], gb[:, c, 1:2])


def inproj(k, l, W, hT):
    S, I = k.S, k.S.I
    with phase(k) as st:
        wf = Ring([k.sb(st, "ipwf%d" % i, [128, 8, 128], F32) for i in range(2)])
        wb = Ring([k.sb(st, "ipwb%d" % i, [128, 8, 128], BF16) for i in range(2)])
        pss = Ring([k.ps(st, "ipps%d" % i, [128, 512], F32) for i in range(4)])
        stb = Ring([k.sb(st, "ipsb%d" % i, [128, 512], BF16) for i in range(3)])
        stf = Ring([k.sb(st, "ipsf%d" % i, [128, 512], F32) for i in range(2)])
        raw = Ring([k.sb(st, "ipraw%d" % i, [128, T + 1], F32) for i in range(2)])
        xs = Ring([k.sb(st, "ipxs%d" % i, [128, T], F32) for i in range(1)])
        mu = k.sb(st, "ipmu", [128, 8], F32)
        S.dma("sp", mu[:], W["mu"], wr=[mu])
        for r_ in raw.bufs:
            I("pool", lambda e: e.memset(r_[:, 0:1], 0.0), wr=[r_])
        winF = W["winF"].rearrange("(c p) n -> p c n", p=128)
        ev = 0
        for j in range(50):
            if k.stop_after == "ip_fm%d" % j:
                return
            f_ = wf.next()
            S.dma("sp", f_[:], winF[:, :, j * 128:(j + 1) * 128], wr=[f_])
            b_ = wb.next()
            I("pool", lambda e: e.tensor_copy(b_[:], f_[:]), rd=[f_], wr=[b_])
            if 6 <= j < 14:
                rw = raw.next()
            for tt in range(NT):
                pt = pss.next()
                for c in range(8):
                    mm(k, pt[:], b_[:, c, :], hT[:, c, tt * 512:(tt + 1) * 512], c == 0, c == 7,
                       rd=[b_, hT], wr=[pt])
                eng = "act" if ev % 2 == 0 else "dve"
                ev += 1
                tsl = slice(tt * 512, (tt + 1) * 512)
                if j < 2:
                    sf = stf.next()
                    if eng == "act":
                        I("act", lambda e: e.activation(out=sf[:], in_=pt[:], func=AF.Copy), rd=[pt], wr=[sf])
                    else:
                        I("dve", lambda e: e.tensor_copy(sf[:], pt[:]), rd=[pt], wr=[sf])
                    S.dma("pool", k.uTf[:, j, tsl], sf[:], rd=[sf], wr=[k.uTf])
                elif j < 6 or 14 <= j < 18:
                    which = (j - 2) // 2 if j < 6 else 2 + (j - 14) // 2
                    cc = j % 2
                    scale = 0.125 if which in (0, 2) else 1.0
                    sb_ = stb.next()
                    I("act", lambda e: e.activation(out=sb_[:], in_=pt[:], func=AF.Copy, scale=scale), rd=[pt], wr=[sb_])
                    S.dma("pool", k.qk[which, :, cc, tsl], sb_[:], rd=[sb_], wr=[k.qk])
                elif j < 14:
                    if eng == "act":
                        I("act", lambda e: e.activation(out=rw[:, 1 + tt * 512:1 + (tt + 1) * 512], in_=pt[:], func=AF.Copy),
                          rd=[pt], wr=[rw])
                    else:
                        I("dve", lambda e: e.tensor_copy(rw[:, 1 + tt * 512:1 + (tt + 1) * 512], pt[:]), rd=[pt], wr=[rw])
                else:
                    gc = j - 18
                    sb_ = stb.next()
                    I("act", lambda e: e.activation(out=sb_[:], in_=pt[:], func=AF.Sigmoid), rd=[pt], wr=[sb_])
                    S.dma("pool", k.gsc[gc, :, tsl], sb_[:], rd=[sb_], wr=[k.gsc])
            if 6 <= j < 14:
                jj = j - 6
                x_ = xs.next()
                I("pool", lambda e: e.tensor_tensor(out=x_[:], in0=rw[:, 0:T], in1=rw[:, 1:T + 1], op=ALU.subtract),
                  rd=[rw], wr=[x_])
                I("dve", lambda e: e.scalar_tensor_tensor(out=x_[:], in0=x_[:], scalar=mu[:, jj:jj + 1], in1=rw[:, 1:T + 1],
                                                          op0=ALU.mult, op1=ALU.add), rd=[x_, rw, mu], wr=[x_])
                S.dma("sp", k.rwxs[jj], x_[:], rd=[x_], wr=[k.rwxs])
        if k.stop_after == "ip_fm":
            return
        wtf = k.sb(st, "ipwtf", [128, 8, 512], F32)
        wtb = k.sb(st, "ipwtb", [128, 8, 512], BF16)
        S.dma("sp", wtf[:], W["winT"].rearrange("(c p) n -> p c n", p=128), wr=[wtf])
        I("pool", lambda e: e.tensor_copy(wtb[:], wtf[:]), rd=[wtf], wr=[wtb])
        for blk in range(NB):
            pt = pss.next()
            for c in range(8):
                mm(k, pt[:], hT[:, c, blk * 128:(blk + 1) * 128], wtb[:, c, :], c == 0, c == 7, rd=[wtb, hT], wr=[pt])
            sb_ = stb.next()
            if blk % 2 == 0:
                I("act", lambda e: e.activation(out=sb_[:], in_=pt[:], func=AF.Copy), rd=[pt], wr=[sb_])
            else:
                I("dve", lambda e: e.tensor_copy(sb_[:], pt[:]), rd=[pt], wr=[sb_])
            S.dma("pool", k.vv[:, :, blk, :].rearrange("w p n -> p w n"), sb_[:].rearrange("p (w n) -> p w n", w=2),
                  rd=[sb_], wr=[k.vv])
        if k.stop_after == "ip_tm":
            return
        wff_f = k.sb(st, "ipwff", [128, 8, 4], F32)
        wff_b = k.sb(st, "ipwffb", [128, 8, 4], BF16)
        fb = k.sb(st, "ipfb", [4, 1], F32)
        fl = k.sb(st, "ipfl", [4, T], F32)
        S.dma("sp", wff_f[:], W["wff"].rearrange("(c p) n -> p c n", p=128), wr=[wff_f])
        S.dma("sp", fb[:], W["foxb"], wr=[fb])
        I("pool", lambda e: e.tensor_copy(wff_b[:], wff_f[:]), rd=[wff_f], wr=[wff_b])
        I("dve", lambda e: e.tensor_scalar(out=fb[:], in0=fb[:], scalar1=-1.0, scalar2=None, op0=ALU.mult), rd=[fb], wr=[fb])
        for tt in range(NT):
            pt = pss.next()
            for c in range(8):
                mm(k, pt[0:4, :], wff_b[:, c, :], hT[:, c, tt * 512:(tt + 1) * 512], c == 0, c == 7, rd=[wff_b, hT], wr=[pt])
            I("act", lambda e: e.activation(out=fl[:, tt * 512:(tt + 1) * 512], in_=pt[0:4, :], func=AF.Exp,
                                            bias=fb[:, 0:1], scale=-1.0), rd=[pt, fb], wr=[fl])
        I("act", lambda e: e.activation(out=fl[:], in_=fl[:], func=AF.Ln, bias=k.one1[0:4, 0:1], scale=1.0),
          rd=[fl, k.one1], wr=[fl])
        for tt in range(NT):
            init = 0.0 if tt == 0 else fl[:, tt * 512 - 1:tt * 512]
            I("dve", lambda e: e.tensor_tensor_scan(out=fl[:, tt * 512:(tt + 1) * 512], data0=k.ones[0:4, :],
                                                    data1=fl[:, tt * 512:(tt + 1) * 512], initial=init,
                                                    op0=ALU.mult, op1=ALU.add), rd=[fl, k.ones], wr=[fl])
        S.dma("sp", k.negc[:], fl[:], rd=[fl], wr=[k.negc])


def layer(k, l, W, x_in, p_l, out, last):
    S, I = k.S, k.S.I
    with phase(k) as st:
        hT = k.sb(st, "hT%d" % l, [128, 8, T], BF16)
        if l == 0:
            phase0(k, x_in, hT)
            if k.stop_after == "phase0":
                return
        else:
            S.dma("sp", hT[:], k.hTs[:], rd=[k.hTs], wr=[hT])
        inproj(k, l, W, hT)
    if k.stop_after is not None and k.stop_after.startswith("ip") or k.stop_after == "inproj":
        return
    if k.only in (None, "s5"):
        s5_phase(k, l, W)
    if k.stop_after == "s5":
        return
    if k.only in (None, "fox"):
        attn_phase(k, l, 0)
    if k.only in (None, "sb"):
        attn_phase(k, l, 1)
    if k.stop_after == "attn":
        return
    if k.only in (None, "rwkv"):
        rwkv_prep(k, l, W)
        rwkv_chunked(k, l, W, nsteps=k.nsteps)
    if k.stop_after == "rwkv":
        return
    precast_mlp(k, l, W)
    with phase(k) as st:
        h1T = k.sb(st, "h1T%d" % l, [128, 8, T], BF16)
        merge_phase(k, l, W, h1T)
        if k.stop_after == "merge":
            return
        mlp_phase(k, l, W, h1T, p_l, out, last)


def range_reduce(k, out, in_, shift, qi, qf, rd, wr):
    I = k.S.I
    TWO_PI = 2.0 * PI
    I("dve", lambda e: e.tensor_scalar(out=out, in0=in_, scalar1=float(shift), scalar2=None, op0=ALU.add), rd=rd, wr=wr)
    I("dve", lambda e: e.tensor_scalar(out=qi, in0=out, scalar1=1.0 / TWO_PI, scalar2=0.5, op0=ALU.mult, op1=ALU.add), rd=rd, wr=wr)
    I("dve", lambda e: e.tensor_copy(qf, qi), rd=rd, wr=wr)
    I("dve", lambda e: e.scalar_tensor_tensor(out=out, in0=qf, scalar=-TWO_PI, in1=out, op0=ALU.mult, op1=ALU.add), rd=rd, wr=wr)
    I("dve", lambda e: e.tensor_scalar(out=qf, in0=out, scalar1=-PI, scalar2=None, op0=ALU.is_lt), rd=rd, wr=wr)
    I("dve", lambda e: e.scalar_tensor_tensor(out=out, in0=qf, scalar=TWO_PI, in1=out, op0=ALU.mult, op1=ALU.add), rd=rd, wr=wr)
    I("dve", lambda e: e.tensor_scalar(out=out, in0=out, scalar1=-3.1415925, scalar2=3.1415925, op0=ALU.max, op1=ALU.min), rd=rd, wr=wr)


def s5_phase(k, l, W):
    S, I = k.S, k.S.I
    TWO_PI = 2.0 * PI
    with phase(k) as st:
        sb, ps = k.sb, k.ps
        par = sb(st, "s5par", [128, 8, 3], F32)
        S.dma("sp", par[:], W["s5par"], wr=[par])
        Bb = sb(st, "s5Bb", [128, 8, 2, 128], BF16)
        Cb = sb(st, "s5Cb", [128, 8, 2, 128], BF16)
        gwb = sb(st, "s5gwb", [128, 2, 256], BF16)
        Dc = sb(st, "s5D", [128, 2], F32)
        S.dma("sp", Dc[:], W["s5D"], wr=[Dc])
        with phase(k) as st2:
            Bf = sb(st2, "s5Bf", [128, 8, 2, 128], F32)
            Cf = sb(st2, "s5Cf", [128, 8, 2, 128], F32)
            gwf = sb(st2, "s5gwf", [128, 2, 256], F32)
            S.dma("sp", Bf[:], W["s5B"], wr=[Bf])
            S.dma("sp", Cf[:], W["s5C"], wr=[Cf])
            S.dma("sp", gwf[:], W["gluw"].rearrange("(c p) n -> p c n", p=128), wr=[gwf])
            I("pool", lambda e: e.tensor_copy(Bb[:], Bf[:]), rd=[Bf], wr=[Bb])
            I("pool", lambda e: e.tensor_copy(Cb[:, :, 0, :], Cf[:, :, 0, :]), rd=[Cf], wr=[Cb])
            I("pool", lambda e: e.tensor_scalar(out=Cb[:, :, 1, :], in0=Cf[:, :, 1, :], scalar1=-1.0, scalar2=None, op0=ALU.mult),
              rd=[Cf], wr=[Cb])
            I("pool", lambda e: e.tensor_copy(gwb[:], gwf[:]), rd=[gwf], wr=[gwb])
        gb = sb(st, "s5gb", [128, 2], F32)
        S.dma("sp", gb[:], W["glub"], wr=[gb])
        sm = sb(st, "s5sm", [128, 16, 8], F32)
        def sl(i):
            return sm[:, i, :]
        lre, lim, ldt = par[:, :, 0], par[:, :, 1], par[:, :, 2]
        DT, A_, TH, NA, EM, SN, CS, LBR, LBI, DEN, FRE, FIM, NFRE, T1, T2, THR = range(16)
        dsm = Dep()
        def dv(fn):
            I("dve", fn, rd=[par, dsm], wr=[dsm])
        def ac(fn):
            I("act", fn, rd=[par, dsm], wr=[dsm])
        def pl(fn):
            I("pool", fn, rd=[par, dsm], wr=[dsm])
        ac(lambda e: e.activation(out=sl(DT), in_=ldt, func=AF.Exp))
        dv(lambda e: e.tensor_tensor(out=sl(A_), in0=lre, in1=sl(DT), op=ALU.mult))
        dv(lambda e: e.tensor_tensor(out=sl(TH), in0=lim, in1=sl(DT), op=ALU.mult))
        dv(lambda e: e.tensor_scalar(out=sl(NA), in0=sl(A_), scalar1=-1.0, scalar2=None, op0=ALU.mult))
        smi = sb(st, "s5smi", [128, 8], mybir.dt.int32)
        range_reduce(k, sl(T1), sl(TH), 0.0, smi[:], sl(T2), [par, dsm, smi], [dsm, smi])
        ac(lambda e: e.activation(out=sl(SN), in_=sl(T1), func=AF.Sin))
        range_reduce(k, sl(T1), sl(TH), 0.5 * PI, smi[:], sl(T2), [par, dsm, smi], [dsm, smi])
        ac(lambda e: e.activation(out=sl(CS), in_=sl(T1), func=AF.Sin))
        ac(lambda e: e.activation(out=sl(EM), in_=sl(A_), func=AF.Exp))
        dv(lambda e: e.tensor_tensor(out=sl(LBR), in0=sl(EM), in1=sl(CS), op=ALU.mult))
        dv(lambda e: e.tensor_tensor(out=sl(LBI), in0=sl(EM), in1=sl(SN), op=ALU.mult))
        dv(lambda e: e.tensor_tensor(out=sl(DEN), in0=lre, in1=lre, op=ALU.mult))
        dv(lambda e: e.tensor_tensor(out=sl(T1), in0=lim, in1=lim, op=ALU.mult))
        dv(lambda e: e.tensor_tensor(out=sl(DEN), in0=sl(DEN), in1=sl(T1), op=ALU.add))
        dv(lambda e: e.reciprocal(out=sl(DEN), in_=sl(DEN)))
        dv(lambda e: e.tensor_scalar(out=sl(T2), in0=sl(LBR), scalar1=-1.0, scalar2=None, op0=ALU.add))
        dv(lambda e: e.tensor_tensor(out=sl(FRE), in0=sl(T2), in1=lre, op=ALU.mult))
        dv(lambda e: e.tensor_tensor(out=sl(T1), in0=sl(LBI), in1=lim, op=ALU.mult))
        dv(lambda e: e.tensor_tensor(out=sl(FRE), in0=sl(FRE), in1=sl(T1), op=ALU.add))
        dv(lambda e: e.tensor_tensor(out=sl(FRE), in0=sl(FRE), in1=sl(DEN), op=ALU.mult))
        dv(lambda e: e.tensor_tensor(out=sl(FIM), in0=sl(LBI), in1=lre, op=ALU.mult))
        dv(lambda e: e.tensor_tensor(out=sl(T1), in0=sl(T2), in1=lim, op=ALU.mult))
        dv(lambda e: e.tensor_tensor(out=sl(FIM), in0=sl(FIM), in1=sl(T1), op=ALU.subtract))
        dv(lambda e: e.tensor_tensor(out=sl(FIM), in0=sl(FIM), in1=sl(DEN), op=ALU.mult))
        dv(lambda e: e.tensor_scalar(out=sl(NFRE), in0=sl(FRE), scalar1=-1.0, scalar2=None, op0=ALU.mult))
        range_reduce(k, sl(THR), sl(TH), 0.0, smi[:], sl(T1), [par, dsm, smi], [dsm, smi])
        idx = sb(st, "s5idx", [128, 129], F32)
        I("dve", lambda e: e.memset(idx[:], 1.0), wr=[idx])
        I("dve", lambda e: e.tensor_tensor_scan(out=idx[:], data0=idx[:], data1=idx[:], initial=-1.0,
                                                op0=ALU.mult, op1=ALU.add), rd=[idx], wr=[idx])
        tabr = sb(st, "s5tabr", [128, 8, 129], F32)
        tabi = sb(st, "s5tabi", [128, 8, 129], F32)
        tifr = sb(st, "s5tifr", [128, 8, 128], F32)
        tifi = sb(st, "s5tifi", [128, 8, 128], F32)
        ntl = sb(st, "s5ntl", [128, 8], F32)
        w1 = sb(st, "s5w1", [128, 129], F32)
        w2 = sb(st, "s5w2", [128, 129], F32)
        w3 = sb(st, "s5w3", [128, 129], F32)
        w4 = sb(st, "s5w4", [128, 129], F32)
        w5 = sb(st, "s5w5", [128, 129], F32)
        wi = sb(st, "s5wi", [128, 129], mybir.dt.int32)
        tabs = [tabr, tabi, tifr, tifi]
        wk = [w1, w2, w3, w4, w5, idx]
        for rt in range(8):
            def col(i):
                return sm[:, i, rt:rt + 1]
            def dvw(fn):
                I("dve", fn, rd=wk + [dsm], wr=wk[:5] + tabs)
            def acw(fn):
                I("act", fn, rd=wk + [dsm], wr=wk[:5] + tabs)
            def plw(fn):
                I("pool", fn, rd=wk + [dsm], wr=wk[:5] + tabs)
            dvw(lambda e: e.tensor_scalar(out=w1[:], in0=idx[:], scalar1=col(THR), scalar2=None, op0=ALU.mult))
            range_reduce(k, w2[:], w1[:], 0.0, wi[:], w5[:], wk + [dsm, wi], wk[:5] + tabs + [wi])
            acw(lambda e: e.activation(out=w2[:], in_=w2[:], func=AF.Sin))
            range_reduce(k, w3[:], w1[:], 0.5 * PI, wi[:], w5[:], wk + [dsm, wi], wk[:5] + tabs + [wi])
            acw(lambda e: e.activation(out=w3[:], in_=w3[:], func=AF.Sin))
            acw(lambda e: e.activation(out=w4[:], in_=idx[:], func=AF.Exp, scale=col(A_)))
            acw(lambda e: e.activation(out=w5[:], in_=idx[:], func=AF.Exp, scale=col(NA)))
            dvw(lambda e: e.tensor_tensor(out=tabr[:, rt, :], in0=w4[:], in1=w3[:], op=ALU.mult))
            dvw(lambda e: e.tensor_tensor(out=tabi[:, rt, :], in0=w4[:], in1=w2[:], op=ALU.mult))
            dvw(lambda e: e.tensor_scalar(out=w1[:], in0=w3[:], scalar1=col(FRE), scalar2=None, op0=ALU.mult))
            dvw(lambda e: e.scalar_tensor_tensor(out=w1[:], in0=w2[:], scalar=col(FIM), in1=w1[:], op0=ALU.mult, op1=ALU.add))
            dvw(lambda e: e.tensor_tensor(out=tifr[:, rt, :], in0=w1[:, 0:128], in1=w5[:, 0:128], op=ALU.mult))
            dvw(lambda e: e.tensor_scalar(out=w1[:], in0=w3[:], scalar1=col(FIM), scalar2=None, op0=ALU.mult))
            dvw(lambda e: e.scalar_tensor_tensor(out=w1[:], in0=w2[:], scalar=col(NFRE), in1=w1[:], op0=ALU.mult, op1=ALU.add))
            dvw(lambda e: e.tensor_tensor(out=tifi[:, rt, :], in0=w1[:, 0:128], in1=w5[:, 0:128], op=ALU.mult))
        I("dve", lambda e: e.tensor_scalar(out=ntl[:], in0=tabi[:, :, 128], scalar1=-1.0, scalar2=None, op0=ALU.mult),
          rd=tabs, wr=[ntl])
        uf = sb(st, "s5uf", [128, 2, T], F32)
        ub = sb(st, "s5ub", [128, 2, T], BF16)
        S.dma("sp", uf[:], k.uTf[:], rd=[k.uTf], wr=[uf])
        I("pool", lambda e: e.tensor_copy(ub[:], uf[:]), rd=[uf], wr=[ub])
        ygf = sb(st, "s5ygf", [128, 2, T], F32)
        ygb = sb(st, "s5ygb", [128, 2, T], BF16)
        q0 = sb(st, "s5q0", [128, 8, 2], F32)
        I("dve", lambda e: e.memset(q0[:], 0.0), wr=[q0])
        rawp = Ring([ps(st, "s5rp%d" % i, [128, 512], F32) for i in range(4)])
        yp = Ring([ps(st, "s5yp%d" % i, [128, 512], F32) for i in range(2)])
        rr = Ring([sb(st, "s5rr%d" % i, [128, 2, 512], F32) for i in range(2)])
        zz = Ring([sb(st, "s5zz%d" % i, [128, 2, 512], F32) for i in range(2)])
        mmr = Ring([sb(st, "s5mm%d" % i, [128, 4, 512], F32) for i in range(2)])
        qq = Ring([sb(st, "s5qq%d" % i, [128, 2, 512], F32) for i in range(2)])
        xb = Ring([sb(st, "s5xb%d" % i, [128, 2, 512], BF16) for i in range(3)])
        ew = Ring([sb(st, "s5ew%d" % i, [128, 3, 512], F32) for i in range(2)])
        tq = sb(st, "s5tq", [128, 2], F32)

        def b4(tab, rt):
            return tab[:, rt:rt + 1, 0:128].broadcast_to([128, 4, 128])

        def v4(ap):
            return ap.rearrange("p (a b) -> p a b", a=4)
        for oc in range(2):
            for tt in range(NT):
                tsl = slice(tt * 512, (tt + 1) * 512)
                ypt = yp.next()
                for r4 in range(4):
                    rt = oc * 4 + r4
                    pr, pi_ = rawp.next(), rawp.next()
                    mm(k, pr[:], Bb[:, rt, 0, :], ub[:, oc, tsl], True, True, rd=[Bb, ub], wr=[pr])
                    mm(k, pi_[:], Bb[:, rt, 1, :], ub[:, oc, tsl], True, True, rd=[Bb, ub], wr=[pi_])
                    r_ = rr.next()
                    I("act", lambda e: e.activation(out=r_[:, 0, :], in_=pr[:], func=AF.Copy), rd=[pr], wr=[r_])
                    I("act", lambda e: e.activation(out=r_[:, 1, :], in_=pi_[:], func=AF.Copy), rd=[pi_], wr=[r_])
                    z_, m_ = zz.next(), mmr.next()
                    I("dve", lambda e: e.tensor_tensor(out=v4(m_[:, 0, :]), in0=v4(r_[:, 0, :]), in1=b4(tifr, rt), op=ALU.mult),
                      rd=[r_] + tabs, wr=[m_])
                    I("dve", lambda e: e.tensor_tensor(out=v4(m_[:, 1, :]), in0=v4(r_[:, 1, :]), in1=b4(tifi, rt), op=ALU.mult),
                      rd=[r_] + tabs, wr=[m_])
                    I("pool", lambda e: e.tensor_tensor(out=v4(m_[:, 2, :]), in0=v4(r_[:, 0, :]), in1=b4(tifi, rt), op=ALU.mult),
                      rd=[r_] + tabs, wr=[m_])
                    I("pool", lambda e: e.tensor_tensor(out=v4(m_[:, 3, :]), in0=v4(r_[:, 1, :]), in1=b4(tifr, rt), op=ALU.mult),
                      rd=[r_] + tabs, wr=[m_])
                    I("dve", lambda e: e.tensor_tensor(out=z_[:, 0, :], in0=m_[:, 0, :], in1=m_[:, 1, :], op=ALU.subtract),
                      rd=[m_], wr=[z_])
                    I("pool", lambda e: e.tensor_tensor(out=z_[:, 1, :], in0=m_[:, 2, :], in1=m_[:, 3, :], op=ALU.add),
                      rd=[m_], wr=[z_])
                    q_ = qq.next()
                    for sc in range(4):
                        csl = slice(sc * 128, (sc + 1) * 128)
                        for ri in range(2):
                            I("dve", lambda e: e.tensor_tensor_scan(out=q_[:, ri, csl], data0=k.ones[:, 0:128], data1=z_[:, ri, csl],
                                                                    initial=q0[:, rt, ri:ri + 1], op0=ALU.mult, op1=ALU.add),
                              rd=[z_, q0, k.ones], wr=[q_])
                        qe_r = q_[:, 0, sc * 128 + 127:sc * 128 + 128]
                        qe_i = q_[:, 1, sc * 128 + 127:sc * 128 + 128]
                        lr_ = tabr[:, rt, 128:129]
                        li_ = tabi[:, rt, 128:129]
                        I("dve", lambda e: e.tensor_scalar(out=tq[:, 0:1], in0=qe_r, scalar1=lr_, scalar2=None, op0=ALU.mult),
                          rd=[q_] + tabs, wr=[tq])
                        I("dve", lambda e: e.tensor_scalar(out=tq[:, 1:2], in0=qe_i, scalar1=lr_, scalar2=None, op0=ALU.mult),
                          rd=[q_] + tabs, wr=[tq])
                        I("dve", lambda e: e.scalar_tensor_tensor(out=q0[:, rt, 0:1], in0=qe_i, scalar=ntl[:, rt:rt + 1], in1=tq[:, 0:1],
                                                                  op0=ALU.mult, op1=ALU.add), rd=[q_, ntl, tq], wr=[q0])
                        I("dve", lambda e: e.scalar_tensor_tensor(out=q0[:, rt, 1:2], in0=qe_r, scalar=li_, in1=tq[:, 1:2],
                                                                  op0=ALU.mult, op1=ALU.add), rd=[q_, tq] + tabs, wr=[q0])
                    m2 = mmr.next()
                    x_ = xb.next()
                    I("dve", lambda e: e.tensor_tensor(out=v4(m2[:, 0, :]), in0=v4(q_[:, 0, :]), in1=b4(tabr, rt), op=ALU.mult),
                      rd=[q_] + tabs, wr=[m2])
                    I("dve", lambda e: e.tensor_tensor(out=v4(m2[:, 1, :]), in0=v4(q_[:, 1, :]), in1=b4(tabi, rt), op=ALU.mult),
                      rd=[q_] + tabs, wr=[m2])
                    I("pool", lambda e: e.tensor_tensor(out=v4(m2[:, 2, :]), in0=v4(q_[:, 0, :]), in1=b4(tabi, rt), op=ALU.mult),
                      rd=[q_] + tabs, wr=[m2])
                    I("pool", lambda e: e.tensor_tensor(out=v4(m2[:, 3, :]), in0=v4(q_[:, 1, :]), in1=b4(tabr, rt), op=ALU.mult),
                      rd=[q_] + tabs, wr=[m2])
                    I("dve", lambda e: e.tensor_tensor(out=x_[:, 0, :], in0=m2[:, 0, :], in1=m2[:, 1, :], op=ALU.subtract),
                      rd=[m2], wr=[x_])
                    I("pool", lambda e: e.tensor_tensor(out=x_[:, 1, :], in0=m2[:, 2, :], in1=m2[:, 3, :], op=ALU.add),
                      rd=[m2], wr=[x_])
                    mm(k, ypt[:], Cb[:, rt, 0, :], x_[:, 0, :], r4 == 0, False, rd=[Cb, x_], wr=[ypt])
                    mm(k, ypt[:], Cb[:, rt, 1, :], x_[:, 1, :], False, r4 == 3, rd=[Cb, x_], wr=[ypt])
                e_ = ew.next()
                I("dve", lambda e: e.scalar_tensor_tensor(out=e_[:, 0, :], in0=uf[:, oc, tsl], scalar=Dc[:, oc:oc + 1], in1=ypt[:],
                                                          op0=ALU.mult, op1=ALU.add), rd=[uf, Dc, ypt], wr=[e_])
                I("pool", lambda e: e.tensor_tensor(out=e_[:, 1, :], in0=e_[:, 0, :], in1=e_[:, 0, :], op=ALU.mult), rd=[e_], wr=[e_])
                I("pool", lambda e: e.tensor_scalar(out=e_[:, 1, :], in0=e_[:, 1, :], scalar1=0.044715, scalar2=1.0,
                                                    op0=ALU.mult, op1=ALU.add), rd=[e_], wr=[e_])
                I("pool", lambda e: e.tensor_tensor(out=e_[:, 1, :], in0=e_[:, 1, :], in1=e_[:, 0, :], op=ALU.mult), rd=[e_], wr=[e_])
                I("act", lambda e: e.activation(out=e_[:, 2, :], in_=e_[:, 1, :], func=AF.Sigmoid, scale=1.5957691216057308),
                  rd=[e_], wr=[e_])
                I("dve", lambda e: e.tensor_tensor(out=ygf[:, oc, tsl], in0=e_[:, 0, :], in1=e_[:, 2, :], op=ALU.mult),
                  rd=[e_], wr=[ygf])
                I("pool", lambda e: e.tensor_copy(ygb[:, oc, tsl], ygf[:, oc, tsl]), rd=[ygf], wr=[ygb])
        ob = Ring([sb(st, "s5ob%d" % i, [128, 512], BF16) for i in range(2)])
        for oc in range(2):
            for tt in range(NT):
                tsl = slice(tt * 512, (tt + 1) * 512)
                pt = yp.next()
                for kc in range(2):
                    mm(k, pt[:], gwb[:, kc, oc * 128:(oc + 1) * 128], ygb[:, kc, tsl], kc == 0, kc == 1, rd=[gwb, ygb], wr=[pt])
                e_ = ew.next()
                I("act", lambda e: e.activation(out=e_[:, 0, :], in_=pt[:], func=AF.Sigmoid, bias=gb[:, oc:oc + 1], scale=1.0),
                  rd=[pt, gb], wr=[e_])
                o_ = ob.next()
                I("dve", lambda e: e.tensor_tensor(out=o_[:], in0=e_[:, 0, :], in1=ygf[:, oc, tsl], op=ALU.mult),
                  rd=[e_, ygf], wr=[o_])
                S.dma("sp", k.ycat[0, :, oc, tsl], o_[:], rd=[o_], wr=[k.ycat])


def attn_phase(k, l, kind):
    S, I = k.S, k.S.I
    fox = (kind == 0)
    VW = 65 if fox else 64
    with phase(k) as st:
        sb, ps = k.sb, k.ps
        qT = sb(st, "atq", [128, 2, T], BF16)
        kT = sb(st, "atk", [128, 2, T], BF16)
        S.dma("sp", qT[:], k.qk[2 * kind], rd=[k.qk], wr=[qT])
        S.dma("sp", kT[:], k.qk[2 * kind + 1], rd=[k.qk], wr=[kT])
        V = sb(st, "atv", [128, NB, 4, VW], BF16)
        if fox:
            I("pool", lambda e: e.memset(V[:, :, :, 64:65], 1.0), wr=[V])
        S.dma("sp", V[:, :, :, 0:64], k.vv[kind].rearrange("p b (h d) -> p b h d", h=4), rd=[k.vv], wr=[V])
        yT = sb(st, "atyT", [128, 2, T], BF16)
        ytok = sb(st, "atytok", [128, NB, 256], BF16)
        A = Ring([sb(st, "atA%d" % i, [128, T], F32) for i in range(2)])
        P = Ring([sb(st, "atP%d" % i, [128, T], BF16) for i in range(2)])
        PT = Ring([sb(st, "atPT%d" % i, [128, 4, 128], BF16) for i in range(3)])
        zp = Ring([ps(st, "atzp%d" % i, [128, 512], F32) for i in range(3)])
        tp = Ring([ps(st, "attp%d" % i, [128, 4, 128], BF16) for i in range(2)])
        op = Ring([ps(st, "atop%d" % i, [128, VW], F32) for i in range(2)])
        xp = ps(st, "atxp", [128, 512], F32)
        nb = Ring([sb(st, "atnb%d" % i, [128, 2], F32) for i in range(3)])
        if fox:
            negc = sb(st, "atnegc", [4, T], F32)
            S.dma("sp", negc[:], k.negc[:], rd=[k.negc], wr=[negc])
            crow = sb(st, "atcrow", [128, T], F32)
        else:
            spb = Ring([sb(st, "atsp%d" % i, [128, 512], F32) for i in range(3)])
            Fb = Ring([sb(st, "atF%d" % i, [128, 512], F32) for i in range(3)])
            t1b = Ring([sb(st, "att1%d" % i, [128, 512], F32) for i in range(3)])
        items = [(h, qb) for h in range(4) for qb in range(NB)]
        state = {}

        def stage1(h, qb):
            hc, po = h // 2, (h % 2) * 64
            if fox and qb == 0:
                for tt in range(NT):
                    mm(k, xp[:], k.sel4[:, h, :], negc[:, tt * 512:(tt + 1) * 512], True, True, rd=[k.sel4, negc], wr=[xp])
                    I("act", lambda e: e.activation(out=crow[:, tt * 512:(tt + 1) * 512], in_=xp[:], func=AF.Copy),
                      rd=[xp], wr=[crow])
            nk = qb + 1
            a_ = A.next()
            nb_ = nb.next()
            carry = None
            for kt in range((nk + 3) // 4):
                w = min(4, nk - 4 * kt) * 128
                ks = slice(kt * 512, kt * 512 + w)
                last = (kt == (nk + 3) // 4 - 1)
                z = zp.next()
                mm(k, z[:, 0:w], qT[po:po + 64, hc, qb * 128:(qb + 1) * 128], kT[po:po + 64, hc, ks], True, True,
                   rd=[qT, kT], wr=[z])
                if fox:
                    I("dve", lambda e: e.tensor_tensor(out=a_[:, ks], in0=z[:, 0:w], in1=crow[:, ks], op=ALU.add),
                      rd=[z, crow], wr=[a_])
                else:
                    sp_ = spb.next()
                    I("act", lambda e: e.activation(out=sp_[:, 0:w], in_=z[:, 0:w], func=AF.Exp), rd=[z], wr=[sp_])
                    I("act", lambda e: e.activation(out=sp_[:, 0:w], in_=sp_[:, 0:w], func=AF.Ln, bias=k.one1[:, 0:1], scale=1.0),
                      rd=[sp_, k.one1], wr=[sp_])
                    if last:
                        I("pool", lambda e: e.tensor_tensor(out=sp_[:, w - 128:w], in0=sp_[:, w - 128:w], in1=k.m01[:], op=ALU.mult),
                          rd=[sp_, k.m01], wr=[sp_])
                    f_ = Fb.next()
                    init = 0.0 if carry is None else carry[0][:, carry[1] - 1:carry[1]]
                    rdl = [sp_, k.ones] + ([carry[0]] if carry is not None else [])
                    I("dve", lambda e: e.tensor_tensor_scan(out=f_[:, 0:w], data0=k.ones[:, 0:w], data1=sp_[:, 0:w], initial=init,
                                                            op0=ALU.mult, op1=ALU.add), rd=rdl, wr=[f_])
                    carry = (f_, w)
                    t_ = t1b.next()
                    I("dve", lambda e: e.tensor_tensor(out=t_[:, 0:w], in0=z[:, 0:w], in1=sp_[:, 0:w], op=ALU.subtract),
                      rd=[z, sp_], wr=[t_])
                    I("pool", lambda e: e.tensor_tensor(out=a_[:, ks], in0=t_[:, 0:w], in1=f_[:, 0:w], op=ALU.add),
                      rd=[t_, f_], wr=[a_])
            dsl = slice(qb * 128, (qb + 1) * 128)
            I("pool", lambda e: e.tensor_tensor(out=a_[:, dsl], in0=a_[:, dsl], in1=(k.nmi if fox else k.nms)[:], op=ALU.add),
              rd=[a_, k.nmi, k.nms], wr=[a_])
            if fox:
                I("dve", lambda e: e.reduce_max(out=nb_[:, 0:1], in_=a_[:, 0:nk * 128], axis=AX.X), rd=[a_], wr=[nb_])
                I("dve", lambda e: e.tensor_scalar(out=nb_[:, 1:2], in0=nb_[:, 0:1], scalar1=-1.0, scalar2=None, op0=ALU.mult),
                  rd=[nb_], wr=[nb_])
            else:
                I("dve", lambda e: e.tensor_scalar(out=nb_[:, 1:2], in0=carry[0][:, carry[1] - 1:carry[1]], scalar1=-1.0, scalar2=None,
                                                   op0=ALU.mult), rd=[carry[0]], wr=[nb_])
            state[(h, qb)] = (a_, nb_)

        def stage2(h, qb):
            a_, nb_ = state.pop((h, qb))
            nk = qb + 1
            p_ = P.next()
            I("act", lambda e: e.activation(out=p_[:, 0:nk * 128], in_=a_[:, 0:nk * 128], func=AF.Exp, bias=nb_[:, 1:2], scale=1.0),
              rd=[a_, nb_], wr=[p_])
            o_ = op.next()
            prev = None

            def pv(kt, pt_, nbk):
                for j in range(nbk):
                    kb = kt * 4 + j
                    mm(k, o_[:], pt_[:, j, :], V[:, kb, h, :], kb == 0, kb == nk - 1, rd=[pt_, V], wr=[o_])
            for kt in range((nk + 3) // 4):
                nbk = min(4, nk - 4 * kt)
                t_ = tp.next()
                for j in range(nbk):
                    kb = kt * 4 + j
                    tr(k, t_[:, j, :], p_[:, kb * 128:(kb + 1) * 128], k.identb[:], rd=[p_, k.identb], wr=[t_])
                pt_ = PT.next()
                if kt % 2 == 0:
                    I("act", lambda e: e.activation(out=pt_[:, 0:nbk, :], in_=t_[:, 0:nbk, :], func=AF.Copy), rd=[t_], wr=[pt_])
                else:
                    I("dve", lambda e: e.tensor_copy(pt_[:, 0:nbk, :], t_[:, 0:nbk, :]), rd=[t_], wr=[pt_])
                if prev is not None:
                    pv(*prev)
                prev = (kt, pt_, nbk)
            pv(*prev)
            if fox:
                I("dve", lambda e: e.reciprocal(out=nb_[:, 0:1], in_=o_[:, 64:65]), rd=[o_], wr=[nb_])
                I("act", lambda e: e.activation(out=ytok[:, qb, h * 64:(h + 1) * 64], in_=o_[:, 0:64], func=AF.Copy,
                                                scale=nb_[:, 0:1]), rd=[o_, nb_], wr=[ytok])
            else:
                I("act", lambda e: e.activation(out=ytok[:, qb, h * 64:(h + 1) * 64], in_=o_[:, 0:64], func=AF.Copy),
                  rd=[o_], wr=[ytok])

        for i, it in enumerate(items):
            stage1(*it)
            if i > 0:
                stage2(*items[i - 1])
        stage2(*items[-1])
        for qb in range(NB):
            t_ = tp.next()
            for c in range(2):
                tr(k, t_[:, c, :], ytok[:, qb, c * 128:(c + 1) * 128], k.identb[:], rd=[ytok, k.identb], wr=[t_])
            I("act", lambda e: e.activation(out=yT[:, :, qb * 128:(qb + 1) * 128], in_=t_[:, 0:2, :], func=AF.Copy),
              rd=[t_], wr=[yT])
        S.dma("sp", k.ycat[1 if fox else 3], yT[:], rd=[yT], wr=[k.ycat])


def rwkv_prep(k, l, W):
    S, I = k.S, k.S.I
    with phase(k) as st:
        sb, ps = k.sb, k.ps
        rvec = sb(st, "rpvec", [128, 2, 7], F32)
        S.dma("sp", rvec[:], W["rvec"], wr=[rvec])
        nw0 = sb(st, "rpnw0", [128, 2], F32)
        I("dve", lambda e: e.tensor_scalar(out=nw0[:], in0=rvec[:, :, 0], scalar1=-1.0, scalar2=None, op0=ALU.mult),
          rd=[rvec], wr=[nw0])
        wa2b = sb(st, "rpwa2b", [128, 256], BF16)
        g2b = sb(st, "rpg2b", [128, 256], BF16)
        tw = sb(st, "rptw", [128, T], BF16)
        gs = sb(st, "rpgs", [128, T], BF16)
        with phase(k) as st2:
            wa2f = sb(st2, "rpwa2f", [128, 256], F32)
            g2f = sb(st2, "rpg2f", [128, 256], F32)
            x6 = sb(st2, "rpx6", [128, T], F32)
            x7 = sb(st2, "rpx7", [128, T], F32)
            S.dma("sp", wa2f[:], W["wa2"], wr=[wa2f])
            S.dma("sp", g2f[:], W["g2"], wr=[g2f])
            S.dma("sp", x6[:], k.rwxs[6], rd=[k.rwxs], wr=[x6])
            S.dma("sp", x7[:], k.rwxs[7], rd=[k.rwxs], wr=[x7])
            I("pool", lambda e: e.tensor_copy(wa2b[:], wa2f[:]), rd=[wa2f], wr=[wa2b])
            I("pool", lambda e: e.tensor_copy(g2b[:], g2f[:]), rd=[g2f], wr=[g2b])
            I("act", lambda e: e.activation(out=tw[0:64, :], in_=x6[0:64, :], func=AF.Tanh), rd=[x6], wr=[tw])
            I("pool", lambda e: e.tensor_copy(tw[64:128, :], x6[64:128, :]), rd=[x6], wr=[tw])
            I("act", lambda e: e.activation(out=gs[:], in_=x7[:], func=AF.Sigmoid), rd=[x7], wr=[gs])
        for oc in range(2):
            S.dma("sp", k.sc5[oc, :, 4, :], k.rwxs[oc], rd=[k.rwxs], wr=[k.sc5])
        ld = Ring([sb(st, "rpld%d" % i, [128, 3, 512], F32) for i in range(2)])
        o5 = Ring([sb(st, "rpo5%d" % i, [128, 4, 512], F32) for i in range(2)])
        wk = Ring([sb(st, "rpwk%d" % i, [128, 6, 512], F32) for i in range(2)])
        so = Ring([sb(st, "rpso%d" % i, [128, 3, 512], F32) for i in range(2)])
        pp = Ring([ps(st, "rppp%d" % i, [128, 512], F32) for i in range(6)])
        for oc in range(2):
            osl = slice(oc * 128, (oc + 1) * 128)
            for tt in range(NT):
                tsl = slice(tt * 512, (tt + 1) * 512)
                x_ = ld.next()
                for i, j in enumerate((oc, 2 + oc, 4 + oc)):
                    S.dma("sp", x_[:, i, :], k.rwxs[j, :, tsl], rd=[k.rwxs], wr=[x_])
                r_t, k_t, v_t = x_[:, 0, :], x_[:, 1, :], x_[:, 2, :]
                o_ = o5.next()
                w_ = wk.next()
                s_ = so.next()
                pw = pp.next()
                mm(k, pw[:], wa2b[0:64, osl], tw[0:64, tsl], True, True, rd=[wa2b, tw], wr=[pw])
                I("act", lambda e: e.activation(out=w_[:, 0, :], in_=pw[:], func=AF.Exp, bias=nw0[:, oc:oc + 1], scale=-1.0),
                  rd=[pw, nw0], wr=[w_])
                I("act", lambda e: e.activation(out=w_[:, 0, :], in_=w_[:, 0, :], func=AF.Ln, bias=k.one1[:, 0:1], scale=1.0),
                  rd=[w_, k.one1], wr=[w_])
                I("act", lambda e: e.activation(out=w_[:, 0, :], in_=w_[:, 0, :], func=AF.Exp, bias=k.mhalf[:, 0:1], scale=-1.0),
                  rd=[w_, k.mhalf], wr=[w_])
                I("pool", lambda e: e.tensor_scalar(out=o_[:, 0, :], in0=w_[:, 0, :], scalar1=-1.0, scalar2=None, op0=ALU.mult),
                  rd=[w_], wr=[o_])
                pa = pp.next()
                mm(k, pa[:], wa2b[64:128, osl], tw[64:128, tsl], True, True, rd=[wa2b, tw], wr=[pa])
                I("act", lambda e: e.activation(out=w_[:, 1, :], in_=pa[:], func=AF.Sigmoid, bias=rvec[:, oc, 1:2], scale=1.0),
                  rd=[pa, rvec], wr=[w_])
                a_t = w_[:, 1, :]
                I("dve", lambda e: e.tensor_scalar(out=w_[:, 2, :], in0=k_t, scalar1=rvec[:, oc, 2:3], scalar2=None, op0=ALU.mult),
                  rd=[x_, rvec], wr=[w_])
                I("pool", lambda e: e.tensor_tensor(out=w_[:, 3, :], in0=w_[:, 2, :], in1=w_[:, 2, :], op=ALU.mult), rd=[w_], wr=[w_])
                pn = pp.next()
                mm(k, pn[:], k.blk64[:], w_[:, 3, :], True, True, rd=[k.blk64, w_], wr=[pn])
                I("act", lambda e: e.activation(out=w_[:, 3, :], in_=pn[:], func=AF.Sqrt), rd=[pn], wr=[w_])
                I("dve", lambda e: e.tensor_scalar(out=w_[:, 3, :], in0=w_[:, 3, :], scalar1=1e-12, scalar2=None, op0=ALU.max),
                  rd=[w_], wr=[w_])
                I("dve", lambda e: e.reciprocal(out=w_[:, 3, :], in_=w_[:, 3, :]), rd=[w_], wr=[w_])
                I("dve", lambda e: e.tensor_tensor(out=w_[:, 2, :], in0=w_[:, 2, :], in1=w_[:, 3, :], op=ALU.mult), rd=[w_], wr=[w_])
                I("pool", lambda e: e.tensor_scalar(out=o_[:, 3, :], in0=w_[:, 2, :], scalar1=-1.0, scalar2=None, op0=ALU.mult),
                  rd=[w_], wr=[o_])
                I("pool", lambda e: e.tensor_tensor(out=o_[:, 1, :], in0=w_[:, 2, :], in1=a_t, op=ALU.mult), rd=[w_], wr=[o_])
                I("dve", lambda e: e.tensor_scalar(out=w_[:, 4, :], in0=a_t, scalar1=1.0, scalar2=rvec[:, oc, 3:4],
                                                   op0=ALU.subtract, op1=ALU.mult), rd=[w_, rvec], wr=[w_])
                I("dve", lambda e: e.scalar_tensor_tensor(out=o_[:, 2, :], in0=w_[:, 4, :], scalar=1.0, in1=k_t,
                                                          op0=ALU.add, op1=ALU.mult), rd=[w_, x_], wr=[o_])
                S.dma("sp", k.sc5[oc, :, 0:4, tsl], o_[:], rd=[o_], wr=[k.sc5])
                I("pool", lambda e: e.tensor_tensor(out=w_[:, 5, :], in0=r_t, in1=o_[:, 2, :], op=ALU.mult), rd=[x_, o_], wr=[w_])
                I("pool", lambda e: e.tensor_scalar(out=w_[:, 5, :], in0=w_[:, 5, :], scalar1=rvec[:, oc, 4:5], scalar2=None,
                                                    op0=ALU.mult), rd=[w_, rvec], wr=[w_])
                pb = pp.next()
                mm(k, pb[:], k.blk64[:], w_[:, 5, :], True, True, rd=[k.blk64, w_], wr=[pb])
                I("dve", lambda e: e.tensor_tensor(out=s_[:, 0, :], in0=pb[:], in1=v_t, op=ALU.mult), rd=[pb, x_], wr=[s_])
                S.dma("sp", k.bon[oc, :, tsl], s_[:, 0, :], rd=[s_], wr=[k.bon])
                pg = pp.next()
                mm(k, pg[:], g2b[:, osl], gs[:, tsl], True, True, rd=[g2b, gs], wr=[pg])
                I("act", lambda e: e.activation(out=s_[:, 1, :], in_=pg[:], func=AF.Copy), rd=[pg], wr=[s_])
                S.dma("sp", k.gg[oc, :, tsl], s_[:, 1, :], rd=[s_], wr=[k.gg])
                pv = pp.next()
                for b in range(4):
                    tr(k, pv[:, b * 128:(b + 1) * 128], x_[:, 2, b * 128:(b + 1) * 128], k.identf[:], rd=[x_, k.identf], wr=[pv])
                I("act", lambda e: e.activation(out=s_[:, 2, :], in_=pv[:], func=AF.Copy), rd=[pv], wr=[s_])
                S.dma("sp", k.vtok[tsl, oc, :].rearrange("(b p) c -> p b c", p=128),
                      s_[:, 2, :].rearrange("p (b c) -> p b c", b=4), rd=[s_], wr=[k.vtok])


def rwkv_rec(k, l, W, nsteps=T):
    S, I = k.S, k.S.I
    VC = 16
    with phase(k) as st:
        sb, ps = k.sb, k.ps
        rvec = sb(st, "rrvec", [128, 2, 7], F32)
        S.dma("sp", rvec[:], W["rvec"], wr=[rvec])
        Sx = [Ring([sb(st, "rrS%d_%d" % (hp, i), [128, 128], F32) for i in range(3)]) for hp in range(2)]
        S1 = [Ring([sb(st, "rrS1%d_%d" % (hp, i), [128, 128], F32) for i in range(2)]) for hp in range(2)]
        KK = [Ring([sb(st, "rrKK%d_%d" % (hp, i), [128, 128], F32) for i in range(3)]) for hp in range(2)]
        S2 = [Ring([sb(st, "rrS2%d_%d" % (hp, i), [128, 128], F32) for i in range(2)]) for hp in range(2)]
        sap = [Ring([ps(st, "rrsa%d_%d" % (hp, i), [128, 128], F32) for i in range(2)]) for hp in range(2)]
        yp = [ps(st, "rryp%d" % hp, [128, 512], F32) for hp in range(2)]
        c5 = [Ring([sb(st, "rrc5%d_%d" % (hp, i), [128, 5, 512], F32) for i in range(2)]) for hp in range(2)]
        vr = [Ring([sb(st, "rrvr%d_%d" % (hp, i), [128, VC, 128], F32) for i in range(2)]) for hp in range(2)]
        ysb = [sb(st, "rry%d" % hp, [128, T], F32) for hp in range(2)]
        for hp in range(2):
            for b in vr[hp].bufs:
                I("pool", lambda e: e.memset(b[:], 0.0), wr=[b])
            b0 = Sx[hp].bufs[0]
            I("dve", lambda e: e.memset(b0[:], 0.0), wr=[b0])
            if nsteps < T:
                I("pool", lambda e: e.memset(ysb[hp][:], 0.0), wr=[ysb[hp]])
        cur = [Sx[0].next(), Sx[1].next()]
        c5c = [None, None]
        vrc = [None, None]
        for t in range(nsteps):
            tl = t % 512
            for hp in range(2):
                if tl == 0:
                    c_ = c5[hp].next()
                    S.dma("sp", c_[:], k.sc5[hp, :, :, t:t + 512], rd=[k.sc5], wr=[c_])
                    c5c[hp] = c_
                if t % VC == 0:
                    v_ = vr[hp].next()
                    for hh in range(2):
                        S.dma("sp", v_[hh * 64:(hh + 1) * 64, :, hh * 64:(hh + 1) * 64],
                              k.vtok[t:t + VC, hp, hh * 64:(hh + 1) * 64].partition_broadcast(64), rd=[k.vtok], wr=[v_])
                    vrc[hp] = v_
                c_, v_ = c5c[hp], vrc[hp]
                old = cur[hp]
                new = Sx[hp].next()
                kk_ = KK[hp].next()
                I("act", lambda e: e.activation(out=kk_[:], in_=k.blk64[:], func=AF.Copy, scale=c_[:, 3, tl:tl + 1]),
                  rd=[k.blk64, c_], wr=[kk_])
                sa = sap[hp].next()
                mm(k, sa[:], kk_[:], old[:], True, True, rd=[kk_, old], wr=[sa])
                s1 = S1[hp].next()
                s2 = S2[hp].next()
                I("act", lambda e: e.activation(out=s1[:], in_=old[:], func=AF.Copy, scale=c_[:, 0, tl:tl + 1]),
                  rd=[old, c_], wr=[s1])
                I("pool", lambda e: e.tensor_scalar(out=s2[:], in0=v_[:, t % VC, :], scalar1=c_[:, 2, tl:tl + 1], scalar2=None, op0=ALU.mult),
                  rd=[v_, c_], wr=[s2])
                I("pool", lambda e: e.tensor_tensor(out=s1[:], in0=s1[:], in1=s2[:], op=ALU.add), rd=[s1, s2], wr=[s1])
                I("dve", lambda e: e.scalar_tensor_tensor(out=new[:], in0=sa[:], scalar=c_[:, 1, tl:tl + 1], in1=s1[:],
                                                          op0=ALU.mult, op1=ALU.add), rd=[sa, c_, s1], wr=[new])
                mm(k, yp[hp][:, tl:tl + 1], new[:], c_[:, 4, tl:tl + 1], True, True, rd=[new, c_], wr=[yp[hp]])
                cur[hp] = new
                if tl == 511 or t == nsteps - 1:
                    t0 = t - tl
                    ypb = yp[hp]
                    I("act", lambda e: e.activation(out=ysb[hp][:, t0:t0 + tl + 1], in_=ypb[:, 0:tl + 1], func=AF.Copy),
                      rd=[ypb], wr=[ysb[hp]])
        ld = Ring([sb(st, "rrld%d" % i, [128, 2, 512], F32) for i in range(2)])
        wk = Ring([sb(st, "rrwk%d" % i, [128, 3, 512], F32) for i in range(2)])
        ob = Ring([sb(st, "rrob%d" % i, [128, 512], BF16) for i in range(2)])
        for hp in range(2):
            for tt in range(NT):
                tsl = slice(tt * 512, (tt + 1) * 512)
                x_ = ld.next()
                S.dma("sp", x_[:, 0, :], k.bon[hp, :, tsl], rd=[k.bon], wr=[x_])
                S.dma("sp", x_[:, 1, :], k.gg[hp, :, tsl], rd=[k.gg], wr=[x_])
                w_ = wk.next()
                pm = sap[0].next()
                pv_ = sap[1].next()
                pmt = yp[0]
                mm(k, pmt[:], k.blk64[:], ysb[hp][:, tsl], True, True, rd=[k.blk64, ysb[hp]], wr=[pmt])
                I("dve", lambda e: e.scalar_tensor_tensor(out=w_[:, 0, :], in0=pmt[:], scalar=-1.0 / 64.0, in1=ysb[hp][:, tsl],
                                                          op0=ALU.mult, op1=ALU.add), rd=[pmt, ysb[hp]], wr=[w_])
                I("act", lambda e: e.activation(out=w_[:, 1, :], in_=w_[:, 0, :], func=AF.Square), rd=[w_], wr=[w_])
                pvt = yp[1]
                mm(k, pvt[:], k.blk64[:], w_[:, 1, :], True, True, rd=[k.blk64, w_], wr=[pvt])
                I("act", lambda e: e.activation(out=w_[:, 1, :], in_=pvt[:], func=AF.Ln, bias=k.epsgn[:, 0:1], scale=1.0 / 64.0),
                  rd=[pvt, k.epsgn], wr=[w_])
                I("act", lambda e: e.activation(out=w_[:, 1, :], in_=w_[:, 1, :], func=AF.Exp, scale=-0.5), rd=[w_], wr=[w_])
                I("dve", lambda e: e.tensor_tensor(out=w_[:, 0, :], in0=w_[:, 0, :], in1=w_[:, 1, :], op=ALU.mult), rd=[w_], wr=[w_])
                I("dve", lambda e: e.tensor_scalar(out=w_[:, 0, :], in0=w_[:, 0, :], scalar1=rvec[:, hp, 5:6], scalar2=rvec[:, hp, 6:7],
                                                   op0=ALU.mult, op1=ALU.add), rd=[w_, rvec], wr=[w_])
                I("pool", lambda e: e.tensor_tensor(out=w_[:, 0, :], in0=w_[:, 0, :], in1=x_[:, 0, :], op=ALU.add), rd=[w_, x_], wr=[w_])
                o_ = ob.next()
                I("pool", lambda e: e.tensor_tensor(out=o_[:], in0=w_[:, 0, :], in1=x_[:, 1, :], op=ALU.mult), rd=[w_, x_], wr=[o_])
                S.dma("sp", k.ycat[2, :, hp, tsl], o_[:], rd=[o_], wr=[k.ycat])


def rwkv_chunked(k, l, W, nsteps=T):
    S, I = k.S, k.S.I
    C = 64
    TA = 256
    NCH = nsteps // C
    with phase(k) as st:
        sb, ps = k.sb, k.ps
        rvec = sb(st, "rcvec", [128, 2, 7], F32)
        S.dma("sp", rvec[:], W["rvec"], wr=[rvec])
        ysb = [sb(st, "rcy%d" % hp, [128, T], F32) for hp in range(2)]
        if nsteps < T:
            for hp in range(2):
                I("pool", lambda e: e.memset(ysb[hp][:], 0.0), wr=[ysb[hp]])
        _rwkv_chunked_core(k, ysb, nsteps)
        _rwkv_post(k, ysb, rvec)


def _rwkv_chunked_core(k, ysb, nsteps):
    S, I = k.S, k.S.I
    C = 64
    TA = 256
    NCH = nsteps // C
    with phase(k) as st:
        sb, ps = k.sb, k.ps
        msu = sb(st, "rcmsu", [128, 128], F32)
        msl = sb(st, "rcmsl", [128, 128], F32)
        mui = sb(st, "rcmui", [128, 128], F32)
        rmask = sb(st, "rcrmask", [128, TA], F32)
        I("pool", lambda e: e.affine_select(out=msu[:], in_=k.blk64[:], compare_op=ALU.is_gt, fill=0.0, base=0,
                                            pattern=[[1, 128]], channel_multiplier=-1), rd=[k.blk64], wr=[msu])
        I("pool", lambda e: e.affine_select(out=msl[:], in_=k.blk64[:], compare_op=ALU.is_gt, fill=0.0, base=0,
                                            pattern=[[-1, 128]], channel_multiplier=1), rd=[k.blk64], wr=[msl])
        I("pool", lambda e: e.affine_select(out=mui[:], in_=k.blk64[:], compare_op=ALU.is_ge, fill=0.0, base=0,
                                            pattern=[[1, 128]], channel_multiplier=-1), rd=[k.blk64], wr=[mui])
        I("pool", lambda e: e.memset(rmask[:], 1.0), wr=[rmask])
        I("pool", lambda e: e.memset(rmask[:].rearrange("p (c t) -> p c t", t=C)[:, :, 0:1], 0.0), wr=[rmask])
        c5 = [Ring([sb(st, "rcc5%d_%d" % (hp, i), [128, 5, TA], F32) for i in range(2)]) for hp in range(2)]
        wkA = [Ring([sb(st, "rcwa%d_%d" % (hp, i), [128, 4, TA], F32) for i in range(2)]) for hp in range(2)]
        eLr = [Ring([sb(st, "rceL%d_%d" % (hp, i), [128, TA], F32) for i in range(3)]) for hp in range(2)]
        pad = [Ring([sb(st, "rcpad%d_%d" % (hp, i), [128, 4, TA // C, 128], F32) for i in range(3)]) for hp in range(2)]
        for hp in range(2):
            for b in pad[hp].bufs:
                I("pool", lambda e: e.memset(b[:], 0.0), wr=[b])
        NPER = 12
        pers = Ring([sb(st, "rcper%d" % i, [128, 6, 128], F32) for i in range(NPER)])
        tmp = Ring([sb(st, "rctmp%d" % i, [128, 8, 128], F32) for i in range(4)])
        Vp = Ring([sb(st, "rcvp%d" % i, [128, 128], F32) for i in range(8)])
        for b in Vp.bufs:
            I("pool", lambda e: e.memset(b[:], 0.0), wr=[b])
        Sx = [Ring([sb(st, "rcS%d_%d" % (hp, i), [128, 128], F32) for i in range(3)]) for hp in range(2)]
        cw = [Ring([sb(st, "rccw%d_%d" % (hp, i), [128, 3, 128], F32) for i in range(2)]) for hp in range(2)]
        pbB = Ring([ps(st, "rcpb%d" % i, [128, 4, 128], F32) for i in range(4)])
        pbA = [ps(st, "rcpa%d" % hp, [128, 3, 128], F32) for hp in range(2)]
        pbY = [ps(st, "rcpy%d" % hp, [128, 128], F32) for hp in range(2)]
        cur = []
        for hp in range(2):
            b0 = Sx[hp].next()
            I("dve", lambda e: e.memset(b0[:], 0.0), wr=[b0])
            cur.append(b0)

        tiles = {}

        def stageA(hp, ti):
            t0 = ti * TA
            c_ = c5[hp].next()
            S.dma("sp", c_[:], k.sc5[hp, :, :, t0:t0 + TA], rd=[k.sc5], wr=[c_])
            w_ = wkA[hp].next()
            e_ = eLr[hp].next()
            p_ = pad[hp].next()
            I("dve", lambda e: e.tensor_tensor_scan(out=w_[:, 0, :], data0=rmask[:], data1=c_[:, 0, :], initial=0.0,
                                                    op0=ALU.mult, op1=ALU.add), rd=[c_, rmask], wr=[w_])
            I("pool", lambda e: e.tensor_tensor(out=w_[:, 1, :], in0=w_[:, 0, :], in1=c_[:, 0, :], op=ALU.subtract), rd=[w_, c_], wr=[w_])
            I("act", lambda e: e.activation(out=e_[:], in_=w_[:, 0, :], func=AF.Exp), rd=[w_], wr=[e_])
            I("act", lambda e: e.activation(out=w_[:, 2, :], in_=w_[:, 0, :], func=AF.Exp, scale=-1.0), rd=[w_], wr=[w_])
            I("act", lambda e: e.activation(out=w_[:, 3, :], in_=w_[:, 1, :], func=AF.Exp), rd=[w_], wr=[w_])

            def v3(ap):
                return ap.rearrange("p (c t) -> p c t", t=C)
            for hh in range(2):
                rs = slice(hh * 64, hh * 64 + 64)
                specs = [(0, c_[rs, 3, :], w_[rs, 3, :]),
                         (1, c_[rs, 4, :], e_[rs, :]),
                         (2, c_[rs, 1, :], w_[rs, 2, :]),
                         (3, c_[rs, 2, :], w_[rs, 2, :])]
                for qi, a_, b_ in specs:
                    eng = "dve" if qi % 2 == 0 else "pool"
                    I(eng, lambda e: e.tensor_tensor(out=p_[rs, qi, :, rs], in0=v3(a_), in1=v3(b_), op=ALU.mult),
                      rd=[c_, w_, e_], wr=[p_])
            tiles[(hp, ti)] = (c_, e_, p_)

        inst = {}

        def stageB(group):
            ctx = []
            for (ch, hp) in group:
                ti, ci = divmod(ch, TA // C)
                c_, e_, p_ = tiles[(hp, ti)]
                Ap, Rp, Bp, Kp = (p_[:, q, ci, :] for q in range(4))
                pr = pers.next()
                tm = tmp.next()
                pb = pbB.next()
                inst[(ch, hp)] = pr
                ctx.append((ch, hp, p_, Ap, Rp, Bp, Kp, pr, tm, pb))
            for (ch, hp, p_, Ap, Rp, Bp, Kp, pr, tm, pb) in ctx:
                mm(k, pb[:, 0, :], Bp, Ap, True, True, rd=[p_], wr=[pb])
                mm(k, pb[:, 1, :], Ap, Bp, True, True, rd=[p_], wr=[pb])
                mm(k, pb[:, 2, :], Kp, Ap, True, True, rd=[p_], wr=[pb])
                mm(k, pb[:, 3, :], Bp, Rp, True, True, rd=[p_], wr=[pb])
            for (ch, hp, p_, Ap, Rp, Bp, Kp, pr, tm, pb) in ctx:
                I("dve", lambda e: e.tensor_tensor(out=tm[:, 0, :], in0=pb[:, 0, :], in1=msu[:], op=ALU.mult), rd=[pb, msu], wr=[tm])
                I("dve", lambda e: e.tensor_tensor(out=tm[:, 1, :], in0=pb[:, 1, :], in1=msl[:], op=ALU.mult), rd=[pb, msl], wr=[tm])
                I("dve", lambda e: e.tensor_tensor(out=pr[:, 1, :], in0=pb[:, 2, :], in1=msu[:], op=ALU.mult), rd=[pb, msu], wr=[pr])
                I("dve", lambda e: e.tensor_tensor(out=pr[:, 2, :], in0=pb[:, 3, :], in1=mui[:], op=ALU.mult), rd=[pb, mui], wr=[pr])
                I("pool", lambda e: e.tensor_tensor(out=tm[:, 4, :], in0=tm[:, 0, :], in1=k.identf[:], op=ALU.add), rd=[tm, k.identf], wr=[tm])
                I("pool", lambda e: e.tensor_tensor(out=tm[:, 5, :], in0=tm[:, 1, :], in1=k.identf[:], op=ALU.add), rd=[tm, k.identf], wr=[tm])
            for (ch, hp, p_, Ap, Rp, Bp, Kp, pr, tm, pb) in ctx:
                mm(k, pb[:, 0, :], Kp, Rp, True, True, rd=[p_], wr=[pb])
                tr(k, pb[:, 1, :], Bp, k.identf[:], rd=[p_, k.identf], wr=[pb])
                tr(k, pb[:, 2, :], Kp, k.identf[:], rd=[p_, k.identf], wr=[pb])
            for (ch, hp, p_, Ap, Rp, Bp, Kp, pr, tm, pb) in ctx:
                I("dve", lambda e: e.tensor_tensor(out=pr[:, 3, :], in0=pb[:, 0, :], in1=mui[:], op=ALU.mult), rd=[pb, mui], wr=[pr])
                I("act", lambda e: e.activation(out=pr[:, 4:6, :], in_=pb[:, 1:3, :], func=AF.Copy), rd=[pb], wr=[pr])
            for j in range(1, 6):
                po, pn = (0, 2) if j % 2 == 1 else (2, 0)
                qo, qn = (1, 3) if j % 2 == 1 else (3, 1)
                go, gn = (4, 6) if j % 2 == 1 else (6, 4)
                ho, hn = (5, 7) if j % 2 == 1 else (7, 5)
                for (ch, hp, p_, Ap, Rp, Bp, Kp, pr, tm, pb) in ctx:
                    mm(k, pb[:, 0, :], tm[:, qo, :], tm[:, po, :], True, True, rd=[tm], wr=[pb])
                    if j <= 4:
                        mm(k, pb[:, 1, :], tm[:, po, :], tm[:, qo, :], True, True, rd=[tm], wr=[pb])
                for (ch, hp, p_, Ap, Rp, Bp, Kp, pr, tm, pb) in ctx:
                    if j <= 4:
                        I("act", lambda e: e.activation(out=tm[:, pn:pn + 2, :] if qn == pn + 1 else tm[:, pn, :],
                                                        in_=pb[:, 0:2, :] if qn == pn + 1 else pb[:, 0, :], func=AF.Copy), rd=[pb], wr=[tm])
                        if qn != pn + 1:
                            I("act", lambda e: e.activation(out=tm[:, qn, :], in_=pb[:, 1, :], func=AF.Copy), rd=[pb], wr=[tm])
                    else:
                        I("act", lambda e: e.activation(out=tm[:, pn, :], in_=pb[:, 0, :], func=AF.Copy), rd=[pb], wr=[tm])
                for (ch, hp, p_, Ap, Rp, Bp, Kp, pr, tm, pb) in ctx:
                    mm(k, pb[:, 2, :], tm[:, ho, :], tm[:, pn, :], True, True, rd=[tm], wr=[pb])
                    if j <= 4:
                        mm(k, pb[:, 3, :], tm[:, pn, :], tm[:, ho, :], True, True, rd=[tm], wr=[pb])
                for (ch, hp, p_, Ap, Rp, Bp, Kp, pr, tm, pb) in ctx:
                    gout = pr[:, 0, :] if j == 5 else tm[:, gn, :]
                    I("dve", lambda e: e.tensor_tensor(out=gout, in0=pb[:, 2, :], in1=tm[:, go, :], op=ALU.add), rd=[pb, tm], wr=[tm, pr])
                    if j <= 4:
                        I("dve", lambda e: e.tensor_tensor(out=tm[:, hn, :], in0=pb[:, 3, :], in1=tm[:, ho, :], op=ALU.add), rd=[pb, tm], wr=[tm])

        def stageC(ch, hp):
            ti, ci = divmod(ch, TA // C)
            c_, e_, p_ = tiles[(hp, ti)]
            Ap, Rp = p_[:, 0, ci, :], p_[:, 1, ci, :]
            pr = inst.pop((ch, hp))
            t0 = ch * C
            vp = Vp.next()
            for hh in range(2):
                rs = slice(hh * 64, hh * 64 + 64)
                S.dma("sp", vp[rs, rs], k.vtok[t0:t0 + C, hp, rs], rd=[k.vtok], wr=[vp])
            old = cur[hp]
            new = Sx[hp].next()
            w_ = cw[hp].next()
            pa, py = pbA[hp], pbY[hp]
            elc = e_[:, ci * C + C - 1:ci * C + C]
            mm(k, pa[:, 0, :], Ap, old[:], True, False, rd=[p_, old], wr=[pa])
            mm(k, pa[:, 0, :], pr[:, 1, :], vp[:], False, True, rd=[pr, vp], wr=[pa])
            I("act", lambda e: e.activation(out=w_[:, 0, :], in_=pa[:, 0, :], func=AF.Copy), rd=[pa], wr=[w_])
            I("pool", lambda e: e.tensor_scalar(out=w_[:, 2, :], in0=old[:], scalar1=elc, scalar2=None, op0=ALU.mult), rd=[old, e_], wr=[w_])
            mm(k, pa[:, 1, :], pr[:, 0, :], w_[:, 0, :], True, True, rd=[pr, w_], wr=[pa])
            I("dve", lambda e: e.tensor_copy(w_[:, 1, :], pa[:, 1, :]), rd=[pa], wr=[w_])
            mm(k, pa[:, 2, :], pr[:, 4, :], w_[:, 1, :], True, False, rd=[pr, w_], wr=[pa])
            mm(k, pa[:, 2, :], pr[:, 5, :], vp[:], False, True, rd=[pr, vp], wr=[pa])
            I("dve", lambda e: e.scalar_tensor_tensor(out=new[:], in0=pa[:, 2, :], scalar=elc, in1=w_[:, 2, :],
                                                      op0=ALU.mult, op1=ALU.add), rd=[pa, e_, w_], wr=[new])
            mm(k, py[:], old[:], Rp, True, False, rd=[old, p_], wr=[py])
            mm(k, py[:], w_[:, 1, :], pr[:, 2, :], False, False, rd=[w_, pr], wr=[py])
            mm(k, py[:], vp[:], pr[:, 3, :], False, True, rd=[vp, pr], wr=[py])
            for hh in range(2):
                rs = slice(hh * 64, hh * 64 + 64)
                I("act", lambda e: e.activation(out=ysb[hp][rs, t0:t0 + C], in_=py[rs, rs], func=AF.Copy), rd=[py], wr=[ysb[hp]])
            cur[hp] = new

        GC = 2
        ngroups = NCH // GC
        def group(g):
            return [(g * GC + c, hp) for c in range(GC) for hp in range(2)]
        def ensureA(g):
            for (ch, hp) in group(g):
                ti = ch // (TA // C)
                if (hp, ti) not in tiles:
                    stageA(hp, ti)
        ensureA(0)
        stageB(group(0))
        for g in range(ngroups):
            if g + 1 < ngroups:
                ensureA(g + 1)
                stageB(group(g + 1))
            for (ch, hp) in group(g):
                stageC(ch, hp)


def _rwkv_post(k, ysb, rvec):
    S, I = k.S, k.S.I
    with phase(k) as st:
        sb, ps = k.sb, k.ps
        pbB = Ring([ps(st, "rcpq%d" % i, [128, 4, 128], F32) for i in range(2)])
        ld = Ring([sb(st, "rcld%d" % i, [128, 2, 512], F32) for i in range(2)])
        wk = Ring([sb(st, "rcwk%d" % i, [128, 3, 512], F32) for i in range(2)])
        ob = Ring([sb(st, "rcob%d" % i, [128, 512], BF16) for i in range(2)])
        pmt, pvt = pbB.next(), pbB.next()
        pmt = Buf(pmt.t.rearrange("p a b -> p (a b)"), excl=True)
        pvt = Buf(pvt.t.rearrange("p a b -> p (a b)"), excl=True)
        for hp in range(2):
            for tt in range(NT):
                tsl = slice(tt * 512, (tt + 1) * 512)
                x_ = ld.next()
                S.dma("sp", x_[:, 0, :], k.bon[hp, :, tsl], rd=[k.bon], wr=[x_])
                S.dma("sp", x_[:, 1, :], k.gg[hp, :, tsl], rd=[k.gg], wr=[x_])
                w_ = wk.next()
                mm(k, pmt[:], k.blk64[:], ysb[hp][:, tsl], True, True, rd=[k.blk64, ysb[hp]], wr=[pmt])
                I("dve", lambda e: e.scalar_tensor_tensor(out=w_[:, 0, :], in0=pmt[:], scalar=-1.0 / 64.0, in1=ysb[hp][:, tsl],
                                                          op0=ALU.mult, op1=ALU.add), rd=[pmt, ysb[hp]], wr=[w_])
                I("act", lambda e: e.activation(out=w_[:, 1, :], in_=w_[:, 0, :], func=AF.Square), rd=[w_], wr=[w_])
                mm(k, pvt[:], k.blk64[:], w_[:, 1, :], True, True, rd=[k.blk64, w_], wr=[pvt])
                I("act", lambda e: e.activation(out=w_[:, 1, :], in_=pvt[:], func=AF.Ln, bias=k.epsgn[:, 0:1], scale=1.0 / 64.0),
                  rd=[pvt, k.epsgn], wr=[w_])
                I("act", lambda e: e.activation(out=w_[:, 1, :], in_=w_[:, 1, :], func=AF.Exp, scale=-0.5), rd=[w_], wr=[w_])
                I("dve", lambda e: e.tensor_tensor(out=w_[:, 0, :], in0=w_[:, 0, :], in1=w_[:, 1, :], op=ALU.mult), rd=[w_], wr=[w_])
                I("dve", lambda e: e.tensor_scalar(out=w_[:, 0, :], in0=w_[:, 0, :], scalar1=rvec[:, hp, 5:6], scalar2=rvec[:, hp, 6:7],
                                                   op0=ALU.mult, op1=ALU.add), rd=[w_, rvec], wr=[w_])
                I("pool", lambda e: e.tensor_tensor(out=w_[:, 0, :], in0=w_[:, 0, :], in1=x_[:, 0, :], op=ALU.add), rd=[w_, x_], wr=[w_])
                o_ = ob.next()
                I("pool", lambda e: e.tensor_tensor(out=o_[:], in0=w_[:, 0, :], in1=x_[:, 1, :], op=ALU.mult), rd=[w_, x_], wr=[o_])
                S.dma("sp", k.ycat[2, :, hp, tsl], o_[:], rd=[o_], wr=[k.ycat])


def precast_mlp(k, l, W):
    with phase(k) as st:
        cast_weight(k, st, W["w1"], k.w1s, k.w1s, 1024, 4096, "w1", to_dram=True)
    with phase(k) as st:
        cast_weight(k, st, W["w2m"], k.w2s, k.w2s, 4096, 1024, "w2", to_dram=True, dst_fn=lambda n0, n1: k.w2s[n0 // 128])


def merge_phase(k, l, W, h1T):
    S, I = k.S, k.S.I
    with phase(k) as st:
        sb, ps = k.sb, k.ps
        wbr = sb(st, "mgwbr", [128, 8, 1024], BF16)
        wout = sb(st, "mgwout", [128, 8, 1024], BF16)
        with phase(k) as st2:
            cast_weight(k, st2, W["wbr"], wbr, wbr, 1024, 1024, "br")
        with phase(k) as st2:
            cast_weight(k, st2, W["wout"], wout, wout, 1024, 1024, "wo")
        ln = sb(st, "mgln", [128, 8, 2], F32)
        S.dma("sp", ln[:], W["ln1"], wr=[ln])
        yt = Ring([sb(st, "mgyt%d" % i, [128, 4, 2, 512], BF16) for i in range(2)])
        gt = Ring([sb(st, "mggt%d" % i, [128, 4, 512], BF16) for i in range(3)])
        acc = Ring([sb(st, "mgacc%d" % i, [128, 2, 512], F32) for i in range(2)])
        mg = Ring([sb(st, "mgmg%d" % i, [128, 8, 512], BF16) for i in range(1)])
        hr = Ring([sb(st, "mghr%d" % i, [128, 8, 512], F32) for i in range(1)])
        z = Ring([sb(st, "mgz%d" % i, [128, 8, 512], F32) for i in range(1)])
        sq = sb(st, "mgsq", [128, 8, 512], F32)
        msb = sb(st, "mgmsb", [128, 512], F32)
        up = Ring([ps(st, "mgup%d" % i, [128, 512], F32) for i in range(4)])
        mps = ps(st, "mgmps", [128, 512], F32)
        vps = ps(st, "mgvps", [128, 512], F32)
        gview = k.gsc.t.rearrange("(n m) p t -> m p n t", n=4)
        for tt in range(NT):
            tsl = slice(tt * 512, (tt + 1) * 512)
            y_ = yt.next()
            for n in range(4):
                S.dma("sp", y_[:, n, :, :], k.ycat[n, :, :, tsl], rd=[k.ycat], wr=[y_])
            h_ = hr.next()
            S.dma("sp", h_[:], k.hres[:, :, tsl], rd=[k.hres], wr=[h_])
            m_ = mg.next()
            for mc in range(8):
                msl = slice(mc * 128, (mc + 1) * 128)
                g_ = gt.next()
                S.dma("sp", g_[:], gview[mc][:, :, tsl], rd=[k.gsc], wr=[g_])
                a_ = acc.next()
                for n in range(4):
                    u_ = up.next()
                    for kc in range(2):
                        mm(k, u_[:], wbr[:, n * 2 + kc, msl], y_[:, n, kc, :], kc == 0, kc == 1, rd=[wbr, y_], wr=[u_])
                    if n == 0:
                        I("dve", lambda e: e.tensor_tensor(out=a_[:, 0, :], in0=u_[:], in1=g_[:, 0, :], op=ALU.mult),
                          rd=[u_, g_], wr=[a_])
                    else:
                        I("dve", lambda e: e.tensor_tensor(out=a_[:, 1, :], in0=u_[:], in1=g_[:, n, :], op=ALU.mult),
                          rd=[u_, g_], wr=[a_])
                        if n < 3:
                            I("pool", lambda e: e.tensor_tensor(out=a_[:, 0, :], in0=a_[:, 0, :], in1=a_[:, 1, :], op=ALU.add),
                              rd=[a_], wr=[a_])
                        else:
                            I("pool", lambda e: e.tensor_tensor(out=m_[:, mc, :], in0=a_[:, 0, :], in1=a_[:, 1, :], op=ALU.add),
                              rd=[a_], wr=[m_])
            z_ = z.next()
            for mc in range(8):
                msl = slice(mc * 128, (mc + 1) * 128)
                u_ = up.next()
                for kc in range(8):
                    mm(k, u_[:], wout[:, kc, msl], m_[:, kc, :], kc == 0, kc == 7, rd=[wout, m_], wr=[u_])
                I("dve", lambda e: e.scalar_tensor_tensor(out=z_[:, mc, :], in0=h_[:, mc, :], scalar=ALPHA, in1=u_[:],
                                                          op0=ALU.mult, op1=ALU.add), rd=[h_, u_], wr=[z_])

            def outs(c, dn, g_ap, b_ap):
                I("pool", lambda e: e.tensor_scalar(out=dn, in0=dn, scalar1=g_ap, scalar2=b_ap, op0=ALU.mult, op1=ALU.add),
                  rd=[z_, ln], wr=[z_])
            layer_norm_tile(k, (mps, msb, sq, vps), z_, ln, LN_EPS, outs)
            I("act", lambda e: e.activation(out=h1T[:, :, tsl], in_=z_[:], func=AF.Copy), rd=[z_], wr=[h1T])
            S.dma("sp", k.h1res[:, :, tsl], z_[:], rd=[z_], wr=[k.h1res])


def mlp_phase(k, l, W, h1T, p_l, out, last):
    S, I = k.S, k.S.I
    with phase(k) as st:
        sb, ps = k.sb, k.ps
        pgw = sb(st, "mlpgw", [128, 8, 1024], BF16)
        plw = sb(st, "mlplw", [128, 2, 1024], BF16)
        with phase(k) as st2:
            cast_weight(k, st2, W["pgw"], pgw, pgw, 1024, 1024, "pg")
        with phase(k) as st2:
            cast_weight(k, st2, W["plew"], plw, plw, 256, 1024, "pl")
        with phase(k) as st2:
            pin = Ring([sb(st2, "mlpin%d" % i, [128, 256], F32) for i in range(2)])
            tps = Ring([ps(st2, "mltps%d" % i, [128, 2, 128], F32) for i in range(2)])
            pT = sb(st2, "mlpT", [128, 2, T], BF16)
            for blk in range(NB):
                pi_ = pin.next()
                S.dma("sp", pi_[:], p_l[blk * 128:(blk + 1) * 128, :], wr=[pi_])
                t_ = tps.next()
                for c in range(2):
                    tr(k, t_[:, c, :], pi_[:, c * 128:(c + 1) * 128], k.identf[:], rd=[pi_, k.identf], wr=[t_])
                I("act", lambda e: e.activation(out=pT[:, :, blk * 128:(blk + 1) * 128], in_=t_[:], func=AF.Copy), rd=[t_], wr=[pT])
            S.dma("sp", k.pTs[:], pT[:], rd=[pT], wr=[k.pTs])
        ln = sb(st, "mlln", [128, 8, 2], F32)
        S.dma("sp", ln[:], W["ln2"], wr=[ln])
        w1r = Ring([sb(st, "mlw1%d" % i, [128, 8, 512], BF16) for i in range(2)])
        w2r = Ring([sb(st, "mlw2%d" % i, [128, 32, 128], BF16) for i in range(2)])
        pTr = Ring([sb(st, "mlpT%d" % i, [128, 2, 512], BF16) for i in range(1)])
        a = sb(st, "mla", [128, 32, 512], BF16)
        rl = Ring([sb(st, "mlrl%d" % i, [128, 512], F32) for i in range(2)])
        hr = sb(st, "mlhr", [128, 8, 512], F32)
        z = sb(st, "mlz", [128, 8, 512], F32)
        msb = sb(st, "mlmsb", [128, 512], F32)
        sg = Ring([sb(st, "mlsg%d" % i, [128, 2, 512], F32) for i in range(1)])
        up = Ring([ps(st, "mlup%d" % i, [128, 512], F32) for i in range(3)])
        gp = Ring([ps(st, "mlgp%d" % i, [128, 512], F32) for i in range(2)])
        mps = ps(st, "mlmps", [128, 512], F32)
        vps = ps(st, "mlvps", [128, 512], F32)
        tpo = ps(st, "mltpo", [128, 4, 128], F32)
        ot = Ring([sb(st, "mlot%d" % i, [128, D], F32) for i in range(1)]) if last else None
        hb = Ring([sb(st, "mlhb%d" % i, [128, 8, 512], BF16) for i in range(1)]) if not last else None
        for tt in range(NT):
            tsl = slice(tt * 512, (tt + 1) * 512)
            S.dma("sp", hr[:], k.h1res[:, :, tsl], rd=[k.h1res], wr=[hr])
            pT = pTr.next()
            S.dma("sp", pT[:], k.pTs[:, :, tsl], rd=[k.pTs], wr=[pT])
            for fg in range(8):
                w1_ = w1r.next()
                S.dma("sp", w1_[:], k.w1s[:, :, fg * 512:(fg + 1) * 512], rd=[k.w1s], wr=[w1_])
                for f8 in range(4):
                    fc = fg * 4 + f8
                    u_ = up.next()
                    for kc in range(8):
                        mm(k, u_[:], w1_[:, kc, f8 * 128:(f8 + 1) * 128], h1T[:, kc, tsl], kc == 0, kc == 7, rd=[w1_, h1T], wr=[u_])
                    r_ = rl.next()
                    I("act", lambda e: e.activation(out=r_[:], in_=u_[:], func=AF.Relu), rd=[u_], wr=[r_])
                    I("pool", lambda e: e.tensor_tensor(out=a[:, fc, :], in0=r_[:], in1=r_[:], op=ALU.mult), rd=[r_], wr=[a])
            for mc in range(8):
                msl = slice(mc * 128, (mc + 1) * 128)
                s_ = sg.next()
                g_ = gp.next()
                for kc in range(8):
                    mm(k, g_[:], pgw[:, kc, msl], h1T[:, kc, tsl], kc == 0, kc == 7, rd=[pgw, h1T], wr=[g_])
                I("act", lambda e: e.activation(out=s_[:, 0, :], in_=g_[:], func=AF.Sigmoid), rd=[g_], wr=[s_])
                g2_ = gp.next()
                for kc in range(2):
                    mm(k, g2_[:], plw[:, kc, msl], pT[:, kc, :], kc == 0, kc == 1, rd=[plw, pT], wr=[g2_])
                I("dve", lambda e: e.tensor_tensor(out=s_[:, 1, :], in0=g2_[:], in1=s_[:, 0, :], op=ALU.mult), rd=[g2_, s_], wr=[s_])
                u_ = up.next()
                w2_ = w2r.next()
                S.dma("sp", w2_[:], k.w2s[mc], rd=[k.w2s], wr=[w2_])
                for fc in range(32):
                    mm(k, u_[:], w2_[:, fc, :], a[:, fc, :], fc == 0, fc == 31, rd=[w2_, a], wr=[u_])
                I("dve", lambda e: e.scalar_tensor_tensor(out=z[:, mc, :], in0=hr[:, mc, :], scalar=ALPHA, in1=u_[:],
                                                          op0=ALU.mult, op1=ALU.add), rd=[hr, u_], wr=[z])
                I("pool", lambda e: e.tensor_tensor(out=z[:, mc, :], in0=z[:, mc, :], in1=s_[:, 1, :], op=ALU.add), rd=[z, s_], wr=[z])

            def outs(c, dn, g_ap, b_ap):
                I("pool", lambda e: e.tensor_scalar(out=dn, in0=dn, scalar1=g_ap, scalar2=b_ap, op0=ALU.mult, op1=ALU.add),
                  rd=[z, ln], wr=[z])
            layer_norm_tile(k, (mps, msb, hr, vps), z, ln, LN_EPS, outs)
            if not last:
                hb_ = hb.next()
                I("act", lambda e: e.activation(out=hb_[:], in_=z[:], func=AF.Copy), rd=[z], wr=[hb_])
                S.dma("sp", k.hTs[:, :, tsl], hb_[:], rd=[hb_], wr=[k.hTs])
                S.dma("sp", k.hres[:, :, tsl], z[:], rd=[z], wr=[k.hres])
            else:
                for b in range(4):
                    o_ = ot.next()
                    for half in range(2):
                        for j in range(4):
                            c = half * 4 + j
                            tr(k, tpo[:, j, :], z[:, c, b * 128:(b + 1) * 128], k.identf[:], rd=[z, k.identf], wr=[tpo])
                        I("act", lambda e: e.activation(out=o_[:, half * 512:(half + 1) * 512],
                                                        in_=tpo[:].rearrange("p a b -> p (a b)"), func=AF.Copy), rd=[tpo], wr=[o_])
                    r0 = tt * 512 + b * 128
                    S.dma("sp", out[r0:r0 + 128, :], o_[:], rd=[o_], wr=[Dep()])


_CACHE = {}


def kernel(**inputs):
    if "nc" not in _CACHE:
        _CACHE["nc"] = build_program(depth=DEPTH)[0]
    nc = _CACHE["nc"]
    shared = {}
    for l in range(DEPTH):
        for n, v in prep_layer(inputs, l).items():
            shared["%s_%d" % (n, l)] = v
    x = np.asarray(inputs["x"], np.float32)
    p = np.asarray(inputs["p"], np.float32)
    in_maps = []
    for b in range(8):
        m = dict(shared)
        m["x"] = np.ascontiguousarray(x[b])
        m["p"] = np.ascontiguousarray(p[:, b])
        in_maps.append(m)
    res = run_bass_kernel_spmd(nc, in_maps, core_ids=list(range(8)))
    return np.stack([np.asarray(r["out"], np.float32) for r in res.results], axis=0)
```

```python
import math
import contextlib
import numpy as np
import concourse.bass as bass
import concourse.mybir as mybir
from concourse.bass_utils import run_bass_kernel_spmd

F32 = mybir.dt.float32
BF16 = mybir.dt.bfloat16
AF = mybir.ActivationFunctionType
ALU = mybir.AluOpType
AX = mybir.AxisListType

T = 4096
D = 1024
NT = 8
NB = 32
DEPTH = 2
ALPHA = (2 * DEPTH) ** 0.25
LN_EPS = 1e-5
GN_EPS = 64e-5
PI = math.pi


class Dep:
    __slots__ = ("w", "r", "excl")

    def __init__(self):
        self.w = {}
        self.r = {}
        self.excl = False


class Buf:
    def __init__(self, t, excl=False):
        self.t = t
        self.dep = Dep()
        self.dep.excl = excl

    def __getitem__(self, k):
        return self.t[k]


class Ring:
    def __init__(self, bufs):
        self.bufs = bufs
        self.i = 0

    def next(self):
        b = self.bufs[self.i % len(self.bufs)]
        self.i += 1
        return b


class Sched:
    EPOCH = 30000
    NDMA = 6
    NEPOCH = 8
    import os as _os
    qmap = {} if _os.environ.get("KNOB_POOLQ") else {"pool": "sp"}

    def __init__(self, nc):
        self.nc = nc
        self.eng = {"pe": nc.tensor, "act": nc.scalar, "dve": nc.vector,
                    "pool": nc.gpsimd, "sp": nc.sync}
        self.sem = {}
        self.cnt = {}
        self.seen = {e: {} for e in self.eng}
        self.nsem = 0
        self.ninst = 0
        self.allsems = []
        self.epool = {e: [self._alloc("e_%s_%d" % (e, i)) for i in range(self.NEPOCH)] for e in ("pe", "act", "dve", "pool")}
        self.dq = {}
        for q in ("sp", "pool"):
            self.dq[q] = {"sems": [self._alloc("dq_%s_%d" % (q, i)) for i in range(self.NDMA)], "n": 0}
        for e in ("pe", "act", "dve", "pool"):
            self._new_epoch(e)

    def _alloc(self, name):
        self.nsem += 1
        h = self.nc.alloc_semaphore(name)
        self.allsems.append(h)
        return h

    def _new_epoch(self, e):
        self.sem[e] = self.epool[e].pop(0)
        self.cnt[e] = 0

    def _wait(self, eng, sem, val):
        key = id(sem)
        if self.seen[eng].get(key, 0) >= val:
            return
        self.eng[eng].wait_ge(sem, val)
        self.seen[eng][key] = val

    def _gather(self, eng, rd, wr, same_ok=True):
        need = {}

        def add(p):
            pe, sem, val = p
            if pe == eng and (same_ok or eng == "pe"):
                return
            k = id(sem)
            if k not in need or need[k][1] < val:
                need[k] = (sem, val)
        for d in rd:
            for p in d.w.values():
                add(p)
        for d in wr:
            for p in d.w.values():
                add(p)
            for p in d.r.values():
                add(p)
        for sem, val in need.values():
            self._wait(eng, sem, val)

    @staticmethod
    def _deps(lst):
        return [x.dep if isinstance(x, Buf) else x for x in lst]

    @staticmethod
    def _merge(dct, tok):
        k = id(tok[1])
        if k not in dct or dct[k][2] < tok[2]:
            dct[k] = tok

    def I(self, eng, fn, rd=(), wr=()):
        rd = self._deps(rd)
        wr = self._deps(wr)
        ex = [d for d in rd if d.excl and d not in wr]
        if ex:
            rd = [d for d in rd if not d.excl]
            wr = list(wr) + ex
        self._gather(eng, rd, wr, same_ok=False)
        if self.cnt[eng] >= self.EPOCH:
            self._new_epoch(eng)
        ins = fn(self.eng[eng])
        self.ninst += 1
        self.cnt[eng] += 1
        ins.then_inc(self.sem[eng], 1)
        tok = (eng, self.sem[eng], self.cnt[eng])
        for d in rd:
            self._merge(d.r, tok)
        for d in wr:
            d.w = {id(tok[1]): tok}
            d.r = {}
        return ins

    def dma(self, q, out, in_, rd=(), wr=()):
        rd = self._deps(rd)
        wr = self._deps(wr)
        q = self.qmap.get(q, q)
        dq = self.dq[q]
        j = dq["n"]
        dq["n"] += 1
        sem = dq["sems"][j % self.NDMA]
        val = 16 * (j // self.NDMA + 1)
        if val > 16:
            self._wait(q, sem, val - 16)
        self._gather(q, rd, wr, same_ok=False)
        ins = self.eng[q].dma_start(out=out, in_=in_)
        self.ninst += 1
        ins.then_inc(sem, 16)
        tok = ("dma", sem, val)
        for d in rd:
            self._merge(d.r, tok)
        for d in wr:
            self._merge(d.w, tok)
            d.r = {}
        return ins

    def barrier(self, force=False):
        for w in ("pe", "act", "dve", "pool", "sp"):
            for q, dq in self.dq.items():
                n = dq["n"]
                for i in range(self.NDMA):
                    cnt_i = len(range(i, n, self.NDMA))
                    if cnt_i > 0:
                        self._wait(w, dq["sems"][i], 16 * cnt_i)
            for e in ("pe", "act", "dve", "pool"):
                if e != w and self.cnt[e] > 0:
                    self._wait(w, self.sem[e], self.cnt[e])

    def finish(self):
        for q, dq in self.dq.items():
            n = dq["n"]
            for i in range(self.NDMA):
                cnt_i = len(range(i, n, self.NDMA))
                if cnt_i > 0:
                    self._wait("sp", dq["sems"][i], 16 * cnt_i)
        for e in ("pe", "act", "dve", "pool"):
            if self.cnt[e] > 0:
                self._wait("sp", self.sem[e], self.cnt[e])


def _chunkcol(v):
    v = np.asarray(v, np.float32)
    return np.ascontiguousarray(v.reshape(-1, 128).T)


def prep_layer(inp, l):
    f = np.float32
    w_in = np.asarray(inp["w_in"][l], f)
    o = {}
    o["winF"] = np.ascontiguousarray(np.concatenate(
        [w_in[:, 0:256], w_in[:, 256:512], w_in[:, 512:768], w_in[:, 1028:2052],
         w_in[:, 2052:2308], w_in[:, 2308:2564], w_in[:, 2820:6916]], axis=1))
    o["winT"] = np.ascontiguousarray(np.concatenate([w_in[:, 768:1024], w_in[:, 2564:2820]], axis=1))
    o["wff"] = np.ascontiguousarray(w_in[:, 1024:1028])
    lre = np.asarray(inp["s5_lambda_re"][l], f)
    lim = np.asarray(inp["s5_lambda_im"][l], f)
    ldt = np.repeat(np.asarray(inp["s5_log_dt"][l], f)[:, None], 64, axis=1)
    def gp(a):
        return a.reshape(8, 2, 64).transpose(1, 2, 0).reshape(128, 8)
    o["s5par"] = np.ascontiguousarray(np.stack([gp(lre), gp(lim), gp(ldt)], axis=2))
    b_re = np.asarray(inp["s5_b_re"][l], f)
    b_im = np.asarray(inp["s5_b_im"][l], f)
    c_re = np.asarray(inp["s5_c_re"][l], f)
    c_im = np.asarray(inp["s5_c_im"][l], f)
    Bw = np.zeros((8, 128, 2, 128), f)
    Cw = np.zeros((8, 128, 2, 128), f)
    for g in range(16):
        rt = g // 2
        k0 = (g % 8) * 16
        m0 = (g % 2) * 64
        Bw[rt, k0:k0 + 16, 0, m0:m0 + 64] = b_re[g].T
        Bw[rt, k0:k0 + 16, 1, m0:m0 + 64] = b_im[g].T
        Cw[rt, m0:m0 + 64, 0, k0:k0 + 16] = c_re[g].T
        Cw[rt, m0:m0 + 64, 1, k0:k0 + 16] = c_im[g].T
    o["s5B"] = np.ascontiguousarray(Bw.transpose(1, 0, 2, 3))
    o["s5C"] = np.ascontiguousarray(Cw.transpose(1, 0, 2, 3))
    o["s5D"] = _chunkcol(np.asarray(inp["s5_d"][l], f).reshape(-1))
    o["gluw"] = np.asarray(inp["s5_glu_w"][l], f)
    o["glub"] = _chunkcol(inp["s5_glu_b"][l])
    o["foxb"] = np.asarray(inp["fox_f_bias"][l], f).reshape(4, 1)
    o["mu"] = _chunkcol(inp["rwkv_mu"][l])
    vec = [inp["rwkv_w0"][l], inp["rwkv_a0"][l], inp["rwkv_k_k"][l], inp["rwkv_k_a"][l],
           np.asarray(inp["rwkv_r_k"][l]).reshape(-1), inp["rwkv_lnx_g"][l], inp["rwkv_lnx_b"][l]]
    o["rvec"] = np.ascontiguousarray(np.stack([_chunkcol(v) for v in vec], axis=2))
    o["wa2"] = np.ascontiguousarray(np.concatenate([np.asarray(inp["rwkv_w2"][l], f),
                                                     np.asarray(inp["rwkv_a2"][l], f)], axis=0))
    o["g2"] = np.asarray(inp["rwkv_g2"][l], f)
    o["wbr"] = np.ascontiguousarray(np.asarray(inp["w_branch"][l], f).reshape(1024, 1024))
    o["wout"] = np.asarray(inp["w_out"][l], f)
    o["ln1"] = np.ascontiguousarray(np.stack([_chunkcol(inp["ln1_g"][l]), _chunkcol(inp["ln1_b"][l])], axis=2))
    o["ln2"] = np.ascontiguousarray(np.stack([_chunkcol(inp["ln2_g"][l]), _chunkcol(inp["ln2_b"][l])], axis=2))
    o["w1"] = np.asarray(inp["mlp_w1"][l], f)
    o["w2m"] = np.asarray(inp["mlp_w2"][l], f)
    o["plew"] = np.asarray(inp["ple_w"][l], f)
    o["pgw"] = np.asarray(inp["ple_gate_w"][l], f)
    return {k: np.ascontiguousarray(v, dtype=f) for k, v in o.items()}


LAYER_SHAPES = {
    "winF": [1024, 6400], "winT": [1024, 512], "wff": [1024, 4], "s5par": [128, 8, 3],
    "s5B": [128, 8, 2, 128], "s5C": [128, 8, 2, 128], "s5D": [128, 2], "gluw": [256, 256], "glub": [128, 2],
    "foxb": [4, 1], "mu": [128, 8], "rvec": [128, 2, 7], "wa2": [128, 256], "g2": [128, 256],
    "wbr": [1024, 1024], "wout": [1024, 1024], "ln1": [128, 8, 2], "ln2": [128, 8, 2],
    "w1": [1024, 4096], "w2m": [4096, 1024], "plew": [256, 1024], "pgw": [1024, 1024],
}


class K:
    pass


@contextlib.contextmanager
def phase(k):
    with contextlib.ExitStack() as st:
        yield st
        k.S.barrier()


def build_program(depth=DEPTH, dbg=(), stop_after=None, only=None, nsteps=T):
    nc = bass.Bass("TRN2", target_bir_lowering=False)
    S = Sched(nc)
    k = K()
    k.nc, k.S = nc, S
    k.dbg = {}
    x_in = nc.dram_tensor("x", [T, D], F32, kind="ExternalInput").ap()
    p_in = nc.dram_tensor("p", [DEPTH, T, 256], F32, kind="ExternalInput").ap()
    out = nc.dram_tensor("out", [T, D], F32, kind="ExternalOutput").ap()
    W = []
    for l in range(DEPTH):
        W.append({n: nc.dram_tensor("%s_%d" % (n, l), s, F32, kind="ExternalInput").ap()
                  for n, s in LAYER_SHAPES.items()})
    k.stop_after = stop_after
    k.only = only
    k.nsteps = nsteps

    def scratch(name, shape, dt):
        kind = "ExternalOutput" if name in dbg else "Internal"
        import os
        if os.environ.get("KNOB_TINY") and name != "hres":
            shape = [2, 2]
        return Buf(nc.dram_tensor(name, shape, dt, kind=kind).ap())
    k.hres = scratch("hres", [128, 8, T], F32)
    k.h1res = scratch("h1res", [128, 8, T], F32)
    k.gsc = scratch("gsc", [32, 128, T], BF16)
    k.ycat = scratch("ycat", [4, 128, 2, T], BF16)
    k.rwxs = scratch("rwxs", [8, 128, T], F32)
    k.sc5 = scratch("sc5", [2, 128, 5, T], F32)
    k.bon = scratch("bon", [2, 128, T], F32)
    k.gg = scratch("gg", [2, 128, T], F32)
    k.vtok = scratch("vtok", [T, 2, 128], F32)
    k.uTf = scratch("uTf", [128, 2, T], F32)
    k.qk = scratch("qk", [4, 128, 2, T], BF16)
    k.vv = scratch("vv", [2, 128, NB, 256], BF16)
    k.negc = scratch("negc", [4, T], F32)
    k.hTs = scratch("hTs", [128, 8, T], BF16)
    k.pTs = scratch("pTs", [128, 2, T], BF16)
    k.w1s = scratch("w1s", [128, 8, 4096], BF16)
    k.w2s = scratch("w2s", [8, 128, 32, 128], BF16)

    es = contextlib.ExitStack()

    uid = [0]

    def sb(stack, name, shape, dt):
        uid[0] += 1
        return Buf(stack.enter_context(nc.sbuf_tensor("%s_u%d" % (name, uid[0]), shape, dt)))

    def ps(stack, name, shape, dt=F32):
        uid[0] += 1
        full = [128, 512] if dt == F32 else [128, 1024]
        t = stack.enter_context(nc.psum_tensor("%s_u%d" % (name, uid[0]), full, dt))
        n = 1
        for d_ in shape[1:]:
            n *= d_
        if len(shape) == 2:
            v = t[0:shape[0], 0:n]
        else:
            assert len(shape) == 3
            v = t[0:shape[0], 0:n].rearrange("p (a b) -> p a b", a=shape[1])
        return Buf(v, excl=True)
    k.sb, k.ps = sb, ps

    with es:
        k.identf = sb(es, "identf", [128, 128], F32)
        k.identb = sb(es, "identb", [128, 128], BF16)
        k.onesm = sb(es, "onesm", [128, 128], F32)
        k.blk64 = sb(es, "blk64", [128, 128], F32)
        k.nmi = sb(es, "nmi", [128, 128], F32)
        k.nms = sb(es, "nms", [128, 128], F32)
        k.m01 = sb(es, "m01", [128, 128], F32)
        k.ones = sb(es, "ones", [128, 512], F32)
        k.sel4 = sb(es, "sel4", [4, 4, 128], F32)
        k.epsln = sb(es, "epsln", [128, 1], F32)
        k.epsgn = sb(es, "epsgn", [128, 1], F32)
        k.one1 = sb(es, "one1", [128, 1], F32)
        k.mhalf = sb(es, "mhalf", [128, 1], F32)
        I = S.I
        I("pool", lambda e: e.memset(k.epsln[:], LN_EPS), wr=[k.epsln])
        I("pool", lambda e: e.memset(k.epsgn[:], GN_EPS), wr=[k.epsgn])
        I("pool", lambda e: e.memset(k.one1[:], 1.0), wr=[k.one1])
        I("pool", lambda e: e.memset(k.mhalf[:], -0.5), wr=[k.mhalf])
        I("pool", lambda e: e.memset(k.identf[:], 0.0), wr=[k.identf])
        I("pool", lambda e: e.affine_select(out=k.identf[:], in_=k.identf[:], compare_op=ALU.not_equal, fill=1.0,
                                            base=0, pattern=[[-1, 128]], channel_multiplier=1),
          rd=[k.identf], wr=[k.identf])
        I("pool", lambda e: e.tensor_copy(k.identb[:], k.identf[:]), rd=[k.identf], wr=[k.identb])
        I("pool", lambda e: e.memset(k.onesm[:], 1.0 / 1024.0), wr=[k.onesm])
        I("pool", lambda e: e.memset(k.ones[:], 1.0), wr=[k.ones])
        I("pool", lambda e: e.memset(k.blk64[:], 0.0), wr=[k.blk64])
        I("pool", lambda e: e.memset(k.blk64[0:64, 0:64], 1.0), wr=[k.blk64])
        I("pool", lambda e: e.memset(k.blk64[64:128, 64:128], 1.0), wr=[k.blk64])
        I("pool", lambda e: e.memset(k.nmi[:], 0.0), wr=[k.nmi])
        I("pool", lambda e: e.affine_select(out=k.nmi[:], in_=k.nmi[:], compare_op=ALU.is_ge, fill=-1e30,
                                            base=0, pattern=[[-1, 128]], channel_multiplier=1), rd=[k.nmi], wr=[k.nmi])
        I("pool", lambda e: e.memset(k.nms[:], 0.0), wr=[k.nms])
        I("pool", lambda e: e.affine_select(out=k.nms[:], in_=k.nms[:], compare_op=ALU.is_gt, fill=-1e30,
                                            base=0, pattern=[[-1, 128]], channel_multiplier=1), rd=[k.nms], wr=[k.nms])
        I("pool", lambda e: e.affine_select(out=k.m01[:], in_=k.ones[:, 0:128], compare_op=ALU.is_gt, fill=0.0,
                                            base=0, pattern=[[-1, 128]], channel_multiplier=1), rd=[k.ones], wr=[k.m01])
        I("pool", lambda e: e.affine_select(out=k.sel4[:], in_=k.ones[0:4, :].rearrange("p (h m) -> p h m", h=4),
                                            compare_op=ALU.is_equal, fill=0.0, base=0, pattern=[[-1, 4], [0, 128]],
                                            channel_multiplier=1), rd=[k.ones], wr=[k.sel4])

        if stop_after != "consts":
            for l in range(depth):
                layer(k, l, W[l], x_in, p_in[l], out, last=(l == depth - 1))
        S.finish()
    k.ninst = S.ninst
    return nc, k


def dbg_dump(k, name, src_buf, src_ap):
    if name in k.dbg:
        k.S.dma("sp", k.dbg[name], src_ap, rd=[src_buf], wr=[Dep()])


def mm(k, out_ap, lhsT, rhs, start, stop, rd, wr):
    return k.S.I("pe", lambda e: e.matmul(out_ap, lhsT, rhs, start=start, stop=stop), rd=rd, wr=wr)


def tr(k, out_ap, in_ap, ident_ap, rd, wr):
    return k.S.I("pe", lambda e: e.transpose(out_ap, in_ap, ident_ap), rd=rd, wr=wr)


def phase0(k, x_in, hT):
    import os
    lvl = int(os.environ.get("KNOB_P0", "9"))
    nblk = int(os.environ.get("KNOB_P0N", str(NB)))
    S, I = k.S, k.S.I
    with phase(k) as st:
        xin = Ring([k.sb(st, "p0x%d" % i, [128, D], F32) for i in range(2)])
        stg = Ring([k.sb(st, "p0s%d" % i, [128, 8, 128], F32) for i in range(2)])
        pss = Ring([k.ps(st, "p0p%d" % i, [128, 4, 128], F32) for i in range(4)])
        for blk in range(nblk):
            xi = xin.next()
            S.dma("sp", xi[:], x_in[blk * 128:(blk + 1) * 128, :], wr=[xi])
            sg = stg.next()
            if lvl < 1:
                continue
            for half in range(2):
                pt = pss.next()
                for j in range(4):
                    c = half * 4 + j
                    tr(k, pt[:, j, :], xi[:, c * 128:(c + 1) * 128], k.identf[:], rd=[xi, k.identf], wr=[pt])
                if lvl >= 2:
                    I("act", lambda e: e.activation(out=hT[:, half * 4:half * 4 + 4, blk * 128:(blk + 1) * 128],
                                                    in_=pt[:], func=AF.Copy), rd=[pt], wr=[hT])
                if lvl >= 3:
                    I("dve", lambda e: e.tensor_copy(sg[:, half * 4:half * 4 + 4, :], pt[:]), rd=[pt], wr=[sg])
            if lvl >= 4:
                S.dma("pool", k.hres[:, :, blk * 128:(blk + 1) * 128], sg[:], rd=[sg], wr=[k.hres])


def cast_weight(k, st, src, dst_buf, dst_ap, K_, N_, tag, to_dram=False, dst_fn=None):
    S, I = k.S, k.S.I
    kc = K_ // 128
    ncol = max(1, min(N_, 4096 // kc))
    f32r = Ring([k.sb(st, "cw%s_f%d" % (tag, i), [128, kc, ncol], F32) for i in range(2)])
    if to_dram:
        bfr = Ring([k.sb(st, "cw%s_b%d" % (tag, i), [128, kc, ncol], BF16) for i in range(2)])
    srcv = src.rearrange("(c p) n -> p c n", p=128)
    engs = ["pool", "dve"]
    i = 0
    for n0 in range(0, N_, ncol):
        n1 = min(N_, n0 + ncol)
        w = n1 - n0
        fb = f32r.next()
        S.dma("sp", fb[:, :, 0:w], srcv[:, :, n0:n1], wr=[fb])
        if to_dram:
            bb = bfr.next()
            I(engs[i % 2], lambda e: e.tensor_copy(bb[:, :, 0:w], fb[:, :, 0:w]), rd=[fb], wr=[bb])
            dst = dst_fn(n0, n1) if dst_fn is not None else dst_ap[:, :, n0:n1]
            S.dma("pool", dst, bb[:, :, 0:w], rd=[bb], wr=[dst_buf])
        else:
            I(engs[i % 2], lambda e: e.tensor_copy(dst_ap[:, :, n0:n1], fb[:, :, 0:w]), rd=[fb], wr=[dst_buf])
        i += 1


def layer_norm_tile(k, st_bufs, z, gb, eps, outs):
    S, I = k.S, k.S.I
    mps, msb, sq, vps = st_bufs
    for c in range(8):
        mm(k, mps[:], k.onesm[:], z[:, c, :], c == 0, c == 7, rd=[k.onesm, z], wr=[mps])
    I("act", lambda e: e.activation(out=msb[:], in_=mps[:], func=AF.Copy), rd=[mps], wr=[msb])
    I("dve", lambda e: e.tensor_tensor(out=z[:], in0=z[:], in1=msb[:, None, :].broadcast_to([128, 8, 512]),
                                       op=ALU.subtract), rd=[z, msb], wr=[z])
    I("act", lambda e: e.activation(out=sq[:], in_=z[:], func=AF.Square), rd=[z], wr=[sq])
    for c in range(8):
        mm(k, vps[:], k.onesm[:], sq[:, c, :], c == 0, c == 7, rd=[k.onesm, sq], wr=[vps])
    I("act", lambda e: e.activation(out=msb[:], in_=vps[:], func=AF.Ln, bias=k.epsln[:, 0:1], scale=1.0),
      rd=[vps, k.epsln], wr=[msb])
    I("act", lambda e: e.activation(out=msb[:], in_=msb[:], func=AF.Exp, scale=-0.5), rd=[msb], wr=[msb])
    I("dve", lambda e: e.tensor_tensor(out=z[:], in0=z[:], in1=msb[:, None, :].broadcast_to([128, 8, 512]),
                                       op=ALU.mult), rd=[z, msb], wr=[z])
    for c in range(8):
        outs(c, z[:, c, :], gb[:, c, 0:1], gb[:, c, 1:2])


def inproj(k, l, W, hT):
    S, I = k.S, k.S.I
    with phase(k) as st:
        wf = Ring([k.sb(st, "ipwf%d" % i, [128, 8, 128], F32) for i in range(2)])
        wb = Ring([k.sb(st, "ipwb%d" % i, [128, 8, 128], BF16) for i in range(2)])
        pss = Ring([k.ps(st, "ipps%d" % i, [128, 512], F32) for i in range(4)])
        stb = Ring([k.sb(st, "ipsb%d" % i, [128, 512], BF16) for i in range(3)])
        stf = Ring([k.sb(st, "ipsf%d" % i, [128, 512], F32) for i in range(2)])
        raw = Ring([k.sb(st, "ipraw%d" % i, [128, T + 1], F32) for i in range(2)])
        xs = Ring([k.sb(st, "ipxs%d" % i, [128, T], F32) for i in range(1)])
        mu = k.sb(st, "ipmu", [128, 8], F32)
        S.dma("sp", mu[:], W["mu"], wr=[mu])
        for r_ in raw.bufs:
            I("pool", lambda e: e.memset(r_[:, 0:1], 0.0), wr=[r_])
        winF = W["winF"].rearrange("(c p) n -> p c n", p=128)
        ev = 0
        for j in range(50):
            if k.stop_after == "ip_fm%d" % j:
                return
            f_ = wf.next()
            S.dma("sp", f_[:], winF[:, :, j * 128:(j + 1) * 128], wr=[f_])
            b_ = wb.next()
            I("pool", lambda e: e.tensor_copy(b_[:], f_[:]), rd=[f_], wr=[b_])
            if 6 <= j < 14:
                rw = raw.next()
            for tt in range(NT):
                pt = pss.next()
                for c in range(8):
                    mm(k, pt[:], b_[:, c, :], hT[:, c, tt * 512:(tt + 1) * 512], c == 0, c == 7,
                       rd=[b_, hT], wr=[pt])
                eng = "act" if ev % 2 == 0 else "dve"
                ev += 1
                tsl = slice(tt * 512, (tt + 1) * 512)
                if j < 2:
                    sf = stf.next()
                    if eng == "act":
                        I("act", lambda e: e.activation(out=sf[:], in_=pt[:], func=AF.Copy), rd=[pt], wr=[sf])
                    else:
                        I("dve", lambda e: e.tensor_copy(sf[:], pt[:]), rd=[pt], wr=[sf])
                    S.dma("pool", k.uTf[:, j, tsl], sf[:], rd=[sf], wr=[k.uTf])
                elif j < 6 or 14 <= j < 18:
                    which = (j - 2) // 2 if j < 6 else 2 + (j - 14) // 2
                    cc = j % 2
                    scale = 0.125 if which in (0, 2) else 1.0
                    sb_ = stb.next()
                    I("act", lambda e: e.activation(out=sb_[:], in_=pt[:], func=AF.Copy, scale=scale), rd=[pt], wr=[sb_])
                    S.dma("pool", k.qk[which, :, cc, tsl], sb_[:], rd=[sb_], wr=[k.qk])
                elif j < 14:
                    if eng == "act":
                        I("act", lambda e: e.activation(out=rw[:, 1 + tt * 512:1 + (tt + 1) * 512], in_=pt[:], func=AF.Copy),
                          rd=[pt], wr=[rw])
                    else:
                        I("dve", lambda e: e.tensor_copy(rw[:, 1 + tt * 512:1 + (tt + 1) * 512], pt[:]), rd=[pt], wr=[rw])
                else:
                    gc = j - 18
                    sb_ = stb.next()
                    I("act", lambda e: e.activation(out=sb_[:], in_=pt[:], func=AF.Sigmoid), rd=[pt], wr=[sb_])
                    S.dma("pool", k.gsc[gc, :, tsl], sb_[:], rd=[sb_], wr=[k.gsc])
            if 6 <= j < 14:
                jj = j - 6
                x_ = xs.next()
                I("pool", lambda e: e.tensor_tensor(out=x_[:], in0=rw[:, 0:T], in1=rw[:, 1:T + 1], op=ALU.subtract),
                  rd=[rw], wr=[x_])
                I("dve", lambda e: e.scalar_tensor_tensor(out=x_[:], in0=x_[:], scalar=mu[:, jj:jj + 1], in1=rw[:, 1:T + 1],
                                                          op0=ALU.mult, op1=ALU.add), rd=[x_, rw, mu], wr=[x_])
                S.dma("sp", k.rwxs[jj], x_[:], rd=[x_], wr=[k.rwxs])
        if k.stop_after == "ip_fm":
            return
        wtf = k.sb(st, "ipwtf", [128, 8, 512], F32)
        wtb = k.sb(st, "ipwtb", [128, 8, 512], BF16)
        S.dma("sp", wtf[:], W["winT"].rearrange("(c p) n -> p c n", p=128), wr=[wtf])
        I("pool", lambda e: e.tensor_copy(wtb[:], wtf[:]), rd=[wtf], wr=[wtb])
        for blk in range(NB):
            pt = pss.next()
            for c in range(8):
                mm(k, pt[:], hT[:, c, blk * 128:(blk + 1) * 128], wtb[:, c, :], c == 0, c == 7, rd=[wtb, hT], wr=[pt])
            sb_ = stb.next()
            if blk % 2 == 0:
                I("act", lambda e: e.activation(out=sb_[:], in_=pt[:], func=AF.Copy), rd=[pt], wr=[sb_])
            else:
                I("dve", lambda e: e.tensor_copy(sb_[:], pt[:]), rd=[pt], wr=[sb_])
            S.dma("pool", k.vv[:, :, blk, :].rearrange("w p n -> p w n"), sb_[:].rearrange("p (w n) -> p w n", w=2),
                  rd=[sb_], wr=[k.vv])
        if k.stop_after == "ip_tm":
            return
        wff_f = k.sb(st, "ipwff", [128, 8, 4], F32)
        wff_b = k.sb(st, "ipwffb", [128, 8, 4], BF16)
        fb = k.sb(st, "ipfb", [4, 1], F32)
        fl = k.sb(st, "ipfl", [4, T], F32)
        S.dma("sp", wff_f[:], W["wff"].rearrange("(c p) n -> p c n", p=128), wr=[wff_f])
        S.dma("sp", fb[:], W["foxb"], wr=[fb])
        I("pool", lambda e: e.tensor_copy(wff_b[:], wff_f[:]), rd=[wff_f], wr=[wff_b])
        I("dve", lambda e: e.tensor_scalar(out=fb[:], in0=fb[:], scalar1=-1.0, scalar2=None, op0=ALU.mult), rd=[fb], wr=[fb])
        for tt in range(NT):
            pt = pss.next()
            for c in range(8):
                mm(k, pt[0:4, :], wff_b[:, c, :], hT[:, c, tt * 512:(tt + 1) * 512], c == 0, c == 7, rd=[wff_b, hT], wr=[pt])
            I("act", lambda e: e.activation(out=fl[:, tt * 512:(tt + 1) * 512], in_=pt[0:4, :], func=AF.Exp,
                                            bias=fb[:, 0:1], scale=-1.0), rd=[pt, fb], wr=[fl])
        I("act", lambda e: e.activation(out=fl[:], in_=fl[:], func=AF.Ln, bias=k.one1[0:4, 0:1], scale=1.0),
          rd=[fl, k.one1], wr=[fl])
        for tt in range(NT):
            init = 0.0 if tt == 0 else fl[:, tt * 512 - 1:tt * 512]
            I("dve", lambda e: e.tensor_tensor_scan(out=fl[:, tt * 512:(tt + 1) * 512], data0=k.ones[0:4, :],
                                                    data1=fl[:, tt * 512:(tt + 1) * 512], initial=init,
                                                    op0=ALU.mult, op1=ALU.add), rd=[fl, k.ones], wr=[fl])
        S.dma("sp", k.negc[:], fl[:], rd=[fl], wr=[k.negc])


def layer(k, l, W, x_in, p_l, out, last):
    S, I = k.S, k.S.I
    with phase(k) as st:
        hT = k.sb(st, "hT%d" % l, [128, 8, T], BF16)
        if l == 0:
            phase0(k, x_in, hT)
            if k.stop_after == "phase0":
                return
        else:
            S.dma("sp", hT[:], k.hTs[:], rd=[k.hTs], wr=[hT])
        inproj(k, l, W, hT)
    if k.stop_after is not None and k.stop_after.startswith("ip") or k.stop_after == "inproj":
        return
    if k.only in (None, "s5"):
        s5_phase(k, l, W)
    if k.stop_after == "s5":
        return
    if k.only in (None, "fox"):
        attn_phase(k, l, 0)
    if k.only in (None, "sb"):
        attn_phase(k, l, 1)
    if k.stop_after == "attn":
        return
    if k.only in (None, "rwkv"):
        rwkv_prep(k, l, W)
        rwkv_chunked(k, l, W, nsteps=k.nsteps)
    if k.stop_after == "rwkv":
        return
    precast_mlp(k, l, W)
    with phase(k) as st:
        h1T = k.sb(st, "h1T%d" % l, [128, 8, T], BF16)
        merge_phase(k, l, W, h1T)
        if k.stop_after == "merge":
            return
        mlp_phase(k, l, W, h1T, p_l, out, last)


def range_reduce(k, out, in_, shift, qi, qf, rd, wr):
    I = k.S.I
    TWO_PI = 2.0 * PI
    I("dve", lambda e: e.tensor_scalar(out=out, in0=in_, scalar1=float(shift), scalar2=None, op0=ALU.add), rd=rd, wr=wr)
    I("dve", lambda e: e.tensor_scalar(out=qi, in0=out, scalar1=1.0 / TWO_PI, scalar2=0.5, op0=ALU.mult, op1=ALU.add), rd=rd, wr=wr)
    I("dve", lambda e: e.tensor_copy(qf, qi), rd=rd, wr=wr)
    I("dve", lambda e: e.scalar_tensor_tensor(out=out, in0=qf, scalar=-TWO_PI, in1=out, op0=ALU.mult, op1=ALU.add), rd=rd, wr=wr)
    I("dve", lambda e: e.tensor_scalar(out=qf, in0=out, scalar1=-PI, scalar2=None, op0=ALU.is_lt), rd=rd, wr=wr)
    I("dve", lambda e: e.scalar_tensor_tensor(out=out, in0=qf, scalar=TWO_PI, in1=out, op0=ALU.mult, op1=ALU.add), rd=rd, wr=wr)
    I("dve", lambda e: e.tensor_scalar(out=out, in0=out, scalar1=-3.1415925, scalar2=3.1415925, op0=ALU.max, op1=ALU.min), rd=rd, wr=wr)


def s5_phase(k, l, W):
    S, I = k.S, k.S.I
    TWO_PI = 2.0 * PI
    with phase(k) as st:
        sb, ps = k.sb, k.ps
        par = sb(st, "s5par", [128, 8, 3], F32)
        S.dma("sp", par[:], W["s5par"], wr=[par])
        Bb = sb(st, "s5Bb", [128, 8, 2, 128], BF16)
        Cb = sb(st, "s5Cb", [128, 8, 2, 128], BF16)
        gwb = sb(st, "s5gwb", [128, 2, 256], BF16)
        Dc = sb(st, "s5D", [128, 2], F32)
        S.dma("sp", Dc[:], W["s5D"], wr=[Dc])
        with phase(k) as st2:
            Bf = sb(st2, "s5Bf", [128, 8, 2, 128], F32)
            Cf = sb(st2, "s5Cf", [128, 8, 2, 128], F32)
            gwf = sb(st2, "s5gwf", [128, 2, 256], F32)
            S.dma("sp", Bf[:], W["s5B"], wr=[Bf])
            S.dma("sp", Cf[:], W["s5C"], wr=[Cf])
            S.dma("sp", gwf[:], W["gluw"].rearrange("(c p) n -> p c n", p=128), wr=[gwf])
            I("pool", lambda e: e.tensor_copy(Bb[:], Bf[:]), rd=[Bf], wr=[Bb])
            I("pool", lambda e: e.tensor_copy(Cb[:, :, 0, :], Cf[:, :, 0, :]), rd=[Cf], wr=[Cb])
            I("pool", lambda e: e.tensor_scalar(out=Cb[:, :, 1, :], in0=Cf[:, :, 1, :], scalar1=-1.0, scalar2=None, op0=ALU.mult),
              rd=[Cf], wr=[Cb])
            I("pool", lambda e: e.tensor_copy(gwb[:], gwf[:]), rd=[gwf], wr=[gwb])
        gb = sb(st, "s5gb", [128, 2], F32)
        S.dma("sp", gb[:], W["glub"], wr=[gb])
        sm = sb(st, "s5sm", [128, 16, 8], F32)
        def sl(i):
            return sm[:, i, :]
        lre, lim, ldt = par[:, :, 0], par[:, :, 1], par[:, :, 2]
        DT, A_, TH, NA, EM, SN, CS, LBR, LBI, DEN, FRE, FIM, NFRE, T1, T2, THR = range(16)
        dsm = Dep()
        def dv(fn):
            I("dve", fn, rd=[par, dsm], wr=[dsm])
        def ac(fn):
            I("act", fn, rd=[par, dsm], wr=[dsm])
        def pl(fn):
            I("pool", fn, rd=[par, dsm], wr=[dsm])
        ac(lambda e: e.activation(out=sl(DT), in_=ldt, func=AF.Exp))
        dv(lambda e: e.tensor_tensor(out=sl(A_), in0=lre, in1=sl(DT), op=ALU.mult))
        dv(lambda e: e.tensor_tensor(out=sl(TH), in0=lim, in1=sl(DT), op=ALU.mult))
        dv(lambda e: e.tensor_scalar(out=sl(NA), in0=sl(A_), scalar1=-1.0, scalar2=None, op0=ALU.mult))
        smi = sb(st, "s5smi", [128, 8], mybir.dt.int32)
        range_reduce(k, sl(T1), sl(TH), 0.0, smi[:], sl(T2), [par, dsm, smi], [dsm, smi])
        ac(lambda e: e.activation(out=sl(SN), in_=sl(T1), func=AF.Sin))
        range_reduce(k, sl(T1), sl(TH), 0.5 * PI, smi[:], sl(T2), [par, dsm, smi], [dsm, smi])
        ac(lambda e: e.activation(out=sl(CS), in_=sl(T1), func=AF.Sin))
        ac(lambda e: e.activation(out=sl(EM), in_=sl(A_), func=AF.Exp))
        dv(lambda e: e.tensor_tensor(out=sl(LBR), in0=sl(EM), in1=sl(CS), op=ALU.mult))
        dv(lambda e: e.tensor_tensor(out=sl(LBI), in0=sl(EM), in1=sl(SN), op=ALU.mult))
        dv(lambda e: e.tensor_tensor(out=sl(DEN), in0=lre, in1=lre, op=ALU.mult))
        dv(lambda e: e.tensor_tensor(out=sl(T1), in0=lim, in1=lim, op=ALU.mult))
        dv(lambda e: e.tensor_tensor(out=sl(DEN), in0=sl(DEN), in1=sl(T1), op=ALU.add))
        dv(lambda e: e.reciprocal(out=sl(DEN), in_=sl(DEN)))
        dv(lambda e: e.tensor_scalar(out=sl(T2), in0=sl(LBR), scalar1=-1.0, scalar2=None, op0=ALU.add))
        dv(lambda e: e.tensor_tensor(out=sl(FRE), in0=sl(T2), in1=lre, op=ALU.mult))
        dv(lambda e: e.tensor_tensor(out=sl(T1), in0=sl(LBI), in1=lim, op=ALU.mult))
        dv(lambda e: e.tensor_tensor(out=sl(FRE), in0=sl(FRE), in1=sl(T1), op=ALU.add))
        dv(lambda e: e.tensor_tensor(out=sl(FRE), in0=sl(FRE), in1=sl(DEN), op=ALU.mult))
        dv(lambda e: e.tensor_tensor(out=sl(FIM), in0=sl(LBI), in1=lre, op=ALU.mult))
        dv(lambda e: e.tensor_tensor(out=sl(T1), in0=sl(T2), in1=lim, op=ALU.mult))
        dv(lambda e: e.tensor_tensor(out=sl(FIM), in0=sl(FIM), in1=sl(T1), op=ALU.subtract))
        dv(lambda e: e.tensor_tensor(out=sl(FIM), in0=sl(FIM), in1=sl(DEN), op=ALU.mult))
        dv(lambda e: e.tensor_scalar(out=sl(NFRE), in0=sl(FRE), scalar1=-1.0, scalar2=None, op0=ALU.mult))
        range_reduce(k, sl(THR), sl(TH), 0.0, smi[:], sl(T1), [par, dsm, smi], [dsm, smi])
        CH = 256
        idx = sb(st, "s5idx", [128, CH + 1], F32)
        I("dve", lambda e: e.memset(idx[:], 1.0), wr=[idx])
        I("dve", lambda e: e.tensor_tensor_scan(out=idx[:], data0=idx[:], data1=idx[:], initial=-1.0,
                                                op0=ALU.mult, op1=ALU.add), rd=[idx], wr=[idx])
        tabr = sb(st, "s5tabr", [128, 8, CH + 1], F32)
        tabi = sb(st, "s5tabi", [128, 8, CH + 1], F32)
        tifr = sb(st, "s5tifr", [128, 8, CH], F32)
        tifi = sb(st, "s5tifi", [128, 8, CH], F32)
        ntl = sb(st, "s5ntl", [128, 8], F32)
        w1 = sb(st, "s5w1", [128, CH + 1], F32)
        w2 = sb(st, "s5w2", [128, CH + 1], F32)
        w3 = sb(st, "s5w3", [128, CH + 1], F32)
        w4 = sb(st, "s5w4", [128, CH + 1], F32)
        w5 = sb(st, "s5w5", [128, CH + 1], F32)
        wi = sb(st, "s5wi", [128, CH + 1], mybir.dt.int32)
        tabs = [tabr, tabi, tifr, tifi]
        wk = [w1, w2, w3, w4, w5, idx]
        for rt in range(8):
            def col(i):
                return sm[:, i, rt:rt + 1]
            def dvw(fn):
                I("dve", fn, rd=wk + [dsm], wr=wk[:5] + tabs)
            def acw(fn):
                I("act", fn, rd=wk + [dsm], wr=wk[:5] + tabs)
            def plw(fn):
                I("pool", fn, rd=wk + [dsm], wr=wk[:5] + tabs)
            dvw(lambda e: e.tensor_scalar(out=w1[:], in0=idx[:], scalar1=col(THR), scalar2=None, op0=ALU.mult))
            range_reduce(k, w2[:], w1[:], 0.0, wi[:], w5[:], wk + [dsm, wi], wk[:5] + tabs + [wi])
            acw(lambda e: e.activation(out=w2[:], in_=w2[:], func=AF.Sin))
            range_reduce(k, w3[:], w1[:], 0.5 * PI, wi[:], w5[:], wk + [dsm, wi], wk[:5] + tabs + [wi])
            acw(lambda e: e.activation(out=w3[:], in_=w3[:], func=AF.Sin))
            acw(lambda e: e.activation(out=w4[:], in_=idx[:], func=AF.Exp, scale=col(A_)))
            acw(lambda e: e.activation(out=w5[:], in_=idx[:], func=AF.Exp, scale=col(NA)))
            dvw(lambda e: e.tensor_tensor(out=tabr[:, rt, :], in0=w4[:], in1=w3[:], op=ALU.mult))
            dvw(lambda e: e.tensor_tensor(out=tabi[:, rt, :], in0=w4[:], in1=w2[:], op=ALU.mult))
            dvw(lambda e: e.tensor_scalar(out=w1[:], in0=w3[:], scalar1=col(FRE), scalar2=None, op0=ALU.mult))
            dvw(lambda e: e.scalar_tensor_tensor(out=w1[:], in0=w2[:], scalar=col(FIM), in1=w1[:], op0=ALU.mult, op1=ALU.add))
            dvw(lambda e: e.tensor_tensor(out=tifr[:, rt, :], in0=w1[:, 0:CH], in1=w5[:, 0:CH], op=ALU.mult))
            dvw(lambda e: e.tensor_scalar(out=w1[:], in0=w3[:], scalar1=col(FIM), scalar2=None, op0=ALU.mult))
            dvw(lambda e: e.scalar_tensor_tensor(out=w1[:], in0=w2[:], scalar=col(NFRE), in1=w1[:], op0=ALU.mult, op1=ALU.add))
            dvw(lambda e: e.tensor_tensor(out=tifi[:, rt, :], in0=w1[:, 0:CH], in1=w5[:, 0:CH], op=ALU.mult))
        I("dve", lambda e: e.tensor_scalar(out=ntl[:], in0=tabi[:, :, CH], scalar1=-1.0, scalar2=None, op0=ALU.mult),
          rd=tabs, wr=[ntl])
        uf = sb(st, "s5uf", [128, 2, T], F32)
        ub = sb(st, "s5ub", [128, 2, T], BF16)
        S.dma("sp", uf[:], k.uTf[:], rd=[k.uTf], wr=[uf])
        I("pool", lambda e: e.tensor_copy(ub[:], uf[:]), rd=[uf], wr=[ub])
        ygf = sb(st, "s5ygf", [128, 2, T], F32)
        ygb = sb(st, "s5ygb", [128, 2, T], BF16)
        q0 = sb(st, "s5q0", [128, 8, 2], F32)
        I("dve", lambda e: e.memset(q0[:], 0.0), wr=[q0])
        rawp = Ring([ps(st, "s5rp%d" % i, [128, 512], F32) for i in range(4)])
        yp = Ring([ps(st, "s5yp%d" % i, [128, 512], F32) for i in range(2)])
        rr = Ring([sb(st, "s5rr%d" % i, [128, 2, 512], F32) for i in range(2)])
        zz = Ring([sb(st, "s5zz%d" % i, [128, 2, 512], F32) for i in range(2)])
        mmr = Ring([sb(st, "s5mm%d" % i, [128, 4, 512], F32) for i in range(2)])
        qq = Ring([sb(st, "s5qq%d" % i, [128, 2, 512], F32) for i in range(2)])
        xb = Ring([sb(st, "s5xb%d" % i, [128, 2, 512], BF16) for i in range(3)])
        ew = Ring([sb(st, "s5ew%d" % i, [128, 3, 512], F32) for i in range(1)])
        tq = sb(st, "s5tq", [128, 2], F32)

        def b4(tab, rt):
            return tab[:, rt:rt + 1, 0:CH].broadcast_to([128, 512 // CH, CH])

        def v4(ap):
            return ap.rearrange("p (a b) -> p a b", a=512 // CH)
        items = [(oc, tt, r4) for oc in range(2) for tt in range(NT) for r4 in range(4)]
        ctx = {}
        ypts = {}

        def stZ(it):
            oc, tt, r4 = it
            rt = oc * 4 + r4
            tsl = slice(tt * 512, (tt + 1) * 512)
            pr, pi_ = rawp.next(), rawp.next()
            mm(k, pr[:], Bb[:, rt, 0, :], ub[:, oc, tsl], True, True, rd=[Bb, ub], wr=[pr])
            mm(k, pi_[:], Bb[:, rt, 1, :], ub[:, oc, tsl], True, True, rd=[Bb, ub], wr=[pi_])
            r_ = rr.next()
            I("act", lambda e: e.activation(out=r_[:, 0, :], in_=pr[:], func=AF.Copy), rd=[pr], wr=[r_])
            I("act", lambda e: e.activation(out=r_[:, 1, :], in_=pi_[:], func=AF.Copy), rd=[pi_], wr=[r_])
            z_, m_ = zz.next(), mmr.next()
            I("dve", lambda e: e.tensor_tensor(out=v4(m_[:, 0, :]), in0=v4(r_[:, 0, :]), in1=b4(tifr, rt), op=ALU.mult),
              rd=[r_] + tabs, wr=[m_])
            I("dve", lambda e: e.tensor_tensor(out=v4(m_[:, 1, :]), in0=v4(r_[:, 1, :]), in1=b4(tifi, rt), op=ALU.mult),
              rd=[r_] + tabs, wr=[m_])
            I("pool", lambda e: e.tensor_tensor(out=v4(m_[:, 2, :]), in0=v4(r_[:, 0, :]), in1=b4(tifi, rt), op=ALU.mult),
              rd=[r_] + tabs, wr=[m_])
            I("pool", lambda e: e.tensor_tensor(out=v4(m_[:, 3, :]), in0=v4(r_[:, 1, :]), in1=b4(tifr, rt), op=ALU.mult),
              rd=[r_] + tabs, wr=[m_])
            I("dve", lambda e: e.tensor_tensor(out=z_[:, 0, :], in0=m_[:, 0, :], in1=m_[:, 1, :], op=ALU.subtract),
              rd=[m_], wr=[z_])
            I("pool", lambda e: e.tensor_tensor(out=z_[:, 1, :], in0=m_[:, 2, :], in1=m_[:, 3, :], op=ALU.add),
              rd=[m_], wr=[z_])
            ctx[it] = z_

        def stSC(it):
            oc, tt, r4 = it
            rt = oc * 4 + r4
            z_ = ctx[it]
            q_ = qq.next()
            for sc in range(512 // CH):
                csl = slice(sc * CH, (sc + 1) * CH)
                for ri in range(2):
                    I("dve", lambda e: e.tensor_tensor_scan(out=q_[:, ri, csl], data0=k.ones[:, 0:CH], data1=z_[:, ri, csl],
                                                            initial=q0[:, rt, ri:ri + 1], op0=ALU.mult, op1=ALU.add),
                      rd=[z_, q0, k.ones], wr=[q_])
                qe_r = q_[:, 0, sc * CH + CH - 1:sc * CH + CH]
                qe_i = q_[:, 1, sc * CH + CH - 1:sc * CH + CH]
                lr_ = tabr[:, rt, CH:CH + 1]
                li_ = tabi[:, rt, CH:CH + 1]
                I("dve", lambda e: e.tensor_scalar(out=tq[:, 0:1], in0=qe_r, scalar1=lr_, scalar2=None, op0=ALU.mult),
                  rd=[q_] + tabs, wr=[tq])
                I("dve", lambda e: e.tensor_scalar(out=tq[:, 1:2], in0=qe_i, scalar1=lr_, scalar2=None, op0=ALU.mult),
                  rd=[q_] + tabs, wr=[tq])
                I("dve", lambda e: e.scalar_tensor_tensor(out=q0[:, rt, 0:1], in0=qe_i, scalar=ntl[:, rt:rt + 1], in1=tq[:, 0:1],
                                                          op0=ALU.mult, op1=ALU.add), rd=[q_, ntl, tq], wr=[q0])
                I("dve", lambda e: e.scalar_tensor_tensor(out=q0[:, rt, 1:2], in0=qe_r, scalar=li_, in1=tq[:, 1:2],
                                                          op0=ALU.mult, op1=ALU.add), rd=[q_, tq] + tabs, wr=[q0])
            ctx[it] = q_

        def stX(it):
            oc, tt, r4 = it
            rt = oc * 4 + r4
            tsl = slice(tt * 512, (tt + 1) * 512)
            q_ = ctx.pop(it)
            if r4 == 0:
                ypts[(oc, tt)] = yp.next()
            ypt = ypts[(oc, tt)]
            m2 = mmr.next()
            x_ = xb.next()
            I("dve", lambda e: e.tensor_tensor(out=v4(m2[:, 0, :]), in0=v4(q_[:, 0, :]), in1=b4(tabr, rt), op=ALU.mult),
              rd=[q_] + tabs, wr=[m2])
            I("dve", lambda e: e.tensor_tensor(out=v4(m2[:, 1, :]), in0=v4(q_[:, 1, :]), in1=b4(tabi, rt), op=ALU.mult),
              rd=[q_] + tabs, wr=[m2])
            I("pool", lambda e: e.tensor_tensor(out=v4(m2[:, 2, :]), in0=v4(q_[:, 0, :]), in1=b4(tabi, rt), op=ALU.mult),
              rd=[q_] + tabs, wr=[m2])
            I("pool", lambda e: e.tensor_tensor(out=v4(m2[:, 3, :]), in0=v4(q_[:, 1, :]), in1=b4(tabr, rt), op=ALU.mult),
              rd=[q_] + tabs, wr=[m2])
            I("dve", lambda e: e.tensor_tensor(out=x_[:, 0, :], in0=m2[:, 0, :], in1=m2[:, 1, :], op=ALU.subtract),
              rd=[m2], wr=[x_])
            I("pool", lambda e: e.tensor_tensor(out=x_[:, 1, :], in0=m2[:, 2, :], in1=m2[:, 3, :], op=ALU.add),
              rd=[m2], wr=[x_])
            mm(k, ypt[:], Cb[:, rt, 0, :], x_[:, 0, :], r4 == 0, False, rd=[Cb, x_], wr=[ypt])
            mm(k, ypt[:], Cb[:, rt, 1, :], x_[:, 1, :], False, r4 == 3, rd=[Cb, x_], wr=[ypt])
            if r4 == 3:
                e_ = ew.next()
                I("dve", lambda e: e.scalar_tensor_tensor(out=e_[:, 0, :], in0=uf[:, oc, tsl], scalar=Dc[:, oc:oc + 1], in1=ypt[:],
                                                          op0=ALU.mult, op1=ALU.add), rd=[uf, Dc, ypt], wr=[e_])
                I("pool", lambda e: e.tensor_tensor(out=e_[:, 1, :], in0=e_[:, 0, :], in1=e_[:, 0, :], op=ALU.mult), rd=[e_], wr=[e_])
                I("pool", lambda e: e.tensor_scalar(out=e_[:, 1, :], in0=e_[:, 1, :], scalar1=0.044715, scalar2=1.0,
                                                    op0=ALU.mult, op1=ALU.add), rd=[e_], wr=[e_])
                I("pool", lambda e: e.tensor_tensor(out=e_[:, 1, :], in0=e_[:, 1, :], in1=e_[:, 0, :], op=ALU.mult), rd=[e_], wr=[e_])
                I("act", lambda e: e.activation(out=e_[:, 2, :], in_=e_[:, 1, :], func=AF.Sigmoid, scale=1.5957691216057308),
                  rd=[e_], wr=[e_])
                I("dve", lambda e: e.tensor_tensor(out=ygf[:, oc, tsl], in0=e_[:, 0, :], in1=e_[:, 2, :], op=ALU.mult),
                  rd=[e_], wr=[ygf])
                I("pool", lambda e: e.tensor_copy(ygb[:, oc, tsl], ygf[:, oc, tsl]), rd=[ygf], wr=[ygb])
        stZ(items[0])
        for i, it in enumerate(items):
            if i + 1 < len(items):
                stZ(items[i + 1])
            stSC(it)
            stX(it)

        ob = Ring([sb(st, "s5ob%d" % i, [128, 512], BF16) for i in range(2)])
        for oc in range(2):
            for tt in range(NT):
                tsl = slice(tt * 512, (tt + 1) * 512)
                pt = yp.next()
                for kc in range(2):
                    mm(k, pt[:], gwb[:, kc, oc * 128:(oc + 1) * 128], ygb[:, kc, tsl], kc == 0, kc == 1, rd=[gwb, ygb], wr=[pt])
                e_ = ew.next()
                I("act", lambda e: e.activation(out=e_[:, 0, :], in_=pt[:], func=AF.Sigmoid, bias=gb[:, oc:oc + 1], scale=1.0),
                  rd=[pt, gb], wr=[e_])
                o_ = ob.next()
                I("dve", lambda e: e.tensor_tensor(out=o_[:], in0=e_[:, 0, :], in1=ygf[:, oc, tsl], op=ALU.mult),
                  rd=[e_, ygf], wr=[o_])
                S.dma("sp", k.ycat[0, :, oc, tsl], o_[:], rd=[o_], wr=[k.ycat])


def attn_phase(k, l, kind):
    S, I = k.S, k.S.I
    fox = (kind == 0)
    VW = 65 if fox else 64
    with phase(k) as st:
        sb, ps = k.sb, k.ps
        qT = sb(st, "atq", [128, 2, T], BF16)
        kT = sb(st, "atk", [128, 2, T], BF16)
        S.dma("sp", qT[:], k.qk[2 * kind], rd=[k.qk], wr=[qT])
        S.dma("sp", kT[:], k.qk[2 * kind + 1], rd=[k.qk], wr=[kT])
        V = sb(st, "atv", [128, NB, 4, VW], BF16)
        if fox:
            I("pool", lambda e: e.memset(V[:, :, :, 64:65], 1.0), wr=[V])
        S.dma("sp", V[:, :, :, 0:64], k.vv[kind].rearrange("p b (h d) -> p b h d", h=4), rd=[k.vv], wr=[V])
        yT = sb(st, "atyT", [128, 2, T], BF16)
        ytok = sb(st, "atytok", [128, NB, 256], BF16)
        A = Ring([sb(st, "atA%d" % i, [128, T], F32) for i in range(2)])
        P = Ring([sb(st, "atP%d" % i, [128, T], BF16) for i in range(2)])
        PT = Ring([sb(st, "atPT%d" % i, [128, 4, 128], BF16) for i in range(3)])
        zp = Ring([ps(st, "atzp%d" % i, [128, 512], F32) for i in range(3)])
        tp = Ring([ps(st, "attp%d" % i, [128, 4, 128], BF16) for i in range(2)])
        op = Ring([ps(st, "atop%d" % i, [128, VW], F32) for i in range(2)])
        xp = ps(st, "atxp", [128, 512], F32)
        nb = Ring([sb(st, "atnb%d" % i, [128, 2], F32) for i in range(3)])
        if fox:
            negc = sb(st, "atnegc", [4, T], F32)
            S.dma("sp", negc[:], k.negc[:], rd=[k.negc], wr=[negc])
            crow = sb(st, "atcrow", [128, T], F32)
        else:
            spb = Ring([sb(st, "atsp%d" % i, [128, 512], F32) for i in range(3)])
            Fb = Ring([sb(st, "atF%d" % i, [128, 512], F32) for i in range(3)])
            t1b = Ring([sb(st, "att1%d" % i, [128, 512], F32) for i in range(3)])
        items = [(h, qb) for h in range(4) for qb in range(NB)]
        state = {}

        def stage1(h, qb):
            hc, po = h // 2, (h % 2) * 64
            if fox and qb == 0:
                for tt in range(NT):
                    mm(k, xp[:], k.sel4[:, h, :], negc[:, tt * 512:(tt + 1) * 512], True, True, rd=[k.sel4, negc], wr=[xp])
                    I("act", lambda e: e.activation(out=crow[:, tt * 512:(tt + 1) * 512], in_=xp[:], func=AF.Copy),
                      rd=[xp], wr=[crow])
            nk = qb + 1
            a_ = A.next()
            nb_ = nb.next()
            carry = None
            for kt in range((nk + 3) // 4):
                w = min(4, nk - 4 * kt) * 128
                ks = slice(kt * 512, kt * 512 + w)
                last = (kt == (nk + 3) // 4 - 1)
                z = zp.next()
                mm(k, z[:, 0:w], qT[po:po + 64, hc, qb * 128:(qb + 1) * 128], kT[po:po + 64, hc, ks], True, True,
                   rd=[qT, kT], wr=[z])
                if fox:
                    I("dve", lambda e: e.tensor_tensor(out=a_[:, ks], in0=z[:, 0:w], in1=crow[:, ks], op=ALU.add),
                      rd=[z, crow], wr=[a_])
                else:
                    sp_ = spb.next()
                    I("act", lambda e: e.activation(out=sp_[:, 0:w], in_=z[:, 0:w], func=AF.Exp), rd=[z], wr=[sp_])
                    I("act", lambda e: e.activation(out=sp_[:, 0:w], in_=sp_[:, 0:w], func=AF.Ln, bias=k.one1[:, 0:1], scale=1.0),
                      rd=[sp_, k.one1], wr=[sp_])
                    if last:
                        I("pool", lambda e: e.tensor_tensor(out=sp_[:, w - 128:w], in0=sp_[:, w - 128:w], in1=k.m01[:], op=ALU.mult),
                          rd=[sp_, k.m01], wr=[sp_])
                    f_ = Fb.next()
                    init = 0.0 if carry is None else carry[0][:, carry[1] - 1:carry[1]]
                    rdl = [sp_, k.ones] + ([carry[0]] if carry is not None else [])
                    I("dve", lambda e: e.tensor_tensor_scan(out=f_[:, 0:w], data0=k.ones[:, 0:w], data1=sp_[:, 0:w], initial=init,
                                                            op0=ALU.mult, op1=ALU.add), rd=rdl, wr=[f_])
                    carry = (f_, w)
                    t_ = t1b.next()
                    I("dve", lambda e: e.tensor_tensor(out=t_[:, 0:w], in0=z[:, 0:w], in1=sp_[:, 0:w], op=ALU.subtract),
                      rd=[z, sp_], wr=[t_])
                    I("pool", lambda e: e.tensor_tensor(out=a_[:, ks], in0=t_[:, 0:w], in1=f_[:, 0:w], op=ALU.add),
                      rd=[t_, f_], wr=[a_])
            dsl = slice(qb * 128, (qb + 1) * 128)
            I("pool", lambda e: e.tensor_tensor(out=a_[:, dsl], in0=a_[:, dsl], in1=(k.nmi if fox else k.nms)[:], op=ALU.add),
              rd=[a_, k.nmi, k.nms], wr=[a_])
            if fox:
                I("dve", lambda e: e.reduce_max(out=nb_[:, 0:1], in_=a_[:, 0:nk * 128], axis=AX.X), rd=[a_], wr=[nb_])
                I("dve", lambda e: e.tensor_scalar(out=nb_[:, 1:2], in0=nb_[:, 0:1], scalar1=-1.0, scalar2=None, op0=ALU.mult),
                  rd=[nb_], wr=[nb_])
            else:
                I("dve", lambda e: e.tensor_scalar(out=nb_[:, 1:2], in0=carry[0][:, carry[1] - 1:carry[1]], scalar1=-1.0, scalar2=None,
                                                   op0=ALU.mult), rd=[carry[0]], wr=[nb_])
            state[(h, qb)] = (a_, nb_)

        def stage2(h, qb):
            a_, nb_ = state.pop((h, qb))
            nk = qb + 1
            p_ = P.next()
            I("act", lambda e: e.activation(out=p_[:, 0:nk * 128], in_=a_[:, 0:nk * 128], func=AF.Exp, bias=nb_[:, 1:2], scale=1.0),
              rd=[a_, nb_], wr=[p_])
            o_ = op.next()
            prev = None

            def pv(kt, pt_, nbk):
                for j in range(nbk):
                    kb = kt * 4 + j
                    mm(k, o_[:], pt_[:, j, :], V[:, kb, h, :], kb == 0, kb == nk - 1, rd=[pt_, V], wr=[o_])
            for kt in range((nk + 3) // 4):
                nbk = min(4, nk - 4 * kt)
                t_ = tp.next()
                for j in range(nbk):
                    kb = kt * 4 + j
                    tr(k, t_[:, j, :], p_[:, kb * 128:(kb + 1) * 128], k.identb[:], rd=[p_, k.identb], wr=[t_])
                pt_ = PT.next()
                if kt % 2 == 0:
                    I("act", lambda e: e.activation(out=pt_[:, 0:nbk, :], in_=t_[:, 0:nbk, :], func=AF.Copy), rd=[t_], wr=[pt_])
                else:
                    I("dve", lambda e: e.tensor_copy(pt_[:, 0:nbk, :], t_[:, 0:nbk, :]), rd=[t_], wr=[pt_])
                if prev is not None:
                    pv(*prev)
                prev = (kt, pt_, nbk)
            pv(*prev)
            if fox:
                I("dve", lambda e: e.reciprocal(out=nb_[:, 0:1], in_=o_[:, 64:65]), rd=[o_], wr=[nb_])
                I("act", lambda e: e.activation(out=ytok[:, qb, h * 64:(h + 1) * 64], in_=o_[:, 0:64], func=AF.Copy,
                                                scale=nb_[:, 0:1]), rd=[o_, nb_], wr=[ytok])
            else:
                I("act", lambda e: e.activation(out=ytok[:, qb, h * 64:(h + 1) * 64], in_=o_[:, 0:64], func=AF.Copy),
                  rd=[o_], wr=[ytok])

        for i, it in enumerate(items):
            stage1(*it)
            if i > 0:
                stage2(*items[i - 1])
        stage2(*items[-1])
        for qb in range(NB):
            t_ = tp.next()
            for c in range(2):
                tr(k, t_[:, c, :], ytok[:, qb, c * 128:(c + 1) * 128], k.identb[:], rd=[ytok, k.identb], wr=[t_])
            I("act", lambda e: e.activation(out=yT[:, :, qb * 128:(qb + 1) * 128], in_=t_[:, 0:2, :], func=AF.Copy),
              rd=[t_], wr=[yT])
        S.dma("sp", k.ycat[1 if fox else 3], yT[:], rd=[yT], wr=[k.ycat])


def rwkv_prep(k, l, W):
    S, I = k.S, k.S.I
    with phase(k) as st:
        sb, ps = k.sb, k.ps
        rvec = sb(st, "rpvec", [128, 2, 7], F32)
        S.dma("sp", rvec[:], W["rvec"], wr=[rvec])
        nw0 = sb(st, "rpnw0", [128, 2], F32)
        I("dve", lambda e: e.tensor_scalar(out=nw0[:], in0=rvec[:, :, 0], scalar1=-1.0, scalar2=None, op0=ALU.mult),
          rd=[rvec], wr=[nw0])
        wa2b = sb(st, "rpwa2b", [128, 256], BF16)
        g2b = sb(st, "rpg2b", [128, 256], BF16)
        tw = sb(st, "rptw", [128, T], BF16)
        gs = sb(st, "rpgs", [128, T], BF16)
        with phase(k) as st2:
            wa2f = sb(st2, "rpwa2f", [128, 256], F32)
            g2f = sb(st2, "rpg2f", [128, 256], F32)
            x6 = sb(st2, "rpx6", [128, T], F32)
            x7 = sb(st2, "rpx7", [128, T], F32)
            S.dma("sp", wa2f[:], W["wa2"], wr=[wa2f])
            S.dma("sp", g2f[:], W["g2"], wr=[g2f])
            S.dma("sp", x6[:], k.rwxs[6], rd=[k.rwxs], wr=[x6])
            S.dma("sp", x7[:], k.rwxs[7], rd=[k.rwxs], wr=[x7])
            I("pool", lambda e: e.tensor_copy(wa2b[:], wa2f[:]), rd=[wa2f], wr=[wa2b])
            I("pool", lambda e: e.tensor_copy(g2b[:], g2f[:]), rd=[g2f], wr=[g2b])
            I("act", lambda e: e.activation(out=tw[0:64, :], in_=x6[0:64, :], func=AF.Tanh), rd=[x6], wr=[tw])
            I("pool", lambda e: e.tensor_copy(tw[64:128, :], x6[64:128, :]), rd=[x6], wr=[tw])
            I("act", lambda e: e.activation(out=gs[:], in_=x7[:], func=AF.Sigmoid), rd=[x7], wr=[gs])
        for oc in range(2):
            S.dma("sp", k.sc5[oc, :, 4, :], k.rwxs[oc], rd=[k.rwxs], wr=[k.sc5])
        ld = Ring([sb(st, "rpld%d" % i, [128, 3, 512], F32) for i in range(2)])
        o5 = Ring([sb(st, "rpo5%d" % i, [128, 4, 512], F32) for i in range(2)])
        wk = Ring([sb(st, "rpwk%d" % i, [128, 6, 512], F32) for i in range(2)])
        so = Ring([sb(st, "rpso%d" % i, [128, 3, 512], F32) for i in range(2)])
        pp = Ring([ps(st, "rppp%d" % i, [128, 512], F32) for i in range(6)])
        for oc in range(2):
            osl = slice(oc * 128, (oc + 1) * 128)
            for tt in range(NT):
                tsl = slice(tt * 512, (tt + 1) * 512)
                x_ = ld.next()
                for i, j in enumerate((oc, 2 + oc, 4 + oc)):
                    S.dma("sp", x_[:, i, :], k.rwxs[j, :, tsl], rd=[k.rwxs], wr=[x_])
                r_t, k_t, v_t = x_[:, 0, :], x_[:, 1, :], x_[:, 2, :]
                o_ = o5.next()
                w_ = wk.next()
                s_ = so.next()
                pw = pp.next()
                mm(k, pw[:], wa2b[0:64, osl], tw[0:64, tsl], True, True, rd=[wa2b, tw], wr=[pw])
                I("act", lambda e: e.activation(out=w_[:, 0, :], in_=pw[:], func=AF.Exp, bias=nw0[:, oc:oc + 1], scale=-1.0),
                  rd=[pw, nw0], wr=[w_])
                I("act", lambda e: e.activation(out=w_[:, 0, :], in_=w_[:, 0, :], func=AF.Ln, bias=k.one1[:, 0:1], scale=1.0),
                  rd=[w_, k.one1], wr=[w_])
                I("act", lambda e: e.activation(out=w_[:, 0, :], in_=w_[:, 0, :], func=AF.Exp, bias=k.mhalf[:, 0:1], scale=-1.0),
                  rd=[w_, k.mhalf], wr=[w_])
                I("pool", lambda e: e.tensor_scalar(out=o_[:, 0, :], in0=w_[:, 0, :], scalar1=-1.0, scalar2=None, op0=ALU.mult),
                  rd=[w_], wr=[o_])
                pa = pp.next()
                mm(k, pa[:], wa2b[64:128, osl], tw[64:128, tsl], True, True, rd=[wa2b, tw], wr=[pa])
                I("act", lambda e: e.activation(out=w_[:, 1, :], in_=pa[:], func=AF.Sigmoid, bias=rvec[:, oc, 1:2], scale=1.0),
                  rd=[pa, rvec], wr=[w_])
                a_t = w_[:, 1, :]
                I("dve", lambda e: e.tensor_scalar(out=w_[:, 2, :], in0=k_t, scalar1=rvec[:, oc, 2:3], scalar2=None, op0=ALU.mult),
                  rd=[x_, rvec], wr=[w_])
                I("pool", lambda e: e.tensor_tensor(out=w_[:, 3, :], in0=w_[:, 2, :], in1=w_[:, 2, :], op=ALU.mult), rd=[w_], wr=[w_])
                pn = pp.next()
                mm(k, pn[:], k.blk64[:], w_[:, 3, :], True, True, rd=[k.blk64, w_], wr=[pn])
                I("act", lambda e: e.activation(out=w_[:, 3, :], in_=pn[:], func=AF.Sqrt), rd=[pn], wr=[w_])
                I("dve", lambda e: e.tensor_scalar(out=w_[:, 3, :], in0=w_[:, 3, :], scalar1=1e-12, scalar2=None, op0=ALU.max),
                  rd=[w_], wr=[w_])
                I("dve", lambda e: e.reciprocal(out=w_[:, 3, :], in_=w_[:, 3, :]), rd=[w_], wr=[w_])
                I("dve", lambda e: e.tensor_tensor(out=w_[:, 2, :], in0=w_[:, 2, :], in1=w_[:, 3, :], op=ALU.mult), rd=[w_], wr=[w_])
                I("pool", lambda e: e.tensor_scalar(out=o_[:, 3, :], in0=w_[:, 2, :], scalar1=-1.0, scalar2=None, op0=ALU.mult),
                  rd=[w_], wr=[o_])
                I("pool", lambda e: e.tensor_tensor(out=o_[:, 1, :], in0=w_[:, 2, :], in1=a_t, op=ALU.mult), rd=[w_], wr=[o_])
                I("dve", lambda e: e.tensor_scalar(out=w_[:, 4, :], in0=a_t, scalar1=1.0, scalar2=rvec[:, oc, 3:4],
                                                   op0=ALU.subtract, op1=ALU.mult), rd=[w_, rvec], wr=[w_])
                I("dve", lambda e: e.scalar_tensor_tensor(out=o_[:, 2, :], in0=w_[:, 4, :], scalar=1.0, in1=k_t,
                                                          op0=ALU.add, op1=ALU.mult), rd=[w_, x_], wr=[o_])
                S.dma("sp", k.sc5[oc, :, 0:4, tsl], o_[:], rd=[o_], wr=[k.sc5])
                I("pool", lambda e: e.tensor_tensor(out=w_[:, 5, :], in0=r_t, in1=o_[:, 2, :], op=ALU.mult), rd=[x_, o_], wr=[w_])
                I("pool", lambda e: e.tensor_scalar(out=w_[:, 5, :], in0=w_[:, 5, :], scalar1=rvec[:, oc, 4:5], scalar2=None,
                                                    op0=ALU.mult), rd=[w_, rvec], wr=[w_])
                pb = pp.next()
                mm(k, pb[:], k.blk64[:], w_[:, 5, :], True, True, rd=[k.blk64, w_], wr=[pb])
                I("dve", lambda e: e.tensor_tensor(out=s_[:, 0, :], in0=pb[:], in1=v_t, op=ALU.mult), rd=[pb, x_], wr=[s_])
                S.dma("sp", k.bon[oc, :, tsl], s_[:, 0, :], rd=[s_], wr=[k.bon])
                pg = pp.next()
                mm(k, pg[:], g2b[:, osl], gs[:, tsl], True, True, rd=[g2b, gs], wr=[pg])
                I("act", lambda e: e.activation(out=s_[:, 1, :], in_=pg[:], func=AF.Copy), rd=[pg], wr=[s_])
                S.dma("sp", k.gg[oc, :, tsl], s_[:, 1, :], rd=[s_], wr=[k.gg])
                pv = pp.next()
                for b in range(4):
                    tr(k, pv[:, b * 128:(b + 1) * 128], x_[:, 2, b * 128:(b + 1) * 128], k.identf[:], rd=[x_, k.identf], wr=[pv])
                I("act", lambda e: e.activation(out=s_[:, 2, :], in_=pv[:], func=AF.Copy), rd=[pv], wr=[s_])
                S.dma("sp", k.vtok[tsl, oc, :].rearrange("(b p) c -> p b c", p=128),
                      s_[:, 2, :].rearrange("p (b c) -> p b c", b=4), rd=[s_], wr=[k.vtok])


def rwkv_rec(k, l, W, nsteps=T):
    S, I = k.S, k.S.I
    VC = 16
    with phase(k) as st:
        sb, ps = k.sb, k.ps
        rvec = sb(st, "rrvec", [128, 2, 7], F32)
        S.dma("sp", rvec[:], W["rvec"], wr=[rvec])
        Sx = [Ring([sb(st, "rrS%d_%d" % (hp, i), [128, 128], F32) for i in range(3)]) for hp in range(2)]
        S1 = [Ring([sb(st, "rrS1%d_%d" % (hp, i), [128, 128], F32) for i in range(2)]) for hp in range(2)]
        KK = [Ring([sb(st, "rrKK%d_%d" % (hp, i), [128, 128], F32) for i in range(3)]) for hp in range(2)]
        S2 = [Ring([sb(st, "rrS2%d_%d" % (hp, i), [128, 128], F32) for i in range(2)]) for hp in range(2)]
        sap = [Ring([ps(st, "rrsa%d_%d" % (hp, i), [128, 128], F32) for i in range(2)]) for hp in range(2)]
        yp = [ps(st, "rryp%d" % hp, [128, 512], F32) for hp in range(2)]
        c5 = [Ring([sb(st, "rrc5%d_%d" % (hp, i), [128, 5, 512], F32) for i in range(2)]) for hp in range(2)]
        vr = [Ring([sb(st, "rrvr%d_%d" % (hp, i), [128, VC, 128], F32) for i in range(2)]) for hp in range(2)]
        ysb = [sb(st, "rry%d" % hp, [128, T], F32) for hp in range(2)]
        for hp in range(2):
            for b in vr[hp].bufs:
                I("pool", lambda e: e.memset(b[:], 0.0), wr=[b])
            b0 = Sx[hp].bufs[0]
            I("dve", lambda e: e.memset(b0[:], 0.0), wr=[b0])
            if nsteps < T:
                I("pool", lambda e: e.memset(ysb[hp][:], 0.0), wr=[ysb[hp]])
        cur = [Sx[0].next(), Sx[1].next()]
        c5c = [None, None]
        vrc = [None, None]
        for t in range(nsteps):
            tl = t % 512
            for hp in range(2):
                if tl == 0:
                    c_ = c5[hp].next()
                    S.dma("sp", c_[:], k.sc5[hp, :, :, t:t + 512], rd=[k.sc5], wr=[c_])
                    c5c[hp] = c_
                if t % VC == 0:
                    v_ = vr[hp].next()
                    for hh in range(2):
                        S.dma("sp", v_[hh * 64:(hh + 1) * 64, :, hh * 64:(hh + 1) * 64],
                              k.vtok[t:t + VC, hp, hh * 64:(hh + 1) * 64].partition_broadcast(64), rd=[k.vtok], wr=[v_])
                    vrc[hp] = v_
                c_, v_ = c5c[hp], vrc[hp]
                old = cur[hp]
                new = Sx[hp].next()
                kk_ = KK[hp].next()
                I("act", lambda e: e.activation(out=kk_[:], in_=k.blk64[:], func=AF.Copy, scale=c_[:, 3, tl:tl + 1]),
                  rd=[k.blk64, c_], wr=[kk_])
                sa = sap[hp].next()
                mm(k, sa[:], kk_[:], old[:], True, True, rd=[kk_, old], wr=[sa])
                s1 = S1[hp].next()
                s2 = S2[hp].next()
                I("act", lambda e: e.activation(out=s1[:], in_=old[:], func=AF.Copy, scale=c_[:, 0, tl:tl + 1]),
                  rd=[old, c_], wr=[s1])
                I("pool", lambda e: e.tensor_scalar(out=s2[:], in0=v_[:, t % VC, :], scalar1=c_[:, 2, tl:tl + 1], scalar2=None, op0=ALU.mult),
                  rd=[v_, c_], wr=[s2])
                I("pool", lambda e: e.tensor_tensor(out=s1[:], in0=s1[:], in1=s2[:], op=ALU.add), rd=[s1, s2], wr=[s1])
                I("dve", lambda e: e.scalar_tensor_tensor(out=new[:], in0=sa[:], scalar=c_[:, 1, tl:tl + 1], in1=s1[:],
                                                          op0=ALU.mult, op1=ALU.add), rd=[sa, c_, s1], wr=[new])
                mm(k, yp[hp][:, tl:tl + 1], new[:], c_[:, 4, tl:tl + 1], True, True, rd=[new, c_], wr=[yp[hp]])
                cur[hp] = new
                if tl == 511 or t == nsteps - 1:
                    t0 = t - tl
                    ypb = yp[hp]
                    I("act", lambda e: e.activation(out=ysb[hp][:, t0:t0 + tl + 1], in_=ypb[:, 0:tl + 1], func=AF.Copy),
                      rd=[ypb], wr=[ysb[hp]])
        ld = Ring([sb(st, "rrld%d" % i, [128, 2, 512], F32) for i in range(2)])
        wk = Ring([sb(st, "rrwk%d" % i, [128, 3, 512], F32) for i in range(2)])
        ob = Ring([sb(st, "rrob%d" % i, [128, 512], BF16) for i in range(2)])
        for hp in range(2):
            for tt in range(NT):
                tsl = slice(tt * 512, (tt + 1) * 512)
                x_ = ld.next()
                S.dma("sp", x_[:, 0, :], k.bon[hp, :, tsl], rd=[k.bon], wr=[x_])
                S.dma("sp", x_[:, 1, :], k.gg[hp, :, tsl], rd=[k.gg], wr=[x_])
                w_ = wk.next()
                pm = sap[0].next()
                pv_ = sap[1].next()
                pmt = yp[0]
                mm(k, pmt[:], k.blk64[:], ysb[hp][:, tsl], True, True, rd=[k.blk64, ysb[hp]], wr=[pmt])
                I("dve", lambda e: e.scalar_tensor_tensor(out=w_[:, 0, :], in0=pmt[:], scalar=-1.0 / 64.0, in1=ysb[hp][:, tsl],
                                                          op0=ALU.mult, op1=ALU.add), rd=[pmt, ysb[hp]], wr=[w_])
                I("act", lambda e: e.activation(out=w_[:, 1, :], in_=w_[:, 0, :], func=AF.Square), rd=[w_], wr=[w_])
                pvt = yp[1]
                mm(k, pvt[:], k.blk64[:], w_[:, 1, :], True, True, rd=[k.blk64, w_], wr=[pvt])
                I("act", lambda e: e.activation(out=w_[:, 1, :], in_=pvt[:], func=AF.Ln, bias=k.epsgn[:, 0:1], scale=1.0 / 64.0),
                  rd=[pvt, k.epsgn], wr=[w_])
                I("act", lambda e: e.activation(out=w_[:, 1, :], in_=w_[:, 1, :], func=AF.Exp, scale=-0.5), rd=[w_], wr=[w_])
                I("dve", lambda e: e.tensor_tensor(out=w_[:, 0, :], in0=w_[:, 0, :], in1=w_[:, 1, :], op=ALU.mult), rd=[w_], wr=[w_])
                I("dve", lambda e: e.tensor_scalar(out=w_[:, 0, :], in0=w_[:, 0, :], scalar1=rvec[:, hp, 5:6], scalar2=rvec[:, hp, 6:7],
                                                   op0=ALU.mult, op1=ALU.add), rd=[w_, rvec], wr=[w_])
                I("pool", lambda e: e.tensor_tensor(out=w_[:, 0, :], in0=w_[:, 0, :], in1=x_[:, 0, :], op=ALU.add), rd=[w_, x_], wr=[w_])
                o_ = ob.next()
                I("pool", lambda e: e.tensor_tensor(out=o_[:], in0=w_[:, 0, :], in1=x_[:, 1, :], op=ALU.mult), rd=[w_, x_], wr=[o_])
                S.dma("sp", k.ycat[2, :, hp, tsl], o_[:], rd=[o_], wr=[k.ycat])


def rwkv_chunked(k, l, W, nsteps=T):
    S, I = k.S, k.S.I
    C = 64
    TA = 256
    NCH = nsteps // C
    with phase(k) as st:
        sb, ps = k.sb, k.ps
        rvec = sb(st, "rcvec", [128, 2, 7], F32)
        S.dma("sp", rvec[:], W["rvec"], wr=[rvec])
        ysb = [sb(st, "rcy%d" % hp, [128, T], F32) for hp in range(2)]
        if nsteps < T:
            for hp in range(2):
                I("pool", lambda e: e.memset(ysb[hp][:], 0.0), wr=[ysb[hp]])
        _rwkv_chunked_core(k, ysb, nsteps)
        _rwkv_post(k, ysb, rvec)


def _rwkv_chunked_core(k, ysb, nsteps):
    S, I = k.S, k.S.I
    C = 64
    TA = 256
    NCH = nsteps // C
    with phase(k) as st:
        sb, ps = k.sb, k.ps
        msu = sb(st, "rcmsu", [128, 128], F32)
        msl = sb(st, "rcmsl", [128, 128], F32)
        mui = sb(st, "rcmui", [128, 128], F32)
        rmask = sb(st, "rcrmask", [128, TA], F32)
        I("pool", lambda e: e.affine_select(out=msu[:], in_=k.blk64[:], compare_op=ALU.is_gt, fill=0.0, base=0,
                                            pattern=[[1, 128]], channel_multiplier=-1), rd=[k.blk64], wr=[msu])
        I("pool", lambda e: e.affine_select(out=msl[:], in_=k.blk64[:], compare_op=ALU.is_gt, fill=0.0, base=0,
                                            pattern=[[-1, 128]], channel_multiplier=1), rd=[k.blk64], wr=[msl])
        I("pool", lambda e: e.affine_select(out=mui[:], in_=k.blk64[:], compare_op=ALU.is_ge, fill=0.0, base=0,
                                            pattern=[[1, 128]], channel_multiplier=-1), rd=[k.blk64], wr=[mui])
        I("pool", lambda e: e.memset(rmask[:], 1.0), wr=[rmask])
        I("pool", lambda e: e.memset(rmask[:].rearrange("p (c t) -> p c t", t=C)[:, :, 0:1], 0.0), wr=[rmask])
        c5 = [Ring([sb(st, "rcc5%d_%d" % (hp, i), [128, 5, TA], F32) for i in range(2)]) for hp in range(2)]
        wkA = [Ring([sb(st, "rcwa%d_%d" % (hp, i), [128, 4, TA], F32) for i in range(2)]) for hp in range(2)]
        eLr = [Ring([sb(st, "rceL%d_%d" % (hp, i), [128, TA], F32) for i in range(3)]) for hp in range(2)]
        pad = [Ring([sb(st, "rcpad%d_%d" % (hp, i), [128, 4, TA // C, 128], F32) for i in range(3)]) for hp in range(2)]
        for hp in range(2):
            for b in pad[hp].bufs:
                I("pool", lambda e: e.memset(b[:], 0.0), wr=[b])
        NPER = 12
        pers = Ring([sb(st, "rcper%d" % i, [128, 6, 128], F32) for i in range(NPER)])
        tmp = Ring([sb(st, "rctmp%d" % i, [128, 8, 128], F32) for i in range(4)])
        Vp = Ring([sb(st, "rcvp%d" % i, [128, 128], F32) for i in range(8)])
        for b in Vp.bufs:
            I("pool", lambda e: e.memset(b[:], 0.0), wr=[b])
        Sx = [Ring([sb(st, "rcS%d_%d" % (hp, i), [128, 128], F32) for i in range(3)]) for hp in range(2)]
        cw = [Ring([sb(st, "rccw%d_%d" % (hp, i), [128, 3, 128], F32) for i in range(2)]) for hp in range(2)]
        pbB = Ring([ps(st, "rcpb%d" % i, [128, 4, 128], F32) for i in range(4)])
        pbA = [ps(st, "rcpa%d" % hp, [128, 3, 128], F32) for hp in range(2)]
        pbY = [ps(st, "rcpy%d" % hp, [128, 128], F32) for hp in range(2)]
        cur = []
        for hp in range(2):
            b0 = Sx[hp].next()
            I("dve", lambda e: e.memset(b0[:], 0.0), wr=[b0])
            cur.append(b0)

        tiles = {}

        def stageA(hp, ti):
            t0 = ti * TA
            c_ = c5[hp].next()
            S.dma("sp", c_[:], k.sc5[hp, :, :, t0:t0 + TA], rd=[k.sc5], wr=[c_])
            w_ = wkA[hp].next()
            e_ = eLr[hp].next()
            p_ = pad[hp].next()
            I("dve", lambda e: e.tensor_tensor_scan(out=w_[:, 0, :], data0=rmask[:], data1=c_[:, 0, :], initial=0.0,
                                                    op0=ALU.mult, op1=ALU.add), rd=[c_, rmask], wr=[w_])
            I("pool", lambda e: e.tensor_tensor(out=w_[:, 1, :], in0=w_[:, 0, :], in1=c_[:, 0, :], op=ALU.subtract), rd=[w_, c_], wr=[w_])
            I("act", lambda e: e.activation(out=e_[:], in_=w_[:, 0, :], func=AF.Exp), rd=[w_], wr=[e_])
            I("act", lambda e: e.activation(out=w_[:, 2, :], in_=w_[:, 0, :], func=AF.Exp, scale=-1.0), rd=[w_], wr=[w_])
            I("act", lambda e: e.activation(out=w_[:, 3, :], in_=w_[:, 1, :], func=AF.Exp), rd=[w_], wr=[w_])

            def v3(ap):
                return ap.rearrange("p (c t) -> p c t", t=C)
            for hh in range(2):
                rs = slice(hh * 64, hh * 64 + 64)
                specs = [(0, c_[rs, 3, :], w_[rs, 3, :]),
                         (1, c_[rs, 4, :], e_[rs, :]),
                         (2, c_[rs, 1, :], w_[rs, 2, :]),
                         (3, c_[rs, 2, :], w_[rs, 2, :])]
                for qi, a_, b_ in specs:
                    eng = "dve" if qi % 2 == 0 else "pool"
                    I(eng, lambda e: e.tensor_tensor(out=p_[rs, qi, :, rs], in0=v3(a_), in1=v3(b_), op=ALU.mult),
                      rd=[c_, w_, e_], wr=[p_])
            tiles[(hp, ti)] = (c_, e_, p_)

        inst = {}

        def stageB(group):
            ctx = []
            for (ch, hp) in group:
                ti, ci = divmod(ch, TA // C)
                c_, e_, p_ = tiles[(hp, ti)]
                Ap, Rp, Bp, Kp = (p_[:, q, ci, :] for q in range(4))
                pr = pers.next()
                tm = tmp.next()
                pb = pbB.next()
                inst[(ch, hp)] = pr
                ctx.append((ch, hp, p_, Ap, Rp, Bp, Kp, pr, tm, pb))
            for (ch, hp, p_, Ap, Rp, Bp, Kp, pr, tm, pb) in ctx:
                ci = ch % (TA // C)
                mm(k, pb[:, 0:2, :], Bp, p_[:, 0:2, ci, :], True, True, rd=[p_], wr=[pb])
                mm(k, pb[:, 2:4, :], Kp, p_[:, 0:2, ci, :], True, True, rd=[p_], wr=[pb])
            for (ch, hp, p_, Ap, Rp, Bp, Kp, pr, tm, pb) in ctx:
                I("dve", lambda e: e.tensor_tensor(out=tm[:, 0, :], in0=pb[:, 0, :], in1=msu[:], op=ALU.mult), rd=[pb, msu], wr=[tm])
                I("dve", lambda e: e.tensor_tensor(out=pr[:, 2, :], in0=pb[:, 1, :], in1=mui[:], op=ALU.mult), rd=[pb, mui], wr=[pr])
                I("dve", lambda e: e.tensor_tensor(out=pr[:, 1, :], in0=pb[:, 2, :], in1=msu[:], op=ALU.mult), rd=[pb, msu], wr=[pr])
                I("dve", lambda e: e.tensor_tensor(out=pr[:, 3, :], in0=pb[:, 3, :], in1=mui[:], op=ALU.mult), rd=[pb, mui], wr=[pr])
                I("pool", lambda e: e.tensor_copy(tm[:, 1, :], k.identf[:]), rd=[k.identf], wr=[tm])
            for (ch, hp, p_, Ap, Rp, Bp, Kp, pr, tm, pb) in ctx:
                mm(k, pb[:, 0, :], Ap, Bp, True, True, rd=[p_], wr=[pb])
                tr(k, pb[:, 1, :], Bp, k.identf[:], rd=[p_, k.identf], wr=[pb])
                tr(k, pb[:, 2, :], Kp, k.identf[:], rd=[p_, k.identf], wr=[pb])
            for (ch, hp, p_, Ap, Rp, Bp, Kp, pr, tm, pb) in ctx:
                I("dve", lambda e: e.tensor_tensor(out=tm[:, 4, :], in0=pb[:, 0, :], in1=msl[:], op=ALU.mult), rd=[pb, msl], wr=[tm])
                I("act", lambda e: e.activation(out=pr[:, 4:6, :], in_=pb[:, 1:3, :], func=AF.Copy), rd=[pb], wr=[pr])
            for r in range(1, 7):
                so, sn = (0, 2) if r % 2 == 1 else (2, 0)
                qo, qn = (4, 5) if r % 2 == 1 else (5, 4)
                for (ch, hp, p_, Ap, Rp, Bp, Kp, pr, tm, pb) in ctx:
                    if r < 6:
                        mm(k, pb[:, 0:2, :], tm[:, qo, :], tm[:, so:so + 2, :], True, True, rd=[tm], wr=[pb])
                        mm(k, pb[:, 2, :], tm[:, so, :], tm[:, qo, :], True, True, rd=[tm], wr=[pb])
                    else:
                        mm(k, pb[:, 1, :], tm[:, qo, :], tm[:, so + 1, :], True, True, rd=[tm], wr=[pb])
                for (ch, hp, p_, Ap, Rp, Bp, Kp, pr, tm, pb) in ctx:
                    if r < 6:
                        I("act", lambda e: e.activation(out=tm[:, sn, :], in_=pb[:, 0, :], func=AF.Copy), rd=[pb], wr=[tm])
                        I("act", lambda e: e.activation(out=tm[:, qn, :], in_=pb[:, 2, :], func=AF.Copy), rd=[pb], wr=[tm])
                        I("dve", lambda e: e.tensor_tensor(out=tm[:, sn + 1, :], in0=pb[:, 1, :], in1=tm[:, so + 1, :], op=ALU.add),
                          rd=[pb, tm], wr=[tm])
                    else:
                        I("dve", lambda e: e.tensor_tensor(out=pr[:, 0, :], in0=pb[:, 1, :], in1=tm[:, so + 1, :], op=ALU.add),
                          rd=[pb, tm], wr=[pr])

        def stageC(ch, hp):
            ti, ci = divmod(ch, TA // C)
            c_, e_, p_ = tiles[(hp, ti)]
            Ap, Rp = p_[:, 0, ci, :], p_[:, 1, ci, :]
            pr = inst.pop((ch, hp))
            t0 = ch * C
            vp = Vp.next()
            for hh in range(2):
                rs = slice(hh * 64, hh * 64 + 64)
                S.dma("sp", vp[rs, rs], k.vtok[t0:t0 + C, hp, rs], rd=[k.vtok], wr=[vp])
            old = cur[hp]
            new = Sx[hp].next()
            w_ = cw[hp].next()
            pa, py = pbA[hp], pbY[hp]
            elc = e_[:, ci * C + C - 1:ci * C + C]
            mm(k, pa[:, 0, :], Ap, old[:], True, False, rd=[p_, old], wr=[pa])
            mm(k, pa[:, 0, :], pr[:, 1, :], vp[:], False, True, rd=[pr, vp], wr=[pa])
            I("act", lambda e: e.activation(out=w_[:, 0, :], in_=pa[:, 0, :], func=AF.Copy), rd=[pa], wr=[w_])
            I("pool", lambda e: e.tensor_scalar(out=w_[:, 2, :], in0=old[:], scalar1=elc, scalar2=None, op0=ALU.mult), rd=[old, e_], wr=[w_])
            mm(k, pa[:, 1, :], pr[:, 0, :], w_[:, 0, :], True, True, rd=[pr, w_], wr=[pa])
            I("dve", lambda e: e.tensor_copy(w_[:, 1, :], pa[:, 1, :]), rd=[pa], wr=[w_])
            mm(k, pa[:, 2, :], pr[:, 4, :], w_[:, 1, :], True, False, rd=[pr, w_], wr=[pa])
            mm(k, pa[:, 2, :], pr[:, 5, :], vp[:], False, True, rd=[pr, vp], wr=[pa])
            I("dve", lambda e: e.scalar_tensor_tensor(out=new[:], in0=pa[:, 2, :], scalar=elc, in1=w_[:, 2, :],
                                                      op0=ALU.mult, op1=ALU.add), rd=[pa, e_, w_], wr=[new])
            mm(k, py[:], old[:], Rp, True, False, rd=[old, p_], wr=[py])
            mm(k, py[:], w_[:, 1, :], pr[:, 2, :], False, False, rd=[w_, pr], wr=[py])
            mm(k, py[:], vp[:], pr[:, 3, :], False, True, rd=[vp, pr], wr=[py])
            for hh in range(2):
                rs = slice(hh * 64, hh * 64 + 64)
                I("act", lambda e: e.activation(out=ysb[hp][rs, t0:t0 + C], in_=py[rs, rs], func=AF.Copy), rd=[py], wr=[ysb[hp]])
            cur[hp] = new

        GC = 2
        ngroups = NCH // GC
        def group(g):
            return [(g * GC + c, hp) for c in range(GC) for hp in range(2)]
        def ensureA(g):
            for (ch, hp) in group(g):
                ti = ch // (TA // C)
                if (hp, ti) not in tiles:
                    stageA(hp, ti)
        ensureA(0)
        stageB(group(0))
        for g in range(ngroups):
            if g + 1 < ngroups:
                ensureA(g + 1)
                stageB(group(g + 1))
            for (ch, hp) in group(g):
                stageC(ch, hp)


def _rwkv_post(k, ysb, rvec):
    S, I = k.S, k.S.I
    with phase(k) as st:
        sb, ps = k.sb, k.ps
        pbB = Ring([ps(st, "rcpq%d" % i, [128, 4, 128], F32) for i in range(2)])
        ld = Ring([sb(st, "rcld%d" % i, [128, 2, 512], F32) for i in range(2)])
        wk = Ring([sb(st, "rcwk%d" % i, [128, 3, 512], F32) for i in range(2)])
        ob = Ring([sb(st, "rcob%d" % i, [128, 512], BF16) for i in range(2)])
        pmt, pvt = pbB.next(), pbB.next()
        pmt = Buf(pmt.t.rearrange("p a b -> p (a b)"), excl=True)
        pvt = Buf(pvt.t.rearrange("p a b -> p (a b)"), excl=True)
        for hp in range(2):
            for tt in range(NT):
                tsl = slice(tt * 512, (tt + 1) * 512)
                x_ = ld.next()
                S.dma("sp", x_[:, 0, :], k.bon[hp, :, tsl], rd=[k.bon], wr=[x_])
                S.dma("sp", x_[:, 1, :], k.gg[hp, :, tsl], rd=[k.gg], wr=[x_])
                w_ = wk.next()
                mm(k, pmt[:], k.blk64[:], ysb[hp][:, tsl], True, True, rd=[k.blk64, ysb[hp]], wr=[pmt])
                I("dve", lambda e: e.scalar_tensor_tensor(out=w_[:, 0, :], in0=pmt[:], scalar=-1.0 / 64.0, in1=ysb[hp][:, tsl],
                                                          op0=ALU.mult, op1=ALU.add), rd=[pmt, ysb[hp]], wr=[w_])
                I("act", lambda e: e.activation(out=w_[:, 1, :], in_=w_[:, 0, :], func=AF.Square), rd=[w_], wr=[w_])
                mm(k, pvt[:], k.blk64[:], w_[:, 1, :], True, True, rd=[k.blk64, w_], wr=[pvt])
                I("act", lambda e: e.activation(out=w_[:, 1, :], in_=pvt[:], func=AF.Ln, bias=k.epsgn[:, 0:1], scale=1.0 / 64.0),
                  rd=[pvt, k.epsgn], wr=[w_])
                I("act", lambda e: e.activation(out=w_[:, 1, :], in_=w_[:, 1, :], func=AF.Exp, scale=-0.5), rd=[w_], wr=[w_])
                I("dve", lambda e: e.tensor_tensor(out=w_[:, 0, :], in0=w_[:, 0, :], in1=w_[:, 1, :], op=ALU.mult), rd=[w_], wr=[w_])
                I("dve", lambda e: e.tensor_scalar(out=w_[:, 0, :], in0=w_[:, 0, :], scalar1=rvec[:, hp, 5:6], scalar2=rvec[:, hp, 6:7],
                                                   op0=ALU.mult, op1=ALU.add), rd=[w_, rvec], wr=[w_])
                I("pool", lambda e: e.tensor_tensor(out=w_[:, 0, :], in0=w_[:, 0, :], in1=x_[:, 0, :], op=ALU.add), rd=[w_, x_], wr=[w_])
                o_ = ob.next()
                I("pool", lambda e: e.tensor_tensor(out=o_[:], in0=w_[:, 0, :], in1=x_[:, 1, :], op=ALU.mult), rd=[w_, x_], wr=[o_])
                S.dma("sp", k.ycat[2, :, hp, tsl], o_[:], rd=[o_], wr=[k.ycat])


def precast_mlp(k, l, W):
    with phase(k) as st:
        cast_weight(k, st, W["w1"], k.w1s, k.w1s, 1024, 4096, "w1", to_dram=True)
    with phase(k) as st:
        cast_weight(k, st, W["w2m"], k.w2s, k.w2s, 4096, 1024, "w2", to_dram=True, dst_fn=lambda n0, n1: k.w2s[n0 // 128])


def merge_phase(k, l, W, h1T):
    S, I = k.S, k.S.I
    with phase(k) as st:
        sb, ps = k.sb, k.ps
        wbr = sb(st, "mgwbr", [128, 8, 1024], BF16)
        wout = sb(st, "mgwout", [128, 8, 1024], BF16)
        with phase(k) as st2:
            cast_weight(k, st2, W["wbr"], wbr, wbr, 1024, 1024, "br")
        with phase(k) as st2:
            cast_weight(k, st2, W["wout"], wout, wout, 1024, 1024, "wo")
        ln = sb(st, "mgln", [128, 8, 2], F32)
        S.dma("sp", ln[:], W["ln1"], wr=[ln])
        yt = Ring([sb(st, "mgyt%d" % i, [128, 4, 2, 512], BF16) for i in range(2)])
        gt = Ring([sb(st, "mggt%d" % i, [128, 4, 512], BF16) for i in range(3)])
        acc = Ring([sb(st, "mgacc%d" % i, [128, 2, 512], F32) for i in range(2)])
        mg = Ring([sb(st, "mgmg%d" % i, [128, 8, 512], BF16) for i in range(1)])
        hr = Ring([sb(st, "mghr%d" % i, [128, 8, 512], F32) for i in range(1)])
        z = Ring([sb(st, "mgz%d" % i, [128, 8, 512], F32) for i in range(1)])
        sq = sb(st, "mgsq", [128, 8, 512], F32)
        msb = sb(st, "mgmsb", [128, 512], F32)
        up = Ring([ps(st, "mgup%d" % i, [128, 512], F32) for i in range(4)])
        mps = ps(st, "mgmps", [128, 512], F32)
        vps = ps(st, "mgvps", [128, 512], F32)
        gview = k.gsc.t.rearrange("(n m) p t -> m p n t", n=4)
        for tt in range(NT):
            tsl = slice(tt * 512, (tt + 1) * 512)
            y_ = yt.next()
            for n in range(4):
                S.dma("sp", y_[:, n, :, :], k.ycat[n, :, :, tsl], rd=[k.ycat], wr=[y_])
            h_ = hr.next()
            S.dma("sp", h_[:], k.hres[:, :, tsl], rd=[k.hres], wr=[h_])
            m_ = mg.next()
            for mc in range(8):
                msl = slice(mc * 128, (mc + 1) * 128)
                g_ = gt.next()
                S.dma("sp", g_[:], gview[mc][:, :, tsl], rd=[k.gsc], wr=[g_])
                a_ = acc.next()
                for n in range(4):
                    u_ = up.next()
                    for kc in range(2):
                        mm(k, u_[:], wbr[:, n * 2 + kc, msl], y_[:, n, kc, :], kc == 0, kc == 1, rd=[wbr, y_], wr=[u_])
                    if n == 0:
                        I("dve", lambda e: e.tensor_tensor(out=a_[:, 0, :], in0=u_[:], in1=g_[:, 0, :], op=ALU.mult),
                          rd=[u_, g_], wr=[a_])
                    else:
                        I("dve", lambda e: e.tensor_tensor(out=a_[:, 1, :], in0=u_[:], in1=g_[:, n, :], op=ALU.mult),
                          rd=[u_, g_], wr=[a_])
                        if n < 3:
                            I("pool", lambda e: e.tensor_tensor(out=a_[:, 0, :], in0=a_[:, 0, :], in1=a_[:, 1, :], op=ALU.add),
                              rd=[a_], wr=[a_])
                        else:
                            I("pool", lambda e: e.tensor_tensor(out=m_[:, mc, :], in0=a_[:, 0, :], in1=a_[:, 1, :], op=ALU.add),
                              rd=[a_], wr=[m_])
            z_ = z.next()
            for mc in range(8):
                msl = slice(mc * 128, (mc + 1) * 128)
                u_ = up.next()
                for kc in range(8):
                    mm(k, u_[:], wout[:, kc, msl], m_[:, kc, :], kc == 0, kc == 7, rd=[wout, m_], wr=[u_])
                I("dve", lambda e: e.scalar_tensor_tensor(out=z_[:, mc, :], in0=h_[:, mc, :], scalar=ALPHA, in1=u_[:],
                                                          op0=ALU.mult, op1=ALU.add), rd=[h_, u_], wr=[z_])

            def outs(c, dn, g_ap, b_ap):
                I("pool", lambda e: e.tensor_scalar(out=dn, in0=dn, scalar1=g_ap, scalar2=b_ap, op0=ALU.mult, op1=ALU.add),
                  rd=[z_, ln], wr=[z_])
            layer_norm_tile(k, (mps, msb, sq, vps), z_, ln, LN_EPS, outs)
            I("act", lambda e: e.activation(out=h1T[:, :, tsl], in_=z_[:], func=AF.Copy), rd=[z_], wr=[h1T])
            S.dma("sp", k.h1res[:, :, tsl], z_[:], rd=[z_], wr=[k.h1res])


def mlp_phase(k, l, W, h1T, p_l, out, last):
    S, I = k.S, k.S.I
    with phase(k) as st:
        sb, ps = k.sb, k.ps
        pgw = sb(st, "mlpgw", [128, 8, 1024], BF16)
        plw = sb(st, "mlplw", [128, 2, 1024], BF16)
        with phase(k) as st2:
            cast_weight(k, st2, W["pgw"], pgw, pgw, 1024, 1024, "pg")
        with phase(k) as st2:
            cast_weight(k, st2, W["plew"], plw, plw, 256, 1024, "pl")
        with phase(k) as st2:
            pin = Ring([sb(st2, "mlpin%d" % i, [128, 256], F32) for i in range(2)])
            tps = Ring([ps(st2, "mltps%d" % i, [128, 2, 128], F32) for i in range(2)])
            pT = sb(st2, "mlpT", [128, 2, T], BF16)
            for blk in range(NB):
                pi_ = pin.next()
                S.dma("sp", pi_[:], p_l[blk * 128:(blk + 1) * 128, :], wr=[pi_])
                t_ = tps.next()
                for c in range(2):
                    tr(k, t_[:, c, :], pi_[:, c * 128:(c + 1) * 128], k.identf[:], rd=[pi_, k.identf], wr=[t_])
                I("act", lambda e: e.activation(out=pT[:, :, blk * 128:(blk + 1) * 128], in_=t_[:], func=AF.Copy), rd=[t_], wr=[pT])
            S.dma("sp", k.pTs[:], pT[:], rd=[pT], wr=[k.pTs])
        ln = sb(st, "mlln", [128, 8, 2], F32)
        S.dma("sp", ln[:], W["ln2"], wr=[ln])
        w1r = Ring([sb(st, "mlw1%d" % i, [128, 8, 512], BF16) for i in range(2)])
        w2r = Ring([sb(st, "mlw2%d" % i, [128, 32, 128], BF16) for i in range(2)])
        pTr = Ring([sb(st, "mlpT%d" % i, [128, 2, 512], BF16) for i in range(1)])
        a = sb(st, "mla", [128, 32, 512], BF16)
        rl = Ring([sb(st, "mlrl%d" % i, [128, 512], F32) for i in range(2)])
        hr = sb(st, "mlhr", [128, 8, 512], F32)
        z = sb(st, "mlz", [128, 8, 512], F32)
        msb = sb(st, "mlmsb", [128, 512], F32)
        sg = Ring([sb(st, "mlsg%d" % i, [128, 2, 512], F32) for i in range(1)])
        up = Ring([ps(st, "mlup%d" % i, [128, 512], F32) for i in range(3)])
        gp = Ring([ps(st, "mlgp%d" % i, [128, 512], F32) for i in range(2)])
        mps = ps(st, "mlmps", [128, 512], F32)
        vps = ps(st, "mlvps", [128, 512], F32)
        tpo = ps(st, "mltpo", [128, 4, 128], F32)
        ot = Ring([sb(st, "mlot%d" % i, [128, D], F32) for i in range(1)]) if last else None
        hb = Ring([sb(st, "mlhb%d" % i, [128, 8, 512], BF16) for i in range(1)]) if not last else None
        for tt in range(NT):
            tsl = slice(tt * 512, (tt + 1) * 512)
            S.dma("sp", hr[:], k.h1res[:, :, tsl], rd=[k.h1res], wr=[hr])
            pT = pTr.next()
            S.dma("sp", pT[:], k.pTs[:, :, tsl], rd=[k.pTs], wr=[pT])
            for fg in range(8):
                w1_ = w1r.next()
                S.dma("sp", w1_[:], k.w1s[:, :, fg * 512:(fg + 1) * 512], rd=[k.w1s], wr=[w1_])
                for f8 in range(4):
                    fc = fg * 4 + f8
                    u_ = up.next()
                    for kc in range(8):
                        mm(k, u_[:], w1_[:, kc, f8 * 128:(f8 + 1) * 128], h1T[:, kc, tsl], kc == 0, kc == 7, rd=[w1_, h1T], wr=[u_])
                    r_ = rl.next()
                    I("act", lambda e: e.activation(out=r_[:], in_=u_[:], func=AF.Relu), rd=[u_], wr=[r_])
                    I("pool", lambda e: e.tensor_tensor(out=a[:, fc, :], in0=r_[:], in1=r_[:], op=ALU.mult), rd=[r_], wr=[a])
            for mc in range(8):
                msl = slice(mc * 128, (mc + 1) * 128)
                s_ = sg.next()
                g_ = gp.next()
                for kc in range(8):
                    mm(k, g_[:], pgw[:, kc, msl], h1T[:, kc, tsl], kc == 0, kc == 7, rd=[pgw, h1T], wr=[g_])
                I("act", lambda e: e.activation(out=s_[:, 0, :], in_=g_[:], func=AF.Sigmoid), rd=[g_], wr=[s_])
                g2_ = gp.next()
                for kc in range(2):
                    mm(k, g2_[:], plw[:, kc, msl], pT[:, kc, :], kc == 0, kc == 1, rd=[plw, pT], wr=[g2_])
                I("dve", lambda e: e.tensor_tensor(out=s_[:, 1, :], in0=g2_[:], in1=s_[:, 0, :], op=ALU.mult), rd=[g2_, s_], wr=[s_])
                u_ = up.next()
                w2_ = w2r.next()
                S.dma("sp", w2_[:], k.w2s[mc], rd=[k.w2s], wr=[w2_])
                for fc in range(32):
                    mm(k, u_[:], w2_[:, fc, :], a[:, fc, :], fc == 0, fc == 31, rd=[w2_, a], wr=[u_])
                I("dve", lambda e: e.scalar_tensor_tensor(out=z[:, mc, :], in0=hr[:, mc, :], scalar=ALPHA, in1=u_[:],
                                                          op0=ALU.mult, op1=ALU.add), rd=[hr, u_], wr=[z])
                I("pool", lambda e: e.tensor_tensor(out=z[:, mc, :], in0=z[:, mc, :], in1=s_[:, 1, :], op=ALU.add), rd=[z, s_], wr=[z])

            def outs(c, dn, g_ap, b_ap):
                I("pool", lambda e: e.tensor_scalar(out=dn, in0=dn, scalar1=g_ap, scalar2=b_ap, op0=ALU.mult, op1=ALU.add),
                  rd=[z, ln], wr=[z])
            layer_norm_tile(k, (mps, msb, hr, vps), z, ln, LN_EPS, outs)
            if not last:
                hb_ = hb.next()
                I("act", lambda e: e.activation(out=hb_[:], in_=z[:], func=AF.Copy), rd=[z], wr=[hb_])
                S.dma("sp", k.hTs[:, :, tsl], hb_[:], rd=[hb_], wr=[k.hTs])
                S.dma("sp", k.hres[:, :, tsl], z[:], rd=[z], wr=[k.hres])
            else:
                for b in range(4):
                    o_ = ot.next()
                    for half in range(2):
                        for j in range(4):
                            c = half * 4 + j
                            tr(k, tpo[:, j, :], z[:, c, b * 128:(b + 1) * 128], k.identf[:], rd=[z, k.identf], wr=[tpo])
                        I("act", lambda e: e.activation(out=o_[:, half * 512:(half + 1) * 512],
                                                        in_=tpo[:].rearrange("p a b -> p (a b)"), func=AF.Copy), rd=[tpo], wr=[o_])
                    r0 = tt * 512 + b * 128
                    S.dma("sp", out[r0:r0 + 128, :], o_[:], rd=[o_], wr=[Dep()])


_CACHE = {}


def kernel(**inputs):
    if "nc" not in _CACHE:
        _CACHE["nc"] = build_program(depth=DEPTH)[0]
    nc = _CACHE["nc"]
    shared = {}
    for l in range(DEPTH):
        for n, v in prep_layer(inputs, l).items():
            shared["%s_%d" % (n, l)] = v
    x = np.asarray(inputs["x"], np.float32)
    p = np.asarray(inputs["p"], np.float32)
    in_maps = []
    for b in range(8):
        m = dict(shared)
        m["x"] = np.ascontiguousarray(x[b])
        m["p"] = np.ascontiguousarray(p[:, b])
        in_maps.append(m)
    res = run_bass_kernel_spmd(nc, in_maps, core_ids=list(range(8)))
    return np.stack([np.asarray(r["out"], np.float32) for r in res.results], axis=0)
```

```python
import math
import contextlib
import numpy as np
import concourse.bass as bass
import concourse.mybir as mybir
from concourse.bass_utils import run_bass_kernel_spmd

F32 = mybir.dt.float32
BF16 = mybir.dt.bfloat16
AF = mybir.ActivationFunctionType
ALU = mybir.AluOpType
AX = mybir.AxisListType

T = 4096
D = 1024
NT = 8
NB = 32
DEPTH = 2
ALPHA = (2 * DEPTH) ** 0.25
LN_EPS = 1e-5
GN_EPS = 64e-5
PI = math.pi


class Dep:
    __slots__ = ("w", "r", "excl")

    def __init__(self):
        self.w = {}
        self.r = {}
        self.excl = False


class Buf:
    def __init__(self, t, excl=False):
        self.t = t
        self.dep = Dep()
        self.dep.excl = excl

    def __getitem__(self, k):
        return self.t[k]


class Ring:
    def __init__(self, bufs):
        self.bufs = bufs
        self.i = 0

    def next(self):
        b = self.bufs[self.i % len(self.bufs)]
        self.i += 1
        return b


class Sched:
    EPOCH = 30000
    NDMA = 6
    NEPOCH = 8
    import os as _os
    qmap = {} if _os.environ.get("KNOB_POOLQ") else {"pool": "sp"}

    def __init__(self, nc):
        self.nc = nc
        self.eng = {"pe": nc.tensor, "act": nc.scalar, "dve": nc.vector,
                    "pool": nc.gpsimd, "sp": nc.sync}
        self.sem = {}
        self.cnt = {}
        self.seen = {e: {} for e in self.eng}
        self.nsem = 0
        self.ninst = 0
        self.allsems = []
        self.epool = {e: [self._alloc("e_%s_%d" % (e, i)) for i in range(self.NEPOCH)] for e in ("pe", "act", "dve", "pool")}
        self.dq = {}
        for q in ("sp", "pool"):
            self.dq[q] = {"sems": [self._alloc("dq_%s_%d" % (q, i)) for i in range(self.NDMA)], "n": 0}
        for e in ("pe", "act", "dve", "pool"):
            self._new_epoch(e)

    def _alloc(self, name):
        self.nsem += 1
        h = self.nc.alloc_semaphore(name)
        self.allsems.append(h)
        return h

    def _new_epoch(self, e):
        self.sem[e] = self.epool[e].pop(0)
        self.cnt[e] = 0

    def _wait(self, eng, sem, val):
        key = id(sem)
        if self.seen[eng].get(key, 0) >= val:
            return
        self.eng[eng].wait_ge(sem, val)
        self.seen[eng][key] = val

    def _gather(self, eng, rd, wr, same_ok=True):
        need = {}

        def add(p):
            pe, sem, val = p
            if pe == eng and (same_ok or eng == "pe"):
                return
            k = id(sem)
            if k not in need or need[k][1] < val:
                need[k] = (sem, val)
        for d in rd:
            for p in d.w.values():
                add(p)
        for d in wr:
            for p in d.w.values():
                add(p)
            for p in d.r.values():
                add(p)
        for sem, val in need.values():
            self._wait(eng, sem, val)

    @staticmethod
    def _deps(lst):
        return [x.dep if isinstance(x, Buf) else x for x in lst]

    @staticmethod
    def _merge(dct, tok):
        k = id(tok[1])
        if k not in dct or dct[k][2] < tok[2]:
            dct[k] = tok

    def I(self, eng, fn, rd=(), wr=()):
        rd = self._deps(rd)
        wr = self._deps(wr)
        ex = [d for d in rd if d.excl and d not in wr]
        if ex:
            rd = [d for d in rd if not d.excl]
            wr = list(wr) + ex
        self._gather(eng, rd, wr, same_ok=False)
        if self.cnt[eng] >= self.EPOCH:
            self._new_epoch(eng)
        ins = fn(self.eng[eng])
        self.ninst += 1
        self.cnt[eng] += 1
        ins.then_inc(self.sem[eng], 1)
        tok = (eng, self.sem[eng], self.cnt[eng])
        for d in rd:
            self._merge(d.r, tok)
        for d in wr:
            d.w = {id(tok[1]): tok}
            d.r = {}
        return ins

    def dma(self, q, out, in_, rd=(), wr=()):
        rd = self._deps(rd)
        wr = self._deps(wr)
        q = self.qmap.get(q, q)
        dq = self.dq[q]
        j = dq["n"]
        dq["n"] += 1
        sem = dq["sems"][j % self.NDMA]
        val = 16 * (j // self.NDMA + 1)
        if val > 16:
            self._wait(q, sem, val - 16)
        self._gather(q, rd, wr, same_ok=False)
        ins = self.eng[q].dma_start(out=out, in_=in_)
        self.ninst += 1
        ins.then_inc(sem, 16)
        tok = ("dma", sem, val)
        for d in rd:
            self._merge(d.r, tok)
        for d in wr:
            self._merge(d.w, tok)
            d.r = {}
        return ins

    def barrier(self, force=False):
        for w in ("pe", "act", "dve", "pool", "sp"):
            for q, dq in self.dq.items():
                n = dq["n"]
                for i in range(self.NDMA):
                    cnt_i = len(range(i, n, self.NDMA))
                    if cnt_i > 0:
                        self._wait(w, dq["sems"][i], 16 * cnt_i)
            for e in ("pe", "act", "dve", "pool"):
                if e != w and self.cnt[e] > 0:
                    self._wait(w, self.sem[e], self.cnt[e])

    def finish(self):
        for q, dq in self.dq.items():
            n = dq["n"]
            for i in range(self.NDMA):
                cnt_i = len(range(i, n, self.NDMA))
                if cnt_i > 0:
                    self._wait("sp", dq["sems"][i], 16 * cnt_i)
        for e in ("pe", "act", "dve", "pool"):
            if self.cnt[e] > 0:
                self._wait("sp", self.sem[e], self.cnt[e])


def _chunkcol(v):
    v = np.asarray(v, np.float32)
    return np.ascontiguousarray(v.reshape(-1, 128).T)


def prep_layer(inp, l):
    f = np.float32
    w_in = np.asarray(inp["w_in"][l], f)
    o = {}
    o["winF"] = np.ascontiguousarray(np.concatenate(
        [w_in[:, 0:256], w_in[:, 256:512], w_in[:, 512:768], w_in[:, 1028:2052],
         w_in[:, 2052:2308], w_in[:, 2308:2564], w_in[:, 2820:6916]], axis=1))
    o["winT"] = np.ascontiguousarray(np.concatenate([w_in[:, 768:1024], w_in[:, 2564:2820]], axis=1))
    o["wff"] = np.ascontiguousarray(w_in[:, 1024:1028])
    lre = np.asarray(inp["s5_lambda_re"][l], f)
    lim = np.asarray(inp["s5_lambda_im"][l], f)
    ldt = np.repeat(np.asarray(inp["s5_log_dt"][l], f)[:, None], 64, axis=1)
    def gp(a):
        return a.reshape(8, 2, 64).transpose(1, 2, 0).reshape(128, 8)
    o["s5par"] = np.ascontiguousarray(np.stack([gp(lre), gp(lim), gp(ldt)], axis=2))
    b_re = np.asarray(inp["s5_b_re"][l], f)
    b_im = np.asarray(inp["s5_b_im"][l], f)
    c_re = np.asarray(inp["s5_c_re"][l], f)
    c_im = np.asarray(inp["s5_c_im"][l], f)
    Bw = np.zeros((8, 128, 2, 128), f)
    Cw = np.zeros((8, 128, 2, 128), f)
    for g in range(16):
        rt = g // 2
        k0 = (g % 8) * 16
        m0 = (g % 2) * 64
        Bw[rt, k0:k0 + 16, 0, m0:m0 + 64] = b_re[g].T
        Bw[rt, k0:k0 + 16, 1, m0:m0 + 64] = b_im[g].T
        Cw[rt, m0:m0 + 64, 0, k0:k0 + 16] = c_re[g].T
        Cw[rt, m0:m0 + 64, 1, k0:k0 + 16] = c_im[g].T
    o["s5B"] = np.ascontiguousarray(Bw.transpose(1, 0, 2, 3))
    o["s5C"] = np.ascontiguousarray(Cw.transpose(1, 0, 2, 3))
    o["s5D"] = _chunkcol(np.asarray(inp["s5_d"][l], f).reshape(-1))
    o["gluw"] = np.asarray(inp["s5_glu_w"][l], f)
    o["glub"] = _chunkcol(inp["s5_glu_b"][l])
    o["foxb"] = np.asarray(inp["fox_f_bias"][l], f).reshape(4, 1)
    o["mu"] = _chunkcol(inp["rwkv_mu"][l])
    vec = [inp["rwkv_w0"][l], inp["rwkv_a0"][l], inp["rwkv_k_k"][l], inp["rwkv_k_a"][l],
           np.asarray(inp["rwkv_r_k"][l]).reshape(-1), inp["rwkv_lnx_g"][l], inp["rwkv_lnx_b"][l]]
    o["rvec"] = np.ascontiguousarray(np.stack([_chunkcol(v) for v in vec], axis=2))
    o["wa2"] = np.ascontiguousarray(np.concatenate([np.asarray(inp["rwkv_w2"][l], f),
                                                     np.asarray(inp["rwkv_a2"][l], f)], axis=0))
    o["g2"] = np.asarray(inp["rwkv_g2"][l], f)
    o["wbr"] = np.ascontiguousarray(np.asarray(inp["w_branch"][l], f).reshape(1024, 1024))
    o["wout"] = np.asarray(inp["w_out"][l], f)
    o["ln1"] = np.ascontiguousarray(np.stack([_chunkcol(inp["ln1_g"][l]), _chunkcol(inp["ln1_b"][l])], axis=2))
    o["ln2"] = np.ascontiguousarray(np.stack([_chunkcol(inp["ln2_g"][l]), _chunkcol(inp["ln2_b"][l])], axis=2))
    o["w1"] = np.asarray(inp["mlp_w1"][l], f)
    o["w2m"] = np.asarray(inp["mlp_w2"][l], f)
    o["plew"] = np.asarray(inp["ple_w"][l], f)
    o["pgw"] = np.asarray(inp["ple_gate_w"][l], f)
    return {k: np.ascontiguousarray(v, dtype=f) for k, v in o.items()}


LAYER_SHAPES = {
    "winF": [1024, 6400], "winT": [1024, 512], "wff": [1024, 4], "s5par": [128, 8, 3],
    "s5B": [128, 8, 2, 128], "s5C": [128, 8, 2, 128], "s5D": [128, 2], "gluw": [256, 256], "glub": [128, 2],
    "foxb": [4, 1], "mu": [128, 8], "rvec": [128, 2, 7], "wa2": [128, 256], "g2": [128, 256],
    "wbr": [1024, 1024], "wout": [1024, 1024], "ln1": [128, 8, 2], "ln2": [128, 8, 2],
    "w1": [1024, 4096], "w2m": [4096, 1024], "plew": [256, 1024], "pgw": [1024, 1024],
}


class K:
    pass


@contextlib.contextmanager
def phase(k):
    with contextlib.ExitStack() as st:
        yield st
        k.S.barrier()


def build_program(depth=DEPTH, dbg=(), stop_after=None, only=None, nsteps=T):
    nc = bass.Bass("TRN2", target_bir_lowering=False)
    S = Sched(nc)
    k = K()
    k.nc, k.S = nc, S
    k.dbg = {}
    x_in = nc.dram_tensor("x", [T, D], F32, kind="ExternalInput").ap()
    p_in = nc.dram_tensor("p", [DEPTH, T, 256], F32, kind="ExternalInput").ap()
    out = nc.dram_tensor("out", [T, D], F32, kind="ExternalOutput").ap()
    W = []
    for l in range(DEPTH):
        W.append({n: nc.dram_tensor("%s_%d" % (n, l), s, F32, kind="ExternalInput").ap()
                  for n, s in LAYER_SHAPES.items()})
    k.stop_after = stop_after
    k.only = only
    k.nsteps = nsteps

    def scratch(name, shape, dt):
        kind = "ExternalOutput" if name in dbg else "Internal"
        import os
        if os.environ.get("KNOB_TINY") and name != "hres":
            shape = [2, 2]
        return Buf(nc.dram_tensor(name, shape, dt, kind=kind).ap())
    k.hres = scratch("hres", [128, 8, T], F32)
    k.h1res = scratch("h1res", [128, 8, T], F32)
    k.gsc = scratch("gsc", [32, 128, T], BF16)
    k.ycat = scratch("ycat", [4, 128, 2, T], BF16)
    k.rwxs = scratch("rwxs", [8, 128, T], F32)
    k.sc5 = scratch("sc5", [2, 128, 5, T], F32)
    k.bon = scratch("bon", [2, 128, T], F32)
    k.gg = scratch("gg", [2, 128, T], F32)
    k.vtok = scratch("vtok", [T, 2, 128], F32)
    k.uTf = scratch("uTf", [128, 2, T], F32)
    k.qk = scratch("qk", [4, 128, 2, T], BF16)
    k.vv = scratch("vv", [2, 128, NB, 256], BF16)
    k.negc = scratch("negc", [4, T], F32)
    k.hTs = scratch("hTs", [128, 8, T], BF16)
    k.pTs = scratch("pTs", [128, 2, T], BF16)
    k.w1s = scratch("w1s", [128, 8, 4096], BF16)
    k.w2s = scratch("w2s", [8, 128, 32, 128], BF16)

    es = contextlib.ExitStack()

    uid = [0]

    def sb(stack, name, shape, dt):
        uid[0] += 1
        return Buf(stack.enter_context(nc.sbuf_tensor("%s_u%d" % (name, uid[0]), shape, dt)))

    def ps(stack, name, shape, dt=F32):
        uid[0] += 1
        full = [128, 512] if dt == F32 else [128, 1024]
        t = stack.enter_context(nc.psum_tensor("%s_u%d" % (name, uid[0]), full, dt))
        n = 1
        for d_ in shape[1:]:
            n *= d_
        if len(shape) == 2:
            v = t[0:shape[0], 0:n]
        else:
            assert len(shape) == 3
            v = t[0:shape[0], 0:n].rearrange("p (a b) -> p a b", a=shape[1])
        return Buf(v, excl=True)
    k.sb, k.ps = sb, ps

    with es:
        k.identf = sb(es, "identf", [128, 128], F32)
        k.identb = sb(es, "identb", [128, 128], BF16)
        k.onesm = sb(es, "onesm", [128, 128], F32)
        k.blk64 = sb(es, "blk64", [128, 128], F32)
        k.nmi = sb(es, "nmi", [128, 128], F32)
        k.nms = sb(es, "nms", [128, 128], F32)
        k.m01 = sb(es, "m01", [128, 128], F32)
        k.ones = sb(es, "ones", [128, 512], F32)
        k.sel4 = sb(es, "sel4", [4, 4, 128], F32)
        k.epsln = sb(es, "epsln", [128, 1], F32)
        k.epsgn = sb(es, "epsgn", [128, 1], F32)
        k.one1 = sb(es, "one1", [128, 1], F32)
        k.mhalf = sb(es, "mhalf", [128, 1], F32)
        I = S.I
        I("pool", lambda e: e.memset(k.epsln[:], LN_EPS), wr=[k.epsln])
        I("pool", lambda e: e.memset(k.epsgn[:], GN_EPS), wr=[k.epsgn])
        I("pool", lambda e: e.memset(k.one1[:], 1.0), wr=[k.one1])
        I("pool", lambda e: e.memset(k.mhalf[:], -0.5), wr=[k.mhalf])
        I("pool", lambda e: e.memset(k.identf[:], 0.0), wr=[k.identf])
        I("pool", lambda e: e.affine_select(out=k.identf[:], in_=k.identf[:], compare_op=ALU.not_equal, fill=1.0,
                                            base=0, pattern=[[-1, 128]], channel_multiplier=1),
          rd=[k.identf], wr=[k.identf])
        I("pool", lambda e: e.tensor_copy(k.identb[:], k.identf[:]), rd=[k.identf], wr=[k.identb])
        I("pool", lambda e: e.memset(k.onesm[:], 1.0 / 1024.0), wr=[k.onesm])
        I("pool", lambda e: e.memset(k.ones[:], 1.0), wr=[k.ones])
        I("pool", lambda e: e.memset(k.blk64[:], 0.0), wr=[k.blk64])
        I("pool", lambda e: e.memset(k.blk64[0:64, 0:64], 1.0), wr=[k.blk64])
        I("pool", lambda e: e.memset(k.blk64[64:128, 64:128], 1.0), wr=[k.blk64])
        I("pool", lambda e: e.memset(k.nmi[:], 0.0), wr=[k.nmi])
        I("pool", lambda e: e.affine_select(out=k.nmi[:], in_=k.nmi[:], compare_op=ALU.is_ge, fill=-1e30,
                                            base=0, pattern=[[-1, 128]], channel_multiplier=1), rd=[k.nmi], wr=[k.nmi])
        I("pool", lambda e: e.memset(k.nms[:], 0.0), wr=[k.nms])
        I("pool", lambda e: e.affine_select(out=k.nms[:], in_=k.nms[:], compare_op=ALU.is_gt, fill=-1e30,
                                            base=0, pattern=[[-1, 128]], channel_multiplier=1), rd=[k.nms], wr=[k.nms])
        I("pool", lambda e: e.affine_select(out=k.m01[:], in_=k.ones[:, 0:128], compare_op=ALU.is_gt, fill=0.0,
                                            base=0, pattern=[[-1, 128]], channel_multiplier=1), rd=[k.ones], wr=[k.m01])
        I("pool", lambda e: e.affine_select(out=k.sel4[:], in_=k.ones[0:4, :].rearrange("p (h m) -> p h m", h=4),
                                            compare_op=ALU.is_equal, fill=0.0, base=0, pattern=[[-1, 4], [0, 128]],
                                            channel_multiplier=1), rd=[k.ones], wr=[k.sel4])

        if stop_after != "consts":
            for l in range(depth):
                layer(k, l, W[l], x_in, p_in[l], out, last=(l == depth - 1))
        S.finish()
    k.ninst = S.ninst
    return nc, k


def run_rr(gens):
    gens = list(gens)
    while gens:
        for g in list(gens):
            try:
                next(g)
            except StopIteration:
                gens.remove(g)


def interleave(make_gen, items, width):
    items = list(items)
    active = []
    nxt = 0
    while nxt < len(items) or active:
        while len(active) < width and nxt < len(items):
            active.append(make_gen(items[nxt]))
            nxt += 1
        for g in list(active):
            try:
                next(g)
            except StopIteration:
                active.remove(g)


def chain_gens(gens):
    for g in gens:
        for _ in g:
            yield


def dbg_dump(k, name, src_buf, src_ap):
    if name in k.dbg:
        k.S.dma("sp", k.dbg[name], src_ap, rd=[src_buf], wr=[Dep()])


def mm(k, out_ap, lhsT, rhs, start, stop, rd, wr):
    return k.S.I("pe", lambda e: e.matmul(out_ap, lhsT, rhs, start=start, stop=stop), rd=rd, wr=wr)


def tr(k, out_ap, in_ap, ident_ap, rd, wr):
    return k.S.I("pe", lambda e: e.transpose(out_ap, in_ap, ident_ap), rd=rd, wr=wr)


def phase0(k, x_in, hT):
    import os
    lvl = int(os.environ.get("KNOB_P0", "9"))
    nblk = int(os.environ.get("KNOB_P0N", str(NB)))
    S, I = k.S, k.S.I
    with phase(k) as st:
        xin = Ring([k.sb(st, "p0x%d" % i, [128, D], F32) for i in range(2)])
        stg = Ring([k.sb(st, "p0s%d" % i, [128, 8, 128], F32) for i in range(2)])
        pss = Ring([k.ps(st, "p0p%d" % i, [128, 4, 128], F32) for i in range(4)])
        for blk in range(nblk):
            xi = xin.next()
            S.dma("sp", xi[:], x_in[blk * 128:(blk + 1) * 128, :], wr=[xi])
            sg = stg.next()
            if lvl < 1:
                continue
            for half in range(2):
                pt = pss.next()
                for j in range(4):
                    c = half * 4 + j
                    tr(k, pt[:, j, :], xi[:, c * 128:(c + 1) * 128], k.identf[:], rd=[xi, k.identf], wr=[pt])
                if lvl >= 2:
                    I("act", lambda e: e.activation(out=hT[:, half * 4:half * 4 + 4, blk * 128:(blk + 1) * 128],
                                                    in_=pt[:], func=AF.Copy), rd=[pt], wr=[hT])
                if lvl >= 3:
                    I("dve", lambda e: e.tensor_copy(sg[:, half * 4:half * 4 + 4, :], pt[:]), rd=[pt], wr=[sg])
            if lvl >= 4:
                S.dma("pool", k.hres[:, :, blk * 128:(blk + 1) * 128], sg[:], rd=[sg], wr=[k.hres])


def cast_weight(k, st, src, dst_buf, dst_ap, K_, N_, tag, to_dram=False, dst_fn=None):
    S, I = k.S, k.S.I
    kc = K_ // 128
    ncol = max(1, min(N_, 4096 // kc))
    f32r = Ring([k.sb(st, "cw%s_f%d" % (tag, i), [128, kc, ncol], F32) for i in range(2)])
    if to_dram:
        bfr = Ring([k.sb(st, "cw%s_b%d" % (tag, i), [128, kc, ncol], BF16) for i in range(2)])
    srcv = src.rearrange("(c p) n -> p c n", p=128)
    engs = ["pool", "dve"]
    i = 0
    for n0 in range(0, N_, ncol):
        n1 = min(N_, n0 + ncol)
        w = n1 - n0
        fb = f32r.next()
        S.dma("sp", fb[:, :, 0:w], srcv[:, :, n0:n1], wr=[fb])
        if to_dram:
            bb = bfr.next()
            I(engs[i % 2], lambda e: e.tensor_copy(bb[:, :, 0:w], fb[:, :, 0:w]), rd=[fb], wr=[bb])
            dst = dst_fn(n0, n1) if dst_fn is not None else dst_ap[:, :, n0:n1]
            S.dma("pool", dst, bb[:, :, 0:w], rd=[bb], wr=[dst_buf])
        else:
            I(engs[i % 2], lambda e: e.tensor_copy(dst_ap[:, :, n0:n1], fb[:, :, 0:w]), rd=[fb], wr=[dst_buf])
        i += 1


def layer_norm_tile(k, st_bufs, z, gb, eps, outs):
    S, I = k.S, k.S.I
    mps, msb, sq, vps = st_bufs
    for c in range(8):
        mm(k, mps[:], k.onesm[:], z[:, c, :], c == 0, c == 7, rd=[k.onesm, z], wr=[mps])
    I("act", lambda e: e.activation(out=msb[:], in_=mps[:], func=AF.Copy), rd=[mps], wr=[msb])
    I("dve", lambda e: e.tensor_tensor(out=z[:], in0=z[:], in1=msb[:, None, :].broadcast_to([128, 8, 512]),
                                       op=ALU.subtract), rd=[z, msb], wr=[z])
    I("act", lambda e: e.activation(out=sq[:], in_=z[:], func=AF.Square), rd=[z], wr=[sq])
    for c in range(8):
        mm(k, vps[:], k.onesm[:], sq[:, c, :], c == 0, c == 7, rd=[k.onesm, sq], wr=[vps])
    I("act", lambda e: e.activation(out=msb[:], in_=vps[:], func=AF.Ln, bias=k.epsln[:, 0:1], scale=1.0),
      rd=[vps, k.epsln], wr=[msb])
    I("act", lambda e: e.activation(out=msb[:], in_=msb[:], func=AF.Exp, scale=-0.5), rd=[msb], wr=[msb])
    I("dve", lambda e: e.tensor_tensor(out=z[:], in0=z[:], in1=msb[:, None, :].broadcast_to([128, 8, 512]),
                                       op=ALU.mult), rd=[z, msb], wr=[z])
    for c in range(8):
        outs(c, z[:, c, :], gb[:, c, 0:1], gb[:, c, 1:2])


def inproj(k, l, W, hT):
    S, I = k.S, k.S.I
    with phase(k) as st:
        wf = Ring([k.sb(st, "ipwf%d" % i, [128, 8, 128], F32) for i in range(2)])
        wb = Ring([k.sb(st, "ipwb%d" % i, [128, 8, 128], BF16) for i in range(2)])
        pss = Ring([k.ps(st, "ipps%d" % i, [128, 512], F32) for i in range(4)])
        stb = Ring([k.sb(st, "ipsb%d" % i, [128, 512], BF16) for i in range(3)])
        stf = Ring([k.sb(st, "ipsf%d" % i, [128, 512], F32) for i in range(2)])
        raw = Ring([k.sb(st, "ipraw%d" % i, [128, T + 1], F32) for i in range(2)])
        xs = Ring([k.sb(st, "ipxs%d" % i, [128, T], F32) for i in range(1)])
        mu = k.sb(st, "ipmu", [128, 8], F32)
        S.dma("sp", mu[:], W["mu"], wr=[mu])
        for r_ in raw.bufs:
            I("pool", lambda e: e.memset(r_[:, 0:1], 0.0), wr=[r_])
        winF = W["winF"].rearrange("(c p) n -> p c n", p=128)
        ev = 0
        for j in range(50):
            if k.stop_after == "ip_fm%d" % j:
                return
            f_ = wf.next()
            S.dma("sp", f_[:], winF[:, :, j * 128:(j + 1) * 128], wr=[f_])
            b_ = wb.next()
            I("pool", lambda e: e.tensor_copy(b_[:], f_[:]), rd=[f_], wr=[b_])
            if 6 <= j < 14:
                rw = raw.next()
            for tt in range(NT):
                pt = pss.next()
                for c in range(8):
                    mm(k, pt[:], b_[:, c, :], hT[:, c, tt * 512:(tt + 1) * 512], c == 0, c == 7,
                       rd=[b_, hT], wr=[pt])
                eng = "act" if ev % 2 == 0 else "dve"
                ev += 1
                tsl = slice(tt * 512, (tt + 1) * 512)
                if j < 2:
                    sf = stf.next()
                    if eng == "act":
                        I("act", lambda e: e.activation(out=sf[:], in_=pt[:], func=AF.Copy), rd=[pt], wr=[sf])
                    else:
                        I("dve", lambda e: e.tensor_copy(sf[:], pt[:]), rd=[pt], wr=[sf])
                    S.dma("pool", k.uTf[:, j, tsl], sf[:], rd=[sf], wr=[k.uTf])
                elif j < 6 or 14 <= j < 18:
                    which = (j - 2) // 2 if j < 6 else 2 + (j - 14) // 2
                    cc = j % 2
                    scale = 0.125 if which in (0, 2) else 1.0
                    sb_ = stb.next()
                    I("act", lambda e: e.activation(out=sb_[:], in_=pt[:], func=AF.Copy, scale=scale), rd=[pt], wr=[sb_])
                    S.dma("pool", k.qk[which, :, cc, tsl], sb_[:], rd=[sb_], wr=[k.qk])
                elif j < 14:
                    if eng == "act":
                        I("act", lambda e: e.activation(out=rw[:, 1 + tt * 512:1 + (tt + 1) * 512], in_=pt[:], func=AF.Copy),
                          rd=[pt], wr=[rw])
                    else:
                        I("dve", lambda e: e.tensor_copy(rw[:, 1 + tt * 512:1 + (tt + 1) * 512], pt[:]), rd=[pt], wr=[rw])
                else:
                    gc = j - 18
                    sb_ = stb.next()
                    I("act", lambda e: e.activation(out=sb_[:], in_=pt[:], func=AF.Sigmoid), rd=[pt], wr=[sb_])
                    S.dma("pool", k.gsc[gc, :, tsl], sb_[:], rd=[sb_], wr=[k.gsc])
            if 6 <= j < 14:
                jj = j - 6
                x_ = xs.next()
                I("pool", lambda e: e.tensor_tensor(out=x_[:], in0=rw[:, 0:T], in1=rw[:, 1:T + 1], op=ALU.subtract),
                  rd=[rw], wr=[x_])
                I("dve", lambda e: e.scalar_tensor_tensor(out=x_[:], in0=x_[:], scalar=mu[:, jj:jj + 1], in1=rw[:, 1:T + 1],
                                                          op0=ALU.mult, op1=ALU.add), rd=[x_, rw, mu], wr=[x_])
                S.dma("sp", k.rwxs[jj], x_[:], rd=[x_], wr=[k.rwxs])
        if k.stop_after == "ip_fm":
            return
        wtf = k.sb(st, "ipwtf", [128, 8, 512], F32)
        wtb = k.sb(st, "ipwtb", [128, 8, 512], BF16)
        S.dma("sp", wtf[:], W["winT"].rearrange("(c p) n -> p c n", p=128), wr=[wtf])
        I("pool", lambda e: e.tensor_copy(wtb[:], wtf[:]), rd=[wtf], wr=[wtb])
        for blk in range(NB):
            pt = pss.next()
            for c in range(8):
                mm(k, pt[:], hT[:, c, blk * 128:(blk + 1) * 128], wtb[:, c, :], c == 0, c == 7, rd=[wtb, hT], wr=[pt])
            sb_ = stb.next()
            if blk % 2 == 0:
                I("act", lambda e: e.activation(out=sb_[:], in_=pt[:], func=AF.Copy), rd=[pt], wr=[sb_])
            else:
                I("dve", lambda e: e.tensor_copy(sb_[:], pt[:]), rd=[pt], wr=[sb_])
            S.dma("pool", k.vv[:, :, blk, :].rearrange("w p n -> p w n"), sb_[:].rearrange("p (w n) -> p w n", w=2),
                  rd=[sb_], wr=[k.vv])
        if k.stop_after == "ip_tm":
            return
        wff_f = k.sb(st, "ipwff", [128, 8, 4], F32)
        wff_b = k.sb(st, "ipwffb", [128, 8, 4], BF16)
        fb = k.sb(st, "ipfb", [4, 1], F32)
        fl = k.sb(st, "ipfl", [4, T], F32)
        S.dma("sp", wff_f[:], W["wff"].rearrange("(c p) n -> p c n", p=128), wr=[wff_f])
        S.dma("sp", fb[:], W["foxb"], wr=[fb])
        I("pool", lambda e: e.tensor_copy(wff_b[:], wff_f[:]), rd=[wff_f], wr=[wff_b])
        I("dve", lambda e: e.tensor_scalar(out=fb[:], in0=fb[:], scalar1=-1.0, scalar2=None, op0=ALU.mult), rd=[fb], wr=[fb])
        for tt in range(NT):
            pt = pss.next()
            for c in range(8):
                mm(k, pt[0:4, :], wff_b[:, c, :], hT[:, c, tt * 512:(tt + 1) * 512], c == 0, c == 7, rd=[wff_b, hT], wr=[pt])
            I("act", lambda e: e.activation(out=fl[:, tt * 512:(tt + 1) * 512], in_=pt[0:4, :], func=AF.Exp,
                                            bias=fb[:, 0:1], scale=-1.0), rd=[pt, fb], wr=[fl])
        I("act", lambda e: e.activation(out=fl[:], in_=fl[:], func=AF.Ln, bias=k.one1[0:4, 0:1], scale=1.0),
          rd=[fl, k.one1], wr=[fl])
        for tt in range(NT):
            init = 0.0 if tt == 0 else fl[:, tt * 512 - 1:tt * 512]
            I("dve", lambda e: e.tensor_tensor_scan(out=fl[:, tt * 512:(tt + 1) * 512], data0=k.ones[0:4, :],
                                                    data1=fl[:, tt * 512:(tt + 1) * 512], initial=init,
                                                    op0=ALU.mult, op1=ALU.add), rd=[fl, k.ones], wr=[fl])
        S.dma("sp", k.negc[:], fl[:], rd=[fl], wr=[k.negc])


def layer(k, l, W, x_in, p_l, out, last):
    S, I = k.S, k.S.I
    with phase(k) as st:
        hT = k.sb(st, "hT%d" % l, [128, 8, T], BF16)
        if l == 0:
            phase0(k, x_in, hT)
            if k.stop_after == "phase0":
                return
        else:
            S.dma("sp", hT[:], k.hTs[:], rd=[k.hTs], wr=[hT])
        inproj(k, l, W, hT)
    if k.stop_after is not None and k.stop_after.startswith("ip") or k.stop_after == "inproj":
        return
    if k.only in (None, "s5"):
        s5_phase(k, l, W)
    if k.stop_after == "s5":
        return
    if k.only in (None, "fox"):
        attn_phase(k, l, 0)
    if k.only in (None, "sb"):
        attn_phase(k, l, 1)
    if k.stop_after == "attn":
        return
    if k.only in (None, "rwkv"):
        rwkv_prep(k, l, W)
        rwkv_chunked(k, l, W, nsteps=k.nsteps)
    if k.stop_after == "rwkv":
        return
    precast_mlp(k, l, W)
    with phase(k) as st:
        h1T = k.sb(st, "h1T%d" % l, [128, 8, T], BF16)
        merge_phase(k, l, W, h1T)
        if k.stop_after == "merge":
            return
        mlp_phase(k, l, W, h1T, p_l, out, last)


def range_reduce(k, out, in_, shift, qi, qf, rd, wr):
    I = k.S.I
    TWO_PI = 2.0 * PI
    I("dve", lambda e: e.tensor_scalar(out=out, in0=in_, scalar1=float(shift), scalar2=None, op0=ALU.add), rd=rd, wr=wr)
    I("dve", lambda e: e.tensor_scalar(out=qi, in0=out, scalar1=1.0 / TWO_PI, scalar2=0.5, op0=ALU.mult, op1=ALU.add), rd=rd, wr=wr)
    I("dve", lambda e: e.tensor_copy(qf, qi), rd=rd, wr=wr)
    I("dve", lambda e: e.scalar_tensor_tensor(out=out, in0=qf, scalar=-TWO_PI, in1=out, op0=ALU.mult, op1=ALU.add), rd=rd, wr=wr)
    I("dve", lambda e: e.tensor_scalar(out=qf, in0=out, scalar1=-PI, scalar2=None, op0=ALU.is_lt), rd=rd, wr=wr)
    I("dve", lambda e: e.scalar_tensor_tensor(out=out, in0=qf, scalar=TWO_PI, in1=out, op0=ALU.mult, op1=ALU.add), rd=rd, wr=wr)
    I("dve", lambda e: e.tensor_scalar(out=out, in0=out, scalar1=-3.1415925, scalar2=3.1415925, op0=ALU.max, op1=ALU.min), rd=rd, wr=wr)


def s5_phase(k, l, W):
    S, I = k.S, k.S.I
    TWO_PI = 2.0 * PI
    with phase(k) as st:
        sb, ps = k.sb, k.ps
        par = sb(st, "s5par", [128, 8, 3], F32)
        S.dma("sp", par[:], W["s5par"], wr=[par])
        Bb = sb(st, "s5Bb", [128, 8, 2, 128], BF16)
        Cb = sb(st, "s5Cb", [128, 8, 2, 128], BF16)
        gwb = sb(st, "s5gwb", [128, 2, 256], BF16)
        Dc = sb(st, "s5D", [128, 2], F32)
        S.dma("sp", Dc[:], W["s5D"], wr=[Dc])
        with phase(k) as st2:
            Bf = sb(st2, "s5Bf", [128, 8, 2, 128], F32)
            Cf = sb(st2, "s5Cf", [128, 8, 2, 128], F32)
            gwf = sb(st2, "s5gwf", [128, 2, 256], F32)
            S.dma("sp", Bf[:], W["s5B"], wr=[Bf])
            S.dma("sp", Cf[:], W["s5C"], wr=[Cf])
            S.dma("sp", gwf[:], W["gluw"].rearrange("(c p) n -> p c n", p=128), wr=[gwf])
            I("pool", lambda e: e.tensor_copy(Bb[:], Bf[:]), rd=[Bf], wr=[Bb])
            I("pool", lambda e: e.tensor_copy(Cb[:, :, 0, :], Cf[:, :, 0, :]), rd=[Cf], wr=[Cb])
            I("pool", lambda e: e.tensor_scalar(out=Cb[:, :, 1, :], in0=Cf[:, :, 1, :], scalar1=-1.0, scalar2=None, op0=ALU.mult),
              rd=[Cf], wr=[Cb])
            I("pool", lambda e: e.tensor_copy(gwb[:], gwf[:]), rd=[gwf], wr=[gwb])
        gb = sb(st, "s5gb", [128, 2], F32)
        S.dma("sp", gb[:], W["glub"], wr=[gb])
        sm = sb(st, "s5sm", [128, 16, 8], F32)
        def sl(i):
            return sm[:, i, :]
        lre, lim, ldt = par[:, :, 0], par[:, :, 1], par[:, :, 2]
        DT, A_, TH, NA, EM, SN, CS, LBR, LBI, DEN, FRE, FIM, NFRE, T1, T2, THR = range(16)
        dsm = Dep()
        def dv(fn):
            I("dve", fn, rd=[par, dsm], wr=[dsm])
        def ac(fn):
            I("act", fn, rd=[par, dsm], wr=[dsm])
        def pl(fn):
            I("pool", fn, rd=[par, dsm], wr=[dsm])
        ac(lambda e: e.activation(out=sl(DT), in_=ldt, func=AF.Exp))
        dv(lambda e: e.tensor_tensor(out=sl(A_), in0=lre, in1=sl(DT), op=ALU.mult))
        dv(lambda e: e.tensor_tensor(out=sl(TH), in0=lim, in1=sl(DT), op=ALU.mult))
        dv(lambda e: e.tensor_scalar(out=sl(NA), in0=sl(A_), scalar1=-1.0, scalar2=None, op0=ALU.mult))
        smi = sb(st, "s5smi", [128, 8], mybir.dt.int32)
        range_reduce(k, sl(T1), sl(TH), 0.0, smi[:], sl(T2), [par, dsm, smi], [dsm, smi])
        ac(lambda e: e.activation(out=sl(SN), in_=sl(T1), func=AF.Sin))
        range_reduce(k, sl(T1), sl(TH), 0.5 * PI, smi[:], sl(T2), [par, dsm, smi], [dsm, smi])
        ac(lambda e: e.activation(out=sl(CS), in_=sl(T1), func=AF.Sin))
        ac(lambda e: e.activation(out=sl(EM), in_=sl(A_), func=AF.Exp))
        dv(lambda e: e.tensor_tensor(out=sl(LBR), in0=sl(EM), in1=sl(CS), op=ALU.mult))
        dv(lambda e: e.tensor_tensor(out=sl(LBI), in0=sl(EM), in1=sl(SN), op=ALU.mult))
        dv(lambda e: e.tensor_tensor(out=sl(DEN), in0=lre, in1=lre, op=ALU.mult))
        dv(lambda e: e.tensor_tensor(out=sl(T1), in0=lim, in1=lim, op=ALU.mult))
        dv(lambda e: e.tensor_tensor(out=sl(DEN), in0=sl(DEN), in1=sl(T1), op=ALU.add))
        dv(lambda e: e.reciprocal(out=sl(DEN), in_=sl(DEN)))
        dv(lambda e: e.tensor_scalar(out=sl(T2), in0=sl(LBR), scalar1=-1.0, scalar2=None, op0=ALU.add))
        dv(lambda e: e.tensor_tensor(out=sl(FRE), in0=sl(T2), in1=lre, op=ALU.mult))
        dv(lambda e: e.tensor_tensor(out=sl(T1), in0=sl(LBI), in1=lim, op=ALU.mult))
        dv(lambda e: e.tensor_tensor(out=sl(FRE), in0=sl(FRE), in1=sl(T1), op=ALU.add))
        dv(lambda e: e.tensor_tensor(out=sl(FRE), in0=sl(FRE), in1=sl(DEN), op=ALU.mult))
        dv(lambda e: e.tensor_tensor(out=sl(FIM), in0=sl(LBI), in1=lre, op=ALU.mult))
        dv(lambda e: e.tensor_tensor(out=sl(T1), in0=sl(T2), in1=lim, op=ALU.mult))
        dv(lambda e: e.tensor_tensor(out=sl(FIM), in0=sl(FIM), in1=sl(T1), op=ALU.subtract))
        dv(lambda e: e.tensor_tensor(out=sl(FIM), in0=sl(FIM), in1=sl(DEN), op=ALU.mult))
        dv(lambda e: e.tensor_scalar(out=sl(NFRE), in0=sl(FRE), scalar1=-1.0, scalar2=None, op0=ALU.mult))
        range_reduce(k, sl(THR), sl(TH), 0.0, smi[:], sl(T1), [par, dsm, smi], [dsm, smi])
        CH = 256
        idx = sb(st, "s5idx", [128, CH + 1], F32)
        I("dve", lambda e: e.memset(idx[:], 1.0), wr=[idx])
        I("dve", lambda e: e.tensor_tensor_scan(out=idx[:], data0=idx[:], data1=idx[:], initial=-1.0,
                                                op0=ALU.mult, op1=ALU.add), rd=[idx], wr=[idx])
        tabr = sb(st, "s5tabr", [128, 8, CH + 1], F32)
        tabi = sb(st, "s5tabi", [128, 8, CH + 1], F32)
        tifr = sb(st, "s5tifr", [128, 8, CH], F32)
        tifi = sb(st, "s5tifi", [128, 8, CH], F32)
        ntl = sb(st, "s5ntl", [128, 8], F32)
        w1 = sb(st, "s5w1", [128, CH + 1], F32)
        w2 = sb(st, "s5w2", [128, CH + 1], F32)
        w3 = sb(st, "s5w3", [128, CH + 1], F32)
        w4 = sb(st, "s5w4", [128, CH + 1], F32)
        w5 = sb(st, "s5w5", [128, CH + 1], F32)
        wi = sb(st, "s5wi", [128, CH + 1], mybir.dt.int32)
        tabs = [tabr, tabi, tifr, tifi]
        wk = [w1, w2, w3, w4, w5, idx]
        for rt in range(8):
            def col(i):
                return sm[:, i, rt:rt + 1]
            def dvw(fn):
                I("dve", fn, rd=wk + [dsm], wr=wk[:5] + tabs)
            def acw(fn):
                I("act", fn, rd=wk + [dsm], wr=wk[:5] + tabs)
            def plw(fn):
                I("pool", fn, rd=wk + [dsm], wr=wk[:5] + tabs)
            dvw(lambda e: e.tensor_scalar(out=w1[:], in0=idx[:], scalar1=col(THR), scalar2=None, op0=ALU.mult))
            range_reduce(k, w2[:], w1[:], 0.0, wi[:], w5[:], wk + [dsm, wi], wk[:5] + tabs + [wi])
            acw(lambda e: e.activation(out=w2[:], in_=w2[:], func=AF.Sin))
            range_reduce(k, w3[:], w1[:], 0.5 * PI, wi[:], w5[:], wk + [dsm, wi], wk[:5] + tabs + [wi])
            acw(lambda e: e.activation(out=w3[:], in_=w3[:], func=AF.Sin))
            acw(lambda e: e.activation(out=w4[:], in_=idx[:], func=AF.Exp, scale=col(A_)))
            acw(lambda e: e.activation(out=w5[:], in_=idx[:], func=AF.Exp, scale=col(NA)))
            dvw(lambda e: e.tensor_tensor(out=tabr[:, rt, :], in0=w4[:], in1=w3[:], op=ALU.mult))
            dvw(lambda e: e.tensor_tensor(out=tabi[:, rt, :], in0=w4[:], in1=w2[:], op=ALU.mult))
            dvw(lambda e: e.tensor_scalar(out=w1[:], in0=w3[:], scalar1=col(FRE), scalar2=None, op0=ALU.mult))
            dvw(lambda e: e.scalar_tensor_tensor(out=w1[:], in0=w2[:], scalar=col(FIM), in1=w1[:], op0=ALU.mult, op1=ALU.add))
            dvw(lambda e: e.tensor_tensor(out=tifr[:, rt, :], in0=w1[:, 0:CH], in1=w5[:, 0:CH], op=ALU.mult))
            dvw(lambda e: e.tensor_scalar(out=w1[:], in0=w3[:], scalar1=col(FIM), scalar2=None, op0=ALU.mult))
            dvw(lambda e: e.scalar_tensor_tensor(out=w1[:], in0=w2[:], scalar=col(NFRE), in1=w1[:], op0=ALU.mult, op1=ALU.add))
            dvw(lambda e: e.tensor_tensor(out=tifi[:, rt, :], in0=w1[:, 0:CH], in1=w5[:, 0:CH], op=ALU.mult))
        I("dve", lambda e: e.tensor_scalar(out=ntl[:], in0=tabi[:, :, CH], scalar1=-1.0, scalar2=None, op0=ALU.mult),
          rd=tabs, wr=[ntl])
        uf = sb(st, "s5uf", [128, 2, T], F32)
        ub = sb(st, "s5ub", [128, 2, T], BF16)
        S.dma("sp", uf[:], k.uTf[:], rd=[k.uTf], wr=[uf])
        I("pool", lambda e: e.tensor_copy(ub[:], uf[:]), rd=[uf], wr=[ub])
        ygf = sb(st, "s5ygf", [128, 2, T], F32)
        ygb = sb(st, "s5ygb", [128, 2, T], BF16)
        q0 = sb(st, "s5q0", [128, 8, 2], F32)
        I("dve", lambda e: e.memset(q0[:], 0.0), wr=[q0])
        rawp = Ring([ps(st, "s5rp%d" % i, [128, 512], F32) for i in range(4)])
        yp = Ring([ps(st, "s5yp%d" % i, [128, 512], F32) for i in range(2)])
        rr = Ring([sb(st, "s5rr%d" % i, [128, 2, 512], F32) for i in range(2)])
        zz = Ring([sb(st, "s5zz%d" % i, [128, 2, 512], F32) for i in range(2)])
        mmr = Ring([sb(st, "s5mm%d" % i, [128, 4, 512], F32) for i in range(2)])
        qq = Ring([sb(st, "s5qq%d" % i, [128, 2, 512], F32) for i in range(2)])
        xb = Ring([sb(st, "s5xb%d" % i, [128, 2, 512], BF16) for i in range(3)])
        ew = Ring([sb(st, "s5ew%d" % i, [128, 3, 512], F32) for i in range(1)])
        tq = sb(st, "s5tq", [128, 2], F32)

        def b4(tab, rt):
            return tab[:, rt:rt + 1, 0:CH].broadcast_to([128, 512 // CH, CH])

        def v4(ap):
            return ap.rearrange("p (a b) -> p a b", a=512 // CH)
        items = [(oc, tt, r4) for oc in range(2) for tt in range(NT) for r4 in range(4)]
        ctx = {}
        ypts = {}

        def stZ(it):
            oc, tt, r4 = it
            rt = oc * 4 + r4
            tsl = slice(tt * 512, (tt + 1) * 512)
            pr, pi_ = rawp.next(), rawp.next()
            mm(k, pr[:], Bb[:, rt, 0, :], ub[:, oc, tsl], True, True, rd=[Bb, ub], wr=[pr])
            mm(k, pi_[:], Bb[:, rt, 1, :], ub[:, oc, tsl], True, True, rd=[Bb, ub], wr=[pi_])
            r_ = rr.next()
            I("act", lambda e: e.activation(out=r_[:, 0, :], in_=pr[:], func=AF.Copy), rd=[pr], wr=[r_])
            I("act", lambda e: e.activation(out=r_[:, 1, :], in_=pi_[:], func=AF.Copy), rd=[pi_], wr=[r_])
            z_, m_ = zz.next(), mmr.next()
            I("dve", lambda e: e.tensor_tensor(out=v4(m_[:, 0, :]), in0=v4(r_[:, 0, :]), in1=b4(tifr, rt), op=ALU.mult),
              rd=[r_] + tabs, wr=[m_])
            I("dve", lambda e: e.tensor_tensor(out=v4(m_[:, 1, :]), in0=v4(r_[:, 1, :]), in1=b4(tifi, rt), op=ALU.mult),
              rd=[r_] + tabs, wr=[m_])
            I("pool", lambda e: e.tensor_tensor(out=v4(m_[:, 2, :]), in0=v4(r_[:, 0, :]), in1=b4(tifi, rt), op=ALU.mult),
              rd=[r_] + tabs, wr=[m_])
            I("pool", lambda e: e.tensor_tensor(out=v4(m_[:, 3, :]), in0=v4(r_[:, 1, :]), in1=b4(tifr, rt), op=ALU.mult),
              rd=[r_] + tabs, wr=[m_])
            I("dve", lambda e: e.tensor_tensor(out=z_[:, 0, :], in0=m_[:, 0, :], in1=m_[:, 1, :], op=ALU.subtract),
              rd=[m_], wr=[z_])
            I("pool", lambda e: e.tensor_tensor(out=z_[:, 1, :], in0=m_[:, 2, :], in1=m_[:, 3, :], op=ALU.add),
              rd=[m_], wr=[z_])
            ctx[it] = z_

        def stSC(it):
            oc, tt, r4 = it
            rt = oc * 4 + r4
            z_ = ctx[it]
            q_ = qq.next()
            for sc in range(512 // CH):
                csl = slice(sc * CH, (sc + 1) * CH)
                for ri in range(2):
                    I("dve", lambda e: e.tensor_tensor_scan(out=q_[:, ri, csl], data0=k.ones[:, 0:CH], data1=z_[:, ri, csl],
                                                            initial=q0[:, rt, ri:ri + 1], op0=ALU.mult, op1=ALU.add),
                      rd=[z_, q0, k.ones], wr=[q_])
                qe_r = q_[:, 0, sc * CH + CH - 1:sc * CH + CH]
                qe_i = q_[:, 1, sc * CH + CH - 1:sc * CH + CH]
                lr_ = tabr[:, rt, CH:CH + 1]
                li_ = tabi[:, rt, CH:CH + 1]
                I("dve", lambda e: e.tensor_scalar(out=tq[:, 0:1], in0=qe_r, scalar1=lr_, scalar2=None, op0=ALU.mult),
                  rd=[q_] + tabs, wr=[tq])
                I("dve", lambda e: e.tensor_scalar(out=tq[:, 1:2], in0=qe_i, scalar1=lr_, scalar2=None, op0=ALU.mult),
                  rd=[q_] + tabs, wr=[tq])
                I("dve", lambda e: e.scalar_tensor_tensor(out=q0[:, rt, 0:1], in0=qe_i, scalar=ntl[:, rt:rt + 1], in1=tq[:, 0:1],
                                                          op0=ALU.mult, op1=ALU.add), rd=[q_, ntl, tq], wr=[q0])
                I("dve", lambda e: e.scalar_tensor_tensor(out=q0[:, rt, 1:2], in0=qe_r, scalar=li_, in1=tq[:, 1:2],
                                                          op0=ALU.mult, op1=ALU.add), rd=[q_, tq] + tabs, wr=[q0])
            ctx[it] = q_

        def stX(it):
            oc, tt, r4 = it
            rt = oc * 4 + r4
            tsl = slice(tt * 512, (tt + 1) * 512)
            q_ = ctx.pop(it)
            if r4 == 0:
                ypts[(oc, tt)] = yp.next()
            ypt = ypts[(oc, tt)]
            m2 = mmr.next()
            x_ = xb.next()
            I("dve", lambda e: e.tensor_tensor(out=v4(m2[:, 0, :]), in0=v4(q_[:, 0, :]), in1=b4(tabr, rt), op=ALU.mult),
              rd=[q_] + tabs, wr=[m2])
            I("dve", lambda e: e.tensor_tensor(out=v4(m2[:, 1, :]), in0=v4(q_[:, 1, :]), in1=b4(tabi, rt), op=ALU.mult),
              rd=[q_] + tabs, wr=[m2])
            I("pool", lambda e: e.tensor_tensor(out=v4(m2[:, 2, :]), in0=v4(q_[:, 0, :]), in1=b4(tabi, rt), op=ALU.mult),
              rd=[q_] + tabs, wr=[m2])
            I("pool", lambda e: e.tensor_tensor(out=v4(m2[:, 3, :]), in0=v4(q_[:, 1, :]), in1=b4(tabr, rt), op=ALU.mult),
              rd=[q_] + tabs, wr=[m2])
            I("dve", lambda e: e.tensor_tensor(out=x_[:, 0, :], in0=m2[:, 0, :], in1=m2[:, 1, :], op=ALU.subtract),
              rd=[m2], wr=[x_])
            I("pool", lambda e: e.tensor_tensor(out=x_[:, 1, :], in0=m2[:, 2, :], in1=m2[:, 3, :], op=ALU.add),
              rd=[m2], wr=[x_])
            mm(k, ypt[:], Cb[:, rt, 0, :], x_[:, 0, :], r4 == 0, False, rd=[Cb, x_], wr=[ypt])
            mm(k, ypt[:], Cb[:, rt, 1, :], x_[:, 1, :], False, r4 == 3, rd=[Cb, x_], wr=[ypt])
            if r4 == 3:
                e_ = ew.next()
                I("dve", lambda e: e.scalar_tensor_tensor(out=e_[:, 0, :], in0=uf[:, oc, tsl], scalar=Dc[:, oc:oc + 1], in1=ypt[:],
                                                          op0=ALU.mult, op1=ALU.add), rd=[uf, Dc, ypt], wr=[e_])
                I("pool", lambda e: e.tensor_tensor(out=e_[:, 1, :], in0=e_[:, 0, :], in1=e_[:, 0, :], op=ALU.mult), rd=[e_], wr=[e_])
                I("pool", lambda e: e.tensor_scalar(out=e_[:, 1, :], in0=e_[:, 1, :], scalar1=0.044715, scalar2=1.0,
                                                    op0=ALU.mult, op1=ALU.add), rd=[e_], wr=[e_])
                I("pool", lambda e: e.tensor_tensor(out=e_[:, 1, :], in0=e_[:, 1, :], in1=e_[:, 0, :], op=ALU.mult), rd=[e_], wr=[e_])
                I("act", lambda e: e.activation(out=e_[:, 2, :], in_=e_[:, 1, :], func=AF.Sigmoid, scale=1.5957691216057308),
                  rd=[e_], wr=[e_])
                I("dve", lambda e: e.tensor_tensor(out=ygf[:, oc, tsl], in0=e_[:, 0, :], in1=e_[:, 2, :], op=ALU.mult),
                  rd=[e_], wr=[ygf])
                I("pool", lambda e: e.tensor_copy(ygb[:, oc, tsl], ygf[:, oc, tsl]), rd=[ygf], wr=[ygb])
        stZ(items[0])
        for i, it in enumerate(items):
            if i + 1 < len(items):
                stZ(items[i + 1])
            stSC(it)
            stX(it)

        ob = Ring([sb(st, "s5ob%d" % i, [128, 512], BF16) for i in range(2)])
        for oc in range(2):
            for tt in range(NT):
                tsl = slice(tt * 512, (tt + 1) * 512)
                pt = yp.next()
                for kc in range(2):
                    mm(k, pt[:], gwb[:, kc, oc * 128:(oc + 1) * 128], ygb[:, kc, tsl], kc == 0, kc == 1, rd=[gwb, ygb], wr=[pt])
                e_ = ew.next()
                I("act", lambda e: e.activation(out=e_[:, 0, :], in_=pt[:], func=AF.Sigmoid, bias=gb[:, oc:oc + 1], scale=1.0),
                  rd=[pt, gb], wr=[e_])
                o_ = ob.next()
                I("dve", lambda e: e.tensor_tensor(out=o_[:], in0=e_[:, 0, :], in1=ygf[:, oc, tsl], op=ALU.mult),
                  rd=[e_, ygf], wr=[o_])
                S.dma("sp", k.ycat[0, :, oc, tsl], o_[:], rd=[o_], wr=[k.ycat])


def attn_phase(k, l, kind):
    S, I = k.S, k.S.I
    fox = (kind == 0)
    VW = 65 if fox else 64
    with phase(k) as st:
        sb, ps = k.sb, k.ps
        qT = sb(st, "atq", [128, 2, T], BF16)
        kT = sb(st, "atk", [128, 2, T], BF16)
        S.dma("sp", qT[:], k.qk[2 * kind], rd=[k.qk], wr=[qT])
        S.dma("sp", kT[:], k.qk[2 * kind + 1], rd=[k.qk], wr=[kT])
        V = sb(st, "atv", [128, NB, 4, VW], BF16)
        if fox:
            I("pool", lambda e: e.memset(V[:, :, :, 64:65], 1.0), wr=[V])
        S.dma("sp", V[:, :, :, 0:64], k.vv[kind].rearrange("p b (h d) -> p b h d", h=4), rd=[k.vv], wr=[V])
        yT = sb(st, "atyT", [128, 2, T], BF16)
        ytok = sb(st, "atytok", [128, NB, 256], BF16)
        A = Ring([sb(st, "atA%d" % i, [128, T], F32) for i in range(2)])
        P = Ring([sb(st, "atP%d" % i, [128, T], BF16) for i in range(2)])
        PT = Ring([sb(st, "atPT%d" % i, [128, 4, 128], BF16) for i in range(3)])
        zp = Ring([ps(st, "atzp%d" % i, [128, 512], F32) for i in range(3)])
        tp = Ring([ps(st, "attp%d" % i, [128, 4, 128], BF16) for i in range(2)])
        op = Ring([ps(st, "atop%d" % i, [128, VW], F32) for i in range(2)])
        xp = ps(st, "atxp", [128, 512], F32)
        nb = Ring([sb(st, "atnb%d" % i, [128, 2], F32) for i in range(3)])
        if fox:
            negc = sb(st, "atnegc", [4, T], F32)
            S.dma("sp", negc[:], k.negc[:], rd=[k.negc], wr=[negc])
            crow = sb(st, "atcrow", [128, T], F32)
        else:
            spb = Ring([sb(st, "atsp%d" % i, [128, 512], F32) for i in range(3)])
            Fb = Ring([sb(st, "atF%d" % i, [128, 512], F32) for i in range(3)])
            t1b = Ring([sb(st, "att1%d" % i, [128, 512], F32) for i in range(3)])
        items = [(h, qb) for h in range(4) for qb in range(NB)]
        state = {}

        def stage1(h, qb):
            hc, po = h // 2, (h % 2) * 64
            if fox and qb == 0:
                for tt in range(NT):
                    mm(k, xp[:], k.sel4[:, h, :], negc[:, tt * 512:(tt + 1) * 512], True, True, rd=[k.sel4, negc], wr=[xp])
                    I("act", lambda e: e.activation(out=crow[:, tt * 512:(tt + 1) * 512], in_=xp[:], func=AF.Copy),
                      rd=[xp], wr=[crow])
            nk = qb + 1
            a_ = A.next()
            nb_ = nb.next()
            carry = None
            for kt in range((nk + 3) // 4):
                w = min(4, nk - 4 * kt) * 128
                ks = slice(kt * 512, kt * 512 + w)
                last = (kt == (nk + 3) // 4 - 1)
                z = zp.next()
                mm(k, z[:, 0:w], qT[po:po + 64, hc, qb * 128:(qb + 1) * 128], kT[po:po + 64, hc, ks], True, True,
                   rd=[qT, kT], wr=[z])
                if fox:
                    I("dve", lambda e: e.tensor_tensor(out=a_[:, ks], in0=z[:, 0:w], in1=crow[:, ks], op=ALU.add),
                      rd=[z, crow], wr=[a_])
                else:
                    sp_ = spb.next()
                    I("act", lambda e: e.activation(out=sp_[:, 0:w], in_=z[:, 0:w], func=AF.Exp), rd=[z], wr=[sp_])
                    I("act", lambda e: e.activation(out=sp_[:, 0:w], in_=sp_[:, 0:w], func=AF.Ln, bias=k.one1[:, 0:1], scale=1.0),
                      rd=[sp_, k.one1], wr=[sp_])
                    if last:
                        I("pool", lambda e: e.tensor_tensor(out=sp_[:, w - 128:w], in0=sp_[:, w - 128:w], in1=k.m01[:], op=ALU.mult),
                          rd=[sp_, k.m01], wr=[sp_])
                    f_ = Fb.next()
                    init = 0.0 if carry is None else carry[0][:, carry[1] - 1:carry[1]]
                    rdl = [sp_, k.ones] + ([carry[0]] if carry is not None else [])
                    I("dve", lambda e: e.tensor_tensor_scan(out=f_[:, 0:w], data0=k.ones[:, 0:w], data1=sp_[:, 0:w], initial=init,
                                                            op0=ALU.mult, op1=ALU.add), rd=rdl, wr=[f_])
                    carry = (f_, w)
                    t_ = t1b.next()
                    I("dve", lambda e: e.tensor_tensor(out=t_[:, 0:w], in0=z[:, 0:w], in1=sp_[:, 0:w], op=ALU.subtract),
                      rd=[z, sp_], wr=[t_])
                    I("pool", lambda e: e.tensor_tensor(out=a_[:, ks], in0=t_[:, 0:w], in1=f_[:, 0:w], op=ALU.add),
                      rd=[t_, f_], wr=[a_])
                yield
            dsl = slice(qb * 128, (qb + 1) * 128)
            I("pool", lambda e: e.tensor_tensor(out=a_[:, dsl], in0=a_[:, dsl], in1=(k.nmi if fox else k.nms)[:], op=ALU.add),
              rd=[a_, k.nmi, k.nms], wr=[a_])
            if fox:
                I("dve", lambda e: e.reduce_max(out=nb_[:, 0:1], in_=a_[:, 0:nk * 128], axis=AX.X), rd=[a_], wr=[nb_])
                I("dve", lambda e: e.tensor_scalar(out=nb_[:, 1:2], in0=nb_[:, 0:1], scalar1=-1.0, scalar2=None, op0=ALU.mult),
                  rd=[nb_], wr=[nb_])
            else:
                I("dve", lambda e: e.tensor_scalar(out=nb_[:, 1:2], in0=carry[0][:, carry[1] - 1:carry[1]], scalar1=-1.0, scalar2=None,
                                                   op0=ALU.mult), rd=[carry[0]], wr=[nb_])
            state[(h, qb)] = (a_, nb_)
            yield

        def stage2(h, qb):
            a_, nb_ = state.pop((h, qb))
            nk = qb + 1
            p_ = P.next()
            I("act", lambda e: e.activation(out=p_[:, 0:nk * 128], in_=a_[:, 0:nk * 128], func=AF.Exp, bias=nb_[:, 1:2], scale=1.0),
              rd=[a_, nb_], wr=[p_])
            o_ = op.next()
            prev = None

            def pv(kt, pt_, nbk):
                for j in range(nbk):
                    kb = kt * 4 + j
                    mm(k, o_[:], pt_[:, j, :], V[:, kb, h, :], kb == 0, kb == nk - 1, rd=[pt_, V], wr=[o_])
            for kt in range((nk + 3) // 4):
                nbk = min(4, nk - 4 * kt)
                t_ = tp.next()
                for j in range(nbk):
                    kb = kt * 4 + j
                    tr(k, t_[:, j, :], p_[:, kb * 128:(kb + 1) * 128], k.identb[:], rd=[p_, k.identb], wr=[t_])
                pt_ = PT.next()
                if kt % 2 == 0:
                    I("act", lambda e: e.activation(out=pt_[:, 0:nbk, :], in_=t_[:, 0:nbk, :], func=AF.Copy), rd=[t_], wr=[pt_])
                else:
                    I("dve", lambda e: e.tensor_copy(pt_[:, 0:nbk, :], t_[:, 0:nbk, :]), rd=[t_], wr=[pt_])
                if prev is not None:
                    pv(*prev)
                prev = (kt, pt_, nbk)
                yield
            pv(*prev)
            if fox:
                I("dve", lambda e: e.reciprocal(out=nb_[:, 0:1], in_=o_[:, 64:65]), rd=[o_], wr=[nb_])
                I("act", lambda e: e.activation(out=ytok[:, qb, h * 64:(h + 1) * 64], in_=o_[:, 0:64], func=AF.Copy,
                                                scale=nb_[:, 0:1]), rd=[o_, nb_], wr=[ytok])
            else:
                I("act", lambda e: e.activation(out=ytok[:, qb, h * 64:(h + 1) * 64], in_=o_[:, 0:64], func=AF.Copy),
                  rd=[o_], wr=[ytok])

        for i, it in enumerate(items):
            run_rr([stage1(*it)])
            if i > 0:
                run_rr([stage2(*items[i - 1])])
        run_rr([stage2(*items[-1])])
        for qb in range(NB):
            t_ = tp.next()
            for c in range(2):
                tr(k, t_[:, c, :], ytok[:, qb, c * 128:(c + 1) * 128], k.identb[:], rd=[ytok, k.identb], wr=[t_])
            I("act", lambda e: e.activation(out=yT[:, :, qb * 128:(qb + 1) * 128], in_=t_[:, 0:2, :], func=AF.Copy),
              rd=[t_], wr=[yT])
        S.dma("sp", k.ycat[1 if fox else 3], yT[:], rd=[yT], wr=[k.ycat])


def rwkv_prep(k, l, W):
    S, I = k.S, k.S.I
    with phase(k) as st:
        sb, ps = k.sb, k.ps
        rvec = sb(st, "rpvec", [128, 2, 7], F32)
        S.dma("sp", rvec[:], W["rvec"], wr=[rvec])
        nw0 = sb(st, "rpnw0", [128, 2], F32)
        I("dve", lambda e: e.tensor_scalar(out=nw0[:], in0=rvec[:, :, 0], scalar1=-1.0, scalar2=None, op0=ALU.mult),
          rd=[rvec], wr=[nw0])
        wa2b = sb(st, "rpwa2b", [128, 256], BF16)
        g2b = sb(st, "rpg2b", [128, 256], BF16)
        tw = sb(st, "rptw", [128, T], BF16)
        gs = sb(st, "rpgs", [128, T], BF16)
        with phase(k) as st2:
            wa2f = sb(st2, "rpwa2f", [128, 256], F32)
            g2f = sb(st2, "rpg2f", [128, 256], F32)
            x6 = sb(st2, "rpx6", [128, T], F32)
            x7 = sb(st2, "rpx7", [128, T], F32)
            S.dma("sp", wa2f[:], W["wa2"], wr=[wa2f])
            S.dma("sp", g2f[:], W["g2"], wr=[g2f])
            S.dma("sp", x6[:], k.rwxs[6], rd=[k.rwxs], wr=[x6])
            S.dma("sp", x7[:], k.rwxs[7], rd=[k.rwxs], wr=[x7])
            I("pool", lambda e: e.tensor_copy(wa2b[:], wa2f[:]), rd=[wa2f], wr=[wa2b])
            I("pool", lambda e: e.tensor_copy(g2b[:], g2f[:]), rd=[g2f], wr=[g2b])
            I("act", lambda e: e.activation(out=tw[0:64, :], in_=x6[0:64, :], func=AF.Tanh), rd=[x6], wr=[tw])
            I("pool", lambda e: e.tensor_copy(tw[64:128, :], x6[64:128, :]), rd=[x6], wr=[tw])
            I("act", lambda e: e.activation(out=gs[:], in_=x7[:], func=AF.Sigmoid), rd=[x7], wr=[gs])
        for oc in range(2):
            S.dma("sp", k.sc5[oc, :, 4, :], k.rwxs[oc], rd=[k.rwxs], wr=[k.sc5])
        ld = Ring([sb(st, "rpld%d" % i, [128, 3, 512], F32) for i in range(2)])
        o5 = Ring([sb(st, "rpo5%d" % i, [128, 4, 512], F32) for i in range(2)])
        wk = Ring([sb(st, "rpwk%d" % i, [128, 6, 512], F32) for i in range(2)])
        so = Ring([sb(st, "rpso%d" % i, [128, 3, 512], F32) for i in range(2)])
        pp = Ring([ps(st, "rppp%d" % i, [128, 512], F32) for i in range(8)])
        def tile_gen(it):
            oc, tt = it
            osl = slice(oc * 128, (oc + 1) * 128)
            tsl = slice(tt * 512, (tt + 1) * 512)
            x_ = ld.next()
            for i, j in enumerate((oc, 2 + oc, 4 + oc)):
                S.dma("sp", x_[:, i, :], k.rwxs[j, :, tsl], rd=[k.rwxs], wr=[x_])
            r_t, k_t, v_t = x_[:, 0, :], x_[:, 1, :], x_[:, 2, :]
            o_ = o5.next()
            w_ = wk.next()
            s_ = so.next()
            pw = pp.next()
            mm(k, pw[:], wa2b[0:64, osl], tw[0:64, tsl], True, True, rd=[wa2b, tw], wr=[pw])
            yield
            I("act", lambda e: e.activation(out=w_[:, 0, :], in_=pw[:], func=AF.Exp, bias=nw0[:, oc:oc + 1], scale=-1.0),
              rd=[pw, nw0], wr=[w_])
            I("act", lambda e: e.activation(out=w_[:, 0, :], in_=w_[:, 0, :], func=AF.Ln, bias=k.one1[:, 0:1], scale=1.0),
              rd=[w_, k.one1], wr=[w_])
            yield
            I("act", lambda e: e.activation(out=w_[:, 0, :], in_=w_[:, 0, :], func=AF.Exp, bias=k.mhalf[:, 0:1], scale=-1.0),
              rd=[w_, k.mhalf], wr=[w_])
            I("pool", lambda e: e.tensor_scalar(out=o_[:, 0, :], in0=w_[:, 0, :], scalar1=-1.0, scalar2=None, op0=ALU.mult),
              rd=[w_], wr=[o_])
            pa = pp.next()
            mm(k, pa[:], wa2b[64:128, osl], tw[64:128, tsl], True, True, rd=[wa2b, tw], wr=[pa])
            yield
            I("act", lambda e: e.activation(out=w_[:, 1, :], in_=pa[:], func=AF.Sigmoid, bias=rvec[:, oc, 1:2], scale=1.0),
              rd=[pa, rvec], wr=[w_])
            a_t = w_[:, 1, :]
            I("dve", lambda e: e.tensor_scalar(out=w_[:, 2, :], in0=k_t, scalar1=rvec[:, oc, 2:3], scalar2=None, op0=ALU.mult),
              rd=[x_, rvec], wr=[w_])
            I("pool", lambda e: e.tensor_tensor(out=w_[:, 3, :], in0=w_[:, 2, :], in1=w_[:, 2, :], op=ALU.mult), rd=[w_], wr=[w_])
            yield
            pn = pp.next()
            mm(k, pn[:], k.blk64[:], w_[:, 3, :], True, True, rd=[k.blk64, w_], wr=[pn])
            I("act", lambda e: e.activation(out=w_[:, 3, :], in_=pn[:], func=AF.Sqrt), rd=[pn], wr=[w_])
            I("dve", lambda e: e.tensor_scalar(out=w_[:, 3, :], in0=w_[:, 3, :], scalar1=1e-12, scalar2=None, op0=ALU.max),
              rd=[w_], wr=[w_])
            I("dve", lambda e: e.reciprocal(out=w_[:, 3, :], in_=w_[:, 3, :]), rd=[w_], wr=[w_])
            yield
            I("dve", lambda e: e.tensor_tensor(out=w_[:, 2, :], in0=w_[:, 2, :], in1=w_[:, 3, :], op=ALU.mult), rd=[w_], wr=[w_])
            I("pool", lambda e: e.tensor_scalar(out=o_[:, 3, :], in0=w_[:, 2, :], scalar1=-1.0, scalar2=None, op0=ALU.mult),
              rd=[w_], wr=[o_])
            I("pool", lambda e: e.tensor_tensor(out=o_[:, 1, :], in0=w_[:, 2, :], in1=a_t, op=ALU.mult), rd=[w_], wr=[o_])
            I("dve", lambda e: e.tensor_scalar(out=w_[:, 4, :], in0=a_t, scalar1=1.0, scalar2=rvec[:, oc, 3:4],
                                               op0=ALU.subtract, op1=ALU.mult), rd=[w_, rvec], wr=[w_])
            I("dve", lambda e: e.scalar_tensor_tensor(out=o_[:, 2, :], in0=w_[:, 4, :], scalar=1.0, in1=k_t,
                                                      op0=ALU.add, op1=ALU.mult), rd=[w_, x_], wr=[o_])
            S.dma("sp", k.sc5[oc, :, 0:4, tsl], o_[:], rd=[o_], wr=[k.sc5])
            yield
            I("pool", lambda e: e.tensor_tensor(out=w_[:, 5, :], in0=r_t, in1=o_[:, 2, :], op=ALU.mult), rd=[x_, o_], wr=[w_])
            I("pool", lambda e: e.tensor_scalar(out=w_[:, 5, :], in0=w_[:, 5, :], scalar1=rvec[:, oc, 4:5], scalar2=None,
                                                op0=ALU.mult), rd=[w_, rvec], wr=[w_])
            pb = pp.next()
            mm(k, pb[:], k.blk64[:], w_[:, 5, :], True, True, rd=[k.blk64, w_], wr=[pb])
            I("dve", lambda e: e.tensor_tensor(out=s_[:, 0, :], in0=pb[:], in1=v_t, op=ALU.mult), rd=[pb, x_], wr=[s_])
            yield
            S.dma("sp", k.bon[oc, :, tsl], s_[:, 0, :], rd=[s_], wr=[k.bon])
            pg = pp.next()
            mm(k, pg[:], g2b[:, osl], gs[:, tsl], True, True, rd=[g2b, gs], wr=[pg])
            I("act", lambda e: e.activation(out=s_[:, 1, :], in_=pg[:], func=AF.Copy), rd=[pg], wr=[s_])
            yield
            S.dma("sp", k.gg[oc, :, tsl], s_[:, 1, :], rd=[s_], wr=[k.gg])
            pv = pp.next()
            for b in range(4):
                tr(k, pv[:, b * 128:(b + 1) * 128], x_[:, 2, b * 128:(b + 1) * 128], k.identf[:], rd=[x_, k.identf], wr=[pv])
            I("act", lambda e: e.activation(out=s_[:, 2, :], in_=pv[:], func=AF.Copy), rd=[pv], wr=[s_])
            S.dma("sp", k.vtok[tsl, oc, :].rearrange("(b p) c -> p b c", p=128),
                  s_[:, 2, :].rearrange("p (b c) -> p b c", b=4), rd=[s_], wr=[k.vtok])


        interleave(tile_gen, [(oc, tt) for oc in range(2) for tt in range(NT)], 2)


def rwkv_rec(k, l, W, nsteps=T):
    S, I = k.S, k.S.I
    VC = 16
    with phase(k) as st:
        sb, ps = k.sb, k.ps
        rvec = sb(st, "rrvec", [128, 2, 7], F32)
        S.dma("sp", rvec[:], W["rvec"], wr=[rvec])
        Sx = [Ring([sb(st, "rrS%d_%d" % (hp, i), [128, 128], F32) for i in range(3)]) for hp in range(2)]
        S1 = [Ring([sb(st, "rrS1%d_%d" % (hp, i), [128, 128], F32) for i in range(2)]) for hp in range(2)]
        KK = [Ring([sb(st, "rrKK%d_%d" % (hp, i), [128, 128], F32) for i in range(3)]) for hp in range(2)]
        S2 = [Ring([sb(st, "rrS2%d_%d" % (hp, i), [128, 128], F32) for i in range(2)]) for hp in range(2)]
        sap = [Ring([ps(st, "rrsa%d_%d" % (hp, i), [128, 128], F32) for i in range(2)]) for hp in range(2)]
        yp = [ps(st, "rryp%d" % hp, [128, 512], F32) for hp in range(2)]
        c5 = [Ring([sb(st, "rrc5%d_%d" % (hp, i), [128, 5, 512], F32) for i in range(2)]) for hp in range(2)]
        vr = [Ring([sb(st, "rrvr%d_%d" % (hp, i), [128, VC, 128], F32) for i in range(2)]) for hp in range(2)]
        ysb = [sb(st, "rry%d" % hp, [128, T], F32) for hp in range(2)]
        for hp in range(2):
            for b in vr[hp].bufs:
                I("pool", lambda e: e.memset(b[:], 0.0), wr=[b])
            b0 = Sx[hp].bufs[0]
            I("dve", lambda e: e.memset(b0[:], 0.0), wr=[b0])
            if nsteps < T:
                I("pool", lambda e: e.memset(ysb[hp][:], 0.0), wr=[ysb[hp]])
        cur = [Sx[0].next(), Sx[1].next()]
        c5c = [None, None]
        vrc = [None, None]
        for t in range(nsteps):
            tl = t % 512
            for hp in range(2):
                if tl == 0:
                    c_ = c5[hp].next()
                    S.dma("sp", c_[:], k.sc5[hp, :, :, t:t + 512], rd=[k.sc5], wr=[c_])
                    c5c[hp] = c_
                if t % VC == 0:
                    v_ = vr[hp].next()
                    for hh in range(2):
                        S.dma("sp", v_[hh * 64:(hh + 1) * 64, :, hh * 64:(hh + 1) * 64],
                              k.vtok[t:t + VC, hp, hh * 64:(hh + 1) * 64].partition_broadcast(64), rd=[k.vtok], wr=[v_])
                    vrc[hp] = v_
                c_, v_ = c5c[hp], vrc[hp]
                old = cur[hp]
                new = Sx[hp].next()
                kk_ = KK[hp].next()
                I("act", lambda e: e.activation(out=kk_[:], in_=k.blk64[:], func=AF.Copy, scale=c_[:, 3, tl:tl + 1]),
                  rd=[k.blk64, c_], wr=[kk_])
                sa = sap[hp].next()
                mm(k, sa[:], kk_[:], old[:], True, True, rd=[kk_, old], wr=[sa])
                s1 = S1[hp].next()
                s2 = S2[hp].next()
                I("act", lambda e: e.activation(out=s1[:], in_=old[:], func=AF.Copy, scale=c_[:, 0, tl:tl + 1]),
                  rd=[old, c_], wr=[s1])
                I("pool", lambda e: e.tensor_scalar(out=s2[:], in0=v_[:, t % VC, :], scalar1=c_[:, 2, tl:tl + 1], scalar2=None, op0=ALU.mult),
                  rd=[v_, c_], wr=[s2])
                I("pool", lambda e: e.tensor_tensor(out=s1[:], in0=s1[:], in1=s2[:], op=ALU.add), rd=[s1, s2], wr=[s1])
                I("dve", lambda e: e.scalar_tensor_tensor(out=new[:], in0=sa[:], scalar=c_[:, 1, tl:tl + 1], in1=s1[:],
                                                          op0=ALU.mult, op1=ALU.add), rd=[sa, c_, s1], wr=[new])
                mm(k, yp[hp][:, tl:tl + 1], new[:], c_[:, 4, tl:tl + 1], True, True, rd=[new, c_], wr=[yp[hp]])
                cur[hp] = new
                if tl == 511 or t == nsteps - 1:
                    t0 = t - tl
                    ypb = yp[hp]
                    I("act", lambda e: e.activation(out=ysb[hp][:, t0:t0 + tl + 1], in_=ypb[:, 0:tl + 1], func=AF.Copy),
                      rd=[ypb], wr=[ysb[hp]])
        ld = Ring([sb(st, "rrld%d" % i, [128, 2, 512], F32) for i in range(2)])
        wk = Ring([sb(st, "rrwk%d" % i, [128, 3, 512], F32) for i in range(2)])
        ob = Ring([sb(st, "rrob%d" % i, [128, 512], BF16) for i in range(2)])
        for hp in range(2):
            for tt in range(NT):
                tsl = slice(tt * 512, (tt + 1) * 512)
                x_ = ld.next()
                S.dma("sp", x_[:, 0, :], k.bon[hp, :, tsl], rd=[k.bon], wr=[x_])
                S.dma("sp", x_[:, 1, :], k.gg[hp, :, tsl], rd=[k.gg], wr=[x_])
                w_ = wk.next()
                pm = sap[0].next()
                pv_ = sap[1].next()
                pmt = yp[0]
                mm(k, pmt[:], k.blk64[:], ysb[hp][:, tsl], True, True, rd=[k.blk64, ysb[hp]], wr=[pmt])
                I("dve", lambda e: e.scalar_tensor_tensor(out=w_[:, 0, :], in0=pmt[:], scalar=-1.0 / 64.0, in1=ysb[hp][:, tsl],
                                                          op0=ALU.mult, op1=ALU.add), rd=[pmt, ysb[hp]], wr=[w_])
                I("act", lambda e: e.activation(out=w_[:, 1, :], in_=w_[:, 0, :], func=AF.Square), rd=[w_], wr=[w_])
                pvt = yp[1]
                mm(k, pvt[:], k.blk64[:], w_[:, 1, :], True, True, rd=[k.blk64, w_], wr=[pvt])
                I("act", lambda e: e.activation(out=w_[:, 1, :], in_=pvt[:], func=AF.Ln, bias=k.epsgn[:, 0:1], scale=1.0 / 64.0),
                  rd=[pvt, k.epsgn], wr=[w_])
                I("act", lambda e: e.activation(out=w_[:, 1, :], in_=w_[:, 1, :], func=AF.Exp, scale=-0.5), rd=[w_], wr=[w_])
                I("dve", lambda e: e.tensor_tensor(out=w_[:, 0, :], in0=w_[:, 0, :], in1=w_[:, 1, :], op=ALU.mult), rd=[w_], wr=[w_])
                I("dve", lambda e: e.tensor_scalar(out=w_[:, 0, :], in0=w_[:, 0, :], scalar1=rvec[:, hp, 5:6], scalar2=rvec[:, hp, 6:7],
                                                   op0=ALU.mult, op1=ALU.add), rd=[w_, rvec], wr=[w_])
                I("pool", lambda e: e.tensor_tensor(out=w_[:, 0, :], in0=w_[:, 0, :], in1=x_[:, 0, :], op=ALU.add), rd=[w_, x_], wr=[w_])
                o_ = ob.next()
                I("pool", lambda e: e.tensor_tensor(out=o_[:], in0=w_[:, 0, :], in1=x_[:, 1, :], op=ALU.mult), rd=[w_, x_], wr=[o_])
                S.dma("sp", k.ycat[2, :, hp, tsl], o_[:], rd=[o_], wr=[k.ycat])


def rwkv_chunked(k, l, W, nsteps=T):
    S, I = k.S, k.S.I
    C = 64
    TA = 256
    NCH = nsteps // C
    with phase(k) as st:
        sb, ps = k.sb, k.ps
        rvec = sb(st, "rcvec", [128, 2, 7], F32)
        S.dma("sp", rvec[:], W["rvec"], wr=[rvec])
        ysb = [sb(st, "rcy%d" % hp, [128, T], F32) for hp in range(2)]
        if nsteps < T:
            for hp in range(2):
                I("pool", lambda e: e.memset(ysb[hp][:], 0.0), wr=[ysb[hp]])
        _rwkv_chunked_core(k, ysb, nsteps)
        _rwkv_post(k, ysb, rvec)


def _rwkv_chunked_core(k, ysb, nsteps):
    S, I = k.S, k.S.I
    C = 64
    TA = 256
    NCH = nsteps // C
    with phase(k) as st:
        sb, ps = k.sb, k.ps
        msu = sb(st, "rcmsu", [128, 128], F32)
        msl = sb(st, "rcmsl", [128, 128], F32)
        mui = sb(st, "rcmui", [128, 128], F32)
        rmask = sb(st, "rcrmask", [128, TA], F32)
        I("pool", lambda e: e.affine_select(out=msu[:], in_=k.blk64[:], compare_op=ALU.is_gt, fill=0.0, base=0,
                                            pattern=[[1, 128]], channel_multiplier=-1), rd=[k.blk64], wr=[msu])
        I("pool", lambda e: e.affine_select(out=msl[:], in_=k.blk64[:], compare_op=ALU.is_gt, fill=0.0, base=0,
                                            pattern=[[-1, 128]], channel_multiplier=1), rd=[k.blk64], wr=[msl])
        I("pool", lambda e: e.affine_select(out=mui[:], in_=k.blk64[:], compare_op=ALU.is_ge, fill=0.0, base=0,
                                            pattern=[[1, 128]], channel_multiplier=-1), rd=[k.blk64], wr=[mui])
        I("pool", lambda e: e.memset(rmask[:], 1.0), wr=[rmask])
        I("pool", lambda e: e.memset(rmask[:].rearrange("p (c t) -> p c t", t=C)[:, :, 0:1], 0.0), wr=[rmask])
        c5 = [Ring([sb(st, "rcc5%d_%d" % (hp, i), [128, 5, TA], F32) for i in range(2)]) for hp in range(2)]
        wkA = [Ring([sb(st, "rcwa%d_%d" % (hp, i), [128, 4, TA], F32) for i in range(2)]) for hp in range(2)]
        eLr = [Ring([sb(st, "rceL%d_%d" % (hp, i), [128, TA], F32) for i in range(3)]) for hp in range(2)]
        pad = [Ring([sb(st, "rcpad%d_%d" % (hp, i), [128, 4, TA // C, 128], F32) for i in range(3)]) for hp in range(2)]
        for hp in range(2):
            for b in pad[hp].bufs:
                I("pool", lambda e: e.memset(b[:], 0.0), wr=[b])
        NPER = 12
        pers = Ring([sb(st, "rcper%d" % i, [128, 6, 128], F32) for i in range(NPER)])
        tmp = Ring([sb(st, "rctmp%d" % i, [128, 8, 128], F32) for i in range(4)])
        Vp = Ring([sb(st, "rcvp%d" % i, [128, 128], F32) for i in range(8)])
        for b in Vp.bufs:
            I("pool", lambda e: e.memset(b[:], 0.0), wr=[b])
        Sx = [Ring([sb(st, "rcS%d_%d" % (hp, i), [128, 128], F32) for i in range(3)]) for hp in range(2)]
        cw = [Ring([sb(st, "rccw%d_%d" % (hp, i), [128, 3, 128], F32) for i in range(2)]) for hp in range(2)]
        pbB = Ring([ps(st, "rcpb%d" % i, [128, 4, 128], F32) for i in range(4)])
        pbA = [ps(st, "rcpa%d" % hp, [128, 3, 128], F32) for hp in range(2)]
        pbY = [ps(st, "rcpy%d" % hp, [128, 128], F32) for hp in range(2)]
        cur = []
        for hp in range(2):
            b0 = Sx[hp].next()
            I("dve", lambda e: e.memset(b0[:], 0.0), wr=[b0])
            cur.append(b0)

        tiles = {}

        def stageA(hp, ti):
            t0 = ti * TA
            c_ = c5[hp].next()
            S.dma("sp", c_[:], k.sc5[hp, :, :, t0:t0 + TA], rd=[k.sc5], wr=[c_])
            w_ = wkA[hp].next()
            e_ = eLr[hp].next()
            p_ = pad[hp].next()
            I("dve", lambda e: e.tensor_tensor_scan(out=w_[:, 0, :], data0=rmask[:], data1=c_[:, 0, :], initial=0.0,
                                                    op0=ALU.mult, op1=ALU.add), rd=[c_, rmask], wr=[w_])
            I("pool", lambda e: e.tensor_tensor(out=w_[:, 1, :], in0=w_[:, 0, :], in1=c_[:, 0, :], op=ALU.subtract), rd=[w_, c_], wr=[w_])
            I("act", lambda e: e.activation(out=e_[:], in_=w_[:, 0, :], func=AF.Exp), rd=[w_], wr=[e_])
            I("act", lambda e: e.activation(out=w_[:, 2, :], in_=w_[:, 0, :], func=AF.Exp, scale=-1.0), rd=[w_], wr=[w_])
            I("act", lambda e: e.activation(out=w_[:, 3, :], in_=w_[:, 1, :], func=AF.Exp), rd=[w_], wr=[w_])

            def v3(ap):
                return ap.rearrange("p (c t) -> p c t", t=C)
            for hh in range(2):
                rs = slice(hh * 64, hh * 64 + 64)
                specs = [(0, c_[rs, 3, :], w_[rs, 3, :]),
                         (1, c_[rs, 4, :], e_[rs, :]),
                         (2, c_[rs, 1, :], w_[rs, 2, :]),
                         (3, c_[rs, 2, :], w_[rs, 2, :])]
                for qi, a_, b_ in specs:
                    eng = "dve" if qi % 2 == 0 else "pool"
                    I(eng, lambda e: e.tensor_tensor(out=p_[rs, qi, :, rs], in0=v3(a_), in1=v3(b_), op=ALU.mult),
                      rd=[c_, w_, e_], wr=[p_])
            tiles[(hp, ti)] = (c_, e_, p_)

        inst = {}

        def stageB(group):
            ctx = []
            for (ch, hp) in group:
                ti, ci = divmod(ch, TA // C)
                c_, e_, p_ = tiles[(hp, ti)]
                Ap, Rp, Bp, Kp = (p_[:, q, ci, :] for q in range(4))
                pr = pers.next()
                tm = tmp.next()
                pb = pbB.next()
                inst[(ch, hp)] = pr
                ctx.append((ch, hp, p_, Ap, Rp, Bp, Kp, pr, tm, pb))
            for (ch, hp, p_, Ap, Rp, Bp, Kp, pr, tm, pb) in ctx:
                ci = ch % (TA // C)
                mm(k, pb[:, 0:2, :], Bp, p_[:, 0:2, ci, :], True, True, rd=[p_], wr=[pb])
                mm(k, pb[:, 2:4, :], Kp, p_[:, 0:2, ci, :], True, True, rd=[p_], wr=[pb])
            for (ch, hp, p_, Ap, Rp, Bp, Kp, pr, tm, pb) in ctx:
                I("dve", lambda e: e.tensor_tensor(out=tm[:, 0, :], in0=pb[:, 0, :], in1=msu[:], op=ALU.mult), rd=[pb, msu], wr=[tm])
                I("dve", lambda e: e.tensor_tensor(out=pr[:, 2, :], in0=pb[:, 1, :], in1=mui[:], op=ALU.mult), rd=[pb, mui], wr=[pr])
                I("dve", lambda e: e.tensor_tensor(out=pr[:, 1, :], in0=pb[:, 2, :], in1=msu[:], op=ALU.mult), rd=[pb, msu], wr=[pr])
                I("dve", lambda e: e.tensor_tensor(out=pr[:, 3, :], in0=pb[:, 3, :], in1=mui[:], op=ALU.mult), rd=[pb, mui], wr=[pr])
                I("pool", lambda e: e.tensor_copy(tm[:, 1, :], k.identf[:]), rd=[k.identf], wr=[tm])
            yield
            for (ch, hp, p_, Ap, Rp, Bp, Kp, pr, tm, pb) in ctx:
                mm(k, pb[:, 0, :], Ap, Bp, True, True, rd=[p_], wr=[pb])
                tr(k, pb[:, 1, :], Bp, k.identf[:], rd=[p_, k.identf], wr=[pb])
                tr(k, pb[:, 2, :], Kp, k.identf[:], rd=[p_, k.identf], wr=[pb])
            for (ch, hp, p_, Ap, Rp, Bp, Kp, pr, tm, pb) in ctx:
                I("dve", lambda e: e.tensor_tensor(out=tm[:, 4, :], in0=pb[:, 0, :], in1=msl[:], op=ALU.mult), rd=[pb, msl], wr=[tm])
                I("act", lambda e: e.activation(out=pr[:, 4:6, :], in_=pb[:, 1:3, :], func=AF.Copy), rd=[pb], wr=[pr])
            yield
            for r in range(1, 7):
                so, sn = (0, 2) if r % 2 == 1 else (2, 0)
                qo, qn = (4, 5) if r % 2 == 1 else (5, 4)
                for (ch, hp, p_, Ap, Rp, Bp, Kp, pr, tm, pb) in ctx:
                    if r < 6:
                        mm(k, pb[:, 0:2, :], tm[:, qo, :], tm[:, so:so + 2, :], True, True, rd=[tm], wr=[pb])
                        mm(k, pb[:, 2, :], tm[:, so, :], tm[:, qo, :], True, True, rd=[tm], wr=[pb])
                    else:
                        mm(k, pb[:, 1, :], tm[:, qo, :], tm[:, so + 1, :], True, True, rd=[tm], wr=[pb])
                for (ch, hp, p_, Ap, Rp, Bp, Kp, pr, tm, pb) in ctx:
                    if r < 6:
                        I("act", lambda e: e.activation(out=tm[:, sn, :], in_=pb[:, 0, :], func=AF.Copy), rd=[pb], wr=[tm])
                        I("act", lambda e: e.activation(out=tm[:, qn, :], in_=pb[:, 2, :], func=AF.Copy), rd=[pb], wr=[tm])
                        I("dve", lambda e: e.tensor_tensor(out=tm[:, sn + 1, :], in0=pb[:, 1, :], in1=tm[:, so + 1, :], op=ALU.add),
                          rd=[pb, tm], wr=[tm])
                    else:
                        I("dve", lambda e: e.tensor_tensor(out=pr[:, 0, :], in0=pb[:, 1, :], in1=tm[:, so + 1, :], op=ALU.add),
                          rd=[pb, tm], wr=[pr])
                yield

        def stageC(ch, hp):
            ti, ci = divmod(ch, TA // C)
            c_, e_, p_ = tiles[(hp, ti)]
            Ap, Rp = p_[:, 0, ci, :], p_[:, 1, ci, :]
            pr = inst.pop((ch, hp))
            t0 = ch * C
            vp = Vp.next()
            for hh in range(2):
                rs = slice(hh * 64, hh * 64 + 64)
                S.dma("sp", vp[rs, rs], k.vtok[t0:t0 + C, hp, rs], rd=[k.vtok], wr=[vp])
            old = cur[hp]
            new = Sx[hp].next()
            w_ = cw[hp].next()
            pa, py = pbA[hp], pbY[hp]
            elc = e_[:, ci * C + C - 1:ci * C + C]
            mm(k, pa[:, 0, :], Ap, old[:], True, False, rd=[p_, old], wr=[pa])
            mm(k, pa[:, 0, :], pr[:, 1, :], vp[:], False, True, rd=[pr, vp], wr=[pa])
            I("act", lambda e: e.activation(out=w_[:, 0, :], in_=pa[:, 0, :], func=AF.Copy), rd=[pa], wr=[w_])
            I("pool", lambda e: e.tensor_scalar(out=w_[:, 2, :], in0=old[:], scalar1=elc, scalar2=None, op0=ALU.mult), rd=[old, e_], wr=[w_])
            yield
            mm(k, pa[:, 1, :], pr[:, 0, :], w_[:, 0, :], True, True, rd=[pr, w_], wr=[pa])
            I("dve", lambda e: e.tensor_copy(w_[:, 1, :], pa[:, 1, :]), rd=[pa], wr=[w_])
            yield
            mm(k, pa[:, 2, :], pr[:, 4, :], w_[:, 1, :], True, False, rd=[pr, w_], wr=[pa])
            mm(k, pa[:, 2, :], pr[:, 5, :], vp[:], False, True, rd=[pr, vp], wr=[pa])
            I("dve", lambda e: e.scalar_tensor_tensor(out=new[:], in0=pa[:, 2, :], scalar=elc, in1=w_[:, 2, :],
                                                      op0=ALU.mult, op1=ALU.add), rd=[pa, e_, w_], wr=[new])
            mm(k, py[:], old[:], Rp, True, False, rd=[old, p_], wr=[py])
            mm(k, py[:], w_[:, 1, :], pr[:, 2, :], False, False, rd=[w_, pr], wr=[py])
            mm(k, py[:], vp[:], pr[:, 3, :], False, True, rd=[vp, pr], wr=[py])
            for hh in range(2):
                rs = slice(hh * 64, hh * 64 + 64)
                I("act", lambda e: e.activation(out=ysb[hp][rs, t0:t0 + C], in_=py[rs, rs], func=AF.Copy), rd=[py], wr=[ysb[hp]])
            cur[hp] = new
            yield

        GC = 2
        ngroups = NCH // GC
        def group(g):
            return [(g * GC + c, hp) for c in range(GC) for hp in range(2)]
        def ensureA(g):
            for (ch, hp) in group(g):
                ti = ch // (TA // C)
                if (hp, ti) not in tiles:
                    stageA(hp, ti)
        ensureA(0)
        run_rr([stageB(group(0))])
        for g in range(ngroups):
            gens = []
            if g + 1 < ngroups:
                ensureA(g + 1)
                gens.append(stageB(group(g + 1)))
            for hp in range(2):
                gens.append(chain_gens([stageC(ch, hp_) for (ch, hp_) in group(g) if hp_ == hp]))
            run_rr(gens)


def _rwkv_post(k, ysb, rvec):
    S, I = k.S, k.S.I
    with phase(k) as st:
        sb, ps = k.sb, k.ps
        pbB = Ring([ps(st, "rcpq%d" % i, [128, 4, 128], F32) for i in range(4)])
        ld = Ring([sb(st, "rcld%d" % i, [128, 2, 512], F32) for i in range(2)])
        wk = Ring([sb(st, "rcwk%d" % i, [128, 3, 512], F32) for i in range(2)])
        ob = Ring([sb(st, "rcob%d" % i, [128, 512], BF16) for i in range(2)])
        pflat = [Buf(b_.t.rearrange("p a b -> p (a b)"), excl=True) for b_ in pbB.bufs]
        pcount = [0]

        def post_gen(it):
            if True:
                hp, tt = it
                pmt, pvt = pflat[(pcount[0] * 2) % 4], pflat[(pcount[0] * 2 + 1) % 4]
                pcount[0] += 1
                tsl = slice(tt * 512, (tt + 1) * 512)
                x_ = ld.next()
                S.dma("sp", x_[:, 0, :], k.bon[hp, :, tsl], rd=[k.bon], wr=[x_])
                S.dma("sp", x_[:, 1, :], k.gg[hp, :, tsl], rd=[k.gg], wr=[x_])
                w_ = wk.next()
                mm(k, pmt[:], k.blk64[:], ysb[hp][:, tsl], True, True, rd=[k.blk64, ysb[hp]], wr=[pmt])
                I("dve", lambda e: e.scalar_tensor_tensor(out=w_[:, 0, :], in0=pmt[:], scalar=-1.0 / 64.0, in1=ysb[hp][:, tsl],
                                                          op0=ALU.mult, op1=ALU.add), rd=[pmt, ysb[hp]], wr=[w_])
                yield
                I("act", lambda e: e.activation(out=w_[:, 1, :], in_=w_[:, 0, :], func=AF.Square), rd=[w_], wr=[w_])
                mm(k, pvt[:], k.blk64[:], w_[:, 1, :], True, True, rd=[k.blk64, w_], wr=[pvt])
                I("act", lambda e: e.activation(out=w_[:, 1, :], in_=pvt[:], func=AF.Ln, bias=k.epsgn[:, 0:1], scale=1.0 / 64.0),
                  rd=[pvt, k.epsgn], wr=[w_])
                yield
                I("act", lambda e: e.activation(out=w_[:, 1, :], in_=w_[:, 1, :], func=AF.Exp, scale=-0.5), rd=[w_], wr=[w_])
                yield
                I("dve", lambda e: e.tensor_tensor(out=w_[:, 0, :], in0=w_[:, 0, :], in1=w_[:, 1, :], op=ALU.mult), rd=[w_], wr=[w_])
                I("dve", lambda e: e.tensor_scalar(out=w_[:, 0, :], in0=w_[:, 0, :], scalar1=rvec[:, hp, 5:6], scalar2=rvec[:, hp, 6:7],
                                                   op0=ALU.mult, op1=ALU.add), rd=[w_, rvec], wr=[w_])
                I("pool", lambda e: e.tensor_tensor(out=w_[:, 0, :], in0=w_[:, 0, :], in1=x_[:, 0, :], op=ALU.add), rd=[w_, x_], wr=[w_])
                o_ = ob.next()
                I("pool", lambda e: e.tensor_tensor(out=o_[:], in0=w_[:, 0, :], in1=x_[:, 1, :], op=ALU.mult), rd=[w_, x_], wr=[o_])
                S.dma("sp", k.ycat[2, :, hp, tsl], o_[:], rd=[o_], wr=[k.ycat])
                yield
        interleave(post_gen, [(hp, tt) for hp in range(2) for tt in range(NT)], 1)


def precast_mlp(k, l, W):
    with phase(k) as st:
        cast_weight(k, st, W["w1"], k.w1s, k.w1s, 1024, 4096, "w1", to_dram=True)
    with phase(k) as st:
        cast_weight(k, st, W["w2m"], k.w2s, k.w2s, 4096, 1024, "w2", to_dram=True, dst_fn=lambda n0, n1: k.w2s[n0 // 128])


def merge_phase(k, l, W, h1T):
    S, I = k.S, k.S.I
    with phase(k) as st:
        sb, ps = k.sb, k.ps
        wbr = sb(st, "mgwbr", [128, 8, 1024], BF16)
        wout = sb(st, "mgwout", [128, 8, 1024], BF16)
        with phase(k) as st2:
            cast_weight(k, st2, W["wbr"], wbr, wbr, 1024, 1024, "br")
        with phase(k) as st2:
            cast_weight(k, st2, W["wout"], wout, wout, 1024, 1024, "wo")
        ln = sb(st, "mgln", [128, 8, 2], F32)
        S.dma("sp", ln[:], W["ln1"], wr=[ln])
        yt = Ring([sb(st, "mgyt%d" % i, [128, 4, 2, 512], BF16) for i in range(2)])
        gt = Ring([sb(st, "mggt%d" % i, [128, 4, 512], BF16) for i in range(3)])
        acc = Ring([sb(st, "mgacc%d" % i, [128, 2, 512], F32) for i in range(2)])
        mg = Ring([sb(st, "mgmg%d" % i, [128, 8, 512], BF16) for i in range(1)])
        hr = Ring([sb(st, "mghr%d" % i, [128, 8, 512], F32) for i in range(1)])
        z = Ring([sb(st, "mgz%d" % i, [128, 8, 512], F32) for i in range(1)])
        sq = sb(st, "mgsq", [128, 8, 512], F32)
        msb = sb(st, "mgmsb", [128, 512], F32)
        up = Ring([ps(st, "mgup%d" % i, [128, 512], F32) for i in range(4)])
        mps = ps(st, "mgmps", [128, 512], F32)
        vps = ps(st, "mgvps", [128, 512], F32)
        gview = k.gsc.t.rearrange("(n m) p t -> m p n t", n=4)
        for tt in range(NT):
            tsl = slice(tt * 512, (tt + 1) * 512)
            y_ = yt.next()
            for n in range(4):
                S.dma("sp", y_[:, n, :, :], k.ycat[n, :, :, tsl], rd=[k.ycat], wr=[y_])
            h_ = hr.next()
            S.dma("sp", h_[:], k.hres[:, :, tsl], rd=[k.hres], wr=[h_])
            m_ = mg.next()
            for mc in range(8):
                msl = slice(mc * 128, (mc + 1) * 128)
                g_ = gt.next()
                S.dma("sp", g_[:], gview[mc][:, :, tsl], rd=[k.gsc], wr=[g_])
                a_ = acc.next()
                for n in range(4):
                    u_ = up.next()
                    for kc in range(2):
                        mm(k, u_[:], wbr[:, n * 2 + kc, msl], y_[:, n, kc, :], kc == 0, kc == 1, rd=[wbr, y_], wr=[u_])
                    if n == 0:
                        I("dve", lambda e: e.tensor_tensor(out=a_[:, 0, :], in0=u_[:], in1=g_[:, 0, :], op=ALU.mult),
                          rd=[u_, g_], wr=[a_])
                    else:
                        I("dve", lambda e: e.tensor_tensor(out=a_[:, 1, :], in0=u_[:], in1=g_[:, n, :], op=ALU.mult),
                          rd=[u_, g_], wr=[a_])
                        if n < 3:
                            I("pool", lambda e: e.tensor_tensor(out=a_[:, 0, :], in0=a_[:, 0, :], in1=a_[:, 1, :], op=ALU.add),
                              rd=[a_], wr=[a_])
                        else:
                            I("pool", lambda e: e.tensor_tensor(out=m_[:, mc, :], in0=a_[:, 0, :], in1=a_[:, 1, :], op=ALU.add),
                              rd=[a_], wr=[m_])
            z_ = z.next()
            for mc in range(8):
                msl = slice(mc * 128, (mc + 1) * 128)
                u_ = up.next()
                for kc in range(8):
                    mm(k, u_[:], wout[:, kc, msl], m_[:, kc, :], kc == 0, kc == 7, rd=[wout, m_], wr=[u_])
                I("dve", lambda e: e.scalar_tensor_tensor(out=z_[:, mc, :], in0=h_[:, mc, :], scalar=ALPHA, in1=u_[:],
                                                          op0=ALU.mult, op1=ALU.add), rd=[h_, u_], wr=[z_])

            def outs(c, dn, g_ap, b_ap):
                I("pool", lambda e: e.tensor_scalar(out=dn, in0=dn, scalar1=g_ap, scalar2=b_ap, op0=ALU.mult, op1=ALU.add),
                  rd=[z_, ln], wr=[z_])
            layer_norm_tile(k, (mps, msb, sq, vps), z_, ln, LN_EPS, outs)
            I("act", lambda e: e.activation(out=h1T[:, :, tsl], in_=z_[:], func=AF.Copy), rd=[z_], wr=[h1T])
            S.dma("sp", k.h1res[:, :, tsl], z_[:], rd=[z_], wr=[k.h1res])


def mlp_phase(k, l, W, h1T, p_l, out, last):
    S, I = k.S, k.S.I
    with phase(k) as st:
        sb, ps = k.sb, k.ps
        pgw = sb(st, "mlpgw", [128, 8, 1024], BF16)
        plw = sb(st, "mlplw", [128, 2, 1024], BF16)
        with phase(k) as st2:
            cast_weight(k, st2, W["pgw"], pgw, pgw, 1024, 1024, "pg")
        with phase(k) as st2:
            cast_weight(k, st2, W["plew"], plw, plw, 256, 1024, "pl")
        with phase(k) as st2:
            pin = Ring([sb(st2, "mlpin%d" % i, [128, 256], F32) for i in range(2)])
            tps = Ring([ps(st2, "mltps%d" % i, [128, 2, 128], F32) for i in range(2)])
            pT = sb(st2, "mlpT", [128, 2, T], BF16)
            for blk in range(NB):
                pi_ = pin.next()
                S.dma("sp", pi_[:], p_l[blk * 128:(blk + 1) * 128, :], wr=[pi_])
                t_ = tps.next()
                for c in range(2):
                    tr(k, t_[:, c, :], pi_[:, c * 128:(c + 1) * 128], k.identf[:], rd=[pi_, k.identf], wr=[t_])
                I("act", lambda e: e.activation(out=pT[:, :, blk * 128:(blk + 1) * 128], in_=t_[:], func=AF.Copy), rd=[t_], wr=[pT])
            S.dma("sp", k.pTs[:], pT[:], rd=[pT], wr=[k.pTs])
        ln = sb(st, "mlln", [128, 8, 2], F32)
        S.dma("sp", ln[:], W["ln2"], wr=[ln])
        w1r = Ring([sb(st, "mlw1%d" % i, [128, 8, 512], BF16) for i in range(2)])
        w2r = Ring([sb(st, "mlw2%d" % i, [128, 32, 128], BF16) for i in range(2)])
        pTr = Ring([sb(st, "mlpT%d" % i, [128, 2, 512], BF16) for i in range(1)])
        a = sb(st, "mla", [128, 32, 512], BF16)
        rl = Ring([sb(st, "mlrl%d" % i, [128, 512], F32) for i in range(2)])
        hr = sb(st, "mlhr", [128, 8, 512], F32)
        z = sb(st, "mlz", [128, 8, 512], F32)
        msb = sb(st, "mlmsb", [128, 512], F32)
        sg = Ring([sb(st, "mlsg%d" % i, [128, 2, 512], F32) for i in range(1)])
        up = Ring([ps(st, "mlup%d" % i, [128, 512], F32) for i in range(3)])
        gp = Ring([ps(st, "mlgp%d" % i, [128, 512], F32) for i in range(2)])
        mps = ps(st, "mlmps", [128, 512], F32)
        vps = ps(st, "mlvps", [128, 512], F32)
        tpo = ps(st, "mltpo", [128, 4, 128], F32)
        ot = Ring([sb(st, "mlot%d" % i, [128, D], F32) for i in range(1)]) if last else None
        hb = Ring([sb(st, "mlhb%d" % i, [128, 8, 512], BF16) for i in range(1)]) if not last else None
        for tt in range(NT):
            tsl = slice(tt * 512, (tt + 1) * 512)
            S.dma("sp", hr[:], k.h1res[:, :, tsl], rd=[k.h1res], wr=[hr])
            pT = pTr.next()
            S.dma("sp", pT[:], k.pTs[:, :, tsl], rd=[k.pTs], wr=[pT])
            for fg in range(8):
                w1_ = w1r.next()
                S.dma("sp", w1_[:], k.w1s[:, :, fg * 512:(fg + 1) * 512], rd=[k.w1s], wr=[w1_])
                for f8 in range(4):
                    fc = fg * 4 + f8
                    u_ = up.next()
                    for kc in range(8):
                        mm(k, u_[:], w1_[:, kc, f8 * 128:(f8 + 1) * 128], h1T[:, kc, tsl], kc == 0, kc == 7, rd=[w1_, h1T], wr=[u_])
                    r_ = rl.next()
                    I("act", lambda e: e.activation(out=r_[:], in_=u_[:], func=AF.Relu), rd=[u_], wr=[r_])
                    I("pool", lambda e: e.tensor_tensor(out=a[:, fc, :], in0=r_[:], in1=r_[:], op=ALU.mult), rd=[r_], wr=[a])
            for mc in range(8):
                msl = slice(mc * 128, (mc + 1) * 128)
                s_ = sg.next()
                g_ = gp.next()
                for kc in range(8):
                    mm(k, g_[:], pgw[:, kc, msl], h1T[:, kc, tsl], kc == 0, kc == 7, rd=[pgw, h1T], wr=[g_])
                I("act", lambda e: e.activation(out=s_[:, 0, :], in_=g_[:], func=AF.Sigmoid), rd=[g_], wr=[s_])
                g2_ = gp.next()
                for kc in range(2):
                    mm(k, g2_[:], plw[:, kc, msl], pT[:, kc, :], kc == 0, kc == 1, rd=[plw, pT], wr=[g2_])
                I("dve", lambda e: e.tensor_tensor(out=s_[:, 1, :], in0=g2_[:], in1=s_[:, 0, :], op=ALU.mult), rd=[g2_, s_], wr=[s_])
                u_ = up.next()
                w2_ = w2r.next()
                S.dma("sp", w2_[:], k.w2s[mc], rd=[k.w2s], wr=[w2_])
                for fc in range(32):
                    mm(k, u_[:], w2_[:, fc, :], a[:, fc, :], fc == 0, fc == 31, rd=[w2_, a], wr=[u_])
                I("dve", lambda e: e.scalar_tensor_tensor(out=z[:, mc, :], in0=hr[:, mc, :], scalar=ALPHA, in1=u_[:],
                                                          op0=ALU.mult, op1=ALU.add), rd=[hr, u_], wr=[z])
                I("pool", lambda e: e.tensor_tensor(out=z[:, mc, :], in0=z[:, mc, :], in1=s_[:, 1, :], op=ALU.add), rd=[z, s_], wr=[z])

            def outs(c, dn, g_ap, b_ap):
                I("pool", lambda e: e.tensor_scalar(out=dn, in0=dn, scalar1=g_ap, scalar2=b_ap, op0=ALU.mult, op1=ALU.add),
                  rd=[z, ln], wr=[z])
            layer_norm_tile(k, (mps, msb, hr, vps), z, ln, LN_EPS, outs)
            if not last:
                hb_ = hb.next()
                I("act", lambda e: e.activation(out=hb_[:], in_=z[:], func=AF.Copy), rd=[z], wr=[hb_])
                S.dma("sp", k.hTs[:, :, tsl], hb_[:], rd=[hb_], wr=[k.hTs])
                S.dma("sp", k.hres[:, :, tsl], z[:], rd=[z], wr=[k.hres])
            else:
                for b in range(4):
                    o_ = ot.next()
                    for half in range(2):
                        for j in range(4):
                            c = half * 4 + j
                            tr(k, tpo[:, j, :], z[:, c, b * 128:(b + 1) * 128], k.identf[:], rd=[z, k.identf], wr=[tpo])
                        I("act", lambda e: e.activation(out=o_[:, half * 512:(half + 1) * 512],
                                                        in_=tpo[:].rearrange("p a b -> p (a b)"), func=AF.Copy), rd=[tpo], wr=[o_])
                    r0 = tt * 512 + b * 128
                    S.dma("sp", out[r0:r0 + 128, :], o_[:], rd=[o_], wr=[Dep()])


_CACHE = {}


def kernel(**inputs):
    if "nc" not in _CACHE:
        _CACHE["nc"] = build_program(depth=DEPTH)[0]
    nc = _CACHE["nc"]
    shared = {}
    for l in range(DEPTH):
        for n, v in prep_layer(inputs, l).items():
            shared["%s_%d" % (n, l)] = v
    x = np.asarray(inputs["x"], np.float32)
    p = np.asarray(inputs["p"], np.float32)
    in_maps = []
    for b in range(8):
        m = dict(shared)
        m["x"] = np.ascontiguousarray(x[b])
        m["p"] = np.ascontiguousarray(p[:, b])
        in_maps.append(m)
    res = run_bass_kernel_spmd(nc, in_maps, core_ids=list(range(8)))
    return np.stack([np.asarray(r["out"], np.float32) for r in res.results], axis=0)
```

```python
import math
import contextlib
import numpy as np
import concourse.bass as bass
import concourse.mybir as mybir
from concourse.bass_utils import run_bass_kernel_spmd

F32 = mybir.dt.float32
BF16 = mybir.dt.bfloat16
AF = mybir.ActivationFunctionType
ALU = mybir.AluOpType
AX = mybir.AxisListType

T = 4096
D = 1024
NT = 8
NB = 32
DEPTH = 2
ALPHA = (2 * DEPTH) ** 0.25
LN_EPS = 1e-5
GN_EPS = 64e-5
PI = math.pi


class Dep:
    __slots__ = ("w", "r", "excl")

    def __init__(self):
        self.w = {}
        self.r = {}
        self.excl = False


class Buf:
    def __init__(self, t, excl=False):
        self.t = t
        self.dep = Dep()
        self.dep.excl = excl

    def __getitem__(self, k):
        return self.t[k]


class Ring:
    def __init__(self, bufs):
        self.bufs = bufs
        self.i = 0

    def next(self):
        b = self.bufs[self.i % len(self.bufs)]
        self.i += 1
        return b


class Sched:
    EPOCH = 30000
    NDMA = 6
    NEPOCH = 8
    import os as _os
    qmap = {} if _os.environ.get("KNOB_POOLQ") else {"pool": "sp"}

    def __init__(self, nc):
        self.nc = nc
        self.eng = {"pe": nc.tensor, "act": nc.scalar, "dve": nc.vector,
                    "pool": nc.gpsimd, "sp": nc.sync}
        self.sem = {}
        self.cnt = {}
        self.seen = {e: {} for e in self.eng}
        self.nsem = 0
        self.ninst = 0
        self.allsems = []
        self.epool = {e: [self._alloc("e_%s_%d" % (e, i)) for i in range(self.NEPOCH)] for e in ("pe", "act", "dve", "pool")}
        self.dq = {}
        for q in ("sp", "pool"):
            self.dq[q] = {"sems": [self._alloc("dq_%s_%d" % (q, i)) for i in range(self.NDMA)], "n": 0}
        for e in ("pe", "act", "dve", "pool"):
            self._new_epoch(e)

    def _alloc(self, name):
        self.nsem += 1
        h = self.nc.alloc_semaphore(name)
        self.allsems.append(h)
        return h

    def _new_epoch(self, e):
        self.sem[e] = self.epool[e].pop(0)
        self.cnt[e] = 0

    def _wait(self, eng, sem, val):
        key = id(sem)
        if self.seen[eng].get(key, 0) >= val:
            return
        self.eng[eng].wait_ge(sem, val)
        self.seen[eng][key] = val

    def _gather(self, eng, rd, wr, same_ok=True):
        need = {}

        def add(p):
            pe, sem, val = p
            if pe == eng and (same_ok or eng == "pe"):
                return
            k = id(sem)
            if k not in need or need[k][1] < val:
                need[k] = (sem, val)
        for d in rd:
            for p in d.w.values():
                add(p)
        for d in wr:
            for p in d.w.values():
                add(p)
            for p in d.r.values():
                add(p)
        for sem, val in need.values():
            self._wait(eng, sem, val)

    @staticmethod
    def _deps(lst):
        return [x.dep if isinstance(x, Buf) else x for x in lst]

    @staticmethod
    def _merge(dct, tok):
        k = id(tok[1])
        if k not in dct or dct[k][2] < tok[2]:
            dct[k] = tok

    def I(self, eng, fn, rd=(), wr=()):
        rd = self._deps(rd)
        wr = self._deps(wr)
        ex = [d for d in rd if d.excl and d not in wr]
        if ex:
            rd = [d for d in rd if not d.excl]
            wr = list(wr) + ex
        self._gather(eng, rd, wr, same_ok=False)
        if self.cnt[eng] >= self.EPOCH:
            self._new_epoch(eng)
        ins = fn(self.eng[eng])
        self.ninst += 1
        self.cnt[eng] += 1
        ins.then_inc(self.sem[eng], 1)
        tok = (eng, self.sem[eng], self.cnt[eng])
        for d in rd:
            self._merge(d.r, tok)
        for d in wr:
            d.w = {id(tok[1]): tok}
            d.r = {}
        return ins

    def dma(self, q, out, in_, rd=(), wr=()):
        rd = self._deps(rd)
        wr = self._deps(wr)
        q = self.qmap.get(q, q)
        dq = self.dq[q]
        j = dq["n"]
        dq["n"] += 1
        sem = dq["sems"][j % self.NDMA]
        val = 16 * (j // self.NDMA + 1)
        if val > 16:
            self._wait(q, sem, val - 16)
        self._gather(q, rd, wr, same_ok=False)
        ins = self.eng[q].dma_start(out=out, in_=in_)
        self.ninst += 1
        ins.then_inc(sem, 16)
        tok = ("dma", sem, val)
        for d in rd:
            self._merge(d.r, tok)
        for d in wr:
            self._merge(d.w, tok)
            d.r = {}
        return ins

    def barrier(self, force=False):
        for w in ("pe", "act", "dve", "pool", "sp"):
            for q, dq in self.dq.items():
                n = dq["n"]
                for i in range(self.NDMA):
                    cnt_i = len(range(i, n, self.NDMA))
                    if cnt_i > 0:
                        self._wait(w, dq["sems"][i], 16 * cnt_i)
            for e in ("pe", "act", "dve", "pool"):
                if e != w and self.cnt[e] > 0:
                    self._wait(w, self.sem[e], self.cnt[e])

    def finish(self):
        for q, dq in self.dq.items():
            n = dq["n"]
            for i in range(self.NDMA):
                cnt_i = len(range(i, n, self.NDMA))
                if cnt_i > 0:
                    self._wait("sp", dq["sems"][i], 16 * cnt_i)
        for e in ("pe", "act", "dve", "pool"):
            if self.cnt[e] > 0:
                self._wait("sp", self.sem[e], self.cnt[e])


def _chunkcol(v):
    v = np.asarray(v, np.float32)
    return np.ascontiguousarray(v.reshape(-1, 128).T)


def prep_layer(inp, l):
    f = np.float32
    w_in = np.asarray(inp["w_in"][l], f)
    o = {}
    o["winF"] = np.ascontiguousarray(np.concatenate(
        [w_in[:, 0:256], w_in[:, 256:512], w_in[:, 512:768], w_in[:, 1028:2052],
         w_in[:, 2052:2308], w_in[:, 2308:2564], w_in[:, 2820:6916]], axis=1))
    o["winT"] = np.ascontiguousarray(np.concatenate([w_in[:, 768:1024], w_in[:, 2564:2820]], axis=1))
    o["wff"] = np.ascontiguousarray(w_in[:, 1024:1028])
    lre = np.asarray(inp["s5_lambda_re"][l], f)
    lim = np.asarray(inp["s5_lambda_im"][l], f)
    ldt = np.repeat(np.asarray(inp["s5_log_dt"][l], f)[:, None], 64, axis=1)
    def gp(a):
        return a.reshape(8, 2, 64).transpose(1, 2, 0).reshape(128, 8)
    o["s5par"] = np.ascontiguousarray(np.stack([gp(lre), gp(lim), gp(ldt)], axis=2))
    b_re = np.asarray(inp["s5_b_re"][l], f)
    b_im = np.asarray(inp["s5_b_im"][l], f)
    c_re = np.asarray(inp["s5_c_re"][l], f)
    c_im = np.asarray(inp["s5_c_im"][l], f)
    Bw = np.zeros((8, 128, 2, 128), f)
    Cw = np.zeros((8, 128, 2, 128), f)
    for g in range(16):
        rt = g // 2
        k0 = (g % 8) * 16
        m0 = (g % 2) * 64
        Bw[rt, k0:k0 + 16, 0, m0:m0 + 64] = b_re[g].T
        Bw[rt, k0:k0 + 16, 1, m0:m0 + 64] = b_im[g].T
        Cw[rt, m0:m0 + 64, 0, k0:k0 + 16] = c_re[g].T
        Cw[rt, m0:m0 + 64, 1, k0:k0 + 16] = c_im[g].T
    o["s5B"] = np.ascontiguousarray(Bw.transpose(1, 0, 2, 3))
    o["s5C"] = np.ascontiguousarray(Cw.transpose(1, 0, 2, 3))
    o["s5D"] = _chunkcol(np.asarray(inp["s5_d"][l], f).reshape(-1))
    o["gluw"] = np.asarray(inp["s5_glu_w"][l], f)
    o["glub"] = _chunkcol(inp["s5_glu_b"][l])
    o["foxb"] = np.asarray(inp["fox_f_bias"][l], f).reshape(4, 1)
    o["mu"] = _chunkcol(inp["rwkv_mu"][l])
    vec = [inp["rwkv_w0"][l], inp["rwkv_a0"][l], inp["rwkv_k_k"][l], inp["rwkv_k_a"][l],
           np.asarray(inp["rwkv_r_k"][l]).reshape(-1), inp["rwkv_lnx_g"][l], inp["rwkv_lnx_b"][l]]
    o["rvec"] = np.ascontiguousarray(np.stack([_chunkcol(v) for v in vec], axis=2))
    o["wa2"] = np.ascontiguousarray(np.concatenate([np.asarray(inp["rwkv_w2"][l], f),
                                                     np.asarray(inp["rwkv_a2"][l], f)], axis=0))
    o["g2"] = np.asarray(inp["rwkv_g2"][l], f)
    o["wbr"] = np.ascontiguousarray(np.asarray(inp["w_branch"][l], f).reshape(1024, 1024))
    o["wout"] = np.asarray(inp["w_out"][l], f)
    o["ln1"] = np.ascontiguousarray(np.stack([_chunkcol(inp["ln1_g"][l]), _chunkcol(inp["ln1_b"][l])], axis=2))
    o["ln2"] = np.ascontiguousarray(np.stack([_chunkcol(inp["ln2_g"][l]), _chunkcol(inp["ln2_b"][l])], axis=2))
    o["w1"] = np.asarray(inp["mlp_w1"][l], f)
    o["w2m"] = np.asarray(inp["mlp_w2"][l], f)
    o["plew"] = np.asarray(inp["ple_w"][l], f)
    o["pgw"] = np.asarray(inp["ple_gate_w"][l], f)
    return {k: np.ascontiguousarray(v, dtype=f) for k, v in o.items()}


LAYER_SHAPES = {
    "winF": [1024, 6400], "winT": [1024, 512], "wff": [1024, 4], "s5par": [128, 8, 3],
    "s5B": [128, 8, 2, 128], "s5C": [128, 8, 2, 128], "s5D": [128, 2], "gluw": [256, 256], "glub": [128, 2],
    "foxb": [4, 1], "mu": [128, 8], "rvec": [128, 2, 7], "wa2": [128, 256], "g2": [128, 256],
    "wbr": [1024, 1024], "wout": [1024, 1024], "ln1": [128, 8, 2], "ln2": [128, 8, 2],
    "w1": [1024, 4096], "w2m": [4096, 1024], "plew": [256, 1024], "pgw": [1024, 1024],
}


class K:
    pass


@contextlib.contextmanager
def phase(k):
    with contextlib.ExitStack() as st:
        yield st
        k.S.barrier()


def build_program(depth=DEPTH, dbg=(), stop_after=None, only=None, nsteps=T):
    nc = bass.Bass("TRN2", target_bir_lowering=False)
    S = Sched(nc)
    k = K()
    k.nc, k.S = nc, S
    k.dbg = {}
    x_in = nc.dram_tensor("x", [T, D], F32, kind="ExternalInput").ap()
    p_in = nc.dram_tensor("p", [DEPTH, T, 256], F32, kind="ExternalInput").ap()
    out = nc.dram_tensor("out", [T, D], F32, kind="ExternalOutput").ap()
    W = []
    for l in range(DEPTH):
        W.append({n: nc.dram_tensor("%s_%d" % (n, l), s, F32, kind="ExternalInput").ap()
                  for n, s in LAYER_SHAPES.items()})
    k.stop_after = stop_after
    k.only = only
    k.nsteps = nsteps

    def scratch(name, shape, dt):
        kind = "ExternalOutput" if name in dbg else "Internal"
        import os
        if os.environ.get("KNOB_TINY") and name != "hres":
            shape = [2, 2]
        return Buf(nc.dram_tensor(name, shape, dt, kind=kind).ap())
    k.hres = scratch("hres", [128, 8, T], F32)
    k.h1res = scratch("h1res", [128, 8, T], F32)
    k.gsc = scratch("gsc", [32, 128, T], BF16)
    k.ycat = scratch("ycat", [4, 128, 2, T], BF16)
    k.rwxs = scratch("rwxs", [8, 128, T], F32)
    k.sc5 = scratch("sc5", [2, 128, 5, T], F32)
    k.bon = scratch("bon", [2, 128, T], F32)
    k.gg = scratch("gg", [2, 128, T], F32)
    k.vtok = scratch("vtok", [T, 2, 128], F32)
    k.uTf = scratch("uTf", [128, 2, T], F32)
    k.qk = scratch("qk", [4, 128, 2, T], BF16)
    k.vv = scratch("vv", [2, 128, NB, 256], BF16)
    k.negc = scratch("negc", [4, T], F32)
    k.hTs = scratch("hTs", [128, 8, T], BF16)
    k.pTs = scratch("pTs", [128, 2, T], BF16)
    k.w1s = scratch("w1s", [128, 8, 4096], BF16)
    k.w2s = scratch("w2s", [8, 128, 32, 128], BF16)

    es = contextlib.ExitStack()

    uid = [0]

    def sb(stack, name, shape, dt):
        uid[0] += 1
        return Buf(stack.enter_context(nc.sbuf_tensor("%s_u%d" % (name, uid[0]), shape, dt)))

    def ps(stack, name, shape, dt=F32):
        uid[0] += 1
        full = [128, 512] if dt == F32 else [128, 1024]
        t = stack.enter_context(nc.psum_tensor("%s_u%d" % (name, uid[0]), full, dt))
        n = 1
        for d_ in shape[1:]:
            n *= d_
        if len(shape) == 2:
            v = t[0:shape[0], 0:n]
        else:
            assert len(shape) == 3
            v = t[0:shape[0], 0:n].rearrange("p (a b) -> p a b", a=shape[1])
        return Buf(v, excl=True)
    k.sb, k.ps = sb, ps

    with es:
        k.identf = sb(es, "identf", [128, 128], F32)
        k.identb = sb(es, "identb", [128, 128], BF16)
        k.onesm = sb(es, "onesm", [128, 128], F32)
        k.blk64 = sb(es, "blk64", [128, 128], F32)
        k.nmi = sb(es, "nmi", [128, 128], F32)
        k.nms = sb(es, "nms", [128, 128], F32)
        k.m01 = sb(es, "m01", [128, 128], F32)
        k.ones = sb(es, "ones", [128, 512], F32)
        k.sel4 = sb(es, "sel4", [4, 4, 128], F32)
        k.epsln = sb(es, "epsln", [128, 1], F32)
        k.epsgn = sb(es, "epsgn", [128, 1], F32)
        k.one1 = sb(es, "one1", [128, 1], F32)
        k.mhalf = sb(es, "mhalf", [128, 1], F32)
        I = S.I
        I("pool", lambda e: e.memset(k.epsln[:], LN_EPS), wr=[k.epsln])
        I("pool", lambda e: e.memset(k.epsgn[:], GN_EPS), wr=[k.epsgn])
        I("pool", lambda e: e.memset(k.one1[:], 1.0), wr=[k.one1])
        I("pool", lambda e: e.memset(k.mhalf[:], -0.5), wr=[k.mhalf])
        I("pool", lambda e: e.memset(k.identf[:], 0.0), wr=[k.identf])
        I("pool", lambda e: e.affine_select(out=k.identf[:], in_=k.identf[:], compare_op=ALU.not_equal, fill=1.0,
                                            base=0, pattern=[[-1, 128]], channel_multiplier=1),
          rd=[k.identf], wr=[k.identf])
        I("pool", lambda e: e.tensor_copy(k.identb[:], k.identf[:]), rd=[k.identf], wr=[k.identb])
        I("pool", lambda e: e.memset(k.onesm[:], 1.0 / 1024.0), wr=[k.onesm])
        I("pool", lambda e: e.memset(k.ones[:], 1.0), wr=[k.ones])
        I("pool", lambda e: e.memset(k.blk64[:], 0.0), wr=[k.blk64])
        I("pool", lambda e: e.memset(k.blk64[0:64, 0:64], 1.0), wr=[k.blk64])
        I("pool", lambda e: e.memset(k.blk64[64:128, 64:128], 1.0), wr=[k.blk64])
        I("pool", lambda e: e.memset(k.nmi[:], 0.0), wr=[k.nmi])
        I("pool", lambda e: e.affine_select(out=k.nmi[:], in_=k.nmi[:], compare_op=ALU.is_ge, fill=-1e30,
                                            base=0, pattern=[[-1, 128]], channel_multiplier=1), rd=[k.nmi], wr=[k.nmi])
        I("pool", lambda e: e.memset(k.nms[:], 0.0), wr=[k.nms])
        I("pool", lambda e: e.affine_select(out=k.nms[:], in_=k.nms[:], compare_op=ALU.is_gt, fill=-1e30,
                                            base=0, pattern=[[-1, 128]], channel_multiplier=1), rd=[k.nms], wr=[k.nms])
        I("pool", lambda e: e.affine_select(out=k.m01[:], in_=k.ones[:, 0:128], compare_op=ALU.is_gt, fill=0.0,
                                            base=0, pattern=[[-1, 128]], channel_multiplier=1), rd=[k.ones], wr=[k.m01])
        I("pool", lambda e: e.affine_select(out=k.sel4[:], in_=k.ones[0:4, :].rearrange("p (h m) -> p h m", h=4),
                                            compare_op=ALU.is_equal, fill=0.0, base=0, pattern=[[-1, 4], [0, 128]],
                                            channel_multiplier=1), rd=[k.ones], wr=[k.sel4])

        if stop_after != "consts":
            for l in range(depth):
                layer(k, l, W[l], x_in, p_in[l], out, last=(l == depth - 1))
        S.finish()
    k.ninst = S.ninst
    return nc, k


def run_rr(gens):
    gens = list(gens)
    while gens:
        for g in list(gens):
            try:
                next(g)
            except StopIteration:
                gens.remove(g)


def interleave(make_gen, items, width):
    items = list(items)
    active = []
    nxt = 0
    while nxt < len(items) or active:
        while len(active) < width and nxt < len(items):
            active.append(make_gen(items[nxt]))
            nxt += 1
        for g in list(active):
            try:
                next(g)
            except StopIteration:
                active.remove(g)


def chain_gens(gens):
    for g in gens:
        for _ in g:
            yield


def dbg_dump(k, name, src_buf, src_ap):
    if name in k.dbg:
        k.S.dma("sp", k.dbg[name], src_ap, rd=[src_buf], wr=[Dep()])


def mm(k, out_ap, lhsT, rhs, start, stop, rd, wr):
    return k.S.I("pe", lambda e: e.matmul(out_ap, lhsT, rhs, start=start, stop=stop), rd=rd, wr=wr)


def tr(k, out_ap, in_ap, ident_ap, rd, wr):
    return k.S.I("pe", lambda e: e.transpose(out_ap, in_ap, ident_ap), rd=rd, wr=wr)


def phase0(k, x_in, hT):
    import os
    lvl = int(os.environ.get("KNOB_P0", "9"))
    nblk = int(os.environ.get("KNOB_P0N", str(NB)))
    S, I = k.S, k.S.I
    with phase(k) as st:
        xin = Ring([k.sb(st, "p0x%d" % i, [128, D], F32) for i in range(2)])
        stg = Ring([k.sb(st, "p0s%d" % i, [128, 8, 128], F32) for i in range(2)])
        pss = Ring([k.ps(st, "p0p%d" % i, [128, 4, 128], F32) for i in range(4)])
        for blk in range(nblk):
            xi = xin.next()
            S.dma("sp", xi[:], x_in[blk * 128:(blk + 1) * 128, :], wr=[xi])
            sg = stg.next()
            if lvl < 1:
                continue
            for half in range(2):
                pt = pss.next()
                for j in range(4):
                    c = half * 4 + j
                    tr(k, pt[:, j, :], xi[:, c * 128:(c + 1) * 128], k.identf[:], rd=[xi, k.identf], wr=[pt])
                if lvl >= 2:
                    I("act", lambda e: e.activation(out=hT[:, half * 4:half * 4 + 4, blk * 128:(blk + 1) * 128],
                                                    in_=pt[:], func=AF.Copy), rd=[pt], wr=[hT])
                if lvl >= 3:
                    I("dve", lambda e: e.tensor_copy(sg[:, half * 4:half * 4 + 4, :], pt[:]), rd=[pt], wr=[sg])
            if lvl >= 4:
                S.dma("pool", k.hres[:, :, blk * 128:(blk + 1) * 128], sg[:], rd=[sg], wr=[k.hres])


def cast_weight(k, st, src, dst_buf, dst_ap, K_, N_, tag, to_dram=False, dst_fn=None):
    S, I = k.S, k.S.I
    kc = K_ // 128
    ncol = max(1, min(N_, 4096 // kc))
    f32r = Ring([k.sb(st, "cw%s_f%d" % (tag, i), [128, kc, ncol], F32) for i in range(2)])
    if to_dram:
        bfr = Ring([k.sb(st, "cw%s_b%d" % (tag, i), [128, kc, ncol], BF16) for i in range(2)])
    srcv = src.rearrange("(c p) n -> p c n", p=128)
    engs = ["pool", "dve"]
    i = 0
    for n0 in range(0, N_, ncol):
        n1 = min(N_, n0 + ncol)
        w = n1 - n0
        fb = f32r.next()
        S.dma("sp", fb[:, :, 0:w], srcv[:, :, n0:n1], wr=[fb])
        if to_dram:
            bb = bfr.next()
            I(engs[i % 2], lambda e: e.tensor_copy(bb[:, :, 0:w], fb[:, :, 0:w]), rd=[fb], wr=[bb])
            dst = dst_fn(n0, n1) if dst_fn is not None else dst_ap[:, :, n0:n1]
            S.dma("pool", dst, bb[:, :, 0:w], rd=[bb], wr=[dst_buf])
        else:
            I(engs[i % 2], lambda e: e.tensor_copy(dst_ap[:, :, n0:n1], fb[:, :, 0:w]), rd=[fb], wr=[dst_buf])
        i += 1


def layer_norm_tile(k, st_bufs, z, gb, eps, outs):
    S, I = k.S, k.S.I
    mps, msb, sq, vps = st_bufs
    for c in range(8):
        mm(k, mps[:], k.onesm[:], z[:, c, :], c == 0, c == 7, rd=[k.onesm, z], wr=[mps])
    I("act", lambda e: e.activation(out=msb[:], in_=mps[:], func=AF.Copy), rd=[mps], wr=[msb])
    I("dve", lambda e: e.tensor_tensor(out=z[:], in0=z[:], in1=msb[:, None, :].broadcast_to([128, 8, 512]),
                                       op=ALU.subtract), rd=[z, msb], wr=[z])
    I("act", lambda e: e.activation(out=sq[:], in_=z[:], func=AF.Square), rd=[z], wr=[sq])
    for c in range(8):
        mm(k, vps[:], k.onesm[:], sq[:, c, :], c == 0, c == 7, rd=[k.onesm, sq], wr=[vps])
    I("act", lambda e: e.activation(out=msb[:], in_=vps[:], func=AF.Ln, bias=k.epsln[:, 0:1], scale=1.0),
      rd=[vps, k.epsln], wr=[msb])
    I("act", lambda e: e.activation(out=msb[:], in_=msb[:], func=AF.Exp, scale=-0.5), rd=[msb], wr=[msb])
    I("dve", lambda e: e.tensor_tensor(out=z[:], in0=z[:], in1=msb[:, None, :].broadcast_to([128, 8, 512]),
                                       op=ALU.mult), rd=[z, msb], wr=[z])
    for c in range(8):
        outs(c, z[:, c, :], gb[:, c, 0:1], gb[:, c, 1:2])


def inproj(k, l, W, hT):
    S, I = k.S, k.S.I
    with phase(k) as st:
        wf = Ring([k.sb(st, "ipwf%d" % i, [128, 8, 128], F32) for i in range(2)])
        wb = Ring([k.sb(st, "ipwb%d" % i, [128, 8, 128], BF16) for i in range(2)])
        pss = Ring([k.ps(st, "ipps%d" % i, [128, 512], F32) for i in range(4)])
        stb = Ring([k.sb(st, "ipsb%d" % i, [128, 512], BF16) for i in range(3)])
        stf = Ring([k.sb(st, "ipsf%d" % i, [128, 512], F32) for i in range(2)])
        raw = Ring([k.sb(st, "ipraw%d" % i, [128, T + 1], F32) for i in range(2)])
        xs = Ring([k.sb(st, "ipxs%d" % i, [128, T], F32) for i in range(1)])
        mu = k.sb(st, "ipmu", [128, 8], F32)
        S.dma("sp", mu[:], W["mu"], wr=[mu])
        for r_ in raw.bufs:
            I("pool", lambda e: e.memset(r_[:, 0:1], 0.0), wr=[r_])
        winF = W["winF"].rearrange("(c p) n -> p c n", p=128)
        ev = 0
        for j in range(50):
            if k.stop_after == "ip_fm%d" % j:
                return
            f_ = wf.next()
            S.dma("sp", f_[:], winF[:, :, j * 128:(j + 1) * 128], wr=[f_])
            b_ = wb.next()
            I("pool", lambda e: e.tensor_copy(b_[:], f_[:]), rd=[f_], wr=[b_])
            if 6 <= j < 14:
                rw = raw.next()
            for tt in range(NT):
                pt = pss.next()
                for c in range(8):
                    mm(k, pt[:], b_[:, c, :], hT[:, c, tt * 512:(tt + 1) * 512], c == 0, c == 7,
                       rd=[b_, hT], wr=[pt])
                eng = "act" if ev % 2 == 0 else "dve"
                ev += 1
                tsl = slice(tt * 512, (tt + 1) * 512)
                if j < 2:
                    sf = stf.next()
                    if eng == "act":
                        I("act", lambda e: e.activation(out=sf[:], in_=pt[:], func=AF.Copy), rd=[pt], wr=[sf])
                    else:
                        I("dve", lambda e: e.tensor_copy(sf[:], pt[:]), rd=[pt], wr=[sf])
                    S.dma("pool", k.uTf[:, j, tsl], sf[:], rd=[sf], wr=[k.uTf])
                elif j < 6 or 14 <= j < 18:
                    which = (j - 2) // 2 if j < 6 else 2 + (j - 14) // 2
                    cc = j % 2
                    scale = 0.125 if which in (0, 2) else 1.0
                    sb_ = stb.next()
                    I("act", lambda e: e.activation(out=sb_[:], in_=pt[:], func=AF.Copy, scale=scale), rd=[pt], wr=[sb_])
                    S.dma("pool", k.qk[which, :, cc, tsl], sb_[:], rd=[sb_], wr=[k.qk])
                elif j < 14:
                    if eng == "act":
                        I("act", lambda e: e.activation(out=rw[:, 1 + tt * 512:1 + (tt + 1) * 512], in_=pt[:], func=AF.Copy),
                          rd=[pt], wr=[rw])
                    else:
                        I("dve", lambda e: e.tensor_copy(rw[:, 1 + tt * 512:1 + (tt + 1) * 512], pt[:]), rd=[pt], wr=[rw])
                else:
                    gc = j - 18
                    sb_ = stb.next()
                    I("act", lambda e: e.activation(out=sb_[:], in_=pt[:], func=AF.Sigmoid), rd=[pt], wr=[sb_])
                    S.dma("pool", k.gsc[gc, :, tsl], sb_[:], rd=[sb_], wr=[k.gsc])
            if 6 <= j < 14:
                jj = j - 6
                x_ = xs.next()
                I("pool", lambda e: e.tensor_tensor(out=x_[:], in0=rw[:, 0:T], in1=rw[:, 1:T + 1], op=ALU.subtract),
                  rd=[rw], wr=[x_])
                I("dve", lambda e: e.scalar_tensor_tensor(out=x_[:], in0=x_[:], scalar=mu[:, jj:jj + 1], in1=rw[:, 1:T + 1],
                                                          op0=ALU.mult, op1=ALU.add), rd=[x_, rw, mu], wr=[x_])
                S.dma("sp", k.rwxs[jj], x_[:], rd=[x_], wr=[k.rwxs])
        if k.stop_after == "ip_fm":
            return
        wtf = k.sb(st, "ipwtf", [128, 8, 512], F32)
        wtb = k.sb(st, "ipwtb", [128, 8, 512], BF16)
        S.dma("sp", wtf[:], W["winT"].rearrange("(c p) n -> p c n", p=128), wr=[wtf])
        I("pool", lambda e: e.tensor_copy(wtb[:], wtf[:]), rd=[wtf], wr=[wtb])
        for blk in range(NB):
            pt = pss.next()
            for c in range(8):
                mm(k, pt[:], hT[:, c, blk * 128:(blk + 1) * 128], wtb[:, c, :], c == 0, c == 7, rd=[wtb, hT], wr=[pt])
            sb_ = stb.next()
            if blk % 2 == 0:
                I("act", lambda e: e.activation(out=sb_[:], in_=pt[:], func=AF.Copy), rd=[pt], wr=[sb_])
            else:
                I("dve", lambda e: e.tensor_copy(sb_[:], pt[:]), rd=[pt], wr=[sb_])
            S.dma("pool", k.vv[:, :, blk, :].rearrange("w p n -> p w n"), sb_[:].rearrange("p (w n) -> p w n", w=2),
                  rd=[sb_], wr=[k.vv])
        if k.stop_after == "ip_tm":
            return
        wff_f = k.sb(st, "ipwff", [128, 8, 4], F32)
        wff_b = k.sb(st, "ipwffb", [128, 8, 4], BF16)
        fb = k.sb(st, "ipfb", [4, 1], F32)
        fl = k.sb(st, "ipfl", [4, T], F32)
        S.dma("sp", wff_f[:], W["wff"].rearrange("(c p) n -> p c n", p=128), wr=[wff_f])
        S.dma("sp", fb[:], W["foxb"], wr=[fb])
        I("pool", lambda e: e.tensor_copy(wff_b[:], wff_f[:]), rd=[wff_f], wr=[wff_b])
        I("dve", lambda e: e.tensor_scalar(out=fb[:], in0=fb[:], scalar1=-1.0, scalar2=None, op0=ALU.mult), rd=[fb], wr=[fb])
        for tt in range(NT):
            pt = pss.next()
            for c in range(8):
                mm(k, pt[0:4, :], wff_b[:, c, :], hT[:, c, tt * 512:(tt + 1) * 512], c == 0, c == 7, rd=[wff_b, hT], wr=[pt])
            I("act", lambda e: e.activation(out=fl[:, tt * 512:(tt + 1) * 512], in_=pt[0:4, :], func=AF.Exp,
                                            bias=fb[:, 0:1], scale=-1.0), rd=[pt, fb], wr=[fl])
        I("act", lambda e: e.activation(out=fl[:], in_=fl[:], func=AF.Ln, bias=k.one1[0:4, 0:1], scale=1.0),
          rd=[fl, k.one1], wr=[fl])
        for tt in range(NT):
            init = 0.0 if tt == 0 else fl[:, tt * 512 - 1:tt * 512]
            I("dve", lambda e: e.tensor_tensor_scan(out=fl[:, tt * 512:(tt + 1) * 512], data0=k.ones[0:4, :],
                                                    data1=fl[:, tt * 512:(tt + 1) * 512], initial=init,
                                                    op0=ALU.mult, op1=ALU.add), rd=[fl, k.ones], wr=[fl])
        S.dma("sp", k.negc[:], fl[:], rd=[fl], wr=[k.negc])


def layer(k, l, W, x_in, p_l, out, last):
    S, I = k.S, k.S.I
    with phase(k) as st:
        hT = k.sb(st, "hT%d" % l, [128, 8, T], BF16)
        if l == 0:
            phase0(k, x_in, hT)
            if k.stop_after == "phase0":
                return
        else:
            S.dma("sp", hT[:], k.hTs[:], rd=[k.hTs], wr=[hT])
        inproj(k, l, W, hT)
    if k.stop_after is not None and k.stop_after.startswith("ip") or k.stop_after == "inproj":
        return
    if k.only in (None, "s5"):
        s5_phase(k, l, W)
    if k.stop_after == "s5":
        return
    if k.only in (None, "fox"):
        attn_phase(k, l, 0)
    if k.only in (None, "sb"):
        attn_phase(k, l, 1)
    if k.stop_after == "attn":
        return
    if k.only in (None, "rwkv"):
        rwkv_prep(k, l, W)
        rwkv_chunked(k, l, W, nsteps=k.nsteps)
    if k.stop_after == "rwkv":
        return
    precast_mlp(k, l, W)
    with phase(k) as st:
        h1T = k.sb(st, "h1T%d" % l, [128, 8, T], BF16)
        merge_phase(k, l, W, h1T)
        if k.stop_after == "merge":
            return
        mlp_phase(k, l, W, h1T, p_l, out, last)


def range_reduce(k, out, in_, shift, qi, qf, rd, wr):
    I = k.S.I
    TWO_PI = 2.0 * PI
    I("dve", lambda e: e.tensor_scalar(out=out, in0=in_, scalar1=float(shift), scalar2=None, op0=ALU.add), rd=rd, wr=wr)
    I("dve", lambda e: e.tensor_scalar(out=qi, in0=out, scalar1=1.0 / TWO_PI, scalar2=0.5, op0=ALU.mult, op1=ALU.add), rd=rd, wr=wr)
    I("dve", lambda e: e.tensor_copy(qf, qi), rd=rd, wr=wr)
    I("dve", lambda e: e.scalar_tensor_tensor(out=out, in0=qf, scalar=-TWO_PI, in1=out, op0=ALU.mult, op1=ALU.add), rd=rd, wr=wr)
    I("dve", lambda e: e.tensor_scalar(out=qf, in0=out, scalar1=-PI, scalar2=None, op0=ALU.is_lt), rd=rd, wr=wr)
    I("dve", lambda e: e.scalar_tensor_tensor(out=out, in0=qf, scalar=TWO_PI, in1=out, op0=ALU.mult, op1=ALU.add), rd=rd, wr=wr)
    I("dve", lambda e: e.tensor_scalar(out=out, in0=out, scalar1=-3.1415925, scalar2=3.1415925, op0=ALU.max, op1=ALU.min), rd=rd, wr=wr)


def s5_phase(k, l, W):
    S, I = k.S, k.S.I
    TWO_PI = 2.0 * PI
    with phase(k) as st:
        sb, ps = k.sb, k.ps
        par = sb(st, "s5par", [128, 8, 3], F32)
        S.dma("sp", par[:], W["s5par"], wr=[par])
        Bb = sb(st, "s5Bb", [128, 8, 2, 128], BF16)
        Cb = sb(st, "s5Cb", [128, 8, 2, 128], BF16)
        gwb = sb(st, "s5gwb", [128, 2, 256], BF16)
        Dc = sb(st, "s5D", [128, 2], F32)
        S.dma("sp", Dc[:], W["s5D"], wr=[Dc])
        with phase(k) as st2:
            Bf = sb(st2, "s5Bf", [128, 8, 2, 128], F32)
            Cf = sb(st2, "s5Cf", [128, 8, 2, 128], F32)
            gwf = sb(st2, "s5gwf", [128, 2, 256], F32)
            S.dma("sp", Bf[:], W["s5B"], wr=[Bf])
            S.dma("sp", Cf[:], W["s5C"], wr=[Cf])
            S.dma("sp", gwf[:], W["gluw"].rearrange("(c p) n -> p c n", p=128), wr=[gwf])
            I("pool", lambda e: e.tensor_copy(Bb[:], Bf[:]), rd=[Bf], wr=[Bb])
            I("pool", lambda e: e.tensor_copy(Cb[:, :, 0, :], Cf[:, :, 0, :]), rd=[Cf], wr=[Cb])
            I("pool", lambda e: e.tensor_scalar(out=Cb[:, :, 1, :], in0=Cf[:, :, 1, :], scalar1=-1.0, scalar2=None, op0=ALU.mult),
              rd=[Cf], wr=[Cb])
            I("pool", lambda e: e.tensor_copy(gwb[:], gwf[:]), rd=[gwf], wr=[gwb])
        gb = sb(st, "s5gb", [128, 2], F32)
        S.dma("sp", gb[:], W["glub"], wr=[gb])
        sm = sb(st, "s5sm", [128, 16, 8], F32)
        def sl(i):
            return sm[:, i, :]
        lre, lim, ldt = par[:, :, 0], par[:, :, 1], par[:, :, 2]
        DT, A_, TH, NA, EM, SN, CS, LBR, LBI, DEN, FRE, FIM, NFRE, T1, T2, THR = range(16)
        dsm = Dep()
        def dv(fn):
            I("dve", fn, rd=[par, dsm], wr=[dsm])
        def ac(fn):
            I("act", fn, rd=[par, dsm], wr=[dsm])
        def pl(fn):
            I("pool", fn, rd=[par, dsm], wr=[dsm])
        ac(lambda e: e.activation(out=sl(DT), in_=ldt, func=AF.Exp))
        dv(lambda e: e.tensor_tensor(out=sl(A_), in0=lre, in1=sl(DT), op=ALU.mult))
        dv(lambda e: e.tensor_tensor(out=sl(TH), in0=lim, in1=sl(DT), op=ALU.mult))
        dv(lambda e: e.tensor_scalar(out=sl(NA), in0=sl(A_), scalar1=-1.0, scalar2=None, op0=ALU.mult))
        smi = sb(st, "s5smi", [128, 8], mybir.dt.int32)
        range_reduce(k, sl(T1), sl(TH), 0.0, smi[:], sl(T2), [par, dsm, smi], [dsm, smi])
        ac(lambda e: e.activation(out=sl(SN), in_=sl(T1), func=AF.Sin))
        range_reduce(k, sl(T1), sl(TH), 0.5 * PI, smi[:], sl(T2), [par, dsm, smi], [dsm, smi])
        ac(lambda e: e.activation(out=sl(CS), in_=sl(T1), func=AF.Sin))
        ac(lambda e: e.activation(out=sl(EM), in_=sl(A_), func=AF.Exp))
        dv(lambda e: e.tensor_tensor(out=sl(LBR), in0=sl(EM), in1=sl(CS), op=ALU.mult))
        dv(lambda e: e.tensor_tensor(out=sl(LBI), in0=sl(EM), in1=sl(SN), op=ALU.mult))
        dv(lambda e: e.tensor_tensor(out=sl(DEN), in0=lre, in1=lre, op=ALU.mult))
        dv(lambda e: e.tensor_tensor(out=sl(T1), in0=lim, in1=lim, op=ALU.mult))
        dv(lambda e: e.tensor_tensor(out=sl(DEN), in0=sl(DEN), in1=sl(T1), op=ALU.add))
        dv(lambda e: e.reciprocal(out=sl(DEN), in_=sl(DEN)))
        dv(lambda e: e.tensor_scalar(out=sl(T2), in0=sl(LBR), scalar1=-1.0, scalar2=None, op0=ALU.add))
        dv(lambda e: e.tensor_tensor(out=sl(FRE), in0=sl(T2), in1=lre, op=ALU.mult))
        dv(lambda e: e.tensor_tensor(out=sl(T1), in0=sl(LBI), in1=lim, op=ALU.mult))
        dv(lambda e: e.tensor_tensor(out=sl(FRE), in0=sl(FRE), in1=sl(T1), op=ALU.add))
        dv(lambda e: e.tensor_tensor(out=sl(FRE), in0=sl(FRE), in1=sl(DEN), op=ALU.mult))
        dv(lambda e: e.tensor_tensor(out=sl(FIM), in0=sl(LBI), in1=lre, op=ALU.mult))
        dv(lambda e: e.tensor_tensor(out=sl(T1), in0=sl(T2), in1=lim, op=ALU.mult))
        dv(lambda e: e.tensor_tensor(out=sl(FIM), in0=sl(FIM), in1=sl(T1), op=ALU.subtract))
        dv(lambda e: e.tensor_tensor(out=sl(FIM), in0=sl(FIM), in1=sl(DEN), op=ALU.mult))
        dv(lambda e: e.tensor_scalar(out=sl(NFRE), in0=sl(FRE), scalar1=-1.0, scalar2=None, op0=ALU.mult))
        range_reduce(k, sl(THR), sl(TH), 0.0, smi[:], sl(T1), [par, dsm, smi], [dsm, smi])
        CH = 256
        idx = sb(st, "s5idx", [128, CH + 1], F32)
        I("dve", lambda e: e.memset(idx[:], 1.0), wr=[idx])
        I("dve", lambda e: e.tensor_tensor_scan(out=idx[:], data0=idx[:], data1=idx[:], initial=-1.0,
                                                op0=ALU.mult, op1=ALU.add), rd=[idx], wr=[idx])
        tabr = sb(st, "s5tabr", [128, 8, CH + 1], F32)
        tabi = sb(st, "s5tabi", [128, 8, CH + 1], F32)
        tifr = sb(st, "s5tifr", [128, 8, CH], F32)
        tifi = sb(st, "s5tifi", [128, 8, CH], F32)
        ntl = sb(st, "s5ntl", [128, 8], F32)
        w1 = sb(st, "s5w1", [128, CH + 1], F32)
        w2 = sb(st, "s5w2", [128, CH + 1], F32)
        w3 = sb(st, "s5w3", [128, CH + 1], F32)
        w4 = sb(st, "s5w4", [128, CH + 1], F32)
        w5 = sb(st, "s5w5", [128, CH + 1], F32)
        wi = sb(st, "s5wi", [128, CH + 1], mybir.dt.int32)
        tabs = [tabr, tabi, tifr, tifi]
        wk = [w1, w2, w3, w4, w5, idx]
        for rt in range(8):
            def col(i):
                return sm[:, i, rt:rt + 1]
            def dvw(fn):
                I("dve", fn, rd=wk + [dsm], wr=wk[:5] + tabs)
            def acw(fn):
                I("act", fn, rd=wk + [dsm], wr=wk[:5] + tabs)
            def plw(fn):
                I("pool", fn, rd=wk + [dsm], wr=wk[:5] + tabs)
            dvw(lambda e: e.tensor_scalar(out=w1[:], in0=idx[:], scalar1=col(THR), scalar2=None, op0=ALU.mult))
            range_reduce(k, w2[:], w1[:], 0.0, wi[:], w5[:], wk + [dsm, wi], wk[:5] + tabs + [wi])
            acw(lambda e: e.activation(out=w2[:], in_=w2[:], func=AF.Sin))
            range_reduce(k, w3[:], w1[:], 0.5 * PI, wi[:], w5[:], wk + [dsm, wi], wk[:5] + tabs + [wi])
            acw(lambda e: e.activation(out=w3[:], in_=w3[:], func=AF.Sin))
            acw(lambda e: e.activation(out=w4[:], in_=idx[:], func=AF.Exp, scale=col(A_)))
            acw(lambda e: e.activation(out=w5[:], in_=idx[:], func=AF.Exp, scale=col(NA)))
            dvw(lambda e: e.tensor_tensor(out=tabr[:, rt, :], in0=w4[:], in1=w3[:], op=ALU.mult))
            dvw(lambda e: e.tensor_tensor(out=tabi[:, rt, :], in0=w4[:], in1=w2[:], op=ALU.mult))
            dvw(lambda e: e.tensor_scalar(out=w1[:], in0=w3[:], scalar1=col(FRE), scalar2=None, op0=ALU.mult))
            dvw(lambda e: e.scalar_tensor_tensor(out=w1[:], in0=w2[:], scalar=col(FIM), in1=w1[:], op0=ALU.mult, op1=ALU.add))
            dvw(lambda e: e.tensor_tensor(out=tifr[:, rt, :], in0=w1[:, 0:CH], in1=w5[:, 0:CH], op=ALU.mult))
            dvw(lambda e: e.tensor_scalar(out=w1[:], in0=w3[:], scalar1=col(FIM), scalar2=None, op0=ALU.mult))
            dvw(lambda e: e.scalar_tensor_tensor(out=w1[:], in0=w2[:], scalar=col(NFRE), in1=w1[:], op0=ALU.mult, op1=ALU.add))
            dvw(lambda e: e.tensor_tensor(out=tifi[:, rt, :], in0=w1[:, 0:CH], in1=w5[:, 0:CH], op=ALU.mult))
        I("dve", lambda e: e.tensor_scalar(out=ntl[:], in0=tabi[:, :, CH], scalar1=-1.0, scalar2=None, op0=ALU.mult),
          rd=tabs, wr=[ntl])
        uf = sb(st, "s5uf", [128, 2, T], F32)
        ub = sb(st, "s5ub", [128, 2, T], BF16)
        S.dma("sp", uf[:], k.uTf[:], rd=[k.uTf], wr=[uf])
        I("pool", lambda e: e.tensor_copy(ub[:], uf[:]), rd=[uf], wr=[ub])
        ygf = sb(st, "s5ygf", [128, 2, T], F32)
        ygb = sb(st, "s5ygb", [128, 2, T], BF16)
        q0 = sb(st, "s5q0", [128, 8, 2], F32)
        q0_init = I("dve", lambda e: e.memset(q0[:], 0.0), wr=[q0])
        rawp = Ring([ps(st, "s5rp%d" % i, [128, 512], F32) for i in range(4)])
        yp = Ring([ps(st, "s5yp%d" % i, [128, 512], F32) for i in range(2)])
        rr = Ring([sb(st, "s5rr%d" % i, [128, 2, 512], F32) for i in range(2)])
        zz = Ring([sb(st, "s5zz%d" % i, [128, 2, 512], F32) for i in range(2)])
        mmr = Ring([sb(st, "s5mm%d" % i, [128, 4, 512], F32) for i in range(2)])
        qq = Ring([sb(st, "s5qq%d" % i, [128, 2, 512], F32) for i in range(2)])
        xb = Ring([sb(st, "s5xb%d" % i, [128, 2, 512], BF16) for i in range(3)])
        ew = Ring([sb(st, "s5ew%d" % i, [128, 3, 512], F32) for i in range(1)])
        tq = sb(st, "s5tq", [128, 2], F32)

        def b4(tab, rt):
            return tab[:, rt:rt + 1, 0:CH].broadcast_to([128, 512 // CH, CH])

        def v4(ap):
            return ap.rearrange("p (a b) -> p a b", a=512 // CH)
        items = [(oc, tt, r4) for oc in range(2) for tt in range(NT) for r4 in range(4)]
        ctx = {}
        ypts = {}

        for rg, n in ((rr, 2), (zz, 2), (mmr, 4), (qq, 2), (xb, 2), (ew, 3)):
            for b_ in rg.bufs:
                b_.sd = [Dep() for _ in range(n)]
        tqd = [Dep(), Dep()]
        q0d = [[Dep(), Dep()] for _ in range(8)]

        def stZ(it):
            oc, tt, r4 = it
            rt = oc * 4 + r4
            tsl = slice(tt * 512, (tt + 1) * 512)
            pr, pi_ = rawp.next(), rawp.next()
            mm(k, pr[:], Bb[:, rt, 0, :], ub[:, oc, tsl], True, True, rd=[Bb, ub], wr=[pr])
            mm(k, pi_[:], Bb[:, rt, 1, :], ub[:, oc, tsl], True, True, rd=[Bb, ub], wr=[pi_])
            r_ = rr.next()
            I("act", lambda e: e.activation(out=r_[:, 0, :], in_=pr[:], func=AF.Copy), rd=[pr], wr=[r_.sd[0]])
            I("act", lambda e: e.activation(out=r_[:, 1, :], in_=pi_[:], func=AF.Copy), rd=[pi_], wr=[r_.sd[1]])
            z_, m_ = zz.next(), mmr.next()
            I("dve", lambda e: e.tensor_tensor(out=v4(m_[:, 0, :]), in0=v4(r_[:, 0, :]), in1=b4(tifr, rt), op=ALU.mult),
              rd=[r_.sd[0]] + tabs, wr=[m_.sd[0]])
            I("dve", lambda e: e.tensor_tensor(out=v4(m_[:, 1, :]), in0=v4(r_[:, 1, :]), in1=b4(tifi, rt), op=ALU.mult),
              rd=[r_.sd[1]] + tabs, wr=[m_.sd[1]])
            I("pool", lambda e: e.tensor_tensor(out=v4(m_[:, 2, :]), in0=v4(r_[:, 0, :]), in1=b4(tifi, rt), op=ALU.mult),
              rd=[r_.sd[0]] + tabs, wr=[m_.sd[2]])
            I("pool", lambda e: e.tensor_tensor(out=v4(m_[:, 3, :]), in0=v4(r_[:, 1, :]), in1=b4(tifr, rt), op=ALU.mult),
              rd=[r_.sd[1]] + tabs, wr=[m_.sd[3]])
            I("dve", lambda e: e.tensor_tensor(out=z_[:, 0, :], in0=m_[:, 0, :], in1=m_[:, 1, :], op=ALU.subtract),
              rd=[m_.sd[0], m_.sd[1]], wr=[z_.sd[0]])
            I("pool", lambda e: e.tensor_tensor(out=z_[:, 1, :], in0=m_[:, 2, :], in1=m_[:, 3, :], op=ALU.add),
              rd=[m_.sd[2], m_.sd[3]], wr=[z_.sd[1]])
            ctx[it] = z_

        def stSC(it):
            oc, tt, r4 = it
            rt = oc * 4 + r4
            z_ = ctx[it]
            q_ = qq.next()
            for sc in range(512 // CH):
                csl = slice(sc * CH, (sc + 1) * CH)
                for ri in range(2):
                    I("dve", lambda e: e.tensor_tensor_scan(out=q_[:, ri, csl], data0=k.ones[:, 0:CH], data1=z_[:, ri, csl],
                                                            initial=q0[:, rt, ri:ri + 1], op0=ALU.mult, op1=ALU.add),
                      rd=[z_.sd[ri], q0d[rt][ri], q0, k.ones], wr=[q_.sd[ri]])
                qe_r = q_[:, 0, sc * CH + CH - 1:sc * CH + CH]
                qe_i = q_[:, 1, sc * CH + CH - 1:sc * CH + CH]
                lr_ = tabr[:, rt, CH:CH + 1]
                li_ = tabi[:, rt, CH:CH + 1]
                I("dve", lambda e: e.tensor_scalar(out=tq[:, 0:1], in0=qe_r, scalar1=lr_, scalar2=None, op0=ALU.mult),
                  rd=[q_.sd[0]] + tabs, wr=[tqd[0]])
                I("dve", lambda e: e.tensor_scalar(out=tq[:, 1:2], in0=qe_i, scalar1=lr_, scalar2=None, op0=ALU.mult),
                  rd=[q_.sd[1]] + tabs, wr=[tqd[1]])
                I("dve", lambda e: e.scalar_tensor_tensor(out=q0[:, rt, 0:1], in0=qe_i, scalar=ntl[:, rt:rt + 1], in1=tq[:, 0:1],
                                                          op0=ALU.mult, op1=ALU.add), rd=[q_.sd[1], ntl, tqd[0]], wr=[q0d[rt][0]])
                I("dve", lambda e: e.scalar_tensor_tensor(out=q0[:, rt, 1:2], in0=qe_r, scalar=li_, in1=tq[:, 1:2],
                                                          op0=ALU.mult, op1=ALU.add), rd=[q_.sd[0], tqd[1]] + tabs, wr=[q0d[rt][1]])
            ctx[it] = q_

        def stX(it):
            oc, tt, r4 = it
            rt = oc * 4 + r4
            tsl = slice(tt * 512, (tt + 1) * 512)
            q_ = ctx.pop(it)
            if r4 == 0:
                ypts[(oc, tt)] = yp.next()
            ypt = ypts[(oc, tt)]
            m2 = mmr.next()
            x_ = xb.next()
            I("dve", lambda e: e.tensor_tensor(out=v4(m2[:, 0, :]), in0=v4(q_[:, 0, :]), in1=b4(tabr, rt), op=ALU.mult),
              rd=[q_.sd[0]] + tabs, wr=[m2.sd[0]])
            I("dve", lambda e: e.tensor_tensor(out=v4(m2[:, 1, :]), in0=v4(q_[:, 1, :]), in1=b4(tabi, rt), op=ALU.mult),
              rd=[q_.sd[1]] + tabs, wr=[m2.sd[1]])
            I("pool", lambda e: e.tensor_tensor(out=v4(m2[:, 2, :]), in0=v4(q_[:, 0, :]), in1=b4(tabi, rt), op=ALU.mult),
              rd=[q_.sd[0]] + tabs, wr=[m2.sd[2]])
            I("pool", lambda e: e.tensor_tensor(out=v4(m2[:, 3, :]), in0=v4(q_[:, 1, :]), in1=b4(tabr, rt), op=ALU.mult),
              rd=[q_.sd[1]] + tabs, wr=[m2.sd[3]])
            I("dve", lambda e: e.tensor_tensor(out=x_[:, 0, :], in0=m2[:, 0, :], in1=m2[:, 1, :], op=ALU.subtract),
              rd=[m2.sd[0], m2.sd[1]], wr=[x_.sd[0]])
            I("pool", lambda e: e.tensor_tensor(out=x_[:, 1, :], in0=m2[:, 2, :], in1=m2[:, 3, :], op=ALU.add),
              rd=[m2.sd[2], m2.sd[3]], wr=[x_.sd[1]])
            mm(k, ypt[:], Cb[:, rt, 0, :], x_[:, 0, :], r4 == 0, False, rd=[Cb, x_.sd[0]], wr=[ypt])
            mm(k, ypt[:], Cb[:, rt, 1, :], x_[:, 1, :], False, r4 == 3, rd=[Cb, x_.sd[1]], wr=[ypt])
            if r4 == 3:
                e_ = ew.next()
                d0, d1, d2 = e_.sd
                I("dve", lambda e: e.scalar_tensor_tensor(out=e_[:, 0, :], in0=uf[:, oc, tsl], scalar=Dc[:, oc:oc + 1], in1=ypt[:],
                                                          op0=ALU.mult, op1=ALU.add), rd=[uf, Dc, ypt], wr=[d0])
                I("pool", lambda e: e.tensor_tensor(out=e_[:, 1, :], in0=e_[:, 0, :], in1=e_[:, 0, :], op=ALU.mult), rd=[d0], wr=[d1])
                I("pool", lambda e: e.tensor_scalar(out=e_[:, 1, :], in0=e_[:, 1, :], scalar1=0.044715, scalar2=1.0,
                                                    op0=ALU.mult, op1=ALU.add), rd=[d1], wr=[d1])
                I("pool", lambda e: e.tensor_tensor(out=e_[:, 1, :], in0=e_[:, 1, :], in1=e_[:, 0, :], op=ALU.mult), rd=[d0, d1], wr=[d1])
                I("act", lambda e: e.activation(out=e_[:, 2, :], in_=e_[:, 1, :], func=AF.Sigmoid, scale=1.5957691216057308),
                  rd=[d1], wr=[d2])
                I("dve", lambda e: e.tensor_tensor(out=ygf[:, oc, tsl], in0=e_[:, 0, :], in1=e_[:, 2, :], op=ALU.mult),
                  rd=[d0, d2], wr=[ygf])
                I("pool", lambda e: e.tensor_copy(ygb[:, oc, tsl], ygf[:, oc, tsl]), rd=[ygf], wr=[ygb])
        stZ(items[0])
        for i, it in enumerate(items):
            if i + 1 < len(items):
                stZ(items[i + 1])
            stSC(it)
            stX(it)

        ob = Ring([sb(st, "s5ob%d" % i, [128, 512], BF16) for i in range(2)])
        for oc in range(2):
            for tt in range(NT):
                tsl = slice(tt * 512, (tt + 1) * 512)
                pt = yp.next()
                for kc in range(2):
                    mm(k, pt[:], gwb[:, kc, oc * 128:(oc + 1) * 128], ygb[:, kc, tsl], kc == 0, kc == 1, rd=[gwb, ygb], wr=[pt])
                e_ = ew.next()
                I("act", lambda e: e.activation(out=e_[:, 0, :], in_=pt[:], func=AF.Sigmoid, bias=gb[:, oc:oc + 1], scale=1.0),
                  rd=[pt, gb], wr=[e_.sd[0]])
                o_ = ob.next()
                I("dve", lambda e: e.tensor_tensor(out=o_[:], in0=e_[:, 0, :], in1=ygf[:, oc, tsl], op=ALU.mult),
                  rd=[e_.sd[0], ygf], wr=[o_])
                S.dma("sp", k.ycat[0, :, oc, tsl], o_[:], rd=[o_], wr=[k.ycat])


def attn_phase(k, l, kind):
    S, I = k.S, k.S.I
    fox = (kind == 0)
    VW = 65 if fox else 64
    with phase(k) as st:
        sb, ps = k.sb, k.ps
        qT = sb(st, "atq", [128, 2, T], BF16)
        kT = sb(st, "atk", [128, 2, T], BF16)
        S.dma("sp", qT[:], k.qk[2 * kind], rd=[k.qk], wr=[qT])
        S.dma("sp", kT[:], k.qk[2 * kind + 1], rd=[k.qk], wr=[kT])
        V = sb(st, "atv", [128, NB, 4, VW], BF16)
        if fox:
            I("pool", lambda e: e.memset(V[:, :, :, 64:65], 1.0), wr=[V])
        S.dma("sp", V[:, :, :, 0:64], k.vv[kind].rearrange("p b (h d) -> p b h d", h=4), rd=[k.vv], wr=[V])
        yT = sb(st, "atyT", [128, 2, T], BF16)
        ytok = sb(st, "atytok", [128, NB, 256], BF16)
        A = Ring([sb(st, "atA%d" % i, [128, T], F32) for i in range(2)])
        P = Ring([sb(st, "atP%d" % i, [128, T], BF16) for i in range(2)])
        PT = Ring([sb(st, "atPT%d" % i, [128, 4, 128], BF16) for i in range(3)])
        zp = Ring([ps(st, "atzp%d" % i, [128, 512], F32) for i in range(3)])
        tp = Ring([ps(st, "attp%d" % i, [128, 4, 128], BF16) for i in range(2)])
        op = Ring([ps(st, "atop%d" % i, [128, VW], F32) for i in range(2)])
        xp = ps(st, "atxp", [128, 512], F32)
        nb = Ring([sb(st, "atnb%d" % i, [128, 2], F32) for i in range(3)])
        if fox:
            negc = sb(st, "atnegc", [4, T], F32)
            S.dma("sp", negc[:], k.negc[:], rd=[k.negc], wr=[negc])
            crow = sb(st, "atcrow", [128, T], F32)
        else:
            spb = Ring([sb(st, "atsp%d" % i, [128, 512], F32) for i in range(3)])
            Fb = Ring([sb(st, "atF%d" % i, [128, 512], F32) for i in range(3)])
            t1b = Ring([sb(st, "att1%d" % i, [128, 512], F32) for i in range(3)])
        items = [(h, qb) for h in range(4) for qb in range(NB)]
        state = {}

        def stage1(h, qb):
            hc, po = h // 2, (h % 2) * 64
            if fox and qb == 0:
                for tt in range(NT):
                    mm(k, xp[:], k.sel4[:, h, :], negc[:, tt * 512:(tt + 1) * 512], True, True, rd=[k.sel4, negc], wr=[xp])
                    I("act", lambda e: e.activation(out=crow[:, tt * 512:(tt + 1) * 512], in_=xp[:], func=AF.Copy),
                      rd=[xp], wr=[crow])
            nk = qb + 1
            a_ = A.next()
            nb_ = nb.next()
            carry = None
            for kt in range((nk + 3) // 4):
                w = min(4, nk - 4 * kt) * 128
                ks = slice(kt * 512, kt * 512 + w)
                last = (kt == (nk + 3) // 4 - 1)
                z = zp.next()
                mm(k, z[:, 0:w], qT[po:po + 64, hc, qb * 128:(qb + 1) * 128], kT[po:po + 64, hc, ks], True, True,
                   rd=[qT, kT], wr=[z])
                if fox:
                    I("dve", lambda e: e.tensor_tensor(out=a_[:, ks], in0=z[:, 0:w], in1=crow[:, ks], op=ALU.add),
                      rd=[z, crow], wr=[a_])
                else:
                    sp_ = spb.next()
                    I("act", lambda e: e.activation(out=sp_[:, 0:w], in_=z[:, 0:w], func=AF.Exp), rd=[z], wr=[sp_])
                    I("act", lambda e: e.activation(out=sp_[:, 0:w], in_=sp_[:, 0:w], func=AF.Ln, bias=k.one1[:, 0:1], scale=1.0),
                      rd=[sp_, k.one1], wr=[sp_])
                    if last:
                        I("pool", lambda e: e.tensor_tensor(out=sp_[:, w - 128:w], in0=sp_[:, w - 128:w], in1=k.m01[:], op=ALU.mult),
                          rd=[sp_, k.m01], wr=[sp_])
                    f_ = Fb.next()
                    init = 0.0 if carry is None else carry[0][:, carry[1] - 1:carry[1]]
                    rdl = [sp_, k.ones] + ([carry[0]] if carry is not None else [])
                    I("dve", lambda e: e.tensor_tensor_scan(out=f_[:, 0:w], data0=k.ones[:, 0:w], data1=sp_[:, 0:w], initial=init,
                                                            op0=ALU.mult, op1=ALU.add), rd=rdl, wr=[f_])
                    carry = (f_, w)
                    t_ = t1b.next()
                    I("dve", lambda e: e.tensor_tensor(out=t_[:, 0:w], in0=z[:, 0:w], in1=sp_[:, 0:w], op=ALU.subtract),
                      rd=[z, sp_], wr=[t_])
                    I("pool", lambda e: e.tensor_tensor(out=a_[:, ks], in0=t_[:, 0:w], in1=f_[:, 0:w], op=ALU.add),
                      rd=[t_, f_], wr=[a_])
                yield
            dsl = slice(qb * 128, (qb + 1) * 128)
            I("pool", lambda e: e.tensor_tensor(out=a_[:, dsl], in0=a_[:, dsl], in1=(k.nmi if fox else k.nms)[:], op=ALU.add),
              rd=[a_, k.nmi, k.nms], wr=[a_])
            if fox:
                I("dve", lambda e: e.reduce_max(out=nb_[:, 0:1], in_=a_[:, 0:nk * 128], axis=AX.X), rd=[a_], wr=[nb_])
                I("dve", lambda e: e.tensor_scalar(out=nb_[:, 1:2], in0=nb_[:, 0:1], scalar1=-1.0, scalar2=None, op0=ALU.mult),
                  rd=[nb_], wr=[nb_])
            else:
                I("dve", lambda e: e.tensor_scalar(out=nb_[:, 1:2], in0=carry[0][:, carry[1] - 1:carry[1]], scalar1=-1.0, scalar2=None,
                                                   op0=ALU.mult), rd=[carry[0]], wr=[nb_])
            state[(h, qb)] = (a_, nb_)
            yield

        def stage2(h, qb):
            a_, nb_ = state.pop((h, qb))
            nk = qb + 1
            p_ = P.next()
            I("act", lambda e: e.activation(out=p_[:, 0:nk * 128], in_=a_[:, 0:nk * 128], func=AF.Exp, bias=nb_[:, 1:2], scale=1.0),
              rd=[a_, nb_], wr=[p_])
            o_ = op.next()
            prev = None

            def pv(kt, pt_, nbk):
                for j in range(nbk):
                    kb = kt * 4 + j
                    mm(k, o_[:], pt_[:, j, :], V[:, kb, h, :], kb == 0, kb == nk - 1, rd=[pt_, V], wr=[o_])
            for kt in range((nk + 3) // 4):
                nbk = min(4, nk - 4 * kt)
                t_ = tp.next()
                for j in range(nbk):
                    kb = kt * 4 + j
                    tr(k, t_[:, j, :], p_[:, kb * 128:(kb + 1) * 128], k.identb[:], rd=[p_, k.identb], wr=[t_])
                pt_ = PT.next()
                if kt % 2 == 0:
                    I("act", lambda e: e.activation(out=pt_[:, 0:nbk, :], in_=t_[:, 0:nbk, :], func=AF.Copy), rd=[t_], wr=[pt_])
                else:
                    I("dve", lambda e: e.tensor_copy(pt_[:, 0:nbk, :], t_[:, 0:nbk, :]), rd=[t_], wr=[pt_])
                if prev is not None:
                    pv(*prev)
                prev = (kt, pt_, nbk)
                yield
            pv(*prev)
            if fox:
                I("dve", lambda e: e.reciprocal(out=nb_[:, 0:1], in_=o_[:, 64:65]), rd=[o_], wr=[nb_])
                I("act", lambda e: e.activation(out=ytok[:, qb, h * 64:(h + 1) * 64], in_=o_[:, 0:64], func=AF.Copy,
                                                scale=nb_[:, 0:1]), rd=[o_, nb_], wr=[ytok])
            else:
                I("act", lambda e: e.activation(out=ytok[:, qb, h * 64:(h + 1) * 64], in_=o_[:, 0:64], func=AF.Copy),
                  rd=[o_], wr=[ytok])

        for i, it in enumerate(items):
            run_rr([stage1(*it)])
            if i > 0:
                run_rr([stage2(*items[i - 1])])
        run_rr([stage2(*items[-1])])
        for qb in range(NB):
            t_ = tp.next()
            for c in range(2):
                tr(k, t_[:, c, :], ytok[:, qb, c * 128:(c + 1) * 128], k.identb[:], rd=[ytok, k.identb], wr=[t_])
            I("act", lambda e: e.activation(out=yT[:, :, qb * 128:(qb + 1) * 128], in_=t_[:, 0:2, :], func=AF.Copy),
              rd=[t_], wr=[yT])
        S.dma("sp", k.ycat[1 if fox else 3], yT[:], rd=[yT], wr=[k.ycat])


def rwkv_prep(k, l, W):
    S, I = k.S, k.S.I
    with phase(k) as st:
        sb, ps = k.sb, k.ps
        rvec = sb(st, "rpvec", [128, 2, 7], F32)
        S.dma("sp", rvec[:], W["rvec"], wr=[rvec])
        nw0 = sb(st, "rpnw0", [128, 2], F32)
        I("dve", lambda e: e.tensor_scalar(out=nw0[:], in0=rvec[:, :, 0], scalar1=-1.0, scalar2=None, op0=ALU.mult),
          rd=[rvec], wr=[nw0])
        wa2b = sb(st, "rpwa2b", [128, 256], BF16)
        g2b = sb(st, "rpg2b", [128, 256], BF16)
        tw = sb(st, "rptw", [128, T], BF16)
        gs = sb(st, "rpgs", [128, T], BF16)
        with phase(k) as st2:
            wa2f = sb(st2, "rpwa2f", [128, 256], F32)
            g2f = sb(st2, "rpg2f", [128, 256], F32)
            x6 = sb(st2, "rpx6", [128, T], F32)
            x7 = sb(st2, "rpx7", [128, T], F32)
            S.dma("sp", wa2f[:], W["wa2"], wr=[wa2f])
            S.dma("sp", g2f[:], W["g2"], wr=[g2f])
            S.dma("sp", x6[:], k.rwxs[6], rd=[k.rwxs], wr=[x6])
            S.dma("sp", x7[:], k.rwxs[7], rd=[k.rwxs], wr=[x7])
            I("pool", lambda e: e.tensor_copy(wa2b[:], wa2f[:]), rd=[wa2f], wr=[wa2b])
            I("pool", lambda e: e.tensor_copy(g2b[:], g2f[:]), rd=[g2f], wr=[g2b])
            I("act", lambda e: e.activation(out=tw[0:64, :], in_=x6[0:64, :], func=AF.Tanh), rd=[x6], wr=[tw])
            I("pool", lambda e: e.tensor_copy(tw[64:128, :], x6[64:128, :]), rd=[x6], wr=[tw])
            I("act", lambda e: e.activation(out=gs[:], in_=x7[:], func=AF.Sigmoid), rd=[x7], wr=[gs])
        for oc in range(2):
            S.dma("sp", k.sc5[oc, :, 4, :], k.rwxs[oc], rd=[k.rwxs], wr=[k.sc5])
        ld = Ring([sb(st, "rpld%d" % i, [128, 3, 512], F32) for i in range(2)])
        o5 = Ring([sb(st, "rpo5%d" % i, [128, 4, 512], F32) for i in range(2)])
        wk = Ring([sb(st, "rpwk%d" % i, [128, 6, 512], F32) for i in range(2)])
        so = Ring([sb(st, "rpso%d" % i, [128, 3, 512], F32) for i in range(2)])
        pp = Ring([ps(st, "rppp%d" % i, [128, 512], F32) for i in range(8)])
        def tile_gen(it):
            oc, tt = it
            osl = slice(oc * 128, (oc + 1) * 128)
            tsl = slice(tt * 512, (tt + 1) * 512)
            x_ = ld.next()
            for i, j in enumerate((oc, 2 + oc, 4 + oc)):
                S.dma("sp", x_[:, i, :], k.rwxs[j, :, tsl], rd=[k.rwxs], wr=[x_])
            r_t, k_t, v_t = x_[:, 0, :], x_[:, 1, :], x_[:, 2, :]
            o_ = o5.next()
            w_ = wk.next()
            s_ = so.next()
            pw = pp.next()
            mm(k, pw[:], wa2b[0:64, osl], tw[0:64, tsl], True, True, rd=[wa2b, tw], wr=[pw])
            yield
            I("act", lambda e: e.activation(out=w_[:, 0, :], in_=pw[:], func=AF.Exp, bias=nw0[:, oc:oc + 1], scale=-1.0),
              rd=[pw, nw0], wr=[w_])
            I("act", lambda e: e.activation(out=w_[:, 0, :], in_=w_[:, 0, :], func=AF.Ln, bias=k.one1[:, 0:1], scale=1.0),
              rd=[w_, k.one1], wr=[w_])
            yield
            I("act", lambda e: e.activation(out=w_[:, 0, :], in_=w_[:, 0, :], func=AF.Exp, bias=k.mhalf[:, 0:1], scale=-1.0),
              rd=[w_, k.mhalf], wr=[w_])
            I("pool", lambda e: e.tensor_scalar(out=o_[:, 0, :], in0=w_[:, 0, :], scalar1=-1.0, scalar2=None, op0=ALU.mult),
              rd=[w_], wr=[o_])
            pa = pp.next()
            mm(k, pa[:], wa2b[64:128, osl], tw[64:128, tsl], True, True, rd=[wa2b, tw], wr=[pa])
            yield
            I("act", lambda e: e.activation(out=w_[:, 1, :], in_=pa[:], func=AF.Sigmoid, bias=rvec[:, oc, 1:2], scale=1.0),
              rd=[pa, rvec], wr=[w_])
            a_t = w_[:, 1, :]
            I("dve", lambda e: e.tensor_scalar(out=w_[:, 2, :], in0=k_t, scalar1=rvec[:, oc, 2:3], scalar2=None, op0=ALU.mult),
              rd=[x_, rvec], wr=[w_])
            I("pool", lambda e: e.tensor_tensor(out=w_[:, 3, :], in0=w_[:, 2, :], in1=w_[:, 2, :], op=ALU.mult), rd=[w_], wr=[w_])
            yield
            pn = pp.next()
            mm(k, pn[:], k.blk64[:], w_[:, 3, :], True, True, rd=[k.blk64, w_], wr=[pn])
            I("act", lambda e: e.activation(out=w_[:, 3, :], in_=pn[:], func=AF.Sqrt), rd=[pn], wr=[w_])
            I("dve", lambda e: e.tensor_scalar(out=w_[:, 3, :], in0=w_[:, 3, :], scalar1=1e-12, scalar2=None, op0=ALU.max),
              rd=[w_], wr=[w_])
            I("dve", lambda e: e.reciprocal(out=w_[:, 3, :], in_=w_[:, 3, :]), rd=[w_], wr=[w_])
            yield
            I("dve", lambda e: e.tensor_tensor(out=w_[:, 2, :], in0=w_[:, 2, :], in1=w_[:, 3, :], op=ALU.mult), rd=[w_], wr=[w_])
            I("pool", lambda e: e.tensor_scalar(out=o_[:, 3, :], in0=w_[:, 2, :], scalar1=-1.0, scalar2=None, op0=ALU.mult),
              rd=[w_], wr=[o_])
            I("pool", lambda e: e.tensor_tensor(out=o_[:, 1, :], in0=w_[:, 2, :], in1=a_t, op=ALU.mult), rd=[w_], wr=[o_])
            I("dve", lambda e: e.tensor_scalar(out=w_[:, 4, :], in0=a_t, scalar1=1.0, scalar2=rvec[:, oc, 3:4],
                                               op0=ALU.subtract, op1=ALU.mult), rd=[w_, rvec], wr=[w_])
            I("dve", lambda e: e.scalar_tensor_tensor(out=o_[:, 2, :], in0=w_[:, 4, :], scalar=1.0, in1=k_t,
                                                      op0=ALU.add, op1=ALU.mult), rd=[w_, x_], wr=[o_])
            S.dma("sp", k.sc5[oc, :, 0:4, tsl], o_[:], rd=[o_], wr=[k.sc5])
            yield
            I("pool", lambda e: e.tensor_tensor(out=w_[:, 5, :], in0=r_t, in1=o_[:, 2, :], op=ALU.mult), rd=[x_, o_], wr=[w_])
            I("pool", lambda e: e.tensor_scalar(out=w_[:, 5, :], in0=w_[:, 5, :], scalar1=rvec[:, oc, 4:5], scalar2=None,
                                                op0=ALU.mult), rd=[w_, rvec], wr=[w_])
            pb = pp.next()
            mm(k, pb[:], k.blk64[:], w_[:, 5, :], True, True, rd=[k.blk64, w_], wr=[pb])
            I("dve", lambda e: e.tensor_tensor(out=s_[:, 0, :], in0=pb[:], in1=v_t, op=ALU.mult), rd=[pb, x_], wr=[s_])
            yield
            S.dma("sp", k.bon[oc, :, tsl], s_[:, 0, :], rd=[s_], wr=[k.bon])
            pg = pp.next()
            mm(k, pg[:], g2b[:, osl], gs[:, tsl], True, True, rd=[g2b, gs], wr=[pg])
            I("act", lambda e: e.activation(out=s_[:, 1, :], in_=pg[:], func=AF.Copy), rd=[pg], wr=[s_])
            yield
            S.dma("sp", k.gg[oc, :, tsl], s_[:, 1, :], rd=[s_], wr=[k.gg])
            pv = pp.next()
            for b in range(4):
                tr(k, pv[:, b * 128:(b + 1) * 128], x_[:, 2, b * 128:(b + 1) * 128], k.identf[:], rd=[x_, k.identf], wr=[pv])
            I("act", lambda e: e.activation(out=s_[:, 2, :], in_=pv[:], func=AF.Copy), rd=[pv], wr=[s_])
            S.dma("sp", k.vtok[tsl, oc, :].rearrange("(b p) c -> p b c", p=128),
                  s_[:, 2, :].rearrange("p (b c) -> p b c", b=4), rd=[s_], wr=[k.vtok])


        interleave(tile_gen, [(oc, tt) for oc in range(2) for tt in range(NT)], 2)


def rwkv_rec(k, l, W, nsteps=T):
    S, I = k.S, k.S.I
    VC = 16
    with phase(k) as st:
        sb, ps = k.sb, k.ps
        rvec = sb(st, "rrvec", [128, 2, 7], F32)
        S.dma("sp", rvec[:], W["rvec"], wr=[rvec])
        Sx = [Ring([sb(st, "rrS%d_%d" % (hp, i), [128, 128], F32) for i in range(3)]) for hp in range(2)]
        S1 = [Ring([sb(st, "rrS1%d_%d" % (hp, i), [128, 128], F32) for i in range(2)]) for hp in range(2)]
        KK = [Ring([sb(st, "rrKK%d_%d" % (hp, i), [128, 128], F32) for i in range(3)]) for hp in range(2)]
        S2 = [Ring([sb(st, "rrS2%d_%d" % (hp, i), [128, 128], F32) for i in range(2)]) for hp in range(2)]
        sap = [Ring([ps(st, "rrsa%d_%d" % (hp, i), [128, 128], F32) for i in range(2)]) for hp in range(2)]
        yp = [ps(st, "rryp%d" % hp, [128, 512], F32) for hp in range(2)]
        c5 = [Ring([sb(st, "rrc5%d_%d" % (hp, i), [128, 5, 512], F32) for i in range(2)]) for hp in range(2)]
        vr = [Ring([sb(st, "rrvr%d_%d" % (hp, i), [128, VC, 128], F32) for i in range(2)]) for hp in range(2)]
        ysb = [sb(st, "rry%d" % hp, [128, T], F32) for hp in range(2)]
        for hp in range(2):
            for b in vr[hp].bufs:
                I("pool", lambda e: e.memset(b[:], 0.0), wr=[b])
            b0 = Sx[hp].bufs[0]
            I("dve", lambda e: e.memset(b0[:], 0.0), wr=[b0])
            if nsteps < T:
                I("pool", lambda e: e.memset(ysb[hp][:], 0.0), wr=[ysb[hp]])
        cur = [Sx[0].next(), Sx[1].next()]
        c5c = [None, None]
        vrc = [None, None]
        for t in range(nsteps):
            tl = t % 512
            for hp in range(2):
                if tl == 0:
                    c_ = c5[hp].next()
                    S.dma("sp", c_[:], k.sc5[hp, :, :, t:t + 512], rd=[k.sc5], wr=[c_])
                    c5c[hp] = c_
                if t % VC == 0:
                    v_ = vr[hp].next()
                    for hh in range(2):
                        S.dma("sp", v_[hh * 64:(hh + 1) * 64, :, hh * 64:(hh + 1) * 64],
                              k.vtok[t:t + VC, hp, hh * 64:(hh + 1) * 64].partition_broadcast(64), rd=[k.vtok], wr=[v_])
                    vrc[hp] = v_
                c_, v_ = c5c[hp], vrc[hp]
                old = cur[hp]
                new = Sx[hp].next()
                kk_ = KK[hp].next()
                I("act", lambda e: e.activation(out=kk_[:], in_=k.blk64[:], func=AF.Copy, scale=c_[:, 3, tl:tl + 1]),
                  rd=[k.blk64, c_], wr=[kk_])
                sa = sap[hp].next()
                mm(k, sa[:], kk_[:], old[:], True, True, rd=[kk_, old], wr=[sa])
                s1 = S1[hp].next()
                s2 = S2[hp].next()
                I("act", lambda e: e.activation(out=s1[:], in_=old[:], func=AF.Copy, scale=c_[:, 0, tl:tl + 1]),
                  rd=[old, c_], wr=[s1])
                I("pool", lambda e: e.tensor_scalar(out=s2[:], in0=v_[:, t % VC, :], scalar1=c_[:, 2, tl:tl + 1], scalar2=None, op0=ALU.mult),
                  rd=[v_, c_], wr=[s2])
                I("pool", lambda e: e.tensor_tensor(out=s1[:], in0=s1[:], in1=s2[:], op=ALU.add), rd=[s1, s2], wr=[s1])
                I("dve", lambda e: e.scalar_tensor_tensor(out=new[:], in0=sa[:], scalar=c_[:, 1, tl:tl + 1], in1=s1[:],
                                                          op0=ALU.mult, op1=ALU.add), rd=[sa, c_, s1], wr=[new])
                mm(k, yp[hp][:, tl:tl + 1], new[:], c_[:, 4, tl:tl + 1], True, True, rd=[new, c_], wr=[yp[hp]])
                cur[hp] = new
                if tl == 511 or t == nsteps - 1:
                    t0 = t - tl
                    ypb = yp[hp]
                    I("act", lambda e: e.activation(out=ysb[hp][:, t0:t0 + tl + 1], in_=ypb[:, 0:tl + 1], func=AF.Copy),
                      rd=[ypb], wr=[ysb[hp]])
        ld = Ring([sb(st, "rrld%d" % i, [128, 2, 512], F32) for i in range(2)])
        wk = Ring([sb(st, "rrwk%d" % i, [128, 3, 512], F32) for i in range(2)])
        ob = Ring([sb(st, "rrob%d" % i, [128, 512], BF16) for i in range(2)])
        for hp in range(2):
            for tt in range(NT):
                tsl = slice(tt * 512, (tt + 1) * 512)
                x_ = ld.next()
                S.dma("sp", x_[:, 0, :], k.bon[hp, :, tsl], rd=[k.bon], wr=[x_])
                S.dma("sp", x_[:, 1, :], k.gg[hp, :, tsl], rd=[k.gg], wr=[x_])
                w_ = wk.next()
                pm = sap[0].next()
                pv_ = sap[1].next()
                pmt = yp[0]
                mm(k, pmt[:], k.blk64[:], ysb[hp][:, tsl], True, True, rd=[k.blk64, ysb[hp]], wr=[pmt])
                I("dve", lambda e: e.scalar_tensor_tensor(out=w_[:, 0, :], in0=pmt[:], scalar=-1.0 / 64.0, in1=ysb[hp][:, tsl],
                                                          op0=ALU.mult, op1=ALU.add), rd=[pmt, ysb[hp]], wr=[w_])
                I("act", lambda e: e.activation(out=w_[:, 1, :], in_=w_[:, 0, :], func=AF.Square), rd=[w_], wr=[w_])
                pvt = yp[1]
                mm(k, pvt[:], k.blk64[:], w_[:, 1, :], True, True, rd=[k.blk64, w_], wr=[pvt])
                I("act", lambda e: e.activation(out=w_[:, 1, :], in_=pvt[:], func=AF.Ln, bias=k.epsgn[:, 0:1], scale=1.0 / 64.0),
                  rd=[pvt, k.epsgn], wr=[w_])
                I("act", lambda e: e.activation(out=w_[:, 1, :], in_=w_[:, 1, :], func=AF.Exp, scale=-0.5), rd=[w_], wr=[w_])
                I("dve", lambda e: e.tensor_tensor(out=w_[:, 0, :], in0=w_[:, 0, :], in1=w_[:, 1, :], op=ALU.mult), rd=[w_], wr=[w_])
                I("dve", lambda e: e.tensor_scalar(out=w_[:, 0, :], in0=w_[:, 0, :], scalar1=rvec[:, hp, 5:6], scalar2=rvec[:, hp, 6:7],
                                                   op0=ALU.mult, op1=ALU.add), rd=[w_, rvec], wr=[w_])
                I("pool", lambda e: e.tensor_tensor(out=w_[:, 0, :], in0=w_[:, 0, :], in1=x_[:, 0, :], op=ALU.add), rd=[w_, x_], wr=[w_])
                o_ = ob.next()
                I("pool", lambda e: e.tensor_tensor(out=o_[:], in0=w_[:, 0, :], in1=x_[:, 1, :], op=ALU.mult), rd=[w_, x_], wr=[o_])
                S.dma("sp", k.ycat[2, :, hp, tsl], o_[:], rd=[o_], wr=[k.ycat])


def rwkv_chunked(k, l, W, nsteps=T):
    S, I = k.S, k.S.I
    C = 64
    TA = 256
    NCH = nsteps // C
    with phase(k) as st:
        sb, ps = k.sb, k.ps
        rvec = sb(st, "rcvec", [128, 2, 7], F32)
        S.dma("sp", rvec[:], W["rvec"], wr=[rvec])
        ysb = [sb(st, "rcy%d" % hp, [128, T], F32) for hp in range(2)]
        if nsteps < T:
            for hp in range(2):
                I("pool", lambda e: e.memset(ysb[hp][:], 0.0), wr=[ysb[hp]])
        _rwkv_chunked_core(k, ysb, nsteps)
        _rwkv_post(k, ysb, rvec)


def _rwkv_chunked_core(k, ysb, nsteps):
    S, I = k.S, k.S.I
    C = 64
    TA = 256
    NCH = nsteps // C
    with phase(k) as st:
        sb, ps = k.sb, k.ps
        msu = sb(st, "rcmsu", [128, 128], F32)
        msl = sb(st, "rcmsl", [128, 128], F32)
        mui = sb(st, "rcmui", [128, 128], F32)
        rmask = sb(st, "rcrmask", [128, TA], F32)
        I("pool", lambda e: e.affine_select(out=msu[:], in_=k.blk64[:], compare_op=ALU.is_gt, fill=0.0, base=0,
                                            pattern=[[1, 128]], channel_multiplier=-1), rd=[k.blk64], wr=[msu])
        I("pool", lambda e: e.affine_select(out=msl[:], in_=k.blk64[:], compare_op=ALU.is_gt, fill=0.0, base=0,
                                            pattern=[[-1, 128]], channel_multiplier=1), rd=[k.blk64], wr=[msl])
        I("pool", lambda e: e.affine_select(out=mui[:], in_=k.blk64[:], compare_op=ALU.is_ge, fill=0.0, base=0,
                                            pattern=[[1, 128]], channel_multiplier=-1), rd=[k.blk64], wr=[mui])
        I("pool", lambda e: e.memset(rmask[:], 1.0), wr=[rmask])
        I("pool", lambda e: e.memset(rmask[:].rearrange("p (c t) -> p c t", t=C)[:, :, 0:1], 0.0), wr=[rmask])
        c5 = [Ring([sb(st, "rcc5%d_%d" % (hp, i), [128, 5, TA], F32) for i in range(2)]) for hp in range(2)]
        wkA = [Ring([sb(st, "rcwa%d_%d" % (hp, i), [128, 4, TA], F32) for i in range(2)]) for hp in range(2)]
        eLr = [Ring([sb(st, "rceL%d_%d" % (hp, i), [128, TA], F32) for i in range(3)]) for hp in range(2)]
        pad = [Ring([sb(st, "rcpad%d_%d" % (hp, i), [128, 4, TA // C, 128], F32) for i in range(3)]) for hp in range(2)]
        for hp in range(2):
            for b in pad[hp].bufs:
                I("pool", lambda e: e.memset(b[:], 0.0), wr=[b])
        NPER = 12
        pers = Ring([sb(st, "rcper%d" % i, [128, 6, 128], F32) for i in range(NPER)])
        tmp = Ring([sb(st, "rctmp%d" % i, [128, 8, 128], F32) for i in range(4)])
        Vp = Ring([sb(st, "rcvp%d" % i, [128, 128], F32) for i in range(8)])
        for b in Vp.bufs:
            I("pool", lambda e: e.memset(b[:], 0.0), wr=[b])
        Sx = [Ring([sb(st, "rcS%d_%d" % (hp, i), [128, 128], F32) for i in range(3)]) for hp in range(2)]
        cw = [Ring([sb(st, "rccw%d_%d" % (hp, i), [128, 3, 128], F32) for i in range(2)]) for hp in range(2)]
        pbB = Ring([ps(st, "rcpb%d" % i, [128, 4, 128], F32) for i in range(4)])
        pbA = [ps(st, "rcpa%d" % hp, [128, 3, 128], F32) for hp in range(2)]
        pbY = [ps(st, "rcpy%d" % hp, [128, 128], F32) for hp in range(2)]
        for hp in range(2):
            for b_ in pad[hp].bufs:
                b_.sd = [Dep() for _ in range(4)]
            for b_ in cw[hp].bufs:
                b_.sd = [Dep() for _ in range(3)]
        for b_ in pers.bufs:
            b_.sd = [Dep() for _ in range(6)]
        for b_ in tmp.bufs:
            b_.sd = [Dep() for _ in range(8)]
        cur = []
        for hp in range(2):
            b0 = Sx[hp].next()
            I("dve", lambda e: e.memset(b0[:], 0.0), wr=[b0])
            cur.append(b0)

        tiles = {}

        def stageA(hp, ti):
            t0 = ti * TA
            c_ = c5[hp].next()
            S.dma("sp", c_[:], k.sc5[hp, :, :, t0:t0 + TA], rd=[k.sc5], wr=[c_])
            w_ = wkA[hp].next()
            e_ = eLr[hp].next()
            p_ = pad[hp].next()
            I("dve", lambda e: e.tensor_tensor_scan(out=w_[:, 0, :], data0=rmask[:], data1=c_[:, 0, :], initial=0.0,
                                                    op0=ALU.mult, op1=ALU.add), rd=[c_, rmask], wr=[w_])
            I("pool", lambda e: e.tensor_tensor(out=w_[:, 1, :], in0=w_[:, 0, :], in1=c_[:, 0, :], op=ALU.subtract), rd=[w_, c_], wr=[w_])
            I("act", lambda e: e.activation(out=e_[:], in_=w_[:, 0, :], func=AF.Exp), rd=[w_], wr=[e_])
            I("act", lambda e: e.activation(out=w_[:, 2, :], in_=w_[:, 0, :], func=AF.Exp, scale=-1.0), rd=[w_], wr=[w_])
            I("act", lambda e: e.activation(out=w_[:, 3, :], in_=w_[:, 1, :], func=AF.Exp), rd=[w_], wr=[w_])

            def v3(ap):
                return ap.rearrange("p (c t) -> p c t", t=C)
            for hh in range(2):
                rs = slice(hh * 64, hh * 64 + 64)
                specs = [(0, c_[rs, 3, :], w_[rs, 3, :]),
                         (1, c_[rs, 4, :], e_[rs, :]),
                         (2, c_[rs, 1, :], w_[rs, 2, :]),
                         (3, c_[rs, 2, :], w_[rs, 2, :])]
                for qi, a_, b_ in specs:
                    eng = "dve" if qi % 2 == 0 else "pool"
                    I(eng, lambda e: e.tensor_tensor(out=p_[rs, qi, :, rs], in0=v3(a_), in1=v3(b_), op=ALU.mult),
                      rd=[c_, w_, e_, p_], wr=[p_.sd[qi]])
            tiles[(hp, ti)] = (c_, e_, p_)

        inst = {}

        def stageB(group):
            ctx = []
            for (ch, hp) in group:
                ti, ci = divmod(ch, TA // C)
                c_, e_, p_ = tiles[(hp, ti)]
                Ap, Rp, Bp, Kp = (p_[:, q, ci, :] for q in range(4))
                pr = pers.next()
                tm = tmp.next()
                pb = pbB.next()
                inst[(ch, hp)] = pr
                ctx.append((ch, hp, p_, Ap, Rp, Bp, Kp, pr, tm, pb))
            for (ch, hp, p_, Ap, Rp, Bp, Kp, pr, tm, pb) in ctx:
                ci = ch % (TA // C)
                mm(k, pb[:, 0:2, :], Bp, p_[:, 0:2, ci, :], True, True, rd=[p_.sd[2], p_.sd[0], p_.sd[1]], wr=[pb])
                mm(k, pb[:, 2:4, :], Kp, p_[:, 0:2, ci, :], True, True, rd=[p_.sd[3], p_.sd[0], p_.sd[1]], wr=[pb])
            for (ch, hp, p_, Ap, Rp, Bp, Kp, pr, tm, pb) in ctx:
                I("dve", lambda e: e.tensor_tensor(out=tm[:, 0, :], in0=pb[:, 0, :], in1=msu[:], op=ALU.mult), rd=[pb, msu], wr=[tm.sd[0]])
                I("dve", lambda e: e.tensor_tensor(out=pr[:, 2, :], in0=pb[:, 1, :], in1=mui[:], op=ALU.mult), rd=[pb, mui], wr=[pr.sd[2]])
                I("dve", lambda e: e.tensor_tensor(out=pr[:, 1, :], in0=pb[:, 2, :], in1=msu[:], op=ALU.mult), rd=[pb, msu], wr=[pr.sd[1]])
                I("dve", lambda e: e.tensor_tensor(out=pr[:, 3, :], in0=pb[:, 3, :], in1=mui[:], op=ALU.mult), rd=[pb, mui], wr=[pr.sd[3]])
                I("pool", lambda e: e.tensor_copy(tm[:, 1, :], k.identf[:]), rd=[k.identf], wr=[tm.sd[1]])
            yield
            for (ch, hp, p_, Ap, Rp, Bp, Kp, pr, tm, pb) in ctx:
                mm(k, pb[:, 0, :], Ap, Bp, True, True, rd=[p_.sd[0], p_.sd[2]], wr=[pb])
                tr(k, pb[:, 1, :], Bp, k.identf[:], rd=[p_.sd[2], k.identf], wr=[pb])
                tr(k, pb[:, 2, :], Kp, k.identf[:], rd=[p_.sd[3], k.identf], wr=[pb])
            for (ch, hp, p_, Ap, Rp, Bp, Kp, pr, tm, pb) in ctx:
                I("dve", lambda e: e.tensor_tensor(out=tm[:, 4, :], in0=pb[:, 0, :], in1=msl[:], op=ALU.mult), rd=[pb, msl], wr=[tm.sd[4]])
                I("act", lambda e: e.activation(out=pr[:, 4:6, :], in_=pb[:, 1:3, :], func=AF.Copy), rd=[pb], wr=[pr.sd[4], pr.sd[5]])
            yield
            for r in range(1, 7):
                so, sn = (0, 2) if r % 2 == 1 else (2, 0)
                qo, qn = (4, 5) if r % 2 == 1 else (5, 4)
                for (ch, hp, p_, Ap, Rp, Bp, Kp, pr, tm, pb) in ctx:
                    if r < 6:
                        mm(k, pb[:, 0:2, :], tm[:, qo, :], tm[:, so:so + 2, :], True, True, rd=[tm.sd[qo], tm.sd[so], tm.sd[so + 1]], wr=[pb])
                        mm(k, pb[:, 2, :], tm[:, so, :], tm[:, qo, :], True, True, rd=[tm.sd[so], tm.sd[qo]], wr=[pb])
                    else:
                        mm(k, pb[:, 1, :], tm[:, qo, :], tm[:, so + 1, :], True, True, rd=[tm.sd[qo], tm.sd[so + 1]], wr=[pb])
                for (ch, hp, p_, Ap, Rp, Bp, Kp, pr, tm, pb) in ctx:
                    if r < 6:
                        I("act", lambda e: e.activation(out=tm[:, sn, :], in_=pb[:, 0, :], func=AF.Copy), rd=[pb], wr=[tm.sd[sn]])
                        I("act", lambda e: e.activation(out=tm[:, qn, :], in_=pb[:, 2, :], func=AF.Copy), rd=[pb], wr=[tm.sd[qn]])
                        I("dve", lambda e: e.tensor_tensor(out=tm[:, sn + 1, :], in0=pb[:, 1, :], in1=tm[:, so + 1, :], op=ALU.add),
                          rd=[pb, tm.sd[so + 1]], wr=[tm.sd[sn + 1]])
                    else:
                        I("dve", lambda e: e.tensor_tensor(out=pr[:, 0, :], in0=pb[:, 1, :], in1=tm[:, so + 1, :], op=ALU.add),
                          rd=[pb, tm.sd[so + 1]], wr=[pr.sd[0]])
                yield

        def stageC(ch, hp):
            ti, ci = divmod(ch, TA // C)
            c_, e_, p_ = tiles[(hp, ti)]
            Ap, Rp = p_[:, 0, ci, :], p_[:, 1, ci, :]
            pr = inst.pop((ch, hp))
            t0 = ch * C
            vp = Vp.next()
            for hh in range(2):
                rs = slice(hh * 64, hh * 64 + 64)
                S.dma("sp", vp[rs, rs], k.vtok[t0:t0 + C, hp, rs], rd=[k.vtok], wr=[vp])
            old = cur[hp]
            new = Sx[hp].next()
            w_ = cw[hp].next()
            pa, py = pbA[hp], pbY[hp]
            elc = e_[:, ci * C + C - 1:ci * C + C]
            mm(k, pa[:, 0, :], Ap, old[:], True, False, rd=[p_.sd[0], old], wr=[pa])
            mm(k, pa[:, 0, :], pr[:, 1, :], vp[:], False, True, rd=[pr.sd[1], vp], wr=[pa])
            I("act", lambda e: e.activation(out=w_[:, 0, :], in_=pa[:, 0, :], func=AF.Copy), rd=[pa], wr=[w_.sd[0]])
            I("pool", lambda e: e.tensor_scalar(out=w_[:, 2, :], in0=old[:], scalar1=elc, scalar2=None, op0=ALU.mult), rd=[old, e_], wr=[w_.sd[2]])
            yield
            mm(k, pa[:, 1, :], pr[:, 0, :], w_[:, 0, :], True, True, rd=[pr.sd[0], w_.sd[0]], wr=[pa])
            I("dve", lambda e: e.tensor_copy(w_[:, 1, :], pa[:, 1, :]), rd=[pa], wr=[w_.sd[1]])
            yield
            mm(k, pa[:, 2, :], pr[:, 4, :], w_[:, 1, :], True, False, rd=[pr.sd[4], w_.sd[1]], wr=[pa])
            mm(k, pa[:, 2, :], pr[:, 5, :], vp[:], False, True, rd=[pr.sd[5], vp], wr=[pa])
            I("dve", lambda e: e.scalar_tensor_tensor(out=new[:], in0=pa[:, 2, :], scalar=elc, in1=w_[:, 2, :],
                                                      op0=ALU.mult, op1=ALU.add), rd=[pa, e_, w_.sd[2]], wr=[new])
            mm(k, py[:], old[:], Rp, True, False, rd=[old, p_.sd[1]], wr=[py])
            mm(k, py[:], w_[:, 1, :], pr[:, 2, :], False, False, rd=[w_.sd[1], pr.sd[2]], wr=[py])
            mm(k, py[:], vp[:], pr[:, 3, :], False, True, rd=[vp, pr.sd[3]], wr=[py])
            for hh in range(2):
                rs = slice(hh * 64, hh * 64 + 64)
                I("act", lambda e: e.activation(out=ysb[hp][rs, t0:t0 + C], in_=py[rs, rs], func=AF.Copy), rd=[py], wr=[ysb[hp]])
            cur[hp] = new
            yield

        GC = 2
        ngroups = NCH // GC
        def group(g):
            return [(g * GC + c, hp) for c in range(GC) for hp in range(2)]
        def ensureA(g):
            for (ch, hp) in group(g):
                ti = ch // (TA // C)
                if (hp, ti) not in tiles:
                    stageA(hp, ti)
        ensureA(0)
        run_rr([stageB(group(0))])
        for g in range(ngroups):
            gens = []
            if g + 1 < ngroups:
                ensureA(g + 1)
                gens.append(stageB(group(g + 1)))
            for hp in range(2):
                gens.append(chain_gens([stageC(ch, hp_) for (ch, hp_) in group(g) if hp_ == hp]))
            run_rr(gens)


def _rwkv_post(k, ysb, rvec):
    S, I = k.S, k.S.I
    with phase(k) as st:
        sb, ps = k.sb, k.ps
        pbB = Ring([ps(st, "rcpq%d" % i, [128, 4, 128], F32) for i in range(4)])
        ld = Ring([sb(st, "rcld%d" % i, [128, 2, 512], F32) for i in range(2)])
        wk = Ring([sb(st, "rcwk%d" % i, [128, 3, 512], F32) for i in range(2)])
        ob = Ring([sb(st, "rcob%d" % i, [128, 512], BF16) for i in range(2)])
        pflat = [Buf(b_.t.rearrange("p a b -> p (a b)"), excl=True) for b_ in pbB.bufs]
        pcount = [0]

        def post_gen(it):
            if True:
                hp, tt = it
                pmt, pvt = pflat[(pcount[0] * 2) % 4], pflat[(pcount[0] * 2 + 1) % 4]
                pcount[0] += 1
                tsl = slice(tt * 512, (tt + 1) * 512)
                x_ = ld.next()
                S.dma("sp", x_[:, 0, :], k.bon[hp, :, tsl], rd=[k.bon], wr=[x_])
                S.dma("sp", x_[:, 1, :], k.gg[hp, :, tsl], rd=[k.gg], wr=[x_])
                w_ = wk.next()
                mm(k, pmt[:], k.blk64[:], ysb[hp][:, tsl], True, True, rd=[k.blk64, ysb[hp]], wr=[pmt])
                I("dve", lambda e: e.scalar_tensor_tensor(out=w_[:, 0, :], in0=pmt[:], scalar=-1.0 / 64.0, in1=ysb[hp][:, tsl],
                                                          op0=ALU.mult, op1=ALU.add), rd=[pmt, ysb[hp]], wr=[w_])
                yield
                I("act", lambda e: e.activation(out=w_[:, 1, :], in_=w_[:, 0, :], func=AF.Square), rd=[w_], wr=[w_])
                mm(k, pvt[:], k.blk64[:], w_[:, 1, :], True, True, rd=[k.blk64, w_], wr=[pvt])
                I("act", lambda e: e.activation(out=w_[:, 1, :], in_=pvt[:], func=AF.Ln, bias=k.epsgn[:, 0:1], scale=1.0 / 64.0),
                  rd=[pvt, k.epsgn], wr=[w_])
                yield
                I("act", lambda e: e.activation(out=w_[:, 1, :], in_=w_[:, 1, :], func=AF.Exp, scale=-0.5), rd=[w_], wr=[w_])
                yield
                I("dve", lambda e: e.tensor_tensor(out=w_[:, 0, :], in0=w_[:, 0, :], in1=w_[:, 1, :], op=ALU.mult), rd=[w_], wr=[w_])
                I("dve", lambda e: e.tensor_scalar(out=w_[:, 0, :], in0=w_[:, 0, :], scalar1=rvec[:, hp, 5:6], scalar2=rvec[:, hp, 6:7],
                                                   op0=ALU.mult, op1=ALU.add), rd=[w_, rvec], wr=[w_])
                I("pool", lambda e: e.tensor_tensor(out=w_[:, 0, :], in0=w_[:, 0, :], in1=x_[:, 0, :], op=ALU.add), rd=[w_, x_], wr=[w_])
                o_ = ob.next()
                I("pool", lambda e: e.tensor_tensor(out=o_[:], in0=w_[:, 0, :], in1=x_[:, 1, :], op=ALU.mult), rd=[w_, x_], wr=[o_])
                S.dma("sp", k.ycat[2, :, hp, tsl], o_[:], rd=[o_], wr=[k.ycat])
                yield
        interleave(post_gen, [(hp, tt) for hp in range(2) for tt in range(NT)], 1)


def precast_mlp(k, l, W):
    with phase(k) as st:
        cast_weight(k, st, W["w1"], k.w1s, k.w1s, 1024, 4096, "w1", to_dram=True)
    with phase(k) as st:
        cast_weight(k, st, W["w2m"], k.w2s, k.w2s, 4096, 1024, "w2", to_dram=True, dst_fn=lambda n0, n1: k.w2s[n0 // 128])


def merge_phase(k, l, W, h1T):
    S, I = k.S, k.S.I
    with phase(k) as st:
        sb, ps = k.sb, k.ps
        wbr = sb(st, "mgwbr", [128, 8, 1024], BF16)
        wout = sb(st, "mgwout", [128, 8, 1024], BF16)
        with phase(k) as st2:
            cast_weight(k, st2, W["wbr"], wbr, wbr, 1024, 1024, "br")
        with phase(k) as st2:
            cast_weight(k, st2, W["wout"], wout, wout, 1024, 1024, "wo")
        ln = sb(st, "mgln", [128, 8, 2], F32)
        S.dma("sp", ln[:], W["ln1"], wr=[ln])
        yt = Ring([sb(st, "mgyt%d" % i, [128, 4, 2, 512], BF16) for i in range(2)])
        gt = Ring([sb(st, "mggt%d" % i, [128, 4, 512], BF16) for i in range(3)])
        acc = Ring([sb(st, "mgacc%d" % i, [128, 2, 512], F32) for i in range(2)])
        mg = Ring([sb(st, "mgmg%d" % i, [128, 8, 512], BF16) for i in range(1)])
        hr = Ring([sb(st, "mghr%d" % i, [128, 8, 512], F32) for i in range(1)])
        z = Ring([sb(st, "mgz%d" % i, [128, 8, 512], F32) for i in range(1)])
        sq = sb(st, "mgsq", [128, 8, 512], F32)
        msb = sb(st, "mgmsb", [128, 512], F32)
        up = Ring([ps(st, "mgup%d" % i, [128, 512], F32) for i in range(4)])
        mps = ps(st, "mgmps", [128, 512], F32)
        vps = ps(st, "mgvps", [128, 512], F32)
        gview = k.gsc.t.rearrange("(n m) p t -> m p n t", n=4)
        for tt in range(NT):
            tsl = slice(tt * 512, (tt + 1) * 512)
            y_ = yt.next()
            for n in range(4):
                S.dma("sp", y_[:, n, :, :], k.ycat[n, :, :, tsl], rd=[k.ycat], wr=[y_])
            h_ = hr.next()
            S.dma("sp", h_[:], k.hres[:, :, tsl], rd=[k.hres], wr=[h_])
            m_ = mg.next()
            for mc in range(8):
                msl = slice(mc * 128, (mc + 1) * 128)
                g_ = gt.next()
                S.dma("sp", g_[:], gview[mc][:, :, tsl], rd=[k.gsc], wr=[g_])
                a_ = acc.next()
                for n in range(4):
                    u_ = up.next()
                    for kc in range(2):
                        mm(k, u_[:], wbr[:, n * 2 + kc, msl], y_[:, n, kc, :], kc == 0, kc == 1, rd=[wbr, y_], wr=[u_])
                    if n == 0:
                        I("dve", lambda e: e.tensor_tensor(out=a_[:, 0, :], in0=u_[:], in1=g_[:, 0, :], op=ALU.mult),
                          rd=[u_, g_], wr=[a_])
                    else:
                        I("dve", lambda e: e.tensor_tensor(out=a_[:, 1, :], in0=u_[:], in1=g_[:, n, :], op=ALU.mult),
                          rd=[u_, g_], wr=[a_])
                        if n < 3:
                            I("pool", lambda e: e.tensor_tensor(out=a_[:, 0, :], in0=a_[:, 0, :], in1=a_[:, 1, :], op=ALU.add),
                              rd=[a_], wr=[a_])
                        else:
                            I("pool", lambda e: e.tensor_tensor(out=m_[:, mc, :], in0=a_[:, 0, :], in1=a_[:, 1, :], op=ALU.add),
                              rd=[a_], wr=[m_])
            z_ = z.next()
            for mc in range(8):
                msl = slice(mc * 128, (mc + 1) * 128)
                u_ = up.next()
                for kc in range(8):
                    mm(k, u_[:], wout[:, kc, msl], m_[:, kc, :], kc == 0, kc == 7, rd=[wout, m_], wr=[u_])
                I("dve", lambda e: e.scalar_tensor_tensor(out=z_[:, mc, :], in0=h_[:, mc, :], scalar=ALPHA, in1=u_[:],
                                                          op0=ALU.mult, op1=ALU.add), rd=[h_, u_], wr=[z_])

            def outs(c, dn, g_ap, b_ap):
                I("pool", lambda e: e.tensor_scalar(out=dn, in0=dn, scalar1=g_ap, scalar2=b_ap, op0=ALU.mult, op1=ALU.add),
                  rd=[z_, ln], wr=[z_])
            layer_norm_tile(k, (mps, msb, sq, vps), z_, ln, LN_EPS, outs)
            I("act", lambda e: e.activation(out=h1T[:, :, tsl], in_=z_[:], func=AF.Copy), rd=[z_], wr=[h1T])
            S.dma("sp", k.h1res[:, :, tsl], z_[:], rd=[z_], wr=[k.h1res])


def mlp_phase(k, l, W, h1T, p_l, out, last):
    S, I = k.S, k.S.I
    with phase(k) as st:
        sb, ps = k.sb, k.ps
        pgw = sb(st, "mlpgw", [128, 8, 1024], BF16)
        plw = sb(st, "mlplw", [128, 2, 1024], BF16)
        with phase(k) as st2:
            cast_weight(k, st2, W["pgw"], pgw, pgw, 1024, 1024, "pg")
        with phase(k) as st2:
            cast_weight(k, st2, W["plew"], plw, plw, 256, 1024, "pl")
        with phase(k) as st2:
            pin = Ring([sb(st2, "mlpin%d" % i, [128, 256], F32) for i in range(2)])
            tps = Ring([ps(st2, "mltps%d" % i, [128, 2, 128], F32) for i in range(2)])
            pT = sb(st2, "mlpT", [128, 2, T], BF16)
            for blk in range(NB):
                pi_ = pin.next()
                S.dma("sp", pi_[:], p_l[blk * 128:(blk + 1) * 128, :], wr=[pi_])
                t_ = tps.next()
                for c in range(2):
                    tr(k, t_[:, c, :], pi_[:, c * 128:(c + 1) * 128], k.identf[:], rd=[pi_, k.identf], wr=[t_])
                I("act", lambda e: e.activation(out=pT[:, :, blk * 128:(blk + 1) * 128], in_=t_[:], func=AF.Copy), rd=[t_], wr=[pT])
            S.dma("sp", k.pTs[:], pT[:], rd=[pT], wr=[k.pTs])
        ln = sb(st, "mlln", [128, 8, 2], F32)
        S.dma("sp", ln[:], W["ln2"], wr=[ln])
        w1r = Ring([sb(st, "mlw1%d" % i, [128, 8, 512], BF16) for i in range(2)])
        w2r = Ring([sb(st, "mlw2%d" % i, [128, 32, 128], BF16) for i in range(2)])
        pTr = Ring([sb(st, "mlpT%d" % i, [128, 2, 512], BF16) for i in range(1)])
        a = sb(st, "mla", [128, 32, 512], BF16)
        rl = Ring([sb(st, "mlrl%d" % i, [128, 512], F32) for i in range(2)])
        hr = sb(st, "mlhr", [128, 8, 512], F32)
        z = sb(st, "mlz", [128, 8, 512], F32)
        msb = sb(st, "mlmsb", [128, 512], F32)
        sg = Ring([sb(st, "mlsg%d" % i, [128, 2, 512], F32) for i in range(1)])
        up = Ring([ps(st, "mlup%d" % i, [128, 512], F32) for i in range(3)])
        gp = Ring([ps(st, "mlgp%d" % i, [128, 512], F32) for i in range(2)])
        mps = ps(st, "mlmps", [128, 512], F32)
        vps = ps(st, "mlvps", [128, 512], F32)
        tpo = ps(st, "mltpo", [128, 4, 128], F32)
        ot = Ring([sb(st, "mlot%d" % i, [128, D], F32) for i in range(1)]) if last else None
        hb = Ring([sb(st, "mlhb%d" % i, [128, 8, 512], BF16) for i in range(1)]) if not last else None
        for tt in range(NT):
            tsl = slice(tt * 512, (tt + 1) * 512)
            S.dma("sp", hr[:], k.h1res[:, :, tsl], rd=[k.h1res], wr=[hr])
            pT = pTr.next()
            S.dma("sp", pT[:], k.pTs[:, :, tsl], rd=[k.pTs], wr=[pT])
            for fg in range(8):
                w1_ = w1r.next()
                S.dma("sp", w1_[:], k.w1s[:, :, fg * 512:(fg + 1) * 512], rd=[k.w1s], wr=[w1_])
                for f8 in range(4):
                    fc = fg * 4 + f8
                    u_ = up.next()
                    for kc in range(8):
                        mm(k, u_[:], w1_[:, kc, f8 * 128:(f8 + 1) * 128], h1T[:, kc, tsl], kc == 0, kc == 7, rd=[w1_, h1T], wr=[u_])
                    r_ = rl.next()
                    I("act", lambda e: e.activation(out=r_[:], in_=u_[:], func=AF.Relu), rd=[u_], wr=[r_])
                    I("pool", lambda e: e.tensor_tensor(out=a[:, fc, :], in0=r_[:], in1=r_[:], op=ALU.mult), rd=[r_], wr=[a])
            for mc in range(8):
                msl = slice(mc * 128, (mc + 1) * 128)
                s_ = sg.next()
                g_ = gp.next()
                for kc in range(8):
                    mm(k, g_[:], pgw[:, kc, msl], h1T[:, kc, tsl], kc == 0, kc == 7, rd=[pgw, h1T], wr=[g_])
                I("act", lambda e: e.activation(out=s_[:, 0, :], in_=g_[:], func=AF.Sigmoid), rd=[g_], wr=[s_])
                g2_ = gp.next()
                for kc in range(2):
                    mm(k, g2_[:], plw[:, kc, msl], pT[:, kc, :], kc == 0, kc == 1, rd=[plw, pT], wr=[g2_])
                I("dve", lambda e: e.tensor_tensor(out=s_[:, 1, :], in0=g2_[:], in1=s_[:, 0, :], op=ALU.mult), rd=[g2_, s_], wr=[s_])
                u_ = up.next()
                w2_ = w2r.next()
                S.dma("sp", w2_[:], k.w2s[mc], rd=[k.w2s], wr=[w2_])
                for fc in range(32):
                    mm(k, u_[:], w2_[:, fc, :], a[:, fc, :], fc == 0, fc == 31, rd=[w2_, a], wr=[u_])
                I("dve", lambda e: e.scalar_tensor_tensor(out=z[:, mc, :], in0=hr[:, mc, :], scalar=ALPHA, in1=u_[:],
                                                          op0=ALU.mult, op1=ALU.add), rd=[hr, u_], wr=[z])
                I("pool", lambda e: e.tensor_tensor(out=z[:, mc, :], in0=z[:, mc, :], in1=s_[:, 1, :], op=ALU.add), rd=[z, s_], wr=[z])

            def outs(c, dn, g_ap, b_ap):
                I("pool", lambda e: e.tensor_scalar(out=dn, in0=dn, scalar1=g_ap, scalar2=b_ap, op0=ALU.mult, op1=ALU.add),
                  rd=[z, ln], wr=[z])
            layer_norm_tile(k, (mps, msb, hr, vps), z, ln, LN_EPS, outs)
            if not last:
                hb_ = hb.next()
                I("act", lambda e: e.activation(out=hb_[:], in_=z[:], func=AF.Copy), rd=[z], wr=[hb_])
                S.dma("sp", k.hTs[:, :, tsl], hb_[:], rd=[hb_], wr=[k.hTs])
                S.dma("sp", k.hres[:, :, tsl], z[:], rd=[z], wr=[k.hres])
            else:
                for b in range(4):
                    o_ = ot.next()
                    for half in range(2):
                        for j in range(4):
                            c = half * 4 + j
                            tr(k, tpo[:, j, :], z[:, c, b * 128:(b + 1) * 128], k.identf[:], rd=[z, k.identf], wr=[tpo])
                        I("act", lambda e: e.activation(out=o_[:, half * 512:(half + 1) * 512],
                                                        in_=tpo[:].rearrange("p a b -> p (a b)"), func=AF.Copy), rd=[tpo], wr=[o_])
                    r0 = tt * 512 + b * 128
                    S.dma("sp", out[r0:r0 + 128, :], o_[:], rd=[o_], wr=[Dep()])


_CACHE = {}


def kernel(**inputs):
    if "nc" not in _CACHE:
        _CACHE["nc"] = build_program(depth=DEPTH)[0]
    nc = _CACHE["nc"]
    shared = {}
    for l in range(DEPTH):
        for n, v in prep_layer(inputs, l).items():
            shared["%s_%d" % (n, l)] = v
    x = np.asarray(inputs["x"], np.float32)
    p = np.asarray(inputs["p"], np.float32)
    in_maps = []
    for b in range(8):
        m = dict(shared)
        m["x"] = np.ascontiguousarray(x[b])
        m["p"] = np.ascontiguousarray(p[:, b])
        in_maps.append(m)
    res = run_bass_kernel_spmd(nc, in_maps, core_ids=list(range(8)))
    return np.stack([np.asarray(r["out"], np.float32) for r in res.results], axis=0)
```

```python
import math
import contextlib
import numpy as np
import concourse.bass as bass
import concourse.mybir as mybir
from concourse.bass_utils import run_bass_kernel_spmd

F32 = mybir.dt.float32
BF16 = mybir.dt.bfloat16
AF = mybir.ActivationFunctionType
ALU = mybir.AluOpType
AX = mybir.AxisListType

T = 4096
D = 1024
NT = 8
NB = 32
DEPTH = 2
ALPHA = (2 * DEPTH) ** 0.25
LN_EPS = 1e-5
GN_EPS = 64e-5
PI = math.pi


class Dep:
    __slots__ = ("w", "r", "excl")

    def __init__(self):
        self.w = {}
        self.r = {}
        self.excl = False


class Buf:
    def __init__(self, t, excl=False):
        self.t = t
        self.dep = Dep()
        self.dep.excl = excl

    def __getitem__(self, k):
        return self.t[k]


class Ring:
    def __init__(self, bufs):
        self.bufs = bufs
        self.i = 0

    def next(self):
        b = self.bufs[self.i % len(self.bufs)]
        self.i += 1
        return b


class Sched:
    EPOCH = 30000
    NDMA = 6
    NEPOCH = 8
    import os as _os
    qmap = {} if _os.environ.get("KNOB_POOLQ") else {"pool": "sp"}

    def __init__(self, nc):
        self.nc = nc
        self.eng = {"pe": nc.tensor, "act": nc.scalar, "dve": nc.vector,
                    "pool": nc.gpsimd, "sp": nc.sync}
        self.sem = {}
        self.cnt = {}
        self.seen = {e: {} for e in self.eng}
        self.nsem = 0
        self.ninst = 0
        self.allsems = []
        self.epool = {e: [self._alloc("e_%s_%d" % (e, i)) for i in range(self.NEPOCH)] for e in ("pe", "act", "dve", "pool")}
        self.dq = {}
        for q in ("sp", "pool"):
            self.dq[q] = {"sems": [self._alloc("dq_%s_%d" % (q, i)) for i in range(self.NDMA)], "n": 0}
        for e in ("pe", "act", "dve", "pool"):
            self._new_epoch(e)

    def _alloc(self, name):
        self.nsem += 1
        h = self.nc.alloc_semaphore(name)
        self.allsems.append(h)
        return h

    def _new_epoch(self, e):
        self.sem[e] = self.epool[e].pop(0)
        self.cnt[e] = 0

    def _wait(self, eng, sem, val):
        key = id(sem)
        if self.seen[eng].get(key, 0) >= val:
            return
        self.eng[eng].wait_ge(sem, val)
        self.seen[eng][key] = val

    def _gather(self, eng, rd, wr, same_ok=True):
        need = {}

        def add(p):
            pe, sem, val = p
            if pe == eng and (same_ok or eng == "pe"):
                return
            k = id(sem)
            if k not in need or need[k][1] < val:
                need[k] = (sem, val)
        for d in rd:
            for p in d.w.values():
                add(p)
        for d in wr:
            for p in d.w.values():
                add(p)
            for p in d.r.values():
                add(p)
        for sem, val in need.values():
            self._wait(eng, sem, val)

    @staticmethod
    def _deps(lst):
        return [x.dep if isinstance(x, Buf) else x for x in lst]

    @staticmethod
    def _merge(dct, tok):
        k = id(tok[1])
        if k not in dct or dct[k][2] < tok[2]:
            dct[k] = tok

    def I(self, eng, fn, rd=(), wr=()):
        rd = self._deps(rd)
        wr = self._deps(wr)
        ex = [d for d in rd if d.excl and d not in wr]
        if ex:
            rd = [d for d in rd if not d.excl]
            wr = list(wr) + ex
        self._gather(eng, rd, wr, same_ok=False)
        if self.cnt[eng] >= self.EPOCH:
            self._new_epoch(eng)
        ins = fn(self.eng[eng])
        self.ninst += 1
        self.cnt[eng] += 1
        ins.then_inc(self.sem[eng], 1)
        tok = (eng, self.sem[eng], self.cnt[eng])
        for d in rd:
            self._merge(d.r, tok)
        for d in wr:
            d.w = {id(tok[1]): tok}
            d.r = {}
        return ins

    def dma(self, q, out, in_, rd=(), wr=()):
        rd = self._deps(rd)
        wr = self._deps(wr)
        q = self.qmap.get(q, q)
        dq = self.dq[q]
        j = dq["n"]
        dq["n"] += 1
        sem = dq["sems"][j % self.NDMA]
        val = 16 * (j // self.NDMA + 1)
        if val > 16:
            self._wait(q, sem, val - 16)
        self._gather(q, rd, wr, same_ok=False)
        ins = self.eng[q].dma_start(out=out, in_=in_)
        self.ninst += 1
        ins.then_inc(sem, 16)
        tok = ("dma", sem, val)
        for d in rd:
            self._merge(d.r, tok)
        for d in wr:
            self._merge(d.w, tok)
            d.r = {}
        return ins

    def barrier(self, force=False):
        for w in ("pe", "act", "dve", "pool", "sp"):
            for q, dq in self.dq.items():
                n = dq["n"]
                for i in range(self.NDMA):
                    cnt_i = len(range(i, n, self.NDMA))
                    if cnt_i > 0:
                        self._wait(w, dq["sems"][i], 16 * cnt_i)
            for e in ("pe", "act", "dve", "pool"):
                if e != w and self.cnt[e] > 0:
                    self._wait(w, self.sem[e], self.cnt[e])

    def finish(self):
        for q, dq in self.dq.items():
            n = dq["n"]
            for i in range(self.NDMA):
                cnt_i = len(range(i, n, self.NDMA))
                if cnt_i > 0:
                    self._wait("sp", dq["sems"][i], 16 * cnt_i)
        for e in ("pe", "act", "dve", "pool"):
            if self.cnt[e] > 0:
                self._wait("sp", self.sem[e], self.cnt[e])


def _chunkcol(v):
    v = np.asarray(v, np.float32)
    return np.ascontiguousarray(v.reshape(-1, 128).T)


def prep_layer(inp, l):
    f = np.float32
    w_in = np.asarray(inp["w_in"][l], f)
    o = {}
    o["winF"] = np.ascontiguousarray(np.concatenate(
        [w_in[:, 0:256], w_in[:, 256:512], w_in[:, 512:768], w_in[:, 1028:2052],
         w_in[:, 2052:2308], w_in[:, 2308:2564], w_in[:, 2820:6916]], axis=1))
    o["winT"] = np.ascontiguousarray(np.concatenate([w_in[:, 768:1024], w_in[:, 2564:2820]], axis=1))
    o["wff"] = np.ascontiguousarray(w_in[:, 1024:1028])
    lre = np.asarray(inp["s5_lambda_re"][l], f)
    lim = np.asarray(inp["s5_lambda_im"][l], f)
    ldt = np.repeat(np.asarray(inp["s5_log_dt"][l], f)[:, None], 64, axis=1)
    def gp(a):
        return a.reshape(8, 2, 64).transpose(1, 2, 0).reshape(128, 8)
    o["s5par"] = np.ascontiguousarray(np.stack([gp(lre), gp(lim), gp(ldt)], axis=2))
    b_re = np.asarray(inp["s5_b_re"][l], f)
    b_im = np.asarray(inp["s5_b_im"][l], f)
    c_re = np.asarray(inp["s5_c_re"][l], f)
    c_im = np.asarray(inp["s5_c_im"][l], f)
    Bw = np.zeros((8, 128, 2, 128), f)
    Cw = np.zeros((8, 128, 2, 128), f)
    for g in range(16):
        rt = g // 2
        k0 = (g % 8) * 16
        m0 = (g % 2) * 64
        Bw[rt, k0:k0 + 16, 0, m0:m0 + 64] = b_re[g].T
        Bw[rt, k0:k0 + 16, 1, m0:m0 + 64] = b_im[g].T
        Cw[rt, m0:m0 + 64, 0, k0:k0 + 16] = c_re[g].T
        Cw[rt, m0:m0 + 64, 1, k0:k0 + 16] = c_im[g].T
    o["s5B"] = np.ascontiguousarray(Bw.transpose(1, 0, 2, 3))
    o["s5C"] = np.ascontiguousarray(Cw.transpose(1, 0, 2, 3))
    o["s5D"] = _chunkcol(np.asarray(inp["s5_d"][l], f).reshape(-1))
    o["gluw"] = np.asarray(inp["s5_glu_w"][l], f)
    o["glub"] = _chunkcol(inp["s5_glu_b"][l])
    o["foxb"] = np.asarray(inp["fox_f_bias"][l], f).reshape(4, 1)
    o["mu"] = _chunkcol(inp["rwkv_mu"][l])
    vec = [inp["rwkv_w0"][l], inp["rwkv_a0"][l], inp["rwkv_k_k"][l], inp["rwkv_k_a"][l],
           np.asarray(inp["rwkv_r_k"][l]).reshape(-1), inp["rwkv_lnx_g"][l], inp["rwkv_lnx_b"][l]]
    o["rvec"] = np.ascontiguousarray(np.stack([_chunkcol(v) for v in vec], axis=2))
    o["wa2"] = np.ascontiguousarray(np.concatenate([np.asarray(inp["rwkv_w2"][l], f),
                                                     np.asarray(inp["rwkv_a2"][l], f)], axis=0))
    o["g2"] = np.asarray(inp["rwkv_g2"][l], f)
    o["wbr"] = np.ascontiguousarray(np.asarray(inp["w_branch"][l], f).reshape(1024, 1024))
    o["wout"] = np.asarray(inp["w_out"][l], f)
    o["ln1"] = np.ascontiguousarray(np.stack([_chunkcol(inp["ln1_g"][l]), _chunkcol(inp["ln1_b"][l])], axis=2))
    o["ln2"] = np.ascontiguousarray(np.stack([_chunkcol(inp["ln2_g"][l]), _chunkcol(inp["ln2_b"][l])], axis=2))
    o["w1"] = np.asarray(inp["mlp_w1"][l], f)
    o["w2m"] = np.asarray(inp["mlp_w2"][l], f)
    o["plew"] = np.asarray(inp["ple_w"][l], f)
    o["pgw"] = np.asarray(inp["ple_gate_w"][l], f)
    return {k: np.ascontiguousarray(v, dtype=f) for k, v in o.items()}


LAYER_SHAPES = {
    "winF": [1024, 6400], "winT": [1024, 512], "wff": [1024, 4], "s5par": [128, 8, 3],
    "s5B": [128, 8, 2, 128], "s5C": [128, 8, 2, 128], "s5D": [128, 2], "gluw": [256, 256], "glub": [128, 2],
    "foxb": [4, 1], "mu": [128, 8], "rvec": [128, 2, 7], "wa2": [128, 256], "g2": [128, 256],
    "wbr": [1024, 1024], "wout": [1024, 1024], "ln1": [128, 8, 2], "ln2": [128, 8, 2],
    "w1": [1024, 4096], "w2m": [4096, 1024], "plew": [256, 1024], "pgw": [1024, 1024],
}


class K:
    pass


@contextlib.contextmanager
def phase(k):
    with contextlib.ExitStack() as st:
        yield st
        k.S.barrier()


def build_program(depth=DEPTH, dbg=(), stop_after=None, only=None, nsteps=T):
    nc = bass.Bass("TRN2", target_bir_lowering=False)
    S = Sched(nc)
    k = K()
    k.nc, k.S = nc, S
    k.dbg = {}
    x_in = nc.dram_tensor("x", [T, D], F32, kind="ExternalInput").ap()
    p_in = nc.dram_tensor("p", [DEPTH, T, 256], F32, kind="ExternalInput").ap()
    out = nc.dram_tensor("out", [T, D], F32, kind="ExternalOutput").ap()
    W = []
    for l in range(DEPTH):
        W.append({n: nc.dram_tensor("%s_%d" % (n, l), s, F32, kind="ExternalInput").ap()
                  for n, s in LAYER_SHAPES.items()})
    k.stop_after = stop_after
    k.only = only
    k.nsteps = nsteps

    def scratch(name, shape, dt):
        kind = "ExternalOutput" if name in dbg else "Internal"
        import os
        if os.environ.get("KNOB_TINY") and name != "hres":
            shape = [2, 2]
        return Buf(nc.dram_tensor(name, shape, dt, kind=kind).ap())
    k.hres = scratch("hres", [128, 8, T], F32)
    k.h1res = scratch("h1res", [128, 8, T], F32)
    k.gsc = scratch("gsc", [32, 128, T], BF16)
    k.ycat = scratch("ycat", [4, 128, 2, T], BF16)
    k.rwxs = scratch("rwxs", [8, 128, T], F32)
    k.sc5 = scratch("sc5", [2, 128, 5, T], F32)
    k.bon = scratch("bon", [2, 128, T], F32)
    k.gg = scratch("gg", [2, 128, T], F32)
    k.vtok = scratch("vtok", [T, 2, 128], F32)
    k.uTf = scratch("uTf", [128, 2, T], F32)
    k.qk = scratch("qk", [4, 128, 2, T], BF16)
    k.vv = scratch("vv", [2, 128, NB, 256], BF16)
    k.negc = scratch("negc", [4, T], F32)
    k.hTs = scratch("hTs", [128, 8, T], BF16)
    k.pTs = scratch("pTs", [128, 2, T], BF16)
    k.w1s = scratch("w1s", [128, 8, 4096], BF16)
    k.w2s = scratch("w2s", [8, 128, 32, 128], BF16)

    es = contextlib.ExitStack()

    uid = [0]

    def sb(stack, name, shape, dt):
        uid[0] += 1
        return Buf(stack.enter_context(nc.sbuf_tensor("%s_u%d" % (name, uid[0]), shape, dt)))

    def ps(stack, name, shape, dt=F32):
        uid[0] += 1
        full = [128, 512] if dt == F32 else [128, 1024]
        t = stack.enter_context(nc.psum_tensor("%s_u%d" % (name, uid[0]), full, dt))
        n = 1
        for d_ in shape[1:]:
            n *= d_
        if len(shape) == 2:
            v = t[0:shape[0], 0:n]
        else:
            assert len(shape) == 3
            v = t[0:shape[0], 0:n].rearrange("p (a b) -> p a b", a=shape[1])
        return Buf(v, excl=True)
    k.sb, k.ps = sb, ps

    with es:
        k.identf = sb(es, "identf", [128, 128], F32)
        k.identb = sb(es, "identb", [128, 128], BF16)
        k.onesm = sb(es, "onesm", [128, 128], F32)
        k.blk64 = sb(es, "blk64", [128, 128], F32)
        k.nmi = sb(es, "nmi", [128, 128], F32)
        k.nms = sb(es, "nms", [128, 128], F32)
        k.m01 = sb(es, "m01", [128, 128], F32)
        k.ones = sb(es, "ones", [128, 512], F32)
        k.sel4 = sb(es, "sel4", [4, 4, 128], F32)
        k.epsln = sb(es, "epsln", [128, 1], F32)
        k.epsgn = sb(es, "epsgn", [128, 1], F32)
        k.one1 = sb(es, "one1", [128, 1], F32)
        k.mhalf = sb(es, "mhalf", [128, 1], F32)
        I = S.I
        I("pool", lambda e: e.memset(k.epsln[:], LN_EPS), wr=[k.epsln])
        I("pool", lambda e: e.memset(k.epsgn[:], GN_EPS), wr=[k.epsgn])
        I("pool", lambda e: e.memset(k.one1[:], 1.0), wr=[k.one1])
        I("pool", lambda e: e.memset(k.mhalf[:], -0.5), wr=[k.mhalf])
        I("pool", lambda e: e.memset(k.identf[:], 0.0), wr=[k.identf])
        I("pool", lambda e: e.affine_select(out=k.identf[:], in_=k.identf[:], compare_op=ALU.not_equal, fill=1.0,
                                            base=0, pattern=[[-1, 128]], channel_multiplier=1),
          rd=[k.identf], wr=[k.identf])
        I("pool", lambda e: e.tensor_copy(k.identb[:], k.identf[:]), rd=[k.identf], wr=[k.identb])
        I("pool", lambda e: e.memset(k.onesm[:], 1.0 / 1024.0), wr=[k.onesm])
        I("pool", lambda e: e.memset(k.ones[:], 1.0), wr=[k.ones])
        I("pool", lambda e: e.memset(k.blk64[:], 0.0), wr=[k.blk64])
        I("pool", lambda e: e.memset(k.blk64[0:64, 0:64], 1.0), wr=[k.blk64])
        I("pool", lambda e: e.memset(k.blk64[64:128, 64:128], 1.0), wr=[k.blk64])
        I("pool", lambda e: e.memset(k.nmi[:], 0.0), wr=[k.nmi])
        I("pool", lambda e: e.affine_select(out=k.nmi[:], in_=k.nmi[:], compare_op=ALU.is_ge, fill=-1e30,
                                            base=0, pattern=[[-1, 128]], channel_multiplier=1), rd=[k.nmi], wr=[k.nmi])
        I("pool", lambda e: e.memset(k.nms[:], 0.0), wr=[k.nms])
        I("pool", lambda e: e.affine_select(out=k.nms[:], in_=k.nms[:], compare_op=ALU.is_gt, fill=-1e30,
                                            base=0, pattern=[[-1, 128]], channel_multiplier=1), rd=[k.nms], wr=[k.nms])
        I("pool", lambda e: e.affine_select(out=k.m01[:], in_=k.ones[:, 0:128], compare_op=ALU.is_gt, fill=0.0,
                                            base=0, pattern=[[-1, 128]], channel_multiplier=1), rd=[k.ones], wr=[k.m01])
        I("pool", lambda e: e.affine_select(out=k.sel4[:], in_=k.ones[0:4, :].rearrange("p (h m) -> p h m", h=4),
                                            compare_op=ALU.is_equal, fill=0.0, base=0, pattern=[[-1, 4], [0, 128]],
                                            channel_multiplier=1), rd=[k.ones], wr=[k.sel4])

        if stop_after != "consts":
            for l in range(depth):
                layer(k, l, W[l], x_in, p_in[l], out, last=(l == depth - 1))
        S.finish()
    k.ninst = S.ninst
    return nc, k


def run_rr(gens):
    gens = list(gens)
    while gens:
        for g in list(gens):
            try:
                next(g)
            except StopIteration:
                gens.remove(g)


def interleave(make_gen, items, width):
    items = list(items)
    active = []
    nxt = 0
    while nxt < len(items) or active:
        while len(active) < width and nxt < len(items):
            active.append(make_gen(items[nxt]))
            nxt += 1
        for g in list(active):
            try:
                next(g)
            except StopIteration:
                active.remove(g)


def chain_gens(gens):
    for g in gens:
        for _ in g:
            yield


def dbg_dump(k, name, src_buf, src_ap):
    if name in k.dbg:
        k.S.dma("sp", k.dbg[name], src_ap, rd=[src_buf], wr=[Dep()])


def mm(k, out_ap, lhsT, rhs, start, stop, rd, wr):
    return k.S.I("pe", lambda e: e.matmul(out_ap, lhsT, rhs, start=start, stop=stop), rd=rd, wr=wr)


def tr(k, out_ap, in_ap, ident_ap, rd, wr):
    return k.S.I("pe", lambda e: e.transpose(out_ap, in_ap, ident_ap), rd=rd, wr=wr)


def phase0(k, x_in, hT):
    import os
    lvl = int(os.environ.get("KNOB_P0", "9"))
    nblk = int(os.environ.get("KNOB_P0N", str(NB)))
    S, I = k.S, k.S.I
    with phase(k) as st:
        xin = Ring([k.sb(st, "p0x%d" % i, [128, D], F32) for i in range(2)])
        stg = Ring([k.sb(st, "p0s%d" % i, [128, 8, 128], F32) for i in range(2)])
        pss = Ring([k.ps(st, "p0p%d" % i, [128, 4, 128], F32) for i in range(4)])
        for blk in range(nblk):
            xi = xin.next()
            S.dma("sp", xi[:], x_in[blk * 128:(blk + 1) * 128, :], wr=[xi])
            sg = stg.next()
            if lvl < 1:
                continue
            for half in range(2):
                pt = pss.next()
                for j in range(4):
                    c = half * 4 + j
                    tr(k, pt[:, j, :], xi[:, c * 128:(c + 1) * 128], k.identf[:], rd=[xi, k.identf], wr=[pt])
                if lvl >= 2:
                    I("act", lambda e: e.activation(out=hT[:, half * 4:half * 4 + 4, blk * 128:(blk + 1) * 128],
                                                    in_=pt[:], func=AF.Copy), rd=[pt], wr=[hT])
                if lvl >= 3:
                    I("dve", lambda e: e.tensor_copy(sg[:, half * 4:half * 4 + 4, :], pt[:]), rd=[pt], wr=[sg])
            if lvl >= 4:
                S.dma("pool", k.hres[:, :, blk * 128:(blk + 1) * 128], sg[:], rd=[sg], wr=[k.hres])


def cast_weight(k, st, src, dst_buf, dst_ap, K_, N_, tag, to_dram=False, dst_fn=None):
    S, I = k.S, k.S.I
    kc = K_ // 128
    ncol = max(1, min(N_, 4096 // kc))
    f32r = Ring([k.sb(st, "cw%s_f%d" % (tag, i), [128, kc, ncol], F32) for i in range(2)])
    if to_dram:
        bfr = Ring([k.sb(st, "cw%s_b%d" % (tag, i), [128, kc, ncol], BF16) for i in range(2)])
    srcv = src.rearrange("(c p) n -> p c n", p=128)
    engs = ["pool", "dve"]
    i = 0
    for n0 in range(0, N_, ncol):
        n1 = min(N_, n0 + ncol)
        w = n1 - n0
        fb = f32r.next()
        S.dma("sp", fb[:, :, 0:w], srcv[:, :, n0:n1], wr=[fb])
        if to_dram:
            bb = bfr.next()
            I(engs[i % 2], lambda e: e.tensor_copy(bb[:, :, 0:w], fb[:, :, 0:w]), rd=[fb], wr=[bb])
            dst = dst_fn(n0, n1) if dst_fn is not None else dst_ap[:, :, n0:n1]
            S.dma("pool", dst, bb[:, :, 0:w], rd=[bb], wr=[dst_buf])
        else:
            I(engs[i % 2], lambda e: e.tensor_copy(dst_ap[:, :, n0:n1], fb[:, :, 0:w]), rd=[fb], wr=[dst_buf])
        i += 1


def layer_norm_tile(k, st_bufs, z, gb, eps, outs):
    S, I = k.S, k.S.I
    mps, msb, sq, vps = st_bufs
    for c in range(8):
        mm(k, mps[:], k.onesm[:], z[:, c, :], c == 0, c == 7, rd=[k.onesm, z], wr=[mps])
    I("act", lambda e: e.activation(out=msb[:], in_=mps[:], func=AF.Copy), rd=[mps], wr=[msb])
    I("dve", lambda e: e.tensor_tensor(out=z[:], in0=z[:], in1=msb[:, None, :].broadcast_to([128, 8, 512]),
                                       op=ALU.subtract), rd=[z, msb], wr=[z])
    I("act", lambda e: e.activation(out=sq[:], in_=z[:], func=AF.Square), rd=[z], wr=[sq])
    for c in range(8):
        mm(k, vps[:], k.onesm[:], sq[:, c, :], c == 0, c == 7, rd=[k.onesm, sq], wr=[vps])
    I("act", lambda e: e.activation(out=msb[:], in_=vps[:], func=AF.Ln, bias=k.epsln[:, 0:1], scale=1.0),
      rd=[vps, k.epsln], wr=[msb])
    I("act", lambda e: e.activation(out=msb[:], in_=msb[:], func=AF.Exp, scale=-0.5), rd=[msb], wr=[msb])
    I("dve", lambda e: e.tensor_tensor(out=z[:], in0=z[:], in1=msb[:, None, :].broadcast_to([128, 8, 512]),
                                       op=ALU.mult), rd=[z, msb], wr=[z])
    for c in range(8):
        outs(c, z[:, c, :], gb[:, c, 0:1], gb[:, c, 1:2])


def inproj(k, l, W, hT):
    S, I = k.S, k.S.I
    with phase(k) as st:
        wf = Ring([k.sb(st, "ipwf%d" % i, [128, 8, 128], F32) for i in range(2)])
        wb = Ring([k.sb(st, "ipwb%d" % i, [128, 8, 128], BF16) for i in range(2)])
        pss = Ring([k.ps(st, "ipps%d" % i, [128, 512], F32) for i in range(4)])
        stb = Ring([k.sb(st, "ipsb%d" % i, [128, 512], BF16) for i in range(3)])
        stf = Ring([k.sb(st, "ipsf%d" % i, [128, 512], F32) for i in range(2)])
        raw = Ring([k.sb(st, "ipraw%d" % i, [128, T + 1], F32) for i in range(2)])
        xs = Ring([k.sb(st, "ipxs%d" % i, [128, T], F32) for i in range(1)])
        mu = k.sb(st, "ipmu", [128, 8], F32)
        S.dma("sp", mu[:], W["mu"], wr=[mu])
        for r_ in raw.bufs:
            I("pool", lambda e: e.memset(r_[:, 0:1], 0.0), wr=[r_])
        winF = W["winF"].rearrange("(c p) n -> p c n", p=128)
        ev = 0
        for j in range(50):
            if k.stop_after == "ip_fm%d" % j:
                return
            f_ = wf.next()
            S.dma("sp", f_[:], winF[:, :, j * 128:(j + 1) * 128], wr=[f_])
            b_ = wb.next()
            I("pool", lambda e: e.tensor_copy(b_[:], f_[:]), rd=[f_], wr=[b_])
            if 6 <= j < 14:
                rw = raw.next()
            for tt in range(NT):
                pt = pss.next()
                for c in range(8):
                    mm(k, pt[:], b_[:, c, :], hT[:, c, tt * 512:(tt + 1) * 512], c == 0, c == 7,
                       rd=[b_, hT], wr=[pt])
                eng = "act" if ev % 2 == 0 else "dve"
                ev += 1
                tsl = slice(tt * 512, (tt + 1) * 512)
                if j < 2:
                    sf = stf.next()
                    if eng == "act":
                        I("act", lambda e: e.activation(out=sf[:], in_=pt[:], func=AF.Copy), rd=[pt], wr=[sf])
                    else:
                        I("dve", lambda e: e.tensor_copy(sf[:], pt[:]), rd=[pt], wr=[sf])
                    S.dma("pool", k.uTf[:, j, tsl], sf[:], rd=[sf], wr=[k.uTf])
                elif j < 6 or 14 <= j < 18:
                    which = (j - 2) // 2 if j < 6 else 2 + (j - 14) // 2
                    cc = j % 2
                    scale = 0.125 if which in (0, 2) else 1.0
                    sb_ = stb.next()
                    I("act", lambda e: e.activation(out=sb_[:], in_=pt[:], func=AF.Copy, scale=scale), rd=[pt], wr=[sb_])
                    S.dma("pool", k.qk[which, :, cc, tsl], sb_[:], rd=[sb_], wr=[k.qk])
                elif j < 14:
                    if eng == "act":
                        I("act", lambda e: e.activation(out=rw[:, 1 + tt * 512:1 + (tt + 1) * 512], in_=pt[:], func=AF.Copy),
                          rd=[pt], wr=[rw])
                    else:
                        I("dve", lambda e: e.tensor_copy(rw[:, 1 + tt * 512:1 + (tt + 1) * 512], pt[:]), rd=[pt], wr=[rw])
                else:
                    gc = j - 18
                    sb_ = stb.next()
                    I("act", lambda e: e.activation(out=sb_[:], in_=pt[:], func=AF.Sigmoid), rd=[pt], wr=[sb_])
                    S.dma("pool", k.gsc[gc, :, tsl], sb_[:], rd=[sb_], wr=[k.gsc])
            if 6 <= j < 14:
                jj = j - 6
                x_ = xs.next()
                I("pool", lambda e: e.tensor_tensor(out=x_[:], in0=rw[:, 0:T], in1=rw[:, 1:T + 1], op=ALU.subtract),
                  rd=[rw], wr=[x_])
                I("dve", lambda e: e.scalar_tensor_tensor(out=x_[:], in0=x_[:], scalar=mu[:, jj:jj + 1], in1=rw[:, 1:T + 1],
                                                          op0=ALU.mult, op1=ALU.add), rd=[x_, rw, mu], wr=[x_])
                S.dma("sp", k.rwxs[jj], x_[:], rd=[x_], wr=[k.rwxs])
        if k.stop_after == "ip_fm":
            return
        wtf = k.sb(st, "ipwtf", [128, 8, 512], F32)
        wtb = k.sb(st, "ipwtb", [128, 8, 512], BF16)
        S.dma("sp", wtf[:], W["winT"].rearrange("(c p) n -> p c n", p=128), wr=[wtf])
        I("pool", lambda e: e.tensor_copy(wtb[:], wtf[:]), rd=[wtf], wr=[wtb])
        for blk in range(NB):
            pt = pss.next()
            for c in range(8):
                mm(k, pt[:], hT[:, c, blk * 128:(blk + 1) * 128], wtb[:, c, :], c == 0, c == 7, rd=[wtb, hT], wr=[pt])
            sb_ = stb.next()
            if blk % 2 == 0:
                I("act", lambda e: e.activation(out=sb_[:], in_=pt[:], func=AF.Copy), rd=[pt], wr=[sb_])
            else:
                I("dve", lambda e: e.tensor_copy(sb_[:], pt[:]), rd=[pt], wr=[sb_])
            S.dma("pool", k.vv[:, :, blk, :].rearrange("w p n -> p w n"), sb_[:].rearrange("p (w n) -> p w n", w=2),
                  rd=[sb_], wr=[k.vv])
        if k.stop_after == "ip_tm":
            return
        wff_f = k.sb(st, "ipwff", [128, 8, 4], F32)
        wff_b = k.sb(st, "ipwffb", [128, 8, 4], BF16)
        fb = k.sb(st, "ipfb", [4, 1], F32)
        fl = k.sb(st, "ipfl", [4, T], F32)
        S.dma("sp", wff_f[:], W["wff"].rearrange("(c p) n -> p c n", p=128), wr=[wff_f])
        S.dma("sp", fb[:], W["foxb"], wr=[fb])
        I("pool", lambda e: e.tensor_copy(wff_b[:], wff_f[:]), rd=[wff_f], wr=[wff_b])
        I("dve", lambda e: e.tensor_scalar(out=fb[:], in0=fb[:], scalar1=-1.0, scalar2=None, op0=ALU.mult), rd=[fb], wr=[fb])
        for tt in range(NT):
            pt = pss.next()
            for c in range(8):
                mm(k, pt[0:4, :], wff_b[:, c, :], hT[:, c, tt * 512:(tt + 1) * 512], c == 0, c == 7, rd=[wff_b, hT], wr=[pt])
            I("act", lambda e: e.activation(out=fl[:, tt * 512:(tt + 1) * 512], in_=pt[0:4, :], func=AF.Exp,
                                            bias=fb[:, 0:1], scale=-1.0), rd=[pt, fb], wr=[fl])
        I("act", lambda e: e.activation(out=fl[:], in_=fl[:], func=AF.Ln, bias=k.one1[0:4, 0:1], scale=1.0),
          rd=[fl, k.one1], wr=[fl])
        for tt in range(NT):
            init = 0.0 if tt == 0 else fl[:, tt * 512 - 1:tt * 512]
            I("dve", lambda e: e.tensor_tensor_scan(out=fl[:, tt * 512:(tt + 1) * 512], data0=k.ones[0:4, :],
                                                    data1=fl[:, tt * 512:(tt + 1) * 512], initial=init,
                                                    op0=ALU.mult, op1=ALU.add), rd=[fl, k.ones], wr=[fl])
        S.dma("sp", k.negc[:], fl[:], rd=[fl], wr=[k.negc])


def layer(k, l, W, x_in, p_l, out, last):
    S, I = k.S, k.S.I
    with phase(k) as st:
        hT = k.sb(st, "hT%d" % l, [128, 8, T], BF16)
        if l == 0:
            phase0(k, x_in, hT)
            if k.stop_after == "phase0":
                return
        else:
            S.dma("sp", hT[:], k.hTs[:], rd=[k.hTs], wr=[hT])
        inproj(k, l, W, hT)
    if k.stop_after is not None and k.stop_after.startswith("ip") or k.stop_after == "inproj":
        return
    if k.only in (None, "s5"):
        s5_phase(k, l, W)
    if k.stop_after == "s5":
        return
    if k.only in (None, "fox"):
        attn_phase(k, l, 0)
    if k.only in (None, "sb"):
        attn_phase(k, l, 1)
    if k.stop_after == "attn":
        return
    if k.only in (None, "rwkv"):
        rwkv_prep(k, l, W)
        rwkv_chunked(k, l, W, nsteps=k.nsteps)
    if k.stop_after == "rwkv":
        return
    precast_mlp(k, l, W)
    with phase(k) as st:
        h1T = k.sb(st, "h1T%d" % l, [128, 8, T], BF16)
        merge_phase(k, l, W, h1T)
        if k.stop_after == "merge":
            return
        mlp_phase(k, l, W, h1T, p_l, out, last)


def range_reduce(k, out, in_, shift, qi, qf, rd, wr):
    I = k.S.I
    TWO_PI = 2.0 * PI
    I("dve", lambda e: e.tensor_scalar(out=out, in0=in_, scalar1=float(shift), scalar2=None, op0=ALU.add), rd=rd, wr=wr)
    I("dve", lambda e: e.tensor_scalar(out=qi, in0=out, scalar1=1.0 / TWO_PI, scalar2=0.5, op0=ALU.mult, op1=ALU.add), rd=rd, wr=wr)
    I("dve", lambda e: e.tensor_copy(qf, qi), rd=rd, wr=wr)
    I("dve", lambda e: e.scalar_tensor_tensor(out=out, in0=qf, scalar=-TWO_PI, in1=out, op0=ALU.mult, op1=ALU.add), rd=rd, wr=wr)
    I("dve", lambda e: e.tensor_scalar(out=qf, in0=out, scalar1=-PI, scalar2=None, op0=ALU.is_lt), rd=rd, wr=wr)
    I("dve", lambda e: e.scalar_tensor_tensor(out=out, in0=qf, scalar=TWO_PI, in1=out, op0=ALU.mult, op1=ALU.add), rd=rd, wr=wr)
    I("dve", lambda e: e.tensor_scalar(out=out, in0=out, scalar1=-3.1415925, scalar2=3.1415925, op0=ALU.max, op1=ALU.min), rd=rd, wr=wr)


def s5_phase(k, l, W):
    S, I = k.S, k.S.I
    TWO_PI = 2.0 * PI
    with phase(k) as st:
        sb, ps = k.sb, k.ps
        par = sb(st, "s5par", [128, 8, 3], F32)
        S.dma("sp", par[:], W["s5par"], wr=[par])
        Bb = sb(st, "s5Bb", [128, 8, 2, 128], BF16)
        Cb = sb(st, "s5Cb", [128, 8, 2, 128], BF16)
        gwb = sb(st, "s5gwb", [128, 2, 256], BF16)
        Dc = sb(st, "s5D", [128, 2], F32)
        S.dma("sp", Dc[:], W["s5D"], wr=[Dc])
        with phase(k) as st2:
            Bf = sb(st2, "s5Bf", [128, 8, 2, 128], F32)
            Cf = sb(st2, "s5Cf", [128, 8, 2, 128], F32)
            gwf = sb(st2, "s5gwf", [128, 2, 256], F32)
            S.dma("sp", Bf[:], W["s5B"], wr=[Bf])
            S.dma("sp", Cf[:], W["s5C"], wr=[Cf])
            S.dma("sp", gwf[:], W["gluw"].rearrange("(c p) n -> p c n", p=128), wr=[gwf])
            I("pool", lambda e: e.tensor_copy(Bb[:], Bf[:]), rd=[Bf], wr=[Bb])
            I("pool", lambda e: e.tensor_copy(Cb[:, :, 0, :], Cf[:, :, 0, :]), rd=[Cf], wr=[Cb])
            I("pool", lambda e: e.tensor_scalar(out=Cb[:, :, 1, :], in0=Cf[:, :, 1, :], scalar1=-1.0, scalar2=None, op0=ALU.mult),
              rd=[Cf], wr=[Cb])
            I("pool", lambda e: e.tensor_copy(gwb[:], gwf[:]), rd=[gwf], wr=[gwb])
        gb = sb(st, "s5gb", [128, 2], F32)
        S.dma("sp", gb[:], W["glub"], wr=[gb])
        sm = sb(st, "s5sm", [128, 16, 8], F32)
        def sl(i):
            return sm[:, i, :]
        lre, lim, ldt = par[:, :, 0], par[:, :, 1], par[:, :, 2]
        DT, A_, TH, NA, EM, SN, CS, LBR, LBI, DEN, FRE, FIM, NFRE, T1, T2, THR = range(16)
        dsm = Dep()
        def dv(fn):
            I("dve", fn, rd=[par, dsm], wr=[dsm])
        def ac(fn):
            I("act", fn, rd=[par, dsm], wr=[dsm])
        def pl(fn):
            I("pool", fn, rd=[par, dsm], wr=[dsm])
        ac(lambda e: e.activation(out=sl(DT), in_=ldt, func=AF.Exp))
        dv(lambda e: e.tensor_tensor(out=sl(A_), in0=lre, in1=sl(DT), op=ALU.mult))
        dv(lambda e: e.tensor_tensor(out=sl(TH), in0=lim, in1=sl(DT), op=ALU.mult))
        dv(lambda e: e.tensor_scalar(out=sl(NA), in0=sl(A_), scalar1=-1.0, scalar2=None, op0=ALU.mult))
        smi = sb(st, "s5smi", [128, 8], mybir.dt.int32)
        range_reduce(k, sl(T1), sl(TH), 0.0, smi[:], sl(T2), [par, dsm, smi], [dsm, smi])
        ac(lambda e: e.activation(out=sl(SN), in_=sl(T1), func=AF.Sin))
        range_reduce(k, sl(T1), sl(TH), 0.5 * PI, smi[:], sl(T2), [par, dsm, smi], [dsm, smi])
        ac(lambda e: e.activation(out=sl(CS), in_=sl(T1), func=AF.Sin))
        ac(lambda e: e.activation(out=sl(EM), in_=sl(A_), func=AF.Exp))
        dv(lambda e: e.tensor_tensor(out=sl(LBR), in0=sl(EM), in1=sl(CS), op=ALU.mult))
        dv(lambda e: e.tensor_tensor(out=sl(LBI), in0=sl(EM), in1=sl(SN), op=ALU.mult))
        dv(lambda e: e.tensor_tensor(out=sl(DEN), in0=lre, in1=lre, op=ALU.mult))
        dv(lambda e: e.tensor_tensor(out=sl(T1), in0=lim, in1=lim, op=ALU.mult))
        dv(lambda e: e.tensor_tensor(out=sl(DEN), in0=sl(DEN), in1=sl(T1), op=ALU.add))
        dv(lambda e: e.reciprocal(out=sl(DEN), in_=sl(DEN)))
        dv(lambda e: e.tensor_scalar(out=sl(T2), in0=sl(LBR), scalar1=-1.0, scalar2=None, op0=ALU.add))
        dv(lambda e: e.tensor_tensor(out=sl(FRE), in0=sl(T2), in1=lre, op=ALU.mult))
        dv(lambda e: e.tensor_tensor(out=sl(T1), in0=sl(LBI), in1=lim, op=ALU.mult))
        dv(lambda e: e.tensor_tensor(out=sl(FRE), in0=sl(FRE), in1=sl(T1), op=ALU.add))
        dv(lambda e: e.tensor_tensor(out=sl(FRE), in0=sl(FRE), in1=sl(DEN), op=ALU.mult))
        dv(lambda e: e.tensor_tensor(out=sl(FIM), in0=sl(LBI), in1=lre, op=ALU.mult))
        dv(lambda e: e.tensor_tensor(out=sl(T1), in0=sl(T2), in1=lim, op=ALU.mult))
        dv(lambda e: e.tensor_tensor(out=sl(FIM), in0=sl(FIM), in1=sl(T1), op=ALU.subtract))
        dv(lambda e: e.tensor_tensor(out=sl(FIM), in0=sl(FIM), in1=sl(DEN), op=ALU.mult))
        dv(lambda e: e.tensor_scalar(out=sl(NFRE), in0=sl(FRE), scalar1=-1.0, scalar2=None, op0=ALU.mult))
        range_reduce(k, sl(THR), sl(TH), 0.0, smi[:], sl(T1), [par, dsm, smi], [dsm, smi])
        CH = 256
        idx = sb(st, "s5idx", [128, CH + 1], F32)
        I("dve", lambda e: e.memset(idx[:], 1.0), wr=[idx])
        I("dve", lambda e: e.tensor_tensor_scan(out=idx[:], data0=idx[:], data1=idx[:], initial=-1.0,
                                                op0=ALU.mult, op1=ALU.add), rd=[idx], wr=[idx])
        tabr = sb(st, "s5tabr", [128, 8, CH + 1], F32)
        tabi = sb(st, "s5tabi", [128, 8, CH + 1], F32)
        tifr = sb(st, "s5tifr", [128, 8, CH], F32)
        tifi = sb(st, "s5tifi", [128, 8, CH], F32)
        ntl = sb(st, "s5ntl", [128, 8], F32)
        w1 = sb(st, "s5w1", [128, CH + 1], F32)
        w2 = sb(st, "s5w2", [128, CH + 1], F32)
        w3 = sb(st, "s5w3", [128, CH + 1], F32)
        w4 = sb(st, "s5w4", [128, CH + 1], F32)
        w5 = sb(st, "s5w5", [128, CH + 1], F32)
        wi = sb(st, "s5wi", [128, CH + 1], mybir.dt.int32)
        tabs = [tabr, tabi, tifr, tifi]
        wk = [w1, w2, w3, w4, w5, idx]
        for rt in range(8):
            def col(i):
                return sm[:, i, rt:rt + 1]
            def dvw(fn):
                I("dve", fn, rd=wk + [dsm], wr=wk[:5] + tabs)
            def acw(fn):
                I("act", fn, rd=wk + [dsm], wr=wk[:5] + tabs)
            def plw(fn):
                I("pool", fn, rd=wk + [dsm], wr=wk[:5] + tabs)
            dvw(lambda e: e.tensor_scalar(out=w1[:], in0=idx[:], scalar1=col(THR), scalar2=None, op0=ALU.mult))
            range_reduce(k, w2[:], w1[:], 0.0, wi[:], w5[:], wk + [dsm, wi], wk[:5] + tabs + [wi])
            acw(lambda e: e.activation(out=w2[:], in_=w2[:], func=AF.Sin))
            range_reduce(k, w3[:], w1[:], 0.5 * PI, wi[:], w5[:], wk + [dsm, wi], wk[:5] + tabs + [wi])
            acw(lambda e: e.activation(out=w3[:], in_=w3[:], func=AF.Sin))
            acw(lambda e: e.activation(out=w4[:], in_=idx[:], func=AF.Exp, scale=col(A_)))
            acw(lambda e: e.activation(out=w5[:], in_=idx[:], func=AF.Exp, scale=col(NA)))
            dvw(lambda e: e.tensor_tensor(out=tabr[:, rt, :], in0=w4[:], in1=w3[:], op=ALU.mult))
            dvw(lambda e: e.tensor_tensor(out=tabi[:, rt, :], in0=w4[:], in1=w2[:], op=ALU.mult))
            dvw(lambda e: e.tensor_scalar(out=w1[:], in0=w3[:], scalar1=col(FRE), scalar2=None, op0=ALU.mult))
            dvw(lambda e: e.scalar_tensor_tensor(out=w1[:], in0=w2[:], scalar=col(FIM), in1=w1[:], op0=ALU.mult, op1=ALU.add))
            dvw(lambda e: e.tensor_tensor(out=tifr[:, rt, :], in0=w1[:, 0:CH], in1=w5[:, 0:CH], op=ALU.mult))
            dvw(lambda e: e.tensor_scalar(out=w1[:], in0=w3[:], scalar1=col(FIM), scalar2=None, op0=ALU.mult))
            dvw(lambda e: e.scalar_tensor_tensor(out=w1[:], in0=w2[:], scalar=col(NFRE), in1=w1[:], op0=ALU.mult, op1=ALU.add))
            dvw(lambda e: e.tensor_tensor(out=tifi[:, rt, :], in0=w1[:, 0:CH], in1=w5[:, 0:CH], op=ALU.mult))
        I("dve", lambda e: e.tensor_scalar(out=ntl[:], in0=tabi[:, :, CH], scalar1=-1.0, scalar2=None, op0=ALU.mult),
          rd=tabs, wr=[ntl])
        uf = sb(st, "s5uf", [128, 2, T], F32)
        ub = sb(st, "s5ub", [128, 2, T], BF16)
        S.dma("sp", uf[:], k.uTf[:], rd=[k.uTf], wr=[uf])
        I("pool", lambda e: e.tensor_copy(ub[:], uf[:]), rd=[uf], wr=[ub])
        ygf = sb(st, "s5ygf", [128, 2, T], F32)
        ygb = sb(st, "s5ygb", [128, 2, T], BF16)
        q0 = sb(st, "s5q0", [128, 8, 2], F32)
        q0_init = I("dve", lambda e: e.memset(q0[:], 0.0), wr=[q0])
        rawp = Ring([ps(st, "s5rp%d" % i, [128, 512], F32) for i in range(4)])
        yp = Ring([ps(st, "s5yp%d" % i, [128, 512], F32) for i in range(2)])
        rr = Ring([sb(st, "s5rr%d" % i, [128, 2, 512], F32) for i in range(2)])
        zz = Ring([sb(st, "s5zz%d" % i, [128, 2, 512], F32) for i in range(2)])
        mmr = Ring([sb(st, "s5mm%d" % i, [128, 4, 512], F32) for i in range(2)])
        qq = Ring([sb(st, "s5qq%d" % i, [128, 2, 512], F32) for i in range(2)])
        xb = Ring([sb(st, "s5xb%d" % i, [128, 2, 512], BF16) for i in range(3)])
        ew = Ring([sb(st, "s5ew%d" % i, [128, 3, 512], F32) for i in range(1)])
        tq = sb(st, "s5tq", [128, 2], F32)

        def b4(tab, rt):
            return tab[:, rt:rt + 1, 0:CH].broadcast_to([128, 512 // CH, CH])

        def v4(ap):
            return ap.rearrange("p (a b) -> p a b", a=512 // CH)
        items = [(oc, tt, r4) for oc in range(2) for tt in range(NT) for r4 in range(4)]
        ctx = {}
        ypts = {}

        for rg, n in ((rr, 2), (zz, 2), (mmr, 4), (qq, 2), (xb, 2), (ew, 3)):
            for b_ in rg.bufs:
                b_.sd = [Dep() for _ in range(n)]
        tqd = [Dep(), Dep()]
        q0d = [[Dep(), Dep()] for _ in range(8)]

        def stZ(it):
            oc, tt, r4 = it
            rt = oc * 4 + r4
            tsl = slice(tt * 512, (tt + 1) * 512)
            pr, pi_ = rawp.next(), rawp.next()
            mm(k, pr[:], Bb[:, rt, 0, :], ub[:, oc, tsl], True, True, rd=[Bb, ub], wr=[pr])
            mm(k, pi_[:], Bb[:, rt, 1, :], ub[:, oc, tsl], True, True, rd=[Bb, ub], wr=[pi_])
            r_ = rr.next()
            I("act", lambda e: e.activation(out=r_[:, 0, :], in_=pr[:], func=AF.Copy), rd=[pr], wr=[r_.sd[0]])
            I("act", lambda e: e.activation(out=r_[:, 1, :], in_=pi_[:], func=AF.Copy), rd=[pi_], wr=[r_.sd[1]])
            z_, m_ = zz.next(), mmr.next()
            I("dve", lambda e: e.tensor_tensor(out=v4(m_[:, 0, :]), in0=v4(r_[:, 0, :]), in1=b4(tifr, rt), op=ALU.mult),
              rd=[r_.sd[0]] + tabs, wr=[m_.sd[0]])
            I("dve", lambda e: e.tensor_tensor(out=v4(m_[:, 1, :]), in0=v4(r_[:, 1, :]), in1=b4(tifi, rt), op=ALU.mult),
              rd=[r_.sd[1]] + tabs, wr=[m_.sd[1]])
            I("pool", lambda e: e.tensor_tensor(out=v4(m_[:, 2, :]), in0=v4(r_[:, 0, :]), in1=b4(tifi, rt), op=ALU.mult),
              rd=[r_.sd[0]] + tabs, wr=[m_.sd[2]])
            I("pool", lambda e: e.tensor_tensor(out=v4(m_[:, 3, :]), in0=v4(r_[:, 1, :]), in1=b4(tifr, rt), op=ALU.mult),
              rd=[r_.sd[1]] + tabs, wr=[m_.sd[3]])
            I("dve", lambda e: e.tensor_tensor(out=z_[:, 0, :], in0=m_[:, 0, :], in1=m_[:, 1, :], op=ALU.subtract),
              rd=[m_.sd[0], m_.sd[1]], wr=[z_.sd[0]])
            I("pool", lambda e: e.tensor_tensor(out=z_[:, 1, :], in0=m_[:, 2, :], in1=m_[:, 3, :], op=ALU.add),
              rd=[m_.sd[2], m_.sd[3]], wr=[z_.sd[1]])
            ctx[it] = z_

        def stSC(it):
            oc, tt, r4 = it
            rt = oc * 4 + r4
            z_ = ctx[it]
            q_ = qq.next()
            for sc in range(512 // CH):
                csl = slice(sc * CH, (sc + 1) * CH)
                for ri in range(2):
                    I("dve", lambda e: e.tensor_tensor_scan(out=q_[:, ri, csl], data0=k.ones[:, 0:CH], data1=z_[:, ri, csl],
                                                            initial=q0[:, rt, ri:ri + 1], op0=ALU.mult, op1=ALU.add),
                      rd=[z_.sd[ri], q0d[rt][ri], q0, k.ones], wr=[q_.sd[ri]])
                qe_r = q_[:, 0, sc * CH + CH - 1:sc * CH + CH]
                qe_i = q_[:, 1, sc * CH + CH - 1:sc * CH + CH]
                lr_ = tabr[:, rt, CH:CH + 1]
                li_ = tabi[:, rt, CH:CH + 1]
                I("dve", lambda e: e.tensor_scalar(out=tq[:, 0:1], in0=qe_r, scalar1=lr_, scalar2=None, op0=ALU.mult),
                  rd=[q_.sd[0]] + tabs, wr=[tqd[0]])
                I("dve", lambda e: e.tensor_scalar(out=tq[:, 1:2], in0=qe_i, scalar1=lr_, scalar2=None, op0=ALU.mult),
                  rd=[q_.sd[1]] + tabs, wr=[tqd[1]])
                I("dve", lambda e: e.scalar_tensor_tensor(out=q0[:, rt, 0:1], in0=qe_i, scalar=ntl[:, rt:rt + 1], in1=tq[:, 0:1],
                                                          op0=ALU.mult, op1=ALU.add), rd=[q_.sd[1], ntl, tqd[0]], wr=[q0d[rt][0]])
                I("dve", lambda e: e.scalar_tensor_tensor(out=q0[:, rt, 1:2], in0=qe_r, scalar=li_, in1=tq[:, 1:2],
                                                          op0=ALU.mult, op1=ALU.add), rd=[q_.sd[0], tqd[1]] + tabs, wr=[q0d[rt][1]])
            ctx[it] = q_

        def stX(it):
            oc, tt, r4 = it
            rt = oc * 4 + r4
            tsl = slice(tt * 512, (tt + 1) * 512)
            q_ = ctx.pop(it)
            if r4 == 0:
                ypts[(oc, tt)] = yp.next()
            ypt = ypts[(oc, tt)]
            m2 = mmr.next()
            x_ = xb.next()
            I("dve", lambda e: e.tensor_tensor(out=v4(m2[:, 0, :]), in0=v4(q_[:, 0, :]), in1=b4(tabr, rt), op=ALU.mult),
              rd=[q_.sd[0]] + tabs, wr=[m2.sd[0]])
            I("dve", lambda e: e.tensor_tensor(out=v4(m2[:, 1, :]), in0=v4(q_[:, 1, :]), in1=b4(tabi, rt), op=ALU.mult),
              rd=[q_.sd[1]] + tabs, wr=[m2.sd[1]])
            I("pool", lambda e: e.tensor_tensor(out=v4(m2[:, 2, :]), in0=v4(q_[:, 0, :]), in1=b4(tabi, rt), op=ALU.mult),
              rd=[q_.sd[0]] + tabs, wr=[m2.sd[2]])
            I("pool", lambda e: e.tensor_tensor(out=v4(m2[:, 3, :]), in0=v4(q_[:, 1, :]), in1=b4(tabr, rt), op=ALU.mult),
              rd=[q_.sd[1]] + tabs, wr=[m2.sd[3]])
            I("dve", lambda e: e.tensor_tensor(out=x_[:, 0, :], in0=m2[:, 0, :], in1=m2[:, 1, :], op=ALU.subtract),
              rd=[m2.sd[0], m2.sd[1]], wr=[x_.sd[0]])
            I("pool", lambda e: e.tensor_tensor(out=x_[:, 1, :], in0=m2[:, 2, :], in1=m2[:, 3, :], op=ALU.add),
              rd=[m2.sd[2], m2.sd[3]], wr=[x_.sd[1]])
            mm(k, ypt[:], Cb[:, rt, 0, :], x_[:, 0, :], r4 == 0, False, rd=[Cb, x_.sd[0]], wr=[ypt])
            mm(k, ypt[:], Cb[:, rt, 1, :], x_[:, 1, :], False, r4 == 3, rd=[Cb, x_.sd[1]], wr=[ypt])
            if r4 == 3:
                e_ = ew.next()
                d0, d1, d2 = e_.sd
                I("dve", lambda e: e.scalar_tensor_tensor(out=e_[:, 0, :], in0=uf[:, oc, tsl], scalar=Dc[:, oc:oc + 1], in1=ypt[:],
                                                          op0=ALU.mult, op1=ALU.add), rd=[uf, Dc, ypt], wr=[d0])
                I("pool", lambda e: e.tensor_tensor(out=e_[:, 1, :], in0=e_[:, 0, :], in1=e_[:, 0, :], op=ALU.mult), rd=[d0], wr=[d1])
                I("pool", lambda e: e.tensor_scalar(out=e_[:, 1, :], in0=e_[:, 1, :], scalar1=0.044715, scalar2=1.0,
                                                    op0=ALU.mult, op1=ALU.add), rd=[d1], wr=[d1])
                I("pool", lambda e: e.tensor_tensor(out=e_[:, 1, :], in0=e_[:, 1, :], in1=e_[:, 0, :], op=ALU.mult), rd=[d0, d1], wr=[d1])
                I("act", lambda e: e.activation(out=e_[:, 2, :], in_=e_[:, 1, :], func=AF.Sigmoid, scale=1.5957691216057308),
                  rd=[d1], wr=[d2])
                I("dve", lambda e: e.tensor_tensor(out=ygf[:, oc, tsl], in0=e_[:, 0, :], in1=e_[:, 2, :], op=ALU.mult),
                  rd=[d0, d2], wr=[ygf])
                I("pool", lambda e: e.tensor_copy(ygb[:, oc, tsl], ygf[:, oc, tsl]), rd=[ygf], wr=[ygb])
        stZ(items[0])
        for i, it in enumerate(items):
            if i + 1 < len(items):
                stZ(items[i + 1])
            stSC(it)
            stX(it)

        ob = Ring([sb(st, "s5ob%d" % i, [128, 512], BF16) for i in range(2)])
        for oc in range(2):
            for tt in range(NT):
                tsl = slice(tt * 512, (tt + 1) * 512)
                pt = yp.next()
                for kc in range(2):
                    mm(k, pt[:], gwb[:, kc, oc * 128:(oc + 1) * 128], ygb[:, kc, tsl], kc == 0, kc == 1, rd=[gwb, ygb], wr=[pt])
                e_ = ew.next()
                I("act", lambda e: e.activation(out=e_[:, 0, :], in_=pt[:], func=AF.Sigmoid, bias=gb[:, oc:oc + 1], scale=1.0),
                  rd=[pt, gb], wr=[e_.sd[0]])
                o_ = ob.next()
                I("dve", lambda e: e.tensor_tensor(out=o_[:], in0=e_[:, 0, :], in1=ygf[:, oc, tsl], op=ALU.mult),
                  rd=[e_.sd[0], ygf], wr=[o_])
                S.dma("sp", k.ycat[0, :, oc, tsl], o_[:], rd=[o_], wr=[k.ycat])


def attn_phase(k, l, kind):
    S, I = k.S, k.S.I
    fox = (kind == 0)
    VW = 65 if fox else 64
    with phase(k) as st:
        sb, ps = k.sb, k.ps
        qT = sb(st, "atq", [128, 2, T], BF16)
        kT = sb(st, "atk", [128, 2, T], BF16)
        S.dma("sp", qT[:], k.qk[2 * kind], rd=[k.qk], wr=[qT])
        S.dma("sp", kT[:], k.qk[2 * kind + 1], rd=[k.qk], wr=[kT])
        V = sb(st, "atv", [128, NB, 4, VW], BF16)
        if fox:
            I("pool", lambda e: e.memset(V[:, :, :, 64:65], 1.0), wr=[V])
        S.dma("sp", V[:, :, :, 0:64], k.vv[kind].rearrange("p b (h d) -> p b h d", h=4), rd=[k.vv], wr=[V])
        yT = sb(st, "atyT", [128, 2, T], BF16)
        ytok = sb(st, "atytok", [128, NB, 256], BF16)
        A = Ring([sb(st, "atA%d" % i, [128, T], F32) for i in range(2)])
        P = Ring([sb(st, "atP%d" % i, [128, T], BF16) for i in range(2)])
        PT = Ring([sb(st, "atPT%d" % i, [128, 4, 128], BF16) for i in range(3)])
        zp = Ring([ps(st, "atzp%d" % i, [128, 512], F32) for i in range(3)])
        tp = Ring([ps(st, "attp%d" % i, [128, 4, 128], BF16) for i in range(2)])
        op = Ring([ps(st, "atop%d" % i, [128, VW], F32) for i in range(2)])
        xp = ps(st, "atxp", [128, 512], F32)
        nb = Ring([sb(st, "atnb%d" % i, [128, 2], F32) for i in range(3)])
        if fox:
            negc = sb(st, "atnegc", [4, T], F32)
            S.dma("sp", negc[:], k.negc[:], rd=[k.negc], wr=[negc])
            crow = sb(st, "atcrow", [128, T], F32)
        else:
            spb = Ring([sb(st, "atsp%d" % i, [128, 512], F32) for i in range(3)])
            Fb = Ring([sb(st, "atF%d" % i, [128, 512], F32) for i in range(3)])
            t1b = Ring([sb(st, "att1%d" % i, [128, 512], F32) for i in range(3)])
        items = [(h, qb) for h in range(4) for qb in range(NB)]
        state = {}

        def stage1(h, qb):
            hc, po = h // 2, (h % 2) * 64
            if fox and qb == 0:
                for tt in range(NT):
                    mm(k, xp[:], k.sel4[:, h, :], negc[:, tt * 512:(tt + 1) * 512], True, True, rd=[k.sel4, negc], wr=[xp])
                    I("act", lambda e: e.activation(out=crow[:, tt * 512:(tt + 1) * 512], in_=xp[:], func=AF.Copy),
                      rd=[xp], wr=[crow])
            nk = qb + 1
            a_ = A.next()
            nb_ = nb.next()
            carry = None
            for kt in range((nk + 3) // 4):
                w = min(4, nk - 4 * kt) * 128
                ks = slice(kt * 512, kt * 512 + w)
                last = (kt == (nk + 3) // 4 - 1)
                z = zp.next()
                mm(k, z[:, 0:w], qT[po:po + 64, hc, qb * 128:(qb + 1) * 128], kT[po:po + 64, hc, ks], True, True,
                   rd=[qT, kT], wr=[z])
                if fox:
                    I("dve", lambda e: e.tensor_tensor(out=a_[:, ks], in0=z[:, 0:w], in1=crow[:, ks], op=ALU.add),
                      rd=[z, crow], wr=[a_])
                else:
                    sp_ = spb.next()
                    I("act", lambda e: e.activation(out=sp_[:, 0:w], in_=z[:, 0:w], func=AF.Exp), rd=[z], wr=[sp_])
                    I("act", lambda e: e.activation(out=sp_[:, 0:w], in_=sp_[:, 0:w], func=AF.Ln, bias=k.one1[:, 0:1], scale=1.0),
                      rd=[sp_, k.one1], wr=[sp_])
                    if last:
                        I("pool", lambda e: e.tensor_tensor(out=sp_[:, w - 128:w], in0=sp_[:, w - 128:w], in1=k.m01[:], op=ALU.mult),
                          rd=[sp_, k.m01], wr=[sp_])
                    f_ = Fb.next()
                    init = 0.0 if carry is None else carry[0][:, carry[1] - 1:carry[1]]
                    rdl = [sp_, k.ones] + ([carry[0]] if carry is not None else [])
                    I("dve", lambda e: e.tensor_tensor_scan(out=f_[:, 0:w], data0=k.ones[:, 0:w], data1=sp_[:, 0:w], initial=init,
                                                            op0=ALU.mult, op1=ALU.add), rd=rdl, wr=[f_])
                    carry = (f_, w)
                    t_ = t1b.next()
                    I("dve", lambda e: e.tensor_tensor(out=t_[:, 0:w], in0=z[:, 0:w], in1=sp_[:, 0:w], op=ALU.subtract),
                      rd=[z, sp_], wr=[t_])
                    I("pool", lambda e: e.tensor_tensor(out=a_[:, ks], in0=t_[:, 0:w], in1=f_[:, 0:w], op=ALU.add),
                      rd=[t_, f_], wr=[a_])
                yield
            dsl = slice(qb * 128, (qb + 1) * 128)
            I("pool", lambda e: e.tensor_tensor(out=a_[:, dsl], in0=a_[:, dsl], in1=(k.nmi if fox else k.nms)[:], op=ALU.add),
              rd=[a_, k.nmi, k.nms], wr=[a_])
            if fox:
                I("dve", lambda e: e.reduce_max(out=nb_[:, 0:1], in_=a_[:, 0:nk * 128], axis=AX.X), rd=[a_], wr=[nb_])
                I("dve", lambda e: e.tensor_scalar(out=nb_[:, 1:2], in0=nb_[:, 0:1], scalar1=-1.0, scalar2=None, op0=ALU.mult),
                  rd=[nb_], wr=[nb_])
            else:
                I("dve", lambda e: e.tensor_scalar(out=nb_[:, 1:2], in0=carry[0][:, carry[1] - 1:carry[1]], scalar1=-1.0, scalar2=None,
                                                   op0=ALU.mult), rd=[carry[0]], wr=[nb_])
            state[(h, qb)] = (a_, nb_)
            yield

        def stage2(h, qb):
            a_, nb_ = state.pop((h, qb))
            nk = qb + 1
            p_ = P.next()
            I("act", lambda e: e.activation(out=p_[:, 0:nk * 128], in_=a_[:, 0:nk * 128], func=AF.Exp, bias=nb_[:, 1:2], scale=1.0),
              rd=[a_, nb_], wr=[p_])
            o_ = op.next()
            prev = None

            def pv(kt, pt_, nbk):
                for j in range(nbk):
                    kb = kt * 4 + j
                    mm(k, o_[:], pt_[:, j, :], V[:, kb, h, :], kb == 0, kb == nk - 1, rd=[pt_, V], wr=[o_])
            for kt in range((nk + 3) // 4):
                nbk = min(4, nk - 4 * kt)
                t_ = tp.next()
                for j in range(nbk):
                    kb = kt * 4 + j
                    tr(k, t_[:, j, :], p_[:, kb * 128:(kb + 1) * 128], k.identb[:], rd=[p_, k.identb], wr=[t_])
                pt_ = PT.next()
                if kt % 2 == 0:
                    I("act", lambda e: e.activation(out=pt_[:, 0:nbk, :], in_=t_[:, 0:nbk, :], func=AF.Copy), rd=[t_], wr=[pt_])
                else:
                    I("dve", lambda e: e.tensor_copy(pt_[:, 0:nbk, :], t_[:, 0:nbk, :]), rd=[t_], wr=[pt_])
                if prev is not None:
                    pv(*prev)
                prev = (kt, pt_, nbk)
                yield
            pv(*prev)
            if fox:
                I("dve", lambda e: e.reciprocal(out=nb_[:, 0:1], in_=o_[:, 64:65]), rd=[o_], wr=[nb_])
                I("act", lambda e: e.activation(out=ytok[:, qb, h * 64:(h + 1) * 64], in_=o_[:, 0:64], func=AF.Copy,
                                                scale=nb_[:, 0:1]), rd=[o_, nb_], wr=[ytok])
            else:
                I("act", lambda e: e.activation(out=ytok[:, qb, h * 64:(h + 1) * 64], in_=o_[:, 0:64], func=AF.Copy),
                  rd=[o_], wr=[ytok])

        for i, it in enumerate(items):
            run_rr([stage1(*it)])
            if i > 0:
                run_rr([stage2(*items[i - 1])])
        run_rr([stage2(*items[-1])])
        for qb in range(NB):
            t_ = tp.next()
            for c in range(2):
                tr(k, t_[:, c, :], ytok[:, qb, c * 128:(c + 1) * 128], k.identb[:], rd=[ytok, k.identb], wr=[t_])
            I("act", lambda e: e.activation(out=yT[:, :, qb * 128:(qb + 1) * 128], in_=t_[:, 0:2, :], func=AF.Copy),
              rd=[t_], wr=[yT])
        S.dma("sp", k.ycat[1 if fox else 3], yT[:], rd=[yT], wr=[k.ycat])


def rwkv_prep(k, l, W):
    S, I = k.S, k.S.I
    with phase(k) as st:
        sb, ps = k.sb, k.ps
        rvec = sb(st, "rpvec", [128, 2, 7], F32)
        S.dma("sp", rvec[:], W["rvec"], wr=[rvec])
        nw0 = sb(st, "rpnw0", [128, 2], F32)
        I("dve", lambda e: e.tensor_scalar(out=nw0[:], in0=rvec[:, :, 0], scalar1=-1.0, scalar2=None, op0=ALU.mult),
          rd=[rvec], wr=[nw0])
        wa2b = sb(st, "rpwa2b", [128, 256], BF16)
        g2b = sb(st, "rpg2b", [128, 256], BF16)
        tw = sb(st, "rptw", [128, T], BF16)
        gs = sb(st, "rpgs", [128, T], BF16)
        with phase(k) as st2:
            wa2f = sb(st2, "rpwa2f", [128, 256], F32)
            g2f = sb(st2, "rpg2f", [128, 256], F32)
            x6 = sb(st2, "rpx6", [128, T], F32)
            x7 = sb(st2, "rpx7", [128, T], F32)
            S.dma("sp", wa2f[:], W["wa2"], wr=[wa2f])
            S.dma("sp", g2f[:], W["g2"], wr=[g2f])
            S.dma("sp", x6[:], k.rwxs[6], rd=[k.rwxs], wr=[x6])
            S.dma("sp", x7[:], k.rwxs[7], rd=[k.rwxs], wr=[x7])
            I("pool", lambda e: e.tensor_copy(wa2b[:], wa2f[:]), rd=[wa2f], wr=[wa2b])
            I("pool", lambda e: e.tensor_copy(g2b[:], g2f[:]), rd=[g2f], wr=[g2b])
            I("act", lambda e: e.activation(out=tw[0:64, :], in_=x6[0:64, :], func=AF.Tanh), rd=[x6], wr=[tw])
            I("pool", lambda e: e.tensor_copy(tw[64:128, :], x6[64:128, :]), rd=[x6], wr=[tw])
            I("act", lambda e: e.activation(out=gs[:], in_=x7[:], func=AF.Sigmoid), rd=[x7], wr=[gs])
        for oc in range(2):
            S.dma("sp", k.sc5[oc, :, 4, :], k.rwxs[oc], rd=[k.rwxs], wr=[k.sc5])
        ld = Ring([sb(st, "rpld%d" % i, [128, 3, 512], F32) for i in range(2)])
        o5 = Ring([sb(st, "rpo5%d" % i, [128, 4, 512], F32) for i in range(2)])
        wk = Ring([sb(st, "rpwk%d" % i, [128, 6, 512], F32) for i in range(2)])
        so = Ring([sb(st, "rpso%d" % i, [128, 3, 512], F32) for i in range(2)])
        pp = Ring([ps(st, "rppp%d" % i, [128, 512], F32) for i in range(8)])
        for rg, n in ((ld, 3), (o5, 4), (wk, 6), (so, 3)):
            for b_ in rg.bufs:
                b_.sd = [Dep() for _ in range(n)]
        def tile_gen(it):
            oc, tt = it
            osl = slice(oc * 128, (oc + 1) * 128)
            tsl = slice(tt * 512, (tt + 1) * 512)
            x_ = ld.next()
            for i, j in enumerate((oc, 2 + oc, 4 + oc)):
                S.dma("sp", x_[:, i, :], k.rwxs[j, :, tsl], rd=[k.rwxs], wr=[x_.sd[i]])
            r_t, k_t, v_t = x_[:, 0, :], x_[:, 1, :], x_[:, 2, :]
            o_ = o5.next()
            w_ = wk.next()
            s_ = so.next()
            pw = pp.next()
            mm(k, pw[:], wa2b[0:64, osl], tw[0:64, tsl], True, True, rd=[wa2b, tw], wr=[pw])
            yield
            I("act", lambda e: e.activation(out=w_[:, 0, :], in_=pw[:], func=AF.Exp, bias=nw0[:, oc:oc + 1], scale=-1.0),
              rd=[pw, nw0], wr=[w_.sd[0]])
            I("act", lambda e: e.activation(out=w_[:, 0, :], in_=w_[:, 0, :], func=AF.Ln, bias=k.one1[:, 0:1], scale=1.0),
              rd=[w_.sd[0], k.one1], wr=[w_.sd[0]])
            yield
            I("act", lambda e: e.activation(out=w_[:, 0, :], in_=w_[:, 0, :], func=AF.Exp, bias=k.mhalf[:, 0:1], scale=-1.0),
              rd=[w_.sd[0], k.mhalf], wr=[w_.sd[0]])
            I("pool", lambda e: e.tensor_scalar(out=o_[:, 0, :], in0=w_[:, 0, :], scalar1=-1.0, scalar2=None, op0=ALU.mult),
              rd=[w_.sd[0]], wr=[o_.sd[0]])
            pa = pp.next()
            mm(k, pa[:], wa2b[64:128, osl], tw[64:128, tsl], True, True, rd=[wa2b, tw], wr=[pa])
            yield
            I("act", lambda e: e.activation(out=w_[:, 1, :], in_=pa[:], func=AF.Sigmoid, bias=rvec[:, oc, 1:2], scale=1.0),
              rd=[pa, rvec], wr=[w_.sd[1]])
            a_t = w_[:, 1, :]
            I("dve", lambda e: e.tensor_scalar(out=w_[:, 2, :], in0=k_t, scalar1=rvec[:, oc, 2:3], scalar2=None, op0=ALU.mult),
              rd=[x_.sd[1], rvec], wr=[w_.sd[2]])
            I("pool", lambda e: e.tensor_tensor(out=w_[:, 3, :], in0=w_[:, 2, :], in1=w_[:, 2, :], op=ALU.mult), rd=[w_.sd[2]], wr=[w_.sd[3]])
            yield
            pn = pp.next()
            mm(k, pn[:], k.blk64[:], w_[:, 3, :], True, True, rd=[k.blk64, w_.sd[3]], wr=[pn])
            I("act", lambda e: e.activation(out=w_[:, 3, :], in_=pn[:], func=AF.Sqrt), rd=[pn], wr=[w_.sd[3]])
            I("dve", lambda e: e.tensor_scalar(out=w_[:, 3, :], in0=w_[:, 3, :], scalar1=1e-12, scalar2=None, op0=ALU.max),
              rd=[w_.sd[3]], wr=[w_.sd[3]])
            I("dve", lambda e: e.reciprocal(out=w_[:, 3, :], in_=w_[:, 3, :]), rd=[w_.sd[3]], wr=[w_.sd[3]])
            yield
            I("dve", lambda e: e.tensor_tensor(out=w_[:, 2, :], in0=w_[:, 2, :], in1=w_[:, 3, :], op=ALU.mult), rd=[w_.sd[2], w_.sd[3]], wr=[w_.sd[2]])
            I("pool", lambda e: e.tensor_scalar(out=o_[:, 3, :], in0=w_[:, 2, :], scalar1=-1.0, scalar2=None, op0=ALU.mult),
              rd=[w_.sd[2]], wr=[o_.sd[3]])
            I("pool", lambda e: e.tensor_tensor(out=o_[:, 1, :], in0=w_[:, 2, :], in1=a_t, op=ALU.mult), rd=[w_.sd[2], w_.sd[1]], wr=[o_.sd[1]])
            I("dve", lambda e: e.tensor_scalar(out=w_[:, 4, :], in0=a_t, scalar1=1.0, scalar2=rvec[:, oc, 3:4],
                                               op0=ALU.subtract, op1=ALU.mult), rd=[w_.sd[1], rvec], wr=[w_.sd[4]])
            I("dve", lambda e: e.scalar_tensor_tensor(out=o_[:, 2, :], in0=w_[:, 4, :], scalar=1.0, in1=k_t,
                                                      op0=ALU.add, op1=ALU.mult), rd=[w_.sd[4], x_.sd[1]], wr=[o_.sd[2]])
            S.dma("sp", k.sc5[oc, :, 0:4, tsl], o_[:], rd=[o_.sd[0], o_.sd[1], o_.sd[2], o_.sd[3]], wr=[k.sc5])
            yield
            I("pool", lambda e: e.tensor_tensor(out=w_[:, 5, :], in0=r_t, in1=o_[:, 2, :], op=ALU.mult), rd=[x_.sd[0], o_.sd[2]], wr=[w_.sd[5]])
            I("pool", lambda e: e.tensor_scalar(out=w_[:, 5, :], in0=w_[:, 5, :], scalar1=rvec[:, oc, 4:5], scalar2=None,
                                                op0=ALU.mult), rd=[w_.sd[5], rvec], wr=[w_.sd[5]])
            pb = pp.next()
            mm(k, pb[:], k.blk64[:], w_[:, 5, :], True, True, rd=[k.blk64, w_.sd[5]], wr=[pb])
            I("dve", lambda e: e.tensor_tensor(out=s_[:, 0, :], in0=pb[:], in1=v_t, op=ALU.mult), rd=[pb, x_.sd[2]], wr=[s_.sd[0]])
            yield
            S.dma("sp", k.bon[oc, :, tsl], s_[:, 0, :], rd=[s_.sd[0]], wr=[k.bon])
            pg = pp.next()
            mm(k, pg[:], g2b[:, osl], gs[:, tsl], True, True, rd=[g2b, gs], wr=[pg])
            I("act", lambda e: e.activation(out=s_[:, 1, :], in_=pg[:], func=AF.Copy), rd=[pg], wr=[s_.sd[1]])
            yield
            S.dma("sp", k.gg[oc, :, tsl], s_[:, 1, :], rd=[s_.sd[1]], wr=[k.gg])
            pv = pp.next()
            for b in range(4):
                tr(k, pv[:, b * 128:(b + 1) * 128], x_[:, 2, b * 128:(b + 1) * 128], k.identf[:], rd=[x_.sd[2], k.identf], wr=[pv])
            I("act", lambda e: e.activation(out=s_[:, 2, :], in_=pv[:], func=AF.Copy), rd=[pv], wr=[s_.sd[2]])
            S.dma("sp", k.vtok[tsl, oc, :].rearrange("(b p) c -> p b c", p=128),
                  s_[:, 2, :].rearrange("p (b c) -> p b c", b=4), rd=[s_.sd[2]], wr=[k.vtok])


        interleave(tile_gen, [(oc, tt) for oc in range(2) for tt in range(NT)], 2)


def rwkv_rec(k, l, W, nsteps=T):
    S, I = k.S, k.S.I
    VC = 16
    with phase(k) as st:
        sb, ps = k.sb, k.ps
        rvec = sb(st, "rrvec", [128, 2, 7], F32)
        S.dma("sp", rvec[:], W["rvec"], wr=[rvec])
        Sx = [Ring([sb(st, "rrS%d_%d" % (hp, i), [128, 128], F32) for i in range(3)]) for hp in range(2)]
        S1 = [Ring([sb(st, "rrS1%d_%d" % (hp, i), [128, 128], F32) for i in range(2)]) for hp in range(2)]
        KK = [Ring([sb(st, "rrKK%d_%d" % (hp, i), [128, 128], F32) for i in range(3)]) for hp in range(2)]
        S2 = [Ring([sb(st, "rrS2%d_%d" % (hp, i), [128, 128], F32) for i in range(2)]) for hp in range(2)]
        sap = [Ring([ps(st, "rrsa%d_%d" % (hp, i), [128, 128], F32) for i in range(2)]) for hp in range(2)]
        yp = [ps(st, "rryp%d" % hp, [128, 512], F32) for hp in range(2)]
        c5 = [Ring([sb(st, "rrc5%d_%d" % (hp, i), [128, 5, 512], F32) for i in range(2)]) for hp in range(2)]
        vr = [Ring([sb(st, "rrvr%d_%d" % (hp, i), [128, VC, 128], F32) for i in range(2)]) for hp in range(2)]
        ysb = [sb(st, "rry%d" % hp, [128, T], F32) for hp in range(2)]
        for hp in range(2):
            for b in vr[hp].bufs:
                I("pool", lambda e: e.memset(b[:], 0.0), wr=[b])
            b0 = Sx[hp].bufs[0]
            I("dve", lambda e: e.memset(b0[:], 0.0), wr=[b0])
            if nsteps < T:
                I("pool", lambda e: e.memset(ysb[hp][:], 0.0), wr=[ysb[hp]])
        cur = [Sx[0].next(), Sx[1].next()]
        c5c = [None, None]
        vrc = [None, None]
        for t in range(nsteps):
            tl = t % 512
            for hp in range(2):
                if tl == 0:
                    c_ = c5[hp].next()
                    S.dma("sp", c_[:], k.sc5[hp, :, :, t:t + 512], rd=[k.sc5], wr=[c_])
                    c5c[hp] = c_
                if t % VC == 0:
                    v_ = vr[hp].next()
                    for hh in range(2):
                        S.dma("sp", v_[hh * 64:(hh + 1) * 64, :, hh * 64:(hh + 1) * 64],
                              k.vtok[t:t + VC, hp, hh * 64:(hh + 1) * 64].partition_broadcast(64), rd=[k.vtok], wr=[v_])
                    vrc[hp] = v_
                c_, v_ = c5c[hp], vrc[hp]
                old = cur[hp]
                new = Sx[hp].next()
                kk_ = KK[hp].next()
                I("act", lambda e: e.activation(out=kk_[:], in_=k.blk64[:], func=AF.Copy, scale=c_[:, 3, tl:tl + 1]),
                  rd=[k.blk64, c_], wr=[kk_])
                sa = sap[hp].next()
                mm(k, sa[:], kk_[:], old[:], True, True, rd=[kk_, old], wr=[sa])
                s1 = S1[hp].next()
                s2 = S2[hp].next()
                I("act", lambda e: e.activation(out=s1[:], in_=old[:], func=AF.Copy, scale=c_[:, 0, tl:tl + 1]),
                  rd=[old, c_], wr=[s1])
                I("pool", lambda e: e.tensor_scalar(out=s2[:], in0=v_[:, t % VC, :], scalar1=c_[:, 2, tl:tl + 1], scalar2=None, op0=ALU.mult),
                  rd=[v_, c_], wr=[s2])
                I("pool", lambda e: e.tensor_tensor(out=s1[:], in0=s1[:], in1=s2[:], op=ALU.add), rd=[s1, s2], wr=[s1])
                I("dve", lambda e: e.scalar_tensor_tensor(out=new[:], in0=sa[:], scalar=c_[:, 1, tl:tl + 1], in1=s1[:],
                                                          op0=ALU.mult, op1=ALU.add), rd=[sa, c_, s1], wr=[new])
                mm(k, yp[hp][:, tl:tl + 1], new[:], c_[:, 4, tl:tl + 1], True, True, rd=[new, c_], wr=[yp[hp]])
                cur[hp] = new
                if tl == 511 or t == nsteps - 1:
                    t0 = t - tl
                    ypb = yp[hp]
                    I("act", lambda e: e.activation(out=ysb[hp][:, t0:t0 + tl + 1], in_=ypb[:, 0:tl + 1], func=AF.Copy),
                      rd=[ypb], wr=[ysb[hp]])
        ld = Ring([sb(st, "rrld%d" % i, [128, 2, 512], F32) for i in range(2)])
        wk = Ring([sb(st, "rrwk%d" % i, [128, 3, 512], F32) for i in range(2)])
        ob = Ring([sb(st, "rrob%d" % i, [128, 512], BF16) for i in range(2)])
        for hp in range(2):
            for tt in range(NT):
                tsl = slice(tt * 512, (tt + 1) * 512)
                x_ = ld.next()
                S.dma("sp", x_[:, 0, :], k.bon[hp, :, tsl], rd=[k.bon], wr=[x_])
                S.dma("sp", x_[:, 1, :], k.gg[hp, :, tsl], rd=[k.gg], wr=[x_])
                w_ = wk.next()
                pm = sap[0].next()
                pv_ = sap[1].next()
                pmt = yp[0]
                mm(k, pmt[:], k.blk64[:], ysb[hp][:, tsl], True, True, rd=[k.blk64, ysb[hp]], wr=[pmt])
                I("dve", lambda e: e.scalar_tensor_tensor(out=w_[:, 0, :], in0=pmt[:], scalar=-1.0 / 64.0, in1=ysb[hp][:, tsl],
                                                          op0=ALU.mult, op1=ALU.add), rd=[pmt, ysb[hp]], wr=[w_])
                I("act", lambda e: e.activation(out=w_[:, 1, :], in_=w_[:, 0, :], func=AF.Square), rd=[w_], wr=[w_])
                pvt = yp[1]
                mm(k, pvt[:], k.blk64[:], w_[:, 1, :], True, True, rd=[k.blk64, w_], wr=[pvt])
                I("act", lambda e: e.activation(out=w_[:, 1, :], in_=pvt[:], func=AF.Ln, bias=k.epsgn[:, 0:1], scale=1.0 / 64.0),
                  rd=[pvt, k.epsgn], wr=[w_])
                I("act", lambda e: e.activation(out=w_[:, 1, :], in_=w_[:, 1, :], func=AF.Exp, scale=-0.5), rd=[w_], wr=[w_])
                I("dve", lambda e: e.tensor_tensor(out=w_[:, 0, :], in0=w_[:, 0, :], in1=w_[:, 1, :], op=ALU.mult), rd=[w_], wr=[w_])
                I("dve", lambda e: e.tensor_scalar(out=w_[:, 0, :], in0=w_[:, 0, :], scalar1=rvec[:, hp, 5:6], scalar2=rvec[:, hp, 6:7],
                                                   op0=ALU.mult, op1=ALU.add), rd=[w_, rvec], wr=[w_])
                I("pool", lambda e: e.tensor_tensor(out=w_[:, 0, :], in0=w_[:, 0, :], in1=x_[:, 0, :], op=ALU.add), rd=[w_, x_], wr=[w_])
                o_ = ob.next()
                I("pool", lambda e: e.tensor_tensor(out=o_[:], in0=w_[:, 0, :], in1=x_[:, 1, :], op=ALU.mult), rd=[w_, x_], wr=[o_])
                S.dma("sp", k.ycat[2, :, hp, tsl], o_[:], rd=[o_], wr=[k.ycat])


def rwkv_chunked(k, l, W, nsteps=T):
    S, I = k.S, k.S.I
    C = 64
    TA = 256
    NCH = nsteps // C
    with phase(k) as st:
        sb, ps = k.sb, k.ps
        rvec = sb(st, "rcvec", [128, 2, 7], F32)
        S.dma("sp", rvec[:], W["rvec"], wr=[rvec])
        ysb = [sb(st, "rcy%d" % hp, [128, T], F32) for hp in range(2)]
        if nsteps < T:
            for hp in range(2):
                I("pool", lambda e: e.memset(ysb[hp][:], 0.0), wr=[ysb[hp]])
        _rwkv_chunked_core(k, ysb, nsteps)
        _rwkv_post(k, ysb, rvec)


def _rwkv_chunked_core(k, ysb, nsteps):
    S, I = k.S, k.S.I
    C = 64
    TA = 256
    NCH = nsteps // C
    with phase(k) as st:
        sb, ps = k.sb, k.ps
        msu = sb(st, "rcmsu", [128, 128], F32)
        msl = sb(st, "rcmsl", [128, 128], F32)
        mui = sb(st, "rcmui", [128, 128], F32)
        rmask = sb(st, "rcrmask", [128, TA], F32)
        I("pool", lambda e: e.affine_select(out=msu[:], in_=k.blk64[:], compare_op=ALU.is_gt, fill=0.0, base=0,
                                            pattern=[[1, 128]], channel_multiplier=-1), rd=[k.blk64], wr=[msu])
        I("pool", lambda e: e.affine_select(out=msl[:], in_=k.blk64[:], compare_op=ALU.is_gt, fill=0.0, base=0,
                                            pattern=[[-1, 128]], channel_multiplier=1), rd=[k.blk64], wr=[msl])
        I("pool", lambda e: e.affine_select(out=mui[:], in_=k.blk64[:], compare_op=ALU.is_ge, fill=0.0, base=0,
                                            pattern=[[1, 128]], channel_multiplier=-1), rd=[k.blk64], wr=[mui])
        I("pool", lambda e: e.memset(rmask[:], 1.0), wr=[rmask])
        I("pool", lambda e: e.memset(rmask[:].rearrange("p (c t) -> p c t", t=C)[:, :, 0:1], 0.0), wr=[rmask])
        c5 = [Ring([sb(st, "rcc5%d_%d" % (hp, i), [128, 5, TA], F32) for i in range(2)]) for hp in range(2)]
        wkA = [Ring([sb(st, "rcwa%d_%d" % (hp, i), [128, 4, TA], F32) for i in range(2)]) for hp in range(2)]
        eLr = [Ring([sb(st, "rceL%d_%d" % (hp, i), [128, TA], F32) for i in range(3)]) for hp in range(2)]
        pad = [Ring([sb(st, "rcpad%d_%d" % (hp, i), [128, 4, TA // C, 128], F32) for i in range(3)]) for hp in range(2)]
        for hp in range(2):
            for b in pad[hp].bufs:
                I("pool", lambda e: e.memset(b[:], 0.0), wr=[b])
        NPER = 12
        pers = Ring([sb(st, "rcper%d" % i, [128, 6, 128], F32) for i in range(NPER)])
        tmp = Ring([sb(st, "rctmp%d" % i, [128, 8, 128], F32) for i in range(4)])
        Vp = Ring([sb(st, "rcvp%d" % i, [128, 128], F32) for i in range(8)])
        for b in Vp.bufs:
            I("pool", lambda e: e.memset(b[:], 0.0), wr=[b])
        Sx = [Ring([sb(st, "rcS%d_%d" % (hp, i), [128, 128], F32) for i in range(3)]) for hp in range(2)]
        cw = [Ring([sb(st, "rccw%d_%d" % (hp, i), [128, 3, 128], F32) for i in range(2)]) for hp in range(2)]
        pbB = Ring([ps(st, "rcpb%d" % i, [128, 4, 128], F32) for i in range(4)])
        pbA = [ps(st, "rcpa%d" % hp, [128, 3, 128], F32) for hp in range(2)]
        pbY = [ps(st, "rcpy%d" % hp, [128, 128], F32) for hp in range(2)]
        for hp in range(2):
            for b_ in pad[hp].bufs:
                b_.sd = [Dep() for _ in range(4)]
            for b_ in cw[hp].bufs:
                b_.sd = [Dep() for _ in range(3)]
        for b_ in pers.bufs:
            b_.sd = [Dep() for _ in range(6)]
        for b_ in tmp.bufs:
            b_.sd = [Dep() for _ in range(8)]
        cur = []
        for hp in range(2):
            b0 = Sx[hp].next()
            I("dve", lambda e: e.memset(b0[:], 0.0), wr=[b0])
            cur.append(b0)

        tiles = {}

        def stageA(hp, ti):
            t0 = ti * TA
            c_ = c5[hp].next()
            S.dma("sp", c_[:], k.sc5[hp, :, :, t0:t0 + TA], rd=[k.sc5], wr=[c_])
            w_ = wkA[hp].next()
            e_ = eLr[hp].next()
            p_ = pad[hp].next()
            I("dve", lambda e: e.tensor_tensor_scan(out=w_[:, 0, :], data0=rmask[:], data1=c_[:, 0, :], initial=0.0,
                                                    op0=ALU.mult, op1=ALU.add), rd=[c_, rmask], wr=[w_])
            I("pool", lambda e: e.tensor_tensor(out=w_[:, 1, :], in0=w_[:, 0, :], in1=c_[:, 0, :], op=ALU.subtract), rd=[w_, c_], wr=[w_])
            I("act", lambda e: e.activation(out=e_[:], in_=w_[:, 0, :], func=AF.Exp), rd=[w_], wr=[e_])
            I("act", lambda e: e.activation(out=w_[:, 2, :], in_=w_[:, 0, :], func=AF.Exp, scale=-1.0), rd=[w_], wr=[w_])
            I("act", lambda e: e.activation(out=w_[:, 3, :], in_=w_[:, 1, :], func=AF.Exp), rd=[w_], wr=[w_])

            def v3(ap):
                return ap.rearrange("p (c t) -> p c t", t=C)
            for hh in range(2):
                rs = slice(hh * 64, hh * 64 + 64)
                specs = [(0, c_[rs, 3, :], w_[rs, 3, :]),
                         (1, c_[rs, 4, :], e_[rs, :]),
                         (2, c_[rs, 1, :], w_[rs, 2, :]),
                         (3, c_[rs, 2, :], w_[rs, 2, :])]
                for qi, a_, b_ in specs:
                    eng = "dve" if qi % 2 == 0 else "pool"
                    I(eng, lambda e: e.tensor_tensor(out=p_[rs, qi, :, rs], in0=v3(a_), in1=v3(b_), op=ALU.mult),
                      rd=[c_, w_, e_, p_], wr=[p_.sd[qi]])
            tiles[(hp, ti)] = (c_, e_, p_)

        inst = {}

        def stageB(group):
            ctx = []
            for (ch, hp) in group:
                ti, ci = divmod(ch, TA // C)
                c_, e_, p_ = tiles[(hp, ti)]
                Ap, Rp, Bp, Kp = (p_[:, q, ci, :] for q in range(4))
                pr = pers.next()
                tm = tmp.next()
                pb = pbB.next()
                inst[(ch, hp)] = pr
                ctx.append((ch, hp, p_, Ap, Rp, Bp, Kp, pr, tm, pb))
            for (ch, hp, p_, Ap, Rp, Bp, Kp, pr, tm, pb) in ctx:
                ci = ch % (TA // C)
                mm(k, pb[:, 0:2, :], Bp, p_[:, 0:2, ci, :], True, True, rd=[p_.sd[2], p_.sd[0], p_.sd[1]], wr=[pb])
                mm(k, pb[:, 2:4, :], Kp, p_[:, 0:2, ci, :], True, True, rd=[p_.sd[3], p_.sd[0], p_.sd[1]], wr=[pb])
            for (ch, hp, p_, Ap, Rp, Bp, Kp, pr, tm, pb) in ctx:
                I("dve", lambda e: e.tensor_tensor(out=tm[:, 0, :], in0=pb[:, 0, :], in1=msu[:], op=ALU.mult), rd=[pb, msu], wr=[tm.sd[0]])
                I("dve", lambda e: e.tensor_tensor(out=pr[:, 2, :], in0=pb[:, 1, :], in1=mui[:], op=ALU.mult), rd=[pb, mui], wr=[pr.sd[2]])
                I("dve", lambda e: e.tensor_tensor(out=pr[:, 1, :], in0=pb[:, 2, :], in1=msu[:], op=ALU.mult), rd=[pb, msu], wr=[pr.sd[1]])
                I("dve", lambda e: e.tensor_tensor(out=pr[:, 3, :], in0=pb[:, 3, :], in1=mui[:], op=ALU.mult), rd=[pb, mui], wr=[pr.sd[3]])
                I("pool", lambda e: e.tensor_copy(tm[:, 1, :], k.identf[:]), rd=[k.identf], wr=[tm.sd[1]])
            yield
            for (ch, hp, p_, Ap, Rp, Bp, Kp, pr, tm, pb) in ctx:
                mm(k, pb[:, 0, :], Ap, Bp, True, True, rd=[p_.sd[0], p_.sd[2]], wr=[pb])
                tr(k, pb[:, 1, :], Bp, k.identf[:], rd=[p_.sd[2], k.identf], wr=[pb])
                tr(k, pb[:, 2, :], Kp, k.identf[:], rd=[p_.sd[3], k.identf], wr=[pb])
            for (ch, hp, p_, Ap, Rp, Bp, Kp, pr, tm, pb) in ctx:
                I("dve", lambda e: e.tensor_tensor(out=tm[:, 4, :], in0=pb[:, 0, :], in1=msl[:], op=ALU.mult), rd=[pb, msl], wr=[tm.sd[4]])
                I("act", lambda e: e.activation(out=pr[:, 4:6, :], in_=pb[:, 1:3, :], func=AF.Copy), rd=[pb], wr=[pr.sd[4], pr.sd[5]])
            yield
            for r in range(1, 7):
                so, sn = (0, 2) if r % 2 == 1 else (2, 0)
                qo, qn = (4, 5) if r % 2 == 1 else (5, 4)
                for (ch, hp, p_, Ap, Rp, Bp, Kp, pr, tm, pb) in ctx:
                    if r < 6:
                        mm(k, pb[:, 0:2, :], tm[:, qo, :], tm[:, so:so + 2, :], True, True, rd=[tm.sd[qo], tm.sd[so], tm.sd[so + 1]], wr=[pb])
                        mm(k, pb[:, 2, :], tm[:, so, :], tm[:, qo, :], True, True, rd=[tm.sd[so], tm.sd[qo]], wr=[pb])
                    else:
                        mm(k, pb[:, 1, :], tm[:, qo, :], tm[:, so + 1, :], True, True, rd=[tm.sd[qo], tm.sd[so + 1]], wr=[pb])
                for (ch, hp, p_, Ap, Rp, Bp, Kp, pr, tm, pb) in ctx:
                    if r < 6:
                        I("act", lambda e: e.activation(out=tm[:, sn, :], in_=pb[:, 0, :], func=AF.Copy), rd=[pb], wr=[tm.sd[sn]])
                        I("act", lambda e: e.activation(out=tm[:, qn, :], in_=pb[:, 2, :], func=AF.Copy), rd=[pb], wr=[tm.sd[qn]])
                        I("dve", lambda e: e.tensor_tensor(out=tm[:, sn + 1, :], in0=pb[:, 1, :], in1=tm[:, so + 1, :], op=ALU.add),
                          rd=[pb, tm.sd[so + 1]], wr=[tm.sd[sn + 1]])
                    else:
                        I("dve", lambda e: e.tensor_tensor(out=pr[:, 0, :], in0=pb[:, 1, :], in1=tm[:, so + 1, :], op=ALU.add),
                          rd=[pb, tm.sd[so + 1]], wr=[pr.sd[0]])
                yield

        def stageC(ch, hp):
            ti, ci = divmod(ch, TA // C)
            c_, e_, p_ = tiles[(hp, ti)]
            Ap, Rp = p_[:, 0, ci, :], p_[:, 1, ci, :]
            pr = inst.pop((ch, hp))
            t0 = ch * C
            vp = Vp.next()
            for hh in range(2):
                rs = slice(hh * 64, hh * 64 + 64)
                S.dma("sp", vp[rs, rs], k.vtok[t0:t0 + C, hp, rs], rd=[k.vtok], wr=[vp])
            old = cur[hp]
            new = Sx[hp].next()
            w_ = cw[hp].next()
            pa, py = pbA[hp], pbY[hp]
            elc = e_[:, ci * C + C - 1:ci * C + C]
            mm(k, pa[:, 0, :], Ap, old[:], True, False, rd=[p_.sd[0], old], wr=[pa])
            mm(k, pa[:, 0, :], pr[:, 1, :], vp[:], False, True, rd=[pr.sd[1], vp], wr=[pa])
            I("act", lambda e: e.activation(out=w_[:, 0, :], in_=pa[:, 0, :], func=AF.Copy), rd=[pa], wr=[w_.sd[0]])
            I("pool", lambda e: e.tensor_scalar(out=w_[:, 2, :], in0=old[:], scalar1=elc, scalar2=None, op0=ALU.mult), rd=[old, e_], wr=[w_.sd[2]])
            yield
            mm(k, pa[:, 1, :], pr[:, 0, :], w_[:, 0, :], True, True, rd=[pr.sd[0], w_.sd[0]], wr=[pa])
            I("dve", lambda e: e.tensor_copy(w_[:, 1, :], pa[:, 1, :]), rd=[pa], wr=[w_.sd[1]])
            yield
            mm(k, pa[:, 2, :], pr[:, 4, :], w_[:, 1, :], True, False, rd=[pr.sd[4], w_.sd[1]], wr=[pa])
            mm(k, pa[:, 2, :], pr[:, 5, :], vp[:], False, True, rd=[pr.sd[5], vp], wr=[pa])
            I("dve", lambda e: e.scalar_tensor_tensor(out=new[:], in0=pa[:, 2, :], scalar=elc, in1=w_[:, 2, :],
                                                      op0=ALU.mult, op1=ALU.add), rd=[pa, e_, w_.sd[2]], wr=[new])
            mm(k, py[:], old[:], Rp, True, False, rd=[old, p_.sd[1]], wr=[py])
            mm(k, py[:], w_[:, 1, :], pr[:, 2, :], False, False, rd=[w_.sd[1], pr.sd[2]], wr=[py])
            mm(k, py[:], vp[:], pr[:, 3, :], False, True, rd=[vp, pr.sd[3]], wr=[py])
            for hh in range(2):
                rs = slice(hh * 64, hh * 64 + 64)
                I("act", lambda e: e.activation(out=ysb[hp][rs, t0:t0 + C], in_=py[rs, rs], func=AF.Copy), rd=[py], wr=[ysb[hp]])
            cur[hp] = new
            yield

        GC = 2
        ngroups = NCH // GC
        def group(g):
            return [(g * GC + c, hp) for c in range(GC) for hp in range(2)]
        def ensureA(g):
            for (ch, hp) in group(g):
                ti = ch // (TA // C)
                if (hp, ti) not in tiles:
                    stageA(hp, ti)
        ensureA(0)
        run_rr([stageB(group(0))])
        for g in range(ngroups):
            gens = []
            if g + 1 < ngroups:
                ensureA(g + 1)
                gens.append(stageB(group(g + 1)))
            for hp in range(2):
                gens.append(chain_gens([stageC(ch, hp_) for (ch, hp_) in group(g) if hp_ == hp]))
            run_rr(gens)


def _rwkv_post(k, ysb, rvec):
    S, I = k.S, k.S.I
    with phase(k) as st:
        sb, ps = k.sb, k.ps
        pbB = Ring([ps(st, "rcpq%d" % i, [128, 4, 128], F32) for i in range(4)])
        ld = Ring([sb(st, "rcld%d" % i, [128, 2, 512], F32) for i in range(2)])
        wk = Ring([sb(st, "rcwk%d" % i, [128, 3, 512], F32) for i in range(2)])
        ob = Ring([sb(st, "rcob%d" % i, [128, 512], BF16) for i in range(2)])
        pflat = [Buf(b_.t.rearrange("p a b -> p (a b)"), excl=True) for b_ in pbB.bufs]
        pcount = [0]

        def post_gen(it):
            if True:
                hp, tt = it
                pmt, pvt = pflat[(pcount[0] * 2) % 4], pflat[(pcount[0] * 2 + 1) % 4]
                pcount[0] += 1
                tsl = slice(tt * 512, (tt + 1) * 512)
                x_ = ld.next()
                S.dma("sp", x_[:, 0, :], k.bon[hp, :, tsl], rd=[k.bon], wr=[x_])
                S.dma("sp", x_[:, 1, :], k.gg[hp, :, tsl], rd=[k.gg], wr=[x_])
                w_ = wk.next()
                mm(k, pmt[:], k.blk64[:], ysb[hp][:, tsl], True, True, rd=[k.blk64, ysb[hp]], wr=[pmt])
                I("dve", lambda e: e.scalar_tensor_tensor(out=w_[:, 0, :], in0=pmt[:], scalar=-1.0 / 64.0, in1=ysb[hp][:, tsl],
                                                          op0=ALU.mult, op1=ALU.add), rd=[pmt, ysb[hp]], wr=[w_])
                yield
                I("act", lambda e: e.activation(out=w_[:, 1, :], in_=w_[:, 0, :], func=AF.Square), rd=[w_], wr=[w_])
                mm(k, pvt[:], k.blk64[:], w_[:, 1, :], True, True, rd=[k.blk64, w_], wr=[pvt])
                I("act", lambda e: e.activation(out=w_[:, 1, :], in_=pvt[:], func=AF.Ln, bias=k.epsgn[:, 0:1], scale=1.0 / 64.0),
                  rd=[pvt, k.epsgn], wr=[w_])
                yield
                I("act", lambda e: e.activation(out=w_[:, 1, :], in_=w_[:, 1, :], func=AF.Exp, scale=-0.5), rd=[w_], wr=[w_])
                yield
                I("dve", lambda e: e.tensor_tensor(out=w_[:, 0, :], in0=w_[:, 0, :], in1=w_[:, 1, :], op=ALU.mult), rd=[w_], wr=[w_])
                I("dve", lambda e: e.tensor_scalar(out=w_[:, 0, :], in0=w_[:, 0, :], scalar1=rvec[:, hp, 5:6], scalar2=rvec[:, hp, 6:7],
                                                   op0=ALU.mult, op1=ALU.add), rd=[w_, rvec], wr=[w_])
                I("pool", lambda e: e.tensor_tensor(out=w_[:, 0, :], in0=w_[:, 0, :], in1=x_[:, 0, :], op=ALU.add), rd=[w_, x_], wr=[w_])
                o_ = ob.next()
                I("pool", lambda e: e.tensor_tensor(out=o_[:], in0=w_[:, 0, :], in1=x_[:, 1, :], op=ALU.mult), rd=[w_, x_], wr=[o_])
                S.dma("sp", k.ycat[2, :, hp, tsl], o_[:], rd=[o_], wr=[k.ycat])
                yield
        interleave(post_gen, [(hp, tt) for hp in range(2) for tt in range(NT)], 1)


def precast_mlp(k, l, W):
    with phase(k) as st:
        cast_weight(k, st, W["w1"], k.w1s, k.w1s, 1024, 4096, "w1", to_dram=True)
    with phase(k) as st:
        cast_weight(k, st, W["w2m"], k.w2s, k.w2s, 4096, 1024, "w2", to_dram=True, dst_fn=lambda n0, n1: k.w2s[n0 // 128])


def merge_phase(k, l, W, h1T):
    S, I = k.S, k.S.I
    with phase(k) as st:
        sb, ps = k.sb, k.ps
        wbr = sb(st, "mgwbr", [128, 8, 1024], BF16)
        wout = sb(st, "mgwout", [128, 8, 1024], BF16)
        with phase(k) as st2:
            cast_weight(k, st2, W["wbr"], wbr, wbr, 1024, 1024, "br")
        with phase(k) as st2:
            cast_weight(k, st2, W["wout"], wout, wout, 1024, 1024, "wo")
        ln = sb(st, "mgln", [128, 8, 2], F32)
        S.dma("sp", ln[:], W["ln1"], wr=[ln])
        yt = Ring([sb(st, "mgyt%d" % i, [128, 4, 2, 512], BF16) for i in range(2)])
        gt = Ring([sb(st, "mggt%d" % i, [128, 4, 512], BF16) for i in range(3)])
        acc = Ring([sb(st, "mgacc%d" % i, [128, 2, 512], F32) for i in range(2)])
        mg = Ring([sb(st, "mgmg%d" % i, [128, 8, 512], BF16) for i in range(1)])
        hr = Ring([sb(st, "mghr%d" % i, [128, 8, 512], F32) for i in range(1)])
        z = Ring([sb(st, "mgz%d" % i, [128, 8, 512], F32) for i in range(1)])
        sq = sb(st, "mgsq", [128, 8, 512], F32)
        msb = sb(st, "mgmsb", [128, 512], F32)
        up = Ring([ps(st, "mgup%d" % i, [128, 512], F32) for i in range(4)])
        mps = ps(st, "mgmps", [128, 512], F32)
        vps = ps(st, "mgvps", [128, 512], F32)
        gview = k.gsc.t.rearrange("(n m) p t -> m p n t", n=4)
        for tt in range(NT):
            tsl = slice(tt * 512, (tt + 1) * 512)
            y_ = yt.next()
            for n in range(4):
                S.dma("sp", y_[:, n, :, :], k.ycat[n, :, :, tsl], rd=[k.ycat], wr=[y_])
            h_ = hr.next()
            S.dma("sp", h_[:], k.hres[:, :, tsl], rd=[k.hres], wr=[h_])
            m_ = mg.next()
            for mc in range(8):
                msl = slice(mc * 128, (mc + 1) * 128)
                g_ = gt.next()
                S.dma("sp", g_[:], gview[mc][:, :, tsl], rd=[k.gsc], wr=[g_])
                a_ = acc.next()
                for n in range(4):
                    u_ = up.next()
                    for kc in range(2):
                        mm(k, u_[:], wbr[:, n * 2 + kc, msl], y_[:, n, kc, :], kc == 0, kc == 1, rd=[wbr, y_], wr=[u_])
                    if n == 0:
                        I("dve", lambda e: e.tensor_tensor(out=a_[:, 0, :], in0=u_[:], in1=g_[:, 0, :], op=ALU.mult),
                          rd=[u_, g_], wr=[a_])
                    else:
                        I("dve", lambda e: e.tensor_tensor(out=a_[:, 1, :], in0=u_[:], in1=g_[:, n, :], op=ALU.mult),
                          rd=[u_, g_], wr=[a_])
                        if n < 3:
                            I("pool", lambda e: e.tensor_tensor(out=a_[:, 0, :], in0=a_[:, 0, :], in1=a_[:, 1, :], op=ALU.add),
                              rd=[a_], wr=[a_])
                        else:
                            I("pool", lambda e: e.tensor_tensor(out=m_[:, mc, :], in0=a_[:, 0, :], in1=a_[:, 1, :], op=ALU.add),
                              rd=[a_], wr=[m_])
            z_ = z.next()
            for mc in range(8):
                msl = slice(mc * 128, (mc + 1) * 128)
                u_ = up.next()
                for kc in range(8):
                    mm(k, u_[:], wout[:, kc, msl], m_[:, kc, :], kc == 0, kc == 7, rd=[wout, m_], wr=[u_])
                I("dve", lambda e: e.scalar_tensor_tensor(out=z_[:, mc, :], in0=h_[:, mc, :], scalar=ALPHA, in1=u_[:],
                                                          op0=ALU.mult, op1=ALU.add), rd=[h_, u_], wr=[z_])

            def outs(c, dn, g_ap, b_ap):
                I("pool", lambda e: e.tensor_scalar(out=dn, in0=dn, scalar1=g_ap, scalar2=b_ap, op0=ALU.mult, op1=ALU.add),
                  rd=[z_, ln], wr=[z_])
            layer_norm_tile(k, (mps, msb, sq, vps), z_, ln, LN_EPS, outs)
            I("act", lambda e: e.activation(out=h1T[:, :, tsl], in_=z_[:], func=AF.Copy), rd=[z_], wr=[h1T])
            S.dma("sp", k.h1res[:, :, tsl], z_[:], rd=[z_], wr=[k.h1res])


def mlp_phase(k, l, W, h1T, p_l, out, last):
    S, I = k.S, k.S.I
    with phase(k) as st:
        sb, ps = k.sb, k.ps
        pgw = sb(st, "mlpgw", [128, 8, 1024], BF16)
        plw = sb(st, "mlplw", [128, 2, 1024], BF16)
        with phase(k) as st2:
            cast_weight(k, st2, W["pgw"], pgw, pgw, 1024, 1024, "pg")
        with phase(k) as st2:
            cast_weight(k, st2, W["plew"], plw, plw, 256, 1024, "pl")
        with phase(k) as st2:
            pin = Ring([sb(st2, "mlpin%d" % i, [128, 256], F32) for i in range(2)])
            tps = Ring([ps(st2, "mltps%d" % i, [128, 2, 128], F32) for i in range(2)])
            pT = sb(st2, "mlpT", [128, 2, T], BF16)
            for blk in range(NB):
                pi_ = pin.next()
                S.dma("sp", pi_[:], p_l[blk * 128:(blk + 1) * 128, :], wr=[pi_])
                t_ = tps.next()
                for c in range(2):
                    tr(k, t_[:, c, :], pi_[:, c * 128:(c + 1) * 128], k.identf[:], rd=[pi_, k.identf], wr=[t_])
                I("act", lambda e: e.activation(out=pT[:, :, blk * 128:(blk + 1) * 128], in_=t_[:], func=AF.Copy), rd=[t_], wr=[pT])
            S.dma("sp", k.pTs[:], pT[:], rd=[pT], wr=[k.pTs])
        ln = sb(st, "mlln", [128, 8, 2], F32)
        S.dma("sp", ln[:], W["ln2"], wr=[ln])
        w1r = Ring([sb(st, "mlw1%d" % i, [128, 8, 512], BF16) for i in range(2)])
        w2r = Ring([sb(st, "mlw2%d" % i, [128, 32, 128], BF16) for i in range(2)])
        pTr = Ring([sb(st, "mlpT%d" % i, [128, 2, 512], BF16) for i in range(1)])
        a = sb(st, "mla", [128, 32, 512], BF16)
        rl = Ring([sb(st, "mlrl%d" % i, [128, 512], F32) for i in range(2)])
        hr = sb(st, "mlhr", [128, 8, 512], F32)
        z = sb(st, "mlz", [128, 8, 512], F32)
        msb = sb(st, "mlmsb", [128, 512], F32)
        sg = Ring([sb(st, "mlsg%d" % i, [128, 2, 512], F32) for i in range(1)])
        up = Ring([ps(st, "mlup%d" % i, [128, 512], F32) for i in range(3)])
        gp = Ring([ps(st, "mlgp%d" % i, [128, 512], F32) for i in range(2)])
        mps = ps(st, "mlmps", [128, 512], F32)
        vps = ps(st, "mlvps", [128, 512], F32)
        tpo = ps(st, "mltpo", [128, 4, 128], F32)
        ot = Ring([sb(st, "mlot%d" % i, [128, D], F32) for i in range(1)]) if last else None
        hb = Ring([sb(st, "mlhb%d" % i, [128, 8, 512], BF16) for i in range(1)]) if not last else None
        for tt in range(NT):
            tsl = slice(tt * 512, (tt + 1) * 512)
            S.dma("sp", hr[:], k.h1res[:, :, tsl], rd=[k.h1res], wr=[hr])
            pT = pTr.next()
            S.dma("sp", pT[:], k.pTs[:, :, tsl], rd=[k.pTs], wr=[pT])
            for fg in range(8):
                w1_ = w1r.next()
                S.dma("sp", w1_[:], k.w1s[:, :, fg * 512:(fg + 1) * 512], rd=[k.w1s], wr=[w1_])
                for f8 in range(4):
                    fc = fg * 4 + f8
                    u_ = up.next()
                    for kc in range(8):
                        mm(k, u_[:], w1_[:, kc, f8 * 128:(f8 + 1) * 128], h1T[:, kc, tsl], kc == 0, kc == 7, rd=[w1_, h1T], wr=[u_])
                    r_ = rl.next()
                    I("act", lambda e: e.activation(out=r_[:], in_=u_[:], func=AF.Relu), rd=[u_], wr=[r_])
                    I("pool", lambda e: e.tensor_tensor(out=a[:, fc, :], in0=r_[:], in1=r_[:], op=ALU.mult), rd=[r_], wr=[a])
            for mc in range(8):
                msl = slice(mc * 128, (mc + 1) * 128)
                s_ = sg.next()
                g_ = gp.next()
                for kc in range(8):
                    mm(k, g_[:], pgw[:, kc, msl], h1T[:, kc, tsl], kc == 0, kc == 7, rd=[pgw, h1T], wr=[g_])
                I("act", lambda e: e.activation(out=s_[:, 0, :], in_=g_[:], func=AF.Sigmoid), rd=[g_], wr=[s_])
                g2_ = gp.next()
                for kc in range(2):
                    mm(k, g2_[:], plw[:, kc, msl], pT[:, kc, :], kc == 0, kc == 1, rd=[plw, pT], wr=[g2_])
                I("dve", lambda e: e.tensor_tensor(out=s_[:, 1, :], in0=g2_[:], in1=s_[:, 0, :], op=ALU.mult), rd=[g2_, s_], wr=[s_])
                u_ = up.next()
                w2_ = w2r.next()
                S.dma("sp", w2_[:], k.w2s[mc], rd=[k.w2s], wr=[w2_])
                for fc in range(32):
                    mm(k, u_[:], w2_[:, fc, :], a[:, fc, :], fc == 0, fc == 31, rd=[w2_, a], wr=[u_])
                I("dve", lambda e: e.scalar_tensor_tensor(out=z[:, mc, :], in0=hr[:, mc, :], scalar=ALPHA, in1=u_[:],
                                                          op0=ALU.mult, op1=ALU.add), rd=[hr, u_], wr=[z])
                I("pool", lambda e: e.tensor_tensor(out=z[:, mc, :], in0=z[:, mc, :], in1=s_[:, 1, :], op=ALU.add), rd=[z, s_], wr=[z])

            def outs(c, dn, g_ap, b_ap):
                I("pool", lambda e: e.tensor_scalar(out=dn, in0=dn, scalar1=g_ap, scalar2=b_ap, op0=ALU.mult, op1=ALU.add),
                  rd=[z, ln], wr=[z])
            layer_norm_tile(k, (mps, msb, hr, vps), z, ln, LN_EPS, outs)
            if not last:
                hb_ = hb.next()
                I("act", lambda e: e.activation(out=hb_[:], in_=z[:], func=AF.Copy), rd=[z], wr=[hb_])
                S.dma("sp", k.hTs[:, :, tsl], hb_[:], rd=[hb_], wr=[k.hTs])
                S.dma("sp", k.hres[:, :, tsl], z[:], rd=[z], wr=[k.hres])
            else:
                for b in range(4):
                    o_ = ot.next()
                    for half in range(2):
                        for j in range(4):
                            c = half * 4 + j
                            tr(k, tpo[:, j, :], z[:, c, b * 128:(b + 1) * 128], k.identf[:], rd=[z, k.identf], wr=[tpo])
                        I("act", lambda e: e.activation(out=o_[:, half * 512:(half + 1) * 512],
                                                        in_=tpo[:].rearrange("p a b -> p (a b)"), func=AF.Copy), rd=[tpo], wr=[o_])
                    r0 = tt * 512 + b * 128
                    S.dma("sp", out[r0:r0 + 128, :], o_[:], rd=[o_], wr=[Dep()])


_CACHE = {}


def kernel(**inputs):
    if "nc" not in _CACHE:
        _CACHE["nc"] = build_program(depth=DEPTH)[0]
    nc = _CACHE["nc"]
    shared = {}
    for l in range(DEPTH):
        for n, v in prep_layer(inputs, l).items():
            shared["%s_%d" % (n, l)] = v
    x = np.asarray(inputs["x"], np.float32)
    p = np.asarray(inputs["p"], np.float32)
    in_maps = []
    for b in range(8):
        m = dict(shared)
        m["x"] = np.ascontiguousarray(x[b])
        m["p"] = np.ascontiguousarray(p[:, b])
        in_maps.append(m)
    res = run_bass_kernel_spmd(nc, in_maps, core_ids=list(range(8)))
    return np.stack([np.asarray(r["out"], np.float32) for r in res.results], axis=0)
```
